# Optimizing a Trainium2 kernel written in Bass

```python
import math
import jax, jax.numpy as jnp
from jax import lax
import numpy as np

D_MODEL = 1024
BATCH = 4
SEQ = 8192
DEPTH = 1

CHUNK = 64
MEM_LEN = 256
EPS = 1e-6
NEG = -1e30

FOX_HEAD_DIM = 64
D_FOX = 3 * D_MODEL // 4
FOX_HEADS = D_FOX // FOX_HEAD_DIM
Q_BLOCK = 128

D_S5 = 3 * D_MODEL // 4
S5_GROUP = 16
S5_GROUPS = D_S5 // S5_GROUP
S5_STATE = 64

D_MEM = D_MODEL // 2
MEM_HEADS = 4
MEM_HEAD_DIM = D_MEM // MEM_HEADS

N_BRANCH = 3
IN_SIZES = (D_FOX, D_FOX, D_FOX, FOX_HEADS, D_FOX,
            D_S5, D_S5,
            D_MEM, D_MEM,
            N_BRANCH * D_MODEL)
N_IN = sum(IN_SIZES)

kernel_name = "hybrid_fox_s5_memory_gated_block"


def rms_norm(x, g):
    xf = x.astype(jnp.float32)
    y = xf * lax.rsqrt(jnp.mean(xf * xf, axis=-1, keepdims=True) + EPS)
    return (y * g.astype(jnp.float32)).astype(x.dtype)


def split_cols(z):
    offs = np.cumsum(np.array(IN_SIZES))[:-1].tolist()
    return jnp.split(z, offs, axis=-1)


def forgetting_attention(q, k, v, log_f):
    Bn, L, H, Dh = q.shape
    nb = L // Q_BLOCK
    F = jnp.cumsum(log_f, axis=1).transpose(0, 2, 1)
    kh = k.astype(jnp.float32).transpose(0, 2, 1, 3)
    vh = v.astype(jnp.float32).transpose(0, 2, 1, 3)
    qb = (q.astype(jnp.float32) * (Dh ** -0.5)).reshape(Bn, nb, Q_BLOCK, H, Dh).transpose(1, 0, 3, 2, 4)
    Fq = F.reshape(Bn, H, nb, Q_BLOCK).transpose(2, 0, 1, 3)
    starts = jnp.arange(nb, dtype=jnp.int32) * Q_BLOCK
    key_pos = jnp.arange(L, dtype=jnp.int32)

    def block(args):
        qi, Fi, s0 = args
        s = jnp.einsum('bhqd,bhkd->bhqk', qi, kh)
        s = s + Fi[..., None] - F[:, :, None, :]
        qpos = s0 + jnp.arange(Q_BLOCK, dtype=jnp.int32)
        s = jnp.where(key_pos[None, :] <= qpos[:, None], s, NEG)
        p = jax.nn.softmax(s, axis=-1)
        return jnp.einsum('bhqk,bhkd->bhqd', p, vh)

    out = lax.map(block, (qb, Fq, starts))
    return out.transpose(1, 0, 3, 2, 4).reshape(Bn, L, H * Dh).astype(q.dtype)


def s5_ssm(u, lam_re, lam_im, log_step, b_re, b_im, c_re, c_im, d_skip):
    Bn, L, _ = u.shape
    f32 = jnp.float32
    uf = u.astype(f32).reshape(Bn, L, S5_GROUPS, S5_GROUP)
    step = jnp.exp(log_step.astype(f32))[:, None]
    lr, li = lam_re.astype(f32), lam_im.astype(f32)
    mag = jnp.exp(lr * step)
    ab_re, ab_im = mag * jnp.cos(li * step), mag * jnp.sin(li * step)
    den = lr * lr + li * li
    nr, ni = ab_re - 1.0, ab_im
    f_re = (nr * lr + ni * li) / den
    f_im = (ni * lr - nr * li) / den
    br, bi = b_re.astype(f32), b_im.astype(f32)
    bb_re = f_re[..., None] * br - f_im[..., None] * bi
    bb_im = f_re[..., None] * bi + f_im[..., None] * br
    x_re = jnp.einsum('gph,blgh->blgp', bb_re, uf)
    x_im = jnp.einsum('gph,blgh->blgp', bb_im, uf)
    a_re = jnp.broadcast_to(ab_re[None, None], (1, L, S5_GROUPS, S5_STATE))
    a_im = jnp.broadcast_to(ab_im[None, None], (1, L, S5_GROUPS, S5_STATE))

    def combine(e1, e2):
        a1r, a1i, b1r, b1i = e1
        a2r, a2i, b2r, b2i = e2
        return (a2r * a1r - a2i * a1i,
                a2r * a1i + a2i * a1r,
                a2r * b1r - a2i * b1i + b2r,
                a2r * b1i + a2i * b1r + b2i)

    _, _, h_re, h_im = lax.associative_scan(combine, (a_re, a_im, x_re, x_im), axis=1)
    y = (jnp.einsum('ghp,blgp->blgh', c_re.astype(f32), h_re)
         - jnp.einsum('ghp,blgp->blgh', c_im.astype(f32), h_im))
    y = y + d_skip.astype(f32).reshape(S5_GROUPS, S5_GROUP) * uf
    return y.reshape(Bn, L, D_S5).astype(u.dtype)


def memory_attention(q, mk, mv):
    Bn, L = q.shape[:2]
    s = jnp.einsum('blhd,bmhd->bhlm', q.astype(jnp.float32), mk.astype(jnp.float32)) * (MEM_HEAD_DIM ** -0.5)
    p = jax.nn.softmax(s, axis=-1)
    o = jnp.einsum('bhlm,bmhd->blhd', p, mv.astype(jnp.float32))
    return o.reshape(Bn, L, D_MEM).astype(q.dtype)


def setup_inputs(seed: int = 0) -> dict:
    key = jax.random.key(seed)
    ks = jax.random.split(key, 24)
    f32 = jnp.float32
    nrm = lambda k, shape, scale: jax.random.normal(k, shape, f32) * scale
    x = nrm(ks[0], (BATCH, SEQ, D_MODEL), 1.0)
    mem = nrm(ks[1], (BATCH, MEM_LEN, D_MODEL), 1.0)
    g_norm = 1.0 + nrm(ks[2], (DEPTH, D_MODEL), 0.02)
    g_mem_norm = 1.0 + nrm(ks[3], (DEPTH, D_MODEL), 0.02)
    g_final = 1.0 + nrm(ks[4], (D_MODEL,), 0.02)
    w_in = nrm(ks[5], (DEPTH, D_MODEL, N_IN), D_MODEL ** -0.5)
    b_forget = jax.random.uniform(ks[6], (DEPTH, FOX_HEADS), f32, 1.0, 4.0)
    b_merge = nrm(ks[7], (DEPTH, N_BRANCH * D_MODEL), 0.01)
    w_mem_kv = nrm(ks[8], (DEPTH, D_MODEL, 2 * D_MEM), D_MODEL ** -0.5)
    lam_re = -0.5 + nrm(ks[9], (DEPTH, S5_GROUPS, S5_STATE), 0.01)
    lam_im = (math.pi * jnp.arange(S5_STATE, dtype=f32))[None, None, :] + nrm(ks[10], (DEPTH, S5_GROUPS, S5_STATE), 0.01)
    log_step = jax.random.uniform(ks[11], (DEPTH, S5_GROUPS), f32, math.log(1e-3), math.log(1e-1))
    s5_b_re = nrm(ks[12], (DEPTH, S5_GROUPS, S5_STATE, S5_GROUP), (2 * S5_GROUP) ** -0.5)
    s5_b_im = nrm(ks[13], (DEPTH, S5_GROUPS, S5_STATE, S5_GROUP), (2 * S5_GROUP) ** -0.5)
    s5_c_re = nrm(ks[14], (DEPTH, S5_GROUPS, S5_GROUP, S5_STATE), (2 * S5_STATE) ** -0.5)
    s5_c_im = nrm(ks[15], (DEPTH, S5_GROUPS, S5_GROUP, S5_STATE), (2 * S5_STATE) ** -0.5)
    s5_d = nrm(ks[16], (DEPTH, D_S5), 1.0)
    w_glu = nrm(ks[17], (DEPTH, D_S5, D_S5), D_S5 ** -0.5)
    b_glu = nrm(ks[18], (DEPTH, D_S5), 0.01)
    w_proj_fox = nrm(ks[19], (DEPTH, D_FOX, D_MODEL), D_FOX ** -0.5)
    w_proj_s5 = nrm(ks[20], (DEPTH, D_S5, D_MODEL), D_S5 ** -0.5)
    w_proj_mem = nrm(ks[21], (DEPTH, D_MEM, D_MODEL), D_MEM ** -0.5)
    w_out = nrm(ks[22], (DEPTH, D_MODEL, D_MODEL), D_MODEL ** -0.5)
    return {"x": x, "mem": mem, "g_norm": g_norm, "g_mem_norm": g_mem_norm, "g_final": g_final,
            "w_in": w_in, "b_forget": b_forget, "b_merge": b_merge, "w_mem_kv": w_mem_kv,
            "lam_re": lam_re, "lam_im": lam_im, "log_step": log_step,
            "s5_b_re": s5_b_re, "s5_b_im": s5_b_im, "s5_c_re": s5_c_re, "s5_c_im": s5_c_im,
            "s5_d": s5_d, "w_glu": w_glu, "b_glu": b_glu,
            "w_proj_fox": w_proj_fox, "w_proj_s5": w_proj_s5, "w_proj_mem": w_proj_mem, "w_out": w_out}


def reference(x, mem, g_norm, g_mem_norm, g_final, w_in, b_forget, b_merge, w_mem_kv,
              lam_re, lam_im, log_step, s5_b_re, s5_b_im, s5_c_re, s5_c_im, s5_d,
              w_glu, b_glu, w_proj_fox, w_proj_s5, w_proj_mem, w_out):
    Bn, L, _ = x.shape
    M = mem.shape[1]
    for l in range(DEPTH):
        h = rms_norm(x, g_norm[l])
        z = h @ w_in[l]
        q, k, v, fl, gf, u, gs, qm, gm, gl = split_cols(z)

        log_f = jax.nn.log_sigmoid(fl.astype(jnp.float32) + b_forget[l].astype(jnp.float32))
        y_fox = forgetting_attention(q.reshape(Bn, L, FOX_HEADS, FOX_HEAD_DIM),
                                     k.reshape(Bn, L, FOX_HEADS, FOX_HEAD_DIM),
                                     v.reshape(Bn, L, FOX_HEADS, FOX_HEAD_DIM), log_f)
        y_fox = y_fox * jax.nn.silu(gf)

        y_s5 = s5_ssm(u, lam_re[l], lam_im[l], log_step[l], s5_b_re[l], s5_b_im[l],
                      s5_c_re[l], s5_c_im[l], s5_d[l])
        y_s5 = jax.nn.gelu(y_s5)
        y_s5 = y_s5 * jax.nn.sigmoid(y_s5 @ w_glu[l] + b_glu[l])
        y_s5 = y_s5 * jax.nn.silu(gs)

        kv = rms_norm(mem, g_mem_norm[l]) @ w_mem_kv[l]
        mk, mv = jnp.split(kv, 2, axis=-1)
        y_mem = memory_attention(qm.reshape(Bn, L, MEM_HEADS, MEM_HEAD_DIM),
                                 mk.reshape(Bn, M, MEM_HEADS, MEM_HEAD_DIM),
                                 mv.reshape(Bn, M, MEM_HEADS, MEM_HEAD_DIM))
        y_mem = y_mem * jax.nn.silu(gm)

        gates = jax.nn.sigmoid(gl + b_merge[l]).reshape(Bn, L, N_BRANCH, D_MODEL)
        merged = (gates[:, :, 0] * (y_fox @ w_proj_fox[l])
                  + gates[:, :, 1] * (y_s5 @ w_proj_s5[l])
                  + gates[:, :, 2] * (y_mem @ w_proj_mem[l]))
        x = x + merged @ w_out[l]
    return rms_norm(x, g_final)
```

```python
import math
from contextlib import ExitStack

import ml_dtypes
import numpy as np
import concourse.bass as bass
import concourse.mybir as mybir
from concourse.bass_utils import run_bass_kernel_spmd

F32 = mybir.dt.float32
BF16 = mybir.dt.bfloat16
I32 = mybir.dt.int32
AF = mybir.ActivationFunctionType
ALU = mybir.AluOpType

ENGS = ("pe", "act", "dve", "pool", "sp")
D = 1024
NIN = 8716
TT = 512
NT = 16
LTOK = 8192
OWN0 = 8
C_Q, C_K, C_V, C_FL, C_GF, C_U, C_GS, C_QM, C_GM, C_GL = 0, 768, 1536, 2304, 2316, 3084, 3852, 4620, 5132, 5644
EPS = 1e-6
TWO_PI = 2.0 * math.pi


class Op:
    __slots__ = ("eng", "fn", "deps", "needs_inc", "sigval", "is_dma", "lane", "laneval")

    def __init__(self, eng, fn, is_dma=False):
        self.eng = eng
        self.fn = fn
        self.deps = []
        self.needs_inc = False
        self.sigval = None
        self.is_dma = is_dma
        self.lane = None
        self.laneval = None


class Prog:
    def __init__(self, nc, es, n_lanes=6):
        self.nc = nc
        self.n_lanes = n_lanes
        self.esem = {e: es.enter_context(nc.semaphore("s_" + e)) for e in ENGS}
        self.lsem = {}
        for e in ("sp", "act", "pool"):
            for i in range(n_lanes):
                self.lsem[(e, i)] = es.enter_context(nc.semaphore("l_%s%d" % (e, i)))
        self.ecnt = {e: 0 for e in ENGS}
        self.lane_rr = {e: 0 for e in ENGS}
        self.lane_cnt = {k: 0 for k in self.lsem}
        self.barrier = {}
        self._reset()

    def _reset(self):
        self.ops = {e: [] for e in ENGS}
        self.last_w = {}
        self.readers = {}
        self.lane_last = {}

    def _add(self, op, reads, writes):
        deps = []
        for r in reads:
            w = self.last_w.get(r)
            if w is not None:
                deps.append(w)
        for r in writes:
            w = self.last_w.get(r)
            if w is not None:
                deps.append(w)
            deps.extend(self.readers.get(r, ()))
        seen = set(id(d) for d in op.deps)
        for d in deps:
            if d is op or id(d) in seen:
                continue
            seen.add(id(d))
            op.deps.append(d)
            if not d.is_dma:
                d.needs_inc = True
        for r in reads:
            self.readers.setdefault(r, []).append(op)
        for r in writes:
            self.last_w[r] = op
            self.readers[r] = []
        self.ops[op.eng].append(op)
        return op

    def c(self, eng, fn, reads=(), writes=()):
        return self._add(Op(eng, fn), reads, writes)

    def dma(self, eng, fn, reads=(), writes=()):
        op = Op(eng, fn, is_dma=True)
        lane = (eng, self.lane_rr[eng] % self.n_lanes)
        self.lane_rr[eng] += 1
        prev = self.lane_last.get(lane)
        self.lane_cnt[lane] += 1
        op.lane = lane
        op.laneval = 16 * self.lane_cnt[lane]
        if prev is not None:
            op.deps.append(prev)
        self.lane_last[lane] = op
        return self._add(op, reads, writes)

    def emit(self):
        nc = self.nc
        for e in ENGS:
            last = None
            for op in self.ops[e]:
                if not op.is_dma:
                    last = op
            if last is not None:
                last.needs_inc = True
            for op in self.ops[e]:
                if (not op.is_dma) and op.needs_inc:
                    self.ecnt[e] += 1
                    op.sigval = self.ecnt[e]
        barrier = dict(self.barrier)
        esem, lsem = self.esem, self.lsem

        def tok(d):
            if d.is_dma:
                return lsem[d.lane], d.laneval
            return esem[d.eng], d.sigval

        def run(e, h):
            waited = {}
            for s, v in barrier.values():
                if v > 0:
                    h.wait_ge(s, v)
                    waited[id(s)] = v
            for op in self.ops[e]:
                for d in op.deps:
                    s, v = tok(d)
                    if waited.get(id(s), 0) >= v:
                        continue
                    waited[id(s)] = v
                    h.wait_ge(s, v)
                inst = op.fn(h)
                if op.is_dma:
                    inst.then_inc(lsem[op.lane], 16)
                elif op.needs_inc:
                    inst.then_inc(esem[e], 1)
                import os as _os
                if _os.environ.get('DBGPRINT'):
                    print("OP", e, "dma" if op.is_dma else "c", "lane=%s val=%s" % (op.lane, op.laneval) if op.is_dma else "sig=%s" % op.sigval,
                          "deps=", [((d.lane, d.laneval) if d.is_dma else (d.eng, d.sigval)) for d in op.deps], type(inst).__name__)
            for lane, cnt in self.lane_cnt.items():
                if lane[0] == e and cnt > 0:
                    h.wait_ge(lsem[lane], 16 * cnt)

        with nc.Block() as block:
            @block.tensor
            def _(h):
                run("pe", h)

            @block.scalar
            def _(h):
                run("act", h)

            @block.vector
            def _(h):
                run("dve", h)

            @block.gpsimd
            def _(h):
                run("pool", h)

            @block.sync
            def _(h):
                run("sp", h)

        self.barrier = {}
        for e in ENGS:
            self.barrier[("e", e)] = (esem[e], self.ecnt[e])
        for lane, cnt in self.lane_cnt.items():
            self.barrier[("l", lane)] = (lsem[lane], 16 * cnt)
        self._reset()


def build_nc(dbg=False, upto=9):
    nc = bass.Bass("TRN2", target_bir_lowering=False)

    def din(name, shape, dt=F32):
        return nc.dram_tensor(name, list(shape), dt, kind="ExternalInput").ap()

    def dscr(name, shape, dt):
        return nc.dram_tensor(name, list(shape), dt, kind=("ExternalOutput" if dbg else "Internal")).ap()

    xa = din("xa", [LTOK, D])
    kmrow = din("kmrow", [NT, TT], BF16)
    mem = din("mem", [256, D])
    g_norm = din("g_norm", [1, D])
    g_mem_norm = din("g_mem_norm", [1, D])
    g_final = din("g_final", [1, D])
    w_in = din("w_in", [D, NIN])
    b_forget = din("b_forget", [1, 12])
    b_merge = din("b_merge", [1, 3072])
    w_mem_kv = din("w_mem_kv", [D, 1024])
    lam_re = din("lam_re", [48, 64])
    lam_im = din("lam_im", [48, 64])
    log_step = din("log_step", [1, 48])
    s5_b_re = din("s5_b_re", [48, 64, 16])
    s5_b_im = din("s5_b_im", [48, 64, 16])
    s5_c_re = din("s5_c_re", [48, 16, 64])
    s5_c_im = din("s5_c_im", [48, 16, 64])
    s5_d = din("s5_d", [1, 768])
    w_glu = din("w_glu", [768, 768])
    b_glu = din("b_glu", [1, 768])
    w_proj_fox = din("w_proj_fox", [768, D])
    w_proj_s5 = din("w_proj_s5", [768, D])
    w_proj_mem = din("w_proj_mem", [512, D])
    w_out = din("w_out", [D, D])
    yout = nc.dram_tensor("yout", [4096, D], F32, kind="ExternalOutput").ap()

    WB = dscr("WB", [128, 8, NIN], BF16)
    WKV = dscr("WKV", [128, 8, 1024], BF16)
    WGLU = dscr("WGLU", [128, 6, 768], BF16)
    WPF = dscr("WPF", [128, 6, D], BF16)
    WPS = dscr("WPS", [128, 6, D], BF16)
    WPM = dscr("WPM", [128, 4, D], BF16)
    WOUT = dscr("WOUT", [128, 8, D], BF16)
    BDS = dscr("BDS", [6, 128, 8, 128], BF16)
    WZS = dscr("WZS", [6, 128, 8, 2, 128], BF16)
    WYS = dscr("WYS", [6, 128, 4, 8, 2, 32], BF16)
    KT = dscr("KT", [12, 128, LTOK], BF16)
    QT = dscr("QT", [12, 128, 4096], BF16)
    VS = dscr("VS", [64, 128, 780], BF16)
    M0S = dscr("M0S", [8, 128, 8, TT], F32)
    G0S = dscr("G0S", [8, 128, 8, TT], BF16)
    SGFS = dscr("SGFS", [8, 128, 6, TT], BF16)
    YFS = dscr("YFS", [12, 64, 4096], BF16)

    es = ExitStack()
    with es:
        P = Prog(nc, es)

        def sb(stack, name, shape, dt):
            return stack.enter_context(nc.sbuf_tensor(name, list(shape), dt))

        def ps(stack, name, shape, dt=F32):
            return stack.enter_context(nc.psum_tensor(name, list(shape), dt))

        def act(out, in_, func, r, w, bias=None, scale=None, accum=None):
            kw = {}
            if bias is not None:
                kw["bias"] = bias
            if scale is not None:
                kw["scale"] = scale
            if accum is not None:
                kw["accum_out"] = accum
            return P.c("act", lambda h: h.activation(out=out, in_=in_, func=func, **kw), r, w)

        def tt(eng, out, in0, in1, op, r, w):
            return P.c(eng, lambda h: h.tensor_tensor(out=out, in0=in0, in1=in1, op=op), r, w)

        def ts(eng, out, in0, s1, s2, op0, op1, r, w):
            if op1 is None:
                return P.c(eng, lambda h: h.tensor_scalar(out=out, in0=in0, scalar1=s1, scalar2=None, op0=op0), r, w)
            return P.c(eng, lambda h: h.tensor_scalar(out=out, in0=in0, scalar1=s1, scalar2=s2, op0=op0, op1=op1), r, w)

        def stt(out, in0, scalar, in1, op0, op1, r, w):
            return P.c("dve", lambda h: h.scalar_tensor_tensor(out=out, in0=in0, scalar=scalar, in1=in1, op0=op0, op1=op1), r, w)

        def cp(eng, out, in_, r, w):
            if eng == "act":
                return P.c("act", lambda h: h.activation(out=out, in_=in_, func=AF.Identity), r, w)
            return P.c(eng, lambda h: h.tensor_copy(out=out, in_=in_), r, w)

        def mset(eng, ap, val, w):
            return P.c(eng, lambda h: h.memset(ap, val), (), w)

        def mmg(calls, r, w):
            def fn(h):
                inst = None
                for kw in calls:
                    inst = h.matmul(**kw)
                return inst
            return P.c("pe", fn, r, w)

        def tpg(calls, r, w):
            def fn(h):
                inst = None
                for (o, i, idn) in calls:
                    inst = h.transpose(o, i, idn)
                return inst
            return P.c("pe", fn, r, w)

        uq = [0]

        def U():
            uq[0] += 1
            return "u%d" % uq[0]

        def ld(out, in_, r, w, eng="sp", slow=False):
            if slow:
                return P.dma(eng, lambda h: h.dma_start(out=out, in_=in_, allow_slow_non_contiguous=True), r, w)
            return P.dma(eng, lambda h: h.dma_start(out=out, in_=in_), r, w)

        def rows_pattern(tile_ap, n, lo, hi, w):
            P.c("pool", lambda h: h.memset(tile_ap, 1.0), (), w)
            P.c("pool", lambda h: h.affine_select(out=tile_ap, in_=tile_ap, pattern=[[0, n]], compare_op=ALU.is_ge,
                                                  fill=0.0, base=-lo, channel_multiplier=1), w, w)
            P.c("pool", lambda h: h.affine_select(out=tile_ap, in_=tile_ap, pattern=[[0, n]], compare_op=ALU.is_ge,
                                                  fill=0.0, base=hi, channel_multiplier=-1), w, w)

        G = ExitStack()
        es.enter_context(G)
        ident_f = sb(G, "ident_f", [128, 128], F32)
        ident_b = sb(G, "ident_b", [128, 128], BF16)
        ones_f = sb(G, "ones_f", [128, 512], F32)
        ones_b = sb(G, "ones_b", [128, 128], BF16)
        maskT = sb(G, "maskT", [128, 128], BF16)
        gn = sb(G, "gn", [128, 8], F32)
        gmn = sb(G, "gmn", [128, 8], F32)
        bm = sb(G, "bm", [128, 24], F32)
        bglu = sb(G, "bglu", [128, 6], F32)
        SELK = sb(G, "SELK", [128, 12, 128], BF16)
        SELQ = sb(G, "SELQ", [128, 12, 128], BF16)
        qscale = sb(G, "qscale", [128, 1], F32)
        negb = sb(G, "negb", [128, 1], F32)
        nm1 = sb(G, "nm1", [128, 1], F32)
        nm2 = sb(G, "nm2", [128, 1], F32)
        WFL3 = sb(G, "WFL3", [128, 8, 76], BF16)
        MKT = sb(G, "MKT", [128, 4, 256], BF16)
        MV = sb(G, "MV", [128, 2, 512], BF16)
        A1R = sb(G, "A1R", [128, 24], F32)
        A1I = sb(G, "A1I", [128, 24], F32)
        A8R = sb(G, "A8R", [128, 24], F32)
        A8I = sb(G, "A8I", [128, 24], F32)
        APR = sb(G, "APR", [128, 8, 24], F32)
        API = sb(G, "API", [128, 8, 24], F32)
        Fcar = sb(G, "Fcar", [128, 1], F32)
        TB = sb(G, "TB", [128, 24, 2, 9], F32)
        ssq = sb(G, "ssq", [128, 8], F32)
        rstd = sb(G, "rstd", [128, 8], F32)

        mset("pool", ones_f[:], 1.0, ["ones_f"])
        mset("pool", ident_f[:], 1.0, ["ident_f"])
        P.c("pool", lambda h: h.affine_select(out=ident_f[:], in_=ident_f[:], pattern=[[-1, 128]], compare_op=ALU.is_equal,
                                              fill=0.0, base=0, channel_multiplier=1), ["ident_f"], ["ident_f"])
        cp("dve", ident_b[:], ident_f[:], ["ident_f"], ["ident_b"])
        cp("dve", ones_b[:], ones_f[:, 0:128], ["ones_f"], ["ones_b"])
        mset("pool", maskT[:], 1.0, ["maskT"])
        P.c("pool", lambda h: h.affine_select(out=maskT[:], in_=maskT[:], pattern=[[1, 128]], compare_op=ALU.is_ge,
                                              fill=0.0, base=0, channel_multiplier=-1), ["maskT"], ["maskT"])
        ld(gn[:], g_norm.rearrange("o (kt p) -> p (o kt)", p=128), [], ["gn"], slow=True)
        ld(gmn[:], g_mem_norm.rearrange("o (kt p) -> p (o kt)", p=128), [], ["gmn"], slow=True)
        ld(bm[:], b_merge.rearrange("o (c p) -> p (o c)", p=128), [], ["bm"], slow=True)
        ld(bglu[:], b_glu.rearrange("o (c p) -> p (o c)", p=128), [], ["bglu"], slow=True)
        mset("pool", qscale[:], 1.0, ["qscale"])
        mset("pool", qscale[0:64, :], 0.125, ["qscale"])
        mset("pool", negb[:], 0.0, ["negb"])
        for base in (0, 32, 64):
            ld(negb[base:base + 12, :], b_forget.rearrange("o h -> h o"), [], ["negb"], slow=True)
        ts("dve", negb[:], negb[:], -1.0, None, ALU.mult, None, ["negb"], ["negb"])
        mset("pool", nm1[:], 0.0, ["nm1"])
        mset("pool", nm1[32:64, :], -1.0, ["nm1"])
        mset("pool", nm1[64:96, :], -1.0, ["nm1"])
        mset("pool", nm2[:], 0.0, ["nm2"])
        mset("pool", nm2[64:96, :], -1.0, ["nm2"])
        mset("pool", Fcar[:], 0.0, ["Fcar"])
        mset("pool", TB[:], 0.0, ["TB"])
        mset("pool", SELK[:], 0.0, ["SELK"])
        mset("pool", SELQ[:], 0.0, ["SELQ"])
        for j, base in enumerate((0, 32, 64)):
            ts("dve", SELK[:, :, 96 + j], ident_f[:, base:base + 12], -1.0, None, ALU.mult, None, ["ident_f", "SELK"], ["SELK"])
            cp("dve", SELQ[:, :, 64 + j], ident_f[:, base:base + 12], ["ident_f", "SELQ"], ["SELQ"])
        for h_ in range(12):
            cp("dve", SELK[:, h_, 99:100], ident_f[:, 76:77], ["ident_f", "SELK"], ["SELK"])
            for c_ in (64, 65, 66):
                cp("dve", SELK[:, h_, c_:c_ + 1], ident_f[:, 96:97], ["ident_f", "SELK"], ["SELK"])
            for c_ in (96, 97, 98, 99):
                cp("dve", SELQ[:, h_, c_:c_ + 1], ident_f[:, 96:97], ["ident_f", "SELQ"], ["SELQ"])

        S0 = ExitStack()
        with S0:
            cv = [sb(S0, "cv%d" % i, [128, 8, 512], BF16) for i in range(2)]
            cvn = [0]

            def convert(src, nkt, ncols, dst, wkey=None):
                srcv = src.rearrange("(kt p) c -> p kt c", p=128)
                c0 = 0
                while c0 < ncols:
                    n = min(512, ncols - c0)
                    i = cvn[0] % 2
                    cvn[0] += 1
                    t = cv[i]
                    ld(t[:, 0:nkt, 0:n], srcv[:, :, c0:c0 + n], [], ["cv%d" % i], eng="pool")
                    ld(dst[:, :, c0:c0 + n], t[:, 0:nkt, 0:n], ["cv%d" % i], [wkey] if wkey else [U()], eng="sp")
                    c0 += n

            convert(w_in, 8, NIN, WB)
            convert(w_mem_kv, 8, 1024, WKV, "WKV")
            convert(w_glu, 6, 768, WGLU)
            convert(w_proj_fox, 6, D, WPF)
            convert(w_proj_s5, 6, D, WPS)
            convert(w_proj_mem, 4, D, WPM)
            convert(w_out, 8, D, WOUT)
            mset("pool", WFL3[:], 0.0, ["WFL3"])
            for base in (0, 32, 64):
                ld(WFL3[:, :, base:base + 12], w_in.rearrange("(kt p) c -> p kt c", p=128)[:, :, C_FL:C_FL + 12],
                   ["WFL3"], ["WFL3"], eng="pool")

            def s5t(name, shape, dt=F32):
                return sb(S0, name, shape, dt)
            LR = s5t("LR", [128, 24]); LI = s5t("LI", [128, 24]); LS = s5t("LS", [128, 24])
            ld(LR[:], lam_re.rearrange("(pr g2) p -> (g2 p) pr", g2=2), [], ["LR"], slow=True)
            ld(LI[:], lam_im.rearrange("(pr g2) p -> (g2 p) pr", g2=2), [], ["LI"], slow=True)
            for g2 in range(2):
                ld(LS[g2 * 64:(g2 + 1) * 64, :],
                   log_step.rearrange("o (pr g2) -> o g2 pr", g2=2)[:, g2, :].to_broadcast([64, 24]), [], ["LS"], slow=True)
            STP = s5t("STP", [128, 24]); MAG = s5t("MAG", [128, 24]); ANG = s5t("ANG", [128, 24])
            T0 = s5t("T0", [128, 24]); T1 = s5t("T1", [128, 24]); T2 = s5t("T2", [128, 24]); TI = s5t("TI", [128, 24], I32)
            ABR = s5t("ABR", [128, 24]); ABI = s5t("ABI", [128, 24]); SN = s5t("SN", [128, 24]); CS = s5t("CS", [128, 24])
            FR = s5t("FR", [128, 24]); FI = s5t("FI", [128, 24])
            act(STP[:], LS[:], AF.Exp, ["LS"], ["STP"])
            tt("dve", T0[:], LR[:], STP[:], ALU.mult, ["LR", "STP"], ["T0"])
            act(MAG[:], T0[:], AF.Exp, ["T0"], ["MAG"])
            tt("dve", ANG[:], LI[:], STP[:], ALU.mult, ["LI", "STP"], ["ANG"])

            def sin_of(dst, shift, key):
                ts("dve", T0[:], ANG[:], shift, None, ALU.add, None, ["ANG"], ["T0"])
                ts("dve", T1[:], T0[:], 1.0 / TWO_PI, 0.5, ALU.mult, ALU.add, ["T0"], ["T1"])
                cp("dve", TI[:], T1[:], ["T1"], ["TI"])
                cp("dve", T1[:], TI[:], ["TI"], ["T1"])
                stt(T2[:], T1[:], -TWO_PI, T0[:], ALU.mult, ALU.add, ["T1", "T0"], ["T2"])
                ts("dve", T1[:], T2[:], math.pi, -TWO_PI, ALU.is_gt, ALU.mult, ["T2"], ["T1"])
                tt("dve", T2[:], T2[:], T1[:], ALU.add, ["T2", "T1"], ["T2"])
                ts("dve", T1[:], T2[:], -math.pi, TWO_PI, ALU.is_lt, ALU.mult, ["T2"], ["T1"])
                tt("dve", T2[:], T2[:], T1[:], ALU.add, ["T2", "T1"], ["T2"])
                ts("dve", T2[:], T2[:], math.pi, -math.pi, ALU.min, ALU.max, ["T2"], ["T2"])
                act(dst[:], T2[:], AF.Sin, ["T2"], [key])
            sin_of(SN, 0.0, "SN")
            sin_of(CS, math.pi / 2.0, "CS")
            tt("dve", ABR[:], MAG[:], CS[:], ALU.mult, ["MAG", "CS"], ["ABR"])
            tt("dve", ABI[:], MAG[:], SN[:], ALU.mult, ["MAG", "SN"], ["ABI"])
            DEN = s5t("DEN", [128, 24]); NR = s5t("NR", [128, 24])
            tt("dve", T0[:], LR[:], LR[:], ALU.mult, ["LR"], ["T0"])
            tt("dve", T1[:], LI[:], LI[:], ALU.mult, ["LI"], ["T1"])
            tt("dve", DEN[:], T0[:], T1[:], ALU.add, ["T0", "T1"], ["DEN"])
            P.c("dve", lambda h: h.reciprocal(out=DEN[:], in_=DEN[:]), ["DEN"], ["DEN"])
            ts("dve", NR[:], ABR[:], -1.0, None, ALU.add, None, ["ABR"], ["NR"])
            tt("dve", T0[:], NR[:], LR[:], ALU.mult, ["NR", "LR"], ["T0"])
            tt("dve", T1[:], ABI[:], LI[:], ALU.mult, ["ABI", "LI"], ["T1"])
            tt("dve", T0[:], T0[:], T1[:], ALU.add, ["T0", "T1"], ["T0"])
            tt("dve", FR[:], T0[:], DEN[:], ALU.mult, ["T0", "DEN"], ["FR"])
            tt("dve", T0[:], ABI[:], LR[:], ALU.mult, ["ABI", "LR"], ["T0"])
            tt("dve", T1[:], NR[:], LI[:], ALU.mult, ["NR", "LI"], ["T1"])
            tt("dve", T0[:], T0[:], T1[:], ALU.subtract, ["T0", "T1"], ["T0"])
            tt("dve", FI[:], T0[:], DEN[:], ALU.mult, ["T0", "DEN"], ["FI"])
            PWR = s5t("PWR", [128, 9, 24]); PWI = s5t("PWI", [128, 9, 24])
            AWR = s5t("AWR", [128, 9, 24]); AWI = s5t("AWI", [128, 9, 24])

            def cmul(orr, oi, ar, ai, br, bi, rk, wk):
                tt("dve", T0[:], ar, br, ALU.mult, rk, ["T0"])
                tt("dve", T1[:], ai, bi, ALU.mult, rk, ["T1"])
                tt("dve", T2[:], ar, bi, ALU.mult, rk, ["T2"])
                tt("dve", orr, T0[:], T1[:], ALU.subtract, ["T0", "T1"], wk)
                tt("dve", T0[:], ai, br, ALU.mult, rk + wk, ["T0"])
                tt("dve", oi, T2[:], T0[:], ALU.add, ["T2", "T0"], wk)
            mset("pool", PWR[:, 0, :], 1.0, ["PW"])
            mset("pool", PWI[:, 0, :], 0.0, ["PW"])
            for k in range(1, 9):
                cmul(PWR[:, k, :], PWI[:, k, :], PWR[:, k - 1, :], PWI[:, k - 1, :], ABR[:], ABI[:], ["PW", "ABR", "ABI"], ["PW"])
            mset("pool", AWR[:, 0, :], 1.0, ["AW"])
            mset("pool", AWI[:, 0, :], 0.0, ["AW"])
            for k in range(1, 9):
                cmul(AWR[:, k, :], AWI[:, k, :], AWR[:, k - 1, :], AWI[:, k - 1, :], PWR[:, 8, :], PWI[:, 8, :], ["AW", "PW"], ["AW"])
            cp("dve", A1R[:], AWR[:, 1, :], ["AW"], ["A1"])
            cp("dve", A1I[:], AWI[:, 1, :], ["AW"], ["A1"])
            cp("dve", APR[:], AWR[:, 1:9, :], ["AW"], ["APR"])
            cp("dve", API[:], AWI[:, 1:9, :], ["AW"], ["API"])
            cp("dve", A8R[:], AWR[:, 8, :], ["AW"], ["A8"])
            cp("dve", A8I[:], AWI[:, 8, :], ["AW"], ["A8"])

            BRE = s5t("BRE", [128, 24, 16]); BIM = s5t("BIM", [128, 24, 16])
            ld(BRE[:], s5_b_re.rearrange("(pr g2) p h -> (g2 p) pr h", g2=2), [], ["BRE"])
            ld(BIM[:], s5_b_im.rearrange("(pr g2) p h -> (g2 p) pr h", g2=2), [], ["BIM"])
            BBR = s5t("BBR", [128, 24, 16]); BBI = s5t("BBI", [128, 24, 16])
            U0 = s5t("U0", [128, 24, 16]); U1 = s5t("U1", [128, 24, 16])
            frb = FR[:].unsqueeze(2).to_broadcast([128, 24, 16])
            fib = FI[:].unsqueeze(2).to_broadcast([128, 24, 16])
            tt("dve", U0[:], BRE[:], frb, ALU.mult, ["BRE", "FR"], ["U0"])
            tt("dve", U1[:], BIM[:], fib, ALU.mult, ["BIM", "FI"], ["U1"])
            tt("dve", BBR[:], U0[:], U1[:], ALU.subtract, ["U0", "U1"], ["BBR"])
            tt("dve", U0[:], BIM[:], frb, ALU.mult, ["BIM", "FR"], ["U0"])
            tt("dve", U1[:], BRE[:], fib, ALU.mult, ["BRE", "FI"], ["U1"])
            tt("dve", BBI[:], U0[:], U1[:], ALU.add, ["U0", "U1"], ["BBI"])
            BDR = s5t("BDR", [128, 24, 32]); BDI = s5t("BDI", [128, 24, 32])
            mset("pool", BDR[:], 0.0, ["BDR"]); mset("pool", BDI[:], 0.0, ["BDI"])
            for g2 in range(2):
                sl = slice(g2 * 64, (g2 + 1) * 64)
                cp("dve", BDR[sl, :, g2 * 16:(g2 + 1) * 16], BBR[sl, :, :], ["BBR", "BDR"], ["BDR"])
                cp("dve", BDI[sl, :, g2 * 16:(g2 + 1) * 16], BBI[sl, :, :], ["BBI", "BDI"], ["BDI"])
            BWR = s5t("BWR", [128, 24, 128]); BWI = s5t("BWI", [128, 24, 128])
            mset("pool", BWR[:], 0.0, ["BWR"]); mset("pool", BWI[:], 0.0, ["BWI"])
            for q4 in range(4):
                for o in range(6):
                    pr = 4 * o + q4
                    cp("dve", BWR[:, pr, 32 * q4:32 * q4 + 32], BDR[:, pr, :], ["BDR", "BWR"], ["BWR"])
                    cp("pool", BWI[:, pr, 32 * q4:32 * q4 + 32], BDI[:, pr, :], ["BDI", "BWI"], ["BWI"])
            CTR = s5t("CTR", [128, 24, 16]); CTI = s5t("CTI", [128, 24, 16])
            CN = s5t("CN", [128, 3, 2, 128])
            tp_ps = ps(S0, "tp_ps", [128, 512], F32)
            for ri, (csrc, cdst, key) in enumerate(((s5_c_re, CTR, "CTR"), (s5_c_im, CTI, "CTI"))):
                cv4 = csrc.rearrange("(pr g2) h p -> pr h g2 p", g2=2)
                for pr in range(24):
                    ld(CN[(pr % 8) * 16:(pr % 8) * 16 + 16, pr // 8, ri, :].rearrange("h (a p) -> h a p", a=2),
                       cv4[pr], ["CN%d" % ri], ["CN%d" % ri])
                for j in range(3):
                    tpg([(tp_ps[:, 0:128], CN[:, j, ri, :], ident_f[:])], ["CN%d" % ri, "ident_f", "tp_ps"], ["tp_ps"])
                    cp("act", cdst[:, 8 * j:8 * j + 8, :], tp_ps[:, 0:128].rearrange("p (a h) -> p a h", a=8), ["tp_ps"], [key])
            XR = s5t("XR", [128, 24, 32]); XI = s5t("XI", [128, 24, 32])
            V0 = s5t("V0", [128, 24, 32]); V1 = s5t("V1", [128, 24, 32])
            WZs = [s5t("WZs%d" % i, [128, 6, 2, 128], BF16) for i in range(2)]
            for s in range(8):
                k = 7 - s
                wzs, wzk = WZs[s % 2], "WZs%d" % (s % 2)
                pr_ = PWR[:, k, :].unsqueeze(2).to_broadcast([128, 24, 32])
                pi_ = PWI[:, k, :].unsqueeze(2).to_broadcast([128, 24, 32])
                tt("dve", V0[:], BDR[:], pr_, ALU.mult, ["BDR", "PW"], ["V0"])
                tt("dve", V1[:], BDI[:], pi_, ALU.mult, ["BDI", "PW"], ["V1"])
                tt("dve", XR[:], V0[:], V1[:], ALU.subtract, ["V0", "V1"], ["XR"])
                tt("dve", V0[:], BDI[:], pr_, ALU.mult, ["BDI", "PW"], ["V0"])
                tt("dve", V1[:], BDR[:], pi_, ALU.mult, ["BDR", "PW"], ["V1"])
                tt("dve", XI[:], V0[:], V1[:], ALU.add, ["V0", "V1"], ["XI"])
                for o in range(6):
                    for ri, X in enumerate((XR, XI)):
                        tpg([(tp_ps[:, 0:128], X[:, 4 * o:4 * o + 4, :].rearrange("p a b -> p (a b)"), ident_f[:])], ["XR", "XI", "ident_f", "tp_ps"], ["tp_ps"])
                        cp("act", wzs[:, o, ri, :], tp_ps[:, 0:128], ["tp_ps"], [wzk])
                for o in range(6):
                    ld(WZS[o, :, s, :, :], wzs[:, o, :, :], [wzk], [U()])
            QR = s5t("QR", [128, 24, 16]); QI = s5t("QI", [128, 24, 16])
            QBR = s5t("QBR", [128, 24, 32]); QBI = s5t("QBI", [128, 24, 32])
            WYk = [s5t("WYk%d" % i, [128, 24, 2, 32], BF16) for i in range(2)]
            BDj = [s5t("BDj%d" % i, [128, 6, 128], F32) for i in range(2)]
            BDjb = [s5t("BDjb%d" % i, [128, 6, 128], BF16) for i in range(2)]
            DSK = s5t("DSK", [128, 6], F32)
            ld(DSK[:], s5_d.rearrange("o (c p) -> p (o c)", p=128), [], ["DSK"], slow=True)
            mset("pool", QBR[:], 0.0, ["QBR"]); mset("pool", QBI[:], 0.0, ["QBI"])
            for i in range(2):
                mset("pool", BDj[i][:], 0.0, ["BDj%d" % i])
            bd_ps = ps(S0, "bd_ps", [128, 6, 32], F32)
            for k in range(9):
                pr_ = PWR[:, k, :].unsqueeze(2).to_broadcast([128, 24, 16])
                pi_ = PWI[:, k, :].unsqueeze(2).to_broadcast([128, 24, 16])
                tt("dve", U0[:], CTR[:], pr_, ALU.mult, ["CTR", "PW"], ["U0"])
                tt("dve", U1[:], CTI[:], pi_, ALU.mult, ["CTI", "PW"], ["U1"])
                tt("dve", QR[:], U0[:], U1[:], ALU.subtract, ["U0", "U1"], ["QR"])
                tt("dve", U0[:], CTR[:], pi_, ALU.mult, ["CTR", "PW"], ["U0"])
                tt("dve", U1[:], CTI[:], pr_, ALU.mult, ["CTI", "PW"], ["U1"])
                tt("dve", QI[:], U0[:], U1[:], ALU.add, ["U0", "U1"], ["QI"])
                for g2 in range(2):
                    sl = slice(g2 * 64, (g2 + 1) * 64)
                    cp("dve", QBR[sl, :, g2 * 16:(g2 + 1) * 16], QR[sl, :, :], ["QR", "QBR"], ["QBR"])
                    ts("dve", QBI[sl, :, g2 * 16:(g2 + 1) * 16], QI[sl, :, :], -1.0, None, ALU.mult, None, ["QI", "QBI"], ["QBI"])
                if k >= 1:
                    wyk, wykk = WYk[k % 2], "WYk%d" % (k % 2)
                    cp("act", wyk[:, :, 0, :], QBR[:], ["QBR"], [wykk])
                    cp("act", wyk[:, :, 1, :], QBI[:], ["QBI"], [wykk])
                    for o in range(6):
                        ld(WYS[o, :, :, k - 1, :, :], wyk[:, 4 * o:4 * o + 4, :, :], [wykk], [U()])
                if k <= 7:
                    bdj, bdk = BDj[k % 2], "BDj%d" % (k % 2)
                    bdjb, bdbk = BDjb[k % 2], "BDjb%d" % (k % 2)
                    for o in range(6):
                        calls = []
                        for q4 in range(4):
                            pr = 4 * o + q4
                            calls.append(dict(out=bd_ps[:, o, :], lhsT=BWR[:, pr, :], rhs=QBR[:, pr, :], start=(q4 == 0), stop=False))
                            calls.append(dict(out=bd_ps[:, o, :], lhsT=BWI[:, pr, :], rhs=QBI[:, pr, :], start=False, stop=(q4 == 3)))
                        mmg(calls, ["BWR", "BWI", "QBR", "QBI", "bd_ps"], ["bd_ps"])
                    for q4 in range(4):
                        sl = slice(32 * q4, 32 * q4 + 32)
                        cp("act", bdj[sl, :, 32 * q4:32 * q4 + 32], bd_ps[sl, :, :], ["bd_ps"], [bdk])
                    if k == 0:
                        for o in range(6):
                            stt(bdj[:, o, :], ident_f[:], DSK[:, o:o + 1], bdj[:, o, :], ALU.mult, ALU.add, ["ident_f", "DSK", bdk], [bdk])
                    cp("dve", bdjb[:], bdj[:], [bdk], [bdbk])
                    for o in range(6):
                        ld(BDS[o, :, k, :], bdjb[:, o, :], [bdbk], [U()])
            P.emit()
        if upto < 1:
            return nc
        S0b = ExitStack()
        with S0b:
            S0 = S0b

            def s5t(name, shape, dt=F32):
                return sb(S0b, name, shape, dt)
            mt_x = s5t("mt_x", [128, 2, D], F32)
            mt_n = s5t("mt_n", [128, 2, D], BF16)
            mt_j = s5t("mt_j", [128, D], BF16)
            MHT = s5t("MHT", [128, 8, 256], BF16)
            wkv_t = s5t("wkv_t", [128, 8, 1024], BF16)
            tpb_ps = ps(S0, "tpb_ps", [128, 512], BF16)
            mk_ps = ps(S0, "mk_ps", [128, 512], F32)
            ld(mt_x[:], mem.rearrange("(a p) d -> p a d", p=128), [], ["mt_x"])
            ld(wkv_t[:], WKV, ["WKV"], ["wkv_t"])
            for a in range(2):
                act(mt_j[:], mt_x[:, a, :], AF.Square, ["mt_x"], ["mt_j", "ssq"], accum=ssq[:, a:a + 1])
            ts("dve", rstd[:, 0:2], ssq[:, 0:2], 1.0 / D, EPS, ALU.mult, ALU.add, ["ssq"], ["rstd"])
            act(rstd[:, 0:2], rstd[:, 0:2], AF.Sqrt, ["rstd"], ["rstd"])
            P.c("dve", lambda h: h.reciprocal(out=rstd[:, 0:2], in_=rstd[:, 0:2]), ["rstd"], ["rstd"])
            for a in range(2):
                ts("dve", mt_n[:, a, :], mt_x[:, a, :], rstd[:, a:a + 1], None, ALU.mult, None, ["mt_x", "rstd"], ["mt_n"])
            for kt in range(8):
                tpg([(tpb_ps[:, a * 128:(a + 1) * 128], mt_n[:, a, kt * 128:(kt + 1) * 128], ident_b[:]) for a in range(2)],
                    ["mt_n", "ident_b", "tpb_ps"], ["tpb_ps"])
                ts("dve", MHT[:, kt, :], tpb_ps[:, 0:256], gmn[:, kt:kt + 1], None, ALU.mult, None, ["tpb_ps", "gmn"], ["MHT"])
            for hd in range(4):
                mmg([dict(out=mk_ps[:, 0:256], lhsT=wkv_t[:, kt, hd * 128:(hd + 1) * 128], rhs=MHT[:, kt, :],
                          start=(kt == 0), stop=(kt == 7)) for kt in range(8)], ["wkv_t", "MHT", "mk_ps"], ["mk_ps"])
                cp("act", MKT[:, hd, :], mk_ps[:, 0:256], ["mk_ps"], ["MKT"])
            for mt in range(2):
                mmg([dict(out=mk_ps[:, :], lhsT=MHT[:, kt, mt * 128:(mt + 1) * 128], rhs=wkv_t[:, kt, 512:1024],
                          start=(kt == 0), stop=(kt == 7)) for kt in range(8)], ["wkv_t", "MHT", "mk_ps"], ["mk_ps"])
                cp("act", MV[:, mt, :], mk_ps[:, :], ["mk_ps"], ["MV"])
            P.emit()

        if upto < 2:
            return nc
        S1 = ExitStack()
        with S1:
            xt = sb(S1, "xt", [128, 4, D], F32)
            xn = sb(S1, "xn", [128, 4, D], BF16)
            sqj = sb(S1, "sqj", [128, D], BF16)
            hT = sb(S1, "hT", [128, 8, TT], BF16)
            NWS = 3
            wsl = [sb(S1, "ws%d" % i, [128, 4096], BF16) for i in range(NWS)]
            SPt = [sb(S1, "SPt%d" % i, [128, TT], BF16) for i in range(2)]
            HIb = sb(S1, "HIb", [128, TT], BF16)
            MIDb = sb(S1, "MIDb", [128, TT], BF16)
            kst = [sb(S1, "kst%d" % i, [128, TT], BF16) for i in range(2)]
            Vst = sb(S1, "Vst", [128, 4, 12, 65], BF16)
            uT = sb(S1, "uT", [128, 6, TT], BF16)
            s5w = [(sb(S1, "s5bd%d" % i, [128, 8, 128], BF16), sb(S1, "s5wz%d" % i, [128, 8, 2, 128], BF16),
                    sb(S1, "s5wy%d" % i, [128, 4, 8, 2, 32], BF16)) for i in range(2)]
            Z = sb(S1, "Z", [128, 24, 2, 64], F32)
            Hb = sb(S1, "Hb", [128, 24, 2, 65], BF16)
            RT = [sb(S1, "RT%d" % i, [128, 24, 8], F32) for i in range(4)]
            RS = [sb(S1, "RS%d" % i, [128, 24], F32) for i in range(4)]
            tA = sb(S1, "tA", [128, TT], F32)
            tB = sb(S1, "tB", [128, TT], F32)
            tC = sb(S1, "tC", [128, TT], F32)
            Fe, Ff, R1 = tA, tB, tC
            YG = sb(S1, "YG", [128, 6, TT], BF16)
            SGS = sb(S1, "SGS", [128, 6, TT], BF16)
            YS = SGS
            QM = sb(S1, "QM", [128, 4, TT], BF16)
            SGM = sb(S1, "SGM", [128, 4, TT], BF16)
            PmT = sb(S1, "PmT", [128, 2, TT], BF16)
            YM = SGM
            M0 = [sb(S1, "M0_%d" % i, [128, TT], F32) for i in range(2)]
            G0st = [sb(S1, "G0st%d" % i, [128, TT], BF16) for i in range(2)]
            pj = [ps(S1, "pj%d" % i, [128, 512], F32) for i in range(2)]
            tpall = ps(S1, "tpall", [128, 1024], BF16)
            tpp = [tpall[:, 0:512], tpall[:, 0:512]]
            zpsb = [ps(S1, "zps%d" % i, [128, 512], F32) for i in range(4)]
            yps = ps(S1, "yps", [128, 512], F32)

            import os
            if not os.environ.get('SKIPM'):
                for i in range(2):
                    mset("pool", SPt[i][:], 0.0, ["SPt%d" % i])
                    mset("pool", SPt[i][96:97, :], 1.0, ["SPt%d" % i])
                mset("pool", Vst[:], 1.0, ["Vst"])

            wn = [0]
            pjn = [0]

            def wload(src3, nkt, ncols):
                i = wn[0] % NWS
                wn[0] += 1
                t = wsl[i][:, 0:nkt * ncols].rearrange("p (k c) -> p k c", k=nkt)
                ld(t, src3, [], ["ws%d" % i])
                return t, "ws%d" % i

            def nextpj():
                i = pjn[0] % 2
                pjn[0] += 1
                return pj[i], "pj%d" % i

            import os
            P1T = int(os.environ.get('P1T', NT)); P1S = int(os.environ.get('P1S', 99)); OWNX = int(os.environ.get('OWNX', OWN0))
            for it in range(P1T):
                own = it >= OWNX
                ot = it - OWNX
                sp_i = it % 2
                spt, spk = SPt[sp_i], "SPt%d" % sp_i
                ld(xt[:], xa[it * TT:(it + 1) * TT, :].rearrange("(a p) d -> p a d", p=128), [], ["xt"])
                if not os.environ.get('SKIPK'):
                    ld(spt[76:77, :], kmrow[it:it + 1, :], [], [spk], eng="pool")
                if P1S < -1:
                    continue
                for a in range(4):
                    act(sqj[:], xt[:, a, :], AF.Square, ["xt"], ["sqj", "ssq"], accum=ssq[:, a:a + 1])
                ts("dve", rstd[:, 0:4], ssq[:, 0:4], 1.0 / D, EPS, ALU.mult, ALU.add, ["ssq"], ["rstd"])
                act(rstd[:, 0:4], rstd[:, 0:4], AF.Sqrt, ["rstd"], ["rstd"])
                P.c("dve", lambda h: h.reciprocal(out=rstd[:, 0:4], in_=rstd[:, 0:4]), ["rstd"], ["rstd"])
                if os.environ.get('NOXN'):
                    continue
                XNV = int(os.environ.get('XNV', 0))
                for a in range(1 if XNV == 1 else 4):
                    if XNV == 2:
                        ts("dve", sqj[:], xt[:, a, :], rstd[:, a:a + 1], None, ALU.mult, None, ["xt", "rstd"], ["xn"])
                    elif XNV == 3:
                        ts("dve", xn[:, a, :], xt[:, a, :], 2.0, None, ALU.mult, None, ["xt", "rstd"], ["xn"])
                    elif XNV == 4:
                        act(xn[:, a, :], xt[:, a, :], AF.Identity, ["xt", "rstd"], ["xn"], scale=rstd[:, a:a + 1])
                    else:
                        ts("dve", xn[:, a, :], xt[:, a, :], rstd[:, a:a + 1], None, ALU.mult, None, ["xt", "rstd"], ["xn"])
                if P1S < 0:
                    continue
                for kt in range(8):
                    tp, tk = tpp[kt % 2], "tpp0"
                    tpg([(tp[:, a * 128:(a + 1) * 128], xn[:, a, kt * 128:(kt + 1) * 128], ident_b[:]) for a in range(4)],
                        ["xn", "ident_b", tk], [tk])
                    if kt % 2 == 0:
                        ts("dve", hT[:, kt, :], tp[:, :], gn[:, kt:kt + 1], None, ALU.mult, None, [tk, "gn"], ["hT"])
                    else:
                        act(hT[:, kt, :], tp[:, :], AF.Identity, [tk, "gn"], ["hT"], scale=gn[:, kt:kt + 1])
                if P1S < 2:
                    continue
                pjt, pk = nextpj()
                mmg([dict(out=pjt[0:76, :], lhsT=WFL3[:, kt, :], rhs=hT[:, kt, :], start=(kt == 0), stop=(kt == 7)) for kt in range(8)],
                    ["WFL3", "hT", pk], [pk])
                act(Fe[0:76, :], pjt[0:76, :], AF.Exp, [pk, "negb"], ["tA"], bias=negb[0:76, :], scale=-1.0)
                act(Fe[0:76, :], Fe[0:76, :], AF.Ln, ["tA"], ["tA"], bias=1.0)
                P.c("dve", lambda h: h.tensor_tensor_scan(out=Ff[0:76, :], data0=ones_f[0:76, :], data1=Fe[0:76, :],
                                                          initial=Fcar[0:76, :], op0=ALU.mult, op1=ALU.subtract),
                    ["ones_f", "tA", "Fcar"], ["tB"])
                cp("dve", Fcar[0:76, :], Ff[0:76, TT - 1:TT], ["tB"], ["Fcar"])
                cp("dve", HIb[0:76, :], Ff[0:76, :], ["tB"], ["HIb"])
                stt(R1[0:76, :], HIb[0:76, :], nm1[0:76, :], Ff[0:76, :], ALU.mult, ALU.add, ["HIb", "nm1", "tB"], ["tC"])
                tt("dve", MIDb[0:76, :], Ff[0:76, :], HIb[0:76, :], ALU.subtract, ["tB", "HIb"], ["MIDb"])
                stt(R1[0:76, :], MIDb[0:76, :], nm2[0:76, :], R1[0:76, :], ALU.mult, ALU.add, ["MIDb", "nm2", "tC"], ["tC"])
                cp("dve", spt[0:76, :], R1[0:76, :], ["tC"], [spk])
                if P1S < 3:
                    continue
                wk, wkk = None, None
                for h_ in range(12):
                    if h_ % 8 == 0:
                        n = min(512, 768 - h_ * 64)
                        wk, wkk = wload(WB[:, :, C_K + h_ * 64:C_K + h_ * 64 + n], 8, n)
                    pjt, pk = nextpj()
                    c0 = (h_ % 8) * 64
                    calls = [dict(out=pjt[:, :], lhsT=SELK[:, h_, :], rhs=spt[:, :], start=True, stop=False)]
                    calls += [dict(out=pjt[0:64, :], lhsT=wk[:, kt, c0:c0 + 64], rhs=hT[:, kt, :], start=False, stop=(kt == 7)) for kt in range(8)]
                    mmg(calls, ["SELK", spk, wkk, "hT", pk], [pk])
                    ks, kk = kst[h_ % 2], "kst%d" % (h_ % 2)
                    if h_ % 2 == 0:
                        cp("act", ks[:], pjt[:, :], [pk], [kk])
                        ld(KT[h_, :, it * TT:(it + 1) * TT], ks[:], [kk], [U()], eng="act")
                    else:
                        cp("dve", ks[:], pjt[:, :], [pk], [kk])
                        ld(KT[h_, :, it * TT:(it + 1) * TT], ks[:], [kk], [U()], eng="act")
                if P1S < 4:
                    continue
                if own:
                    for h_ in range(12):
                        if h_ % 8 == 0:
                            n = min(512, 768 - h_ * 64)
                            wk, wkk = wload(WB[:, :, C_Q + h_ * 64:C_Q + h_ * 64 + n], 8, n)
                        pjt, pk = nextpj()
                        c0 = (h_ % 8) * 64
                        calls = [dict(out=pjt[:, :], lhsT=SELQ[:, h_, :], rhs=spt[:, :], start=True, stop=False)]
                        calls += [dict(out=pjt[0:64, :], lhsT=wk[:, kt, c0:c0 + 64], rhs=hT[:, kt, :], start=False, stop=(kt == 7)) for kt in range(8)]
                        mmg(calls, ["SELQ", spk, wkk, "hT", pk], [pk])
                        ks, kk = kst[h_ % 2], "kst%d" % (h_ % 2)
                        act(ks[:], pjt[:, :], AF.Identity, [pk, "qscale"], [kk], scale=qscale[:, :])
                        ld(QT[h_, :, ot * TT:(ot + 1) * TT], ks[:], [kk], [U()], eng="act")
                if P1S < 5:
                    continue
                wv1, wv1k = wload(WB[:, :, C_V:C_V + 512], 8, 512)
                wv2, wv2k = wload(WB[:, :, C_V + 512:C_V + 768], 8, 256)
                for a in range(4):
                    p1, p1k = nextpj()
                    p2, p2k = nextpj()
                    mmg([dict(out=p1[:, :], lhsT=hT[:, kt, a * 128:(a + 1) * 128], rhs=wv1[:, kt, :], start=(kt == 0), stop=(kt == 7)) for kt in range(8)],
                        ["hT", wv1k, p1k], [p1k])
                    mmg([dict(out=p2[:, 0:256], lhsT=hT[:, kt, a * 128:(a + 1) * 128], rhs=wv2[:, kt, :], start=(kt == 0), stop=(kt == 7)) for kt in range(8)],
                        ["hT", wv2k, p2k], [p2k])
                    cp("act", Vst[:, a, 0:8, 0:64], p1[:, :].rearrange("p (h d) -> p h d", h=8), [p1k], ["Vst"])
                    cp("dve", Vst[:, a, 8:12, 0:64], p2[:, 0:256].rearrange("p (h d) -> p h d", h=4), [p2k], ["Vst"])
                ld(VS[4 * it:4 * it + 4].rearrange("a p f -> p a f"), Vst[:].rearrange("p a h d -> p a (h d)"), ["Vst"], [U()], eng="act")
                if P1S < 6:
                    continue
                for o in range(6):
                    if o % 4 == 0:
                        n = min(512, 768 - o * 128)
                        wk, wkk = wload(WB[:, :, C_U + o * 128:C_U + o * 128 + n], 8, n)
                    pjt, pk = nextpj()
                    c0 = (o % 4) * 128
                    mmg([dict(out=pjt[:, :], lhsT=wk[:, kt, c0:c0 + 128], rhs=hT[:, kt, :], start=(kt == 0), stop=(kt == 7)) for kt in range(8)],
                        [wkk, "hT", pk], [pk])
                    cp("act" if o % 2 == 0 else "dve", uT[:, o, :], pjt[:, :], [pk], ["uT"])
                if P1S < 7:
                    continue
                s5slots = []
                for o in range(6):
                    bd_t, wz_t, wy_t = s5w[o % 2]
                    sk = "s5w%d" % (o % 2)
                    ld(wz_t[:], WZS[o], [], [sk + "z"])
                    for q4 in range(4):
                        calls = []
                        for ri in range(2):
                            for s in range(8):
                                calls.append(dict(out=zpsb[q4][:, ri * 64:(ri + 1) * 64], lhsT=wz_t[32 * q4:32 * q4 + 32, s, ri, :],
                                                  rhs=uT[32 * q4:32 * q4 + 32, o, :].rearrange("p (c s) -> p c s", s=8)[:, :, s],
                                                  start=(s == 0), stop=(s == 7), tile_position=(32 * q4, 0)))
                        mmg(calls, [sk + "z", "uT", "zps%d" % q4], ["zps%d" % q4])
                        cp("act" if q4 % 2 == 0 else "dve", Z[:, 4 * o + q4, :, :], zpsb[q4][:, 0:128].rearrange("p (i c) -> p i c", i=2), ["zps%d" % q4], ["Z"])
                if P1S < 8:
                    continue
                if it > 0:
                    cp("pool", TB[:, :, :, 0], TB[:, :, :, 8], ["TB"], ["TB"])
                Zv = Z[:].rearrange("p r i (b j) -> p r i b j", j=8)

                def cmac(dre, dim_, sre, sim, mr, mi, T, key):
                    tt("pool", T[0], mr, sre, ALU.mult, [key, "TB"], ["RT0"])
                    tt("pool", T[1], mi, sim, ALU.mult, [key, "TB"], ["RT1"])
                    tt("pool", T[0], T[0], T[1], ALU.subtract, ["RT0", "RT1"], ["RT0"])
                    tt("pool", T[2], mr, sim, ALU.mult, [key, "TB"], ["RT2"])
                    tt("pool", T[3], mi, sre, ALU.mult, [key, "TB"], ["RT3"])
                    tt("pool", T[2], T[2], T[3], ALU.add, ["RT2", "RT3"], ["RT2"])
                    tt("pool", dre, dre, T[0], ALU.add, ["RT0", key], [key])
                    tt("pool", dim_, dim_, T[2], ALU.add, ["RT2", key], [key])
                RTv = [t[:] for t in RT]
                RSv = [t[:] for t in RS]
                for j in range(1, 8):
                    cmac(Zv[:, :, 0, :, j], Zv[:, :, 1, :, j], Zv[:, :, 0, :, j - 1], Zv[:, :, 1, :, j - 1],
                         A1R[:].unsqueeze(2).to_broadcast([128, 24, 8]), A1I[:].unsqueeze(2).to_broadcast([128, 24, 8]), RTv, "Z")
                for b_ in range(8):
                    cp("pool", TB[:, :, :, b_ + 1], Zv[:, :, :, b_, 7], ["Z", "TB"], ["TB"])
                    cmac(TB[:, :, 0, b_ + 1], TB[:, :, 1, b_ + 1], TB[:, :, 0, b_], TB[:, :, 1, b_], A8R[:], A8I[:], RSv, "TB")
                for j in range(8):
                    cmac(Zv[:, :, 0, :, j], Zv[:, :, 1, :, j], TB[:, :, 0, 0:8], TB[:, :, 1, 0:8],
                         APR[:, j, :].unsqueeze(2).to_broadcast([128, 24, 8]), API[:, j, :].unsqueeze(2).to_broadcast([128, 24, 8]), RTv, "Z")
                if own:
                    cp("pool", Hb[:, :, :, 0], TB[:, :, :, 0], ["TB"], ["Hb"])
                    cp("pool", Hb[:, :, :, 1:65], Z[:, :, :, :], ["Z"], ["Hb"])
                if not own:
                    continue
                if P1S < 9:
                    continue
                for o in range(6):
                    if o % 4 == 0:
                        n = min(512, 768 - o * 128)
                        wk, wkk = wload(WB[:, :, C_GS + o * 128:C_GS + o * 128 + n], 8, n)
                    pjt, pk = nextpj()
                    c0 = (o % 4) * 128
                    mmg([dict(out=pjt[:, :], lhsT=wk[:, kt, c0:c0 + 128], rhs=hT[:, kt, :], start=(kt == 0), stop=(kt == 7)) for kt in range(8)],
                        [wkk, "hT", pk], [pk])
                    act(SGS[:, o, :], pjt[:, :], AF.Silu, [pk], ["SGS"])
                if P1S < 10:
                    continue
                for o in range(6):
                    bd_t, wz_t, wy_t = s5w[o % 2]
                    sk = "s5w%d" % (o % 2)
                    ld(bd_t[:], BDS[o], [], [sk + "b"])
                    ld(wy_t[:], WYS[o], [], [sk + "y"])
                    yv = yps[:, :].rearrange("p (c s) -> p c s", s=8)
                    uv = uT[:, o, :].rearrange("p (c s) -> p c s", s=8)
                    calls = []
                    for j in range(8):
                        calls.append(dict(out=yv[:, :, j:8], lhsT=bd_t[:, j, :], rhs=uv[:, :, 0:8 - j], start=(j == 0), stop=False))
                    for q4 in range(4):
                        for tau in range(8):
                            for ri in range(2):
                                last = (q4 == 3 and tau == 7 and ri == 1)
                                calls.append(dict(out=yv[32 * q4:32 * q4 + 32, :, tau], lhsT=wy_t[:, q4, tau, ri, :],
                                                  rhs=Hb[:, 4 * o + q4, ri, 0:64], start=False, stop=last, tile_position=(0, 32 * q4)))
                    mmg(calls, [sk + "b", sk + "y", "uT", "Hb", "yps"], ["yps"])
                    act(tA[:], yps[:, :], AF.Square, ["yps"], ["tA"])
                    ts("dve", tA[:], tA[:], 0.044715, 1.0, ALU.mult, ALU.add, ["tA"], ["tA"])
                    tt("dve", tB[:], tA[:], yps[:, :], ALU.mult, ["tA", "yps"], ["tB"])
                    act(tA[:], tB[:], AF.Sigmoid, ["tB"], ["tA"], scale=1.5957691216057308)
                    tt("dve", YG[:, o, :], tA[:], yps[:, :], ALU.mult, ["tA", "yps"], ["YG"])
                if P1S < 11:
                    continue
                for half in range(2):
                    wg, wgk = wload(WGLU[:, :, half * 384:(half + 1) * 384], 6, 384)
                    for c3 in range(3):
                        cb = half * 3 + c3
                        pjt, pk = nextpj()
                        mmg([dict(out=pjt[:, :], lhsT=wg[:, kt, c3 * 128:(c3 + 1) * 128], rhs=YG[:, kt, :], start=(kt == 0), stop=(kt == 5)) for kt in range(6)],
                            [wgk, "YG", pk], [pk])
                        act(tA[:], pjt[:, :], AF.Sigmoid, [pk, "bglu"], ["tA"], bias=bglu[:, cb:cb + 1])
                        tt("dve", tB[:], tA[:], YG[:, cb, :], ALU.mult, ["tA", "YG"], ["tB"])
                        tt("dve", YS[:, cb, :], tB[:], SGS[:, cb, :], ALU.mult, ["tB", "SGS"], ["SGS"])
                if P1S < 12:
                    continue
                wk, wkk = wload(WB[:, :, C_QM:C_QM + 512], 8, 512)
                for hd in range(4):
                    pjt, pk = nextpj()
                    mmg([dict(out=pjt[:, :], lhsT=wk[:, kt, hd * 128:(hd + 1) * 128], rhs=hT[:, kt, :], start=(kt == 0), stop=(kt == 7)) for kt in range(8)],
                        [wkk, "hT", pk], [pk])
                    act(QM[:, hd, :], pjt[:, :], AF.Identity, [pk], ["QM"], scale=128.0 ** -0.5)
                wk, wkk = wload(WB[:, :, C_GM:C_GM + 512], 8, 512)
                for hd in range(4):
                    pjt, pk = nextpj()
                    mmg([dict(out=pjt[:, :], lhsT=wk[:, kt, hd * 128:(hd + 1) * 128], rhs=hT[:, kt, :], start=(kt == 0), stop=(kt == 7)) for kt in range(8)],
                        [wkk, "hT", pk], [pk])
                    act(SGM[:, hd, :], pjt[:, :], AF.Silu, [pk], ["SGM"])
                for hd in range(4):
                    for mt in range(2):
                        pjt, pk = nextpj()
                        mmg([dict(out=pjt[:, :], lhsT=MKT[:, hd, mt * 128:(mt + 1) * 128], rhs=QM[:, hd, :], start=True, stop=True)],
                            ["MKT", "QM", pk], [pk])
                        act(PmT[:, mt, :], pjt[:, :], AF.Exp, [pk], ["PmT%d" % mt])
                    pjt, pk = nextpj()
                    mmg([dict(out=pjt[:, :], lhsT=MV[:, mt, hd * 128:(hd + 1) * 128], rhs=PmT[:, mt, :], start=(mt == 0), stop=(mt == 1)) for mt in range(2)],
                        ["MV", "PmT0", "PmT1", pk], [pk])
                    axp, axk = nextpj()
                    mmg([dict(out=axp[:, :], lhsT=ones_b[:, :], rhs=PmT[:, mt, :], start=(mt == 0), stop=(mt == 1)) for mt in range(2)],
                        ["ones_b", "PmT0", "PmT1", axk], [axk])
                    P.c("dve", lambda h, axp=axp: h.reciprocal(out=tC[:], in_=axp[:, :]), [axk], ["tC"])
                    tt("dve", tB[:], tC[:], pjt[:, :], ALU.mult, ["tC", pk], ["tB"])
                    tt("dve", YM[:, hd, :], tB[:], SGM[:, hd, :], ALU.mult, ["tB", "SGM"], ["SGM"])
                if P1S < 13:
                    continue
                for cb in range(8):
                    si = wn[0] % NWS
                    wn[0] += 1
                    wfull = wsl[si][:, 0:26 * 128].rearrange("p (k c) -> p k c", k=26)
                    wpsk = wpmk = wg1k = wg2k = "ws%d" % si
                    wps_t, wpm_t, wg1, wg2 = wfull[:, 0:6, :], wfull[:, 6:10, :], wfull[:, 10:18, :], wfull[:, 18:26, :]
                    ld(wps_t, WPS[:, :, cb * 128:(cb + 1) * 128], [], [wpsk])
                    ld(wpm_t, WPM[:, :, cb * 128:(cb + 1) * 128], [wpsk], [wpsk])
                    ld(wg1, WB[:, :, C_GL + 1024 + cb * 128:C_GL + 1024 + (cb + 1) * 128], [wpsk], [wpsk])
                    ld(wg2, WB[:, :, C_GL + 2048 + cb * 128:C_GL + 2048 + (cb + 1) * 128], [wpsk], [wpsk])
                    c0 = 0
                    m0, m0k = M0[cb % 2], "M0_%d" % (cb % 2)
                    pjt, pk = nextpj()
                    mmg([dict(out=pjt[:, :], lhsT=wg1[:, kt, c0:c0 + 128], rhs=hT[:, kt, :], start=(kt == 0), stop=(kt == 7)) for kt in range(8)],
                        [wg1k, "hT", pk], [pk])
                    act(tA[:], pjt[:, :], AF.Sigmoid, [pk, "bm"], ["tA"], bias=bm[:, 8 + cb:9 + cb])
                    pjt, pk = nextpj()
                    mmg([dict(out=pjt[:, :], lhsT=wps_t[:, kt, c0:c0 + 128], rhs=YS[:, kt, :], start=(kt == 0), stop=(kt == 5)) for kt in range(6)],
                        [wpsk, "SGS", pk], [pk])
                    tt("dve", m0[:], tA[:], pjt[:, :], ALU.mult, ["tA", pk], [m0k])
                    pjt, pk = nextpj()
                    mmg([dict(out=pjt[:, :], lhsT=wg2[:, kt, c0:c0 + 128], rhs=hT[:, kt, :], start=(kt == 0), stop=(kt == 7)) for kt in range(8)],
                        [wg2k, "hT", pk], [pk])
                    act(tB[:], pjt[:, :], AF.Sigmoid, [pk, "bm"], ["tB"], bias=bm[:, 16 + cb:17 + cb])
                    pjt, pk = nextpj()
                    mmg([dict(out=pjt[:, :], lhsT=wpm_t[:, kt, c0:c0 + 128], rhs=YM[:, kt, :], start=(kt == 0), stop=(kt == 3)) for kt in range(4)],
                        [wpmk, "SGM", pk], [pk])
                    tt("dve", tC[:], tB[:], pjt[:, :], ALU.mult, ["tB", pk], ["tC"])
                    tt("dve", m0[:], m0[:], tC[:], ALU.add, [m0k, "tC"], [m0k])
                    ld(M0S[ot, :, cb, :], m0[:], [m0k], [U()], eng="act")
                if P1S < 14:
                    continue
                for cb in range(8):
                    if cb % 4 == 0:
                        wg0, wg0k = wload(WB[:, :, C_GL + cb * 128:C_GL + cb * 128 + 512], 8, 512)
                    c0 = (cb % 4) * 128
                    pjt, pk = nextpj()
                    mmg([dict(out=pjt[:, :], lhsT=wg0[:, kt, c0:c0 + 128], rhs=hT[:, kt, :], start=(kt == 0), stop=(kt == 7)) for kt in range(8)],
                        [wg0k, "hT", pk], [pk])
                    g0, g0k = G0st[cb % 2], "G0st%d" % (cb % 2)
                    act(g0[:], pjt[:, :], AF.Sigmoid, [pk, "bm"], [g0k], bias=bm[:, cb:cb + 1])
                    ld(G0S[ot, :, cb, :], g0[:], [g0k], [U()], eng="act")
                for o in range(6):
                    if o % 4 == 0:
                        n = min(512, 768 - o * 128)
                        wk, wkk = wload(WB[:, :, C_GF + o * 128:C_GF + o * 128 + n], 8, n)
                    pjt, pk = nextpj()
                    c0 = (o % 4) * 128
                    mmg([dict(out=pjt[:, :], lhsT=wk[:, kt, c0:c0 + 128], rhs=hT[:, kt, :], start=(kt == 0), stop=(kt == 7)) for kt in range(8)],
                        [wkk, "hT", pk], [pk])
                    ks, kk = kst[o % 2], "kst%d" % (o % 2)
                    act(ks[:], pjt[:, :], AF.Silu, [pk], [kk])
                    ld(SGFS[ot, :, o, :], ks[:], [kk], [U()], eng="act")
            P.emit()

        if upto < 3:
            return nc
        S2 = ExitStack()
        with S2:
            Kh = [sb(S2, "Kh%d" % i, [128, LTOK], BF16) for i in range(2)]
            Vh = [sb(S2, "Vh%d" % i, [128, 64, 65], BF16) for i in range(2)]
            Qh = [sb(S2, "Qh%d" % i, [128, 4096], BF16) for i in range(2)]
            NPT = 4
            Pt = [sb(S2, "Pt%d" % i, [128, TT], BF16) for i in range(NPT)]
            Osb = sb(S2, "Osb", [65, TT], F32)
            Rd = sb(S2, "Rd", [64, TT], F32)
            Yst = [sb(S2, "Yst%d" % i, [64, TT], BF16) for i in range(2)]
            SEL = sb(S2, "SEL", [65, 64], F32)
            sps = [ps(S2, "sps%d" % i, [128, 512], F32) for i in range(4)]
            ops_ = [ps(S2, "ops%d" % i, [128, 512], F32) for i in range(2)]
            dps = ps(S2, "dps", [128, 512], F32)
            mset("pool", SEL[:], 0.0, ["SEL"])
            mset("pool", SEL[64:65, :], 1.0, ["SEL"])
            pcount = [0]
            scount = [0]
            for h_ in range(12):
                hs = h_ % 2
                kh, vh, qh = Kh[hs], Vh[hs], Qh[hs]
                kk, vk, qk = "Kh%d" % hs, "Vh%d" % hs, "Qh%d" % hs
                for c4 in range(4):
                    ld(kh[:, c4 * 2048:(c4 + 1) * 2048], KT[h_, :, c4 * 2048:(c4 + 1) * 2048], [], [kk])
                ld(qh[:], QT[h_], [], [qk])
                for c4 in range(4):
                    ld(vh[:, c4 * 16:(c4 + 1) * 16, :], VS[c4 * 16:(c4 + 1) * 16, :, h_ * 65:(h_ + 1) * 65].rearrange("b p d -> p b d"), [], [vk])
                for qt in range(8):
                    nkb = 32 + 4 * qt + 4
                    op_t, opk = ops_[qt % 2], "ops%d" % (qt % 2)
                    for kb in range(nkb):
                        diag = kb - (32 + 4 * qt)
                        c0 = max(0, diag) * 128
                        n = TT - c0
                        st, sk = sps[scount[0] % 4], "sps%d" % (scount[0] % 4)
                        scount[0] += 1
                        pt, ptk = Pt[pcount[0] % NPT], "Pt%d" % (pcount[0] % NPT)
                        pcount[0] += 1
                        mmg([dict(out=st[:, c0:TT], lhsT=kh[:, kb * 128:(kb + 1) * 128], rhs=qh[:, qt * TT + c0:(qt + 1) * TT], start=True, stop=True)],
                            [kk, qk, sk], [sk])
                        act(pt[:, c0:TT], st[:, c0:TT], AF.Exp, [sk], [ptk])
                        if diag >= 0:
                            tt("pool", pt[:, c0:c0 + 128], pt[:, c0:c0 + 128], maskT[:], ALU.mult, [ptk, "maskT"], [ptk])
                        mmg([dict(out=op_t[0:65, c0:TT], lhsT=vh[:, kb, :], rhs=pt[:, c0:TT], start=(kb == 0), stop=(kb == nkb - 1))],
                            [vk, ptk, opk], [opk])
                    cp("dve", Osb[:], op_t[0:65, :], [opk], ["Osb"])
                    mmg([dict(out=dps[0:64, :], lhsT=SEL[:, :], rhs=Osb[:, :], start=True, stop=True)], ["SEL", "Osb", "dps"], ["dps"])
                    P.c("dve", lambda h: h.reciprocal(out=Rd[:], in_=dps[0:64, :]), ["dps"], ["Rd"])
                    ys, ysk = Yst[qt % 2], "Yst%d" % (qt % 2)
                    tt("dve", ys[:], Osb[0:64, :], Rd[:], ALU.mult, ["Osb", "Rd"], [ysk])
                    ld(YFS[h_, :, qt * TT:(qt + 1) * TT], ys[:], [ysk], [U()], eng="act")
            P.emit()

        if upto < 4:
            return nc
        S3 = ExitStack()
        with S3:
            gfin = sb(S3, "gfin", [128, D], F32)
            ld(gfin[:], g_final.to_broadcast([128, D]), [], ["gfin"])
            wpf = sb(S3, "wpf", [128, 6, D], BF16)
            wo = sb(S3, "wo", [128, 8, D], BF16)
            yf = sb(S3, "yf", [128, 6, TT], BF16)
            sgf = sb(S3, "sgf", [128, 6, TT], BF16)
            yfg = sb(S3, "yfg", [128, 6, TT], BF16)
            g0t = sb(S3, "g0t", [128, 8, TT], BF16)
            m0t = sb(S3, "m0t", [128, 8, TT], F32)
            mg = sb(S3, "mg", [128, 8, TT], BF16)
            t3 = sb(S3, "t3", [128, TT], F32)
            x3 = sb(S3, "x3", [128, 4, D], F32)
            o3 = [sb(S3, "o3_%d" % i, [128, D], F32) for i in range(2)]
            j3 = sb(S3, "j3", [128, D], BF16)
            pf = [ps(S3, "pf%d" % i, [128, 512], F32) for i in range(2)]
            po = [ps(S3, "po%d" % i, [128, 512], F32) for i in range(4)]
            ld(wpf[:], WPF, [], ["wpf"])
            ld(wo[:], WOUT, [], ["wo"])
            for ot in range(8):
                for a2 in range(2):
                    ld(yf[a2 * 64:(a2 + 1) * 64, :, :], YFS[:, :, ot * TT:(ot + 1) * TT].rearrange("(c a) d t -> a d c t", a=2)[a2], ["yf"], ["yf"])
                ld(sgf[:], SGFS[ot], [], ["sgf"])
                ld(g0t[:], G0S[ot], [], ["g0t"])
                ld(m0t[:], M0S[ot], [], ["m0t"])
                ld(x3[:], xa[(OWN0 + ot) * TT:(OWN0 + ot + 1) * TT, :].rearrange("(a p) d -> p a d", p=128), [], ["x3"])
                for c in range(6):
                    tt("pool" if c % 2 else "dve", yfg[:, c, :], yf[:, c, :], sgf[:, c, :], ALU.mult, ["yf", "sgf"], ["yfg"])
                for cb in range(8):
                    pt_, pk = pf[cb % 2], "pf%d" % (cb % 2)
                    mmg([dict(out=pt_[:, :], lhsT=wpf[:, kt, cb * 128:(cb + 1) * 128], rhs=yfg[:, kt, :], start=(kt == 0), stop=(kt == 5)) for kt in range(6)],
                        ["wpf", "yfg", pk], [pk])
                    tt("dve", t3[:], g0t[:, cb, :], pt_[:, :], ALU.mult, ["g0t", pk], ["t3"])
                    tt("dve", mg[:, cb, :], t3[:], m0t[:, cb, :], ALU.add, ["t3", "m0t"], ["mg"])
                for a in range(4):
                    ob, obk = o3[a % 2], "o3_%d" % (a % 2)
                    for hf in range(2):
                        pt_, pk = po[(2 * a + hf) % 4], "po%d" % ((2 * a + hf) % 4)
                        mmg([dict(out=pt_[:, :], lhsT=mg[:, kt, a * 128:(a + 1) * 128], rhs=wo[:, kt, hf * 512:(hf + 1) * 512], start=(kt == 0), stop=(kt == 7)) for kt in range(8)],
                            ["mg", "wo", pk], [pk])
                        tt("dve", ob[:, hf * 512:(hf + 1) * 512], pt_[:, :], x3[:, a, hf * 512:(hf + 1) * 512], ALU.add, [pk, "x3"], [obk])
                    act(j3[:], ob[:], AF.Square, [obk], ["j3", "ssq"], accum=ssq[:, a:a + 1])
                    ts("dve", rstd[:, a:a + 1], ssq[:, a:a + 1], 1.0 / D, EPS, ALU.mult, ALU.add, ["ssq"], ["rstd"])
                    act(rstd[:, a:a + 1], rstd[:, a:a + 1], AF.Sqrt, ["rstd"], ["rstd"])
                    P.c("dve", lambda h, a=a: h.reciprocal(out=rstd[:, a:a + 1], in_=rstd[:, a:a + 1]), ["rstd"], ["rstd"])
                    stt(ob[:], ob[:], rstd[:, a:a + 1], gfin[:], ALU.mult, ALU.mult, [obk, "rstd", "gfin"], [obk])
                    ld(yout[ot * TT + a * 128:ot * TT + (a + 1) * 128, :], ob[:], [obk], [U()], eng="act")
            P.emit()
    return nc


_NC = None


def kernel(**inputs):
    global _NC
    if _NC is None:
        _NC = build_nc()
    nc = _NC
    x = np.ascontiguousarray(np.asarray(inputs["x"], dtype=np.float32))
    mem = np.asarray(inputs["mem"], dtype=np.float32)
    in_maps = []
    for c in range(8):
        b, half = c // 2, c % 2
        if half == 0:
            xa = np.concatenate([np.zeros((4096, D), np.float32), x[b, 0:4096]], axis=0)
        else:
            xa = x[b]
        km = np.zeros((NT, TT), dtype=ml_dtypes.bfloat16)
        if half == 0:
            km[0:OWN0, :] = -30000.0
        m = {"xa": np.ascontiguousarray(xa), "kmrow": km, "mem": np.ascontiguousarray(mem[b])}
        for name in ("g_norm", "g_mem_norm", "w_in", "b_forget", "b_merge", "w_mem_kv", "lam_re", "lam_im", "log_step",
                     "s5_b_re", "s5_b_im", "s5_c_re", "s5_c_im", "s5_d", "w_glu", "b_glu", "w_proj_fox", "w_proj_s5",
                     "w_proj_mem", "w_out"):
            a = np.asarray(inputs[name], dtype=np.float32)
            a = a[0]
            if a.ndim == 1:
                a = a[None, :]
            m[name] = np.ascontiguousarray(a)
        m["g_final"] = np.ascontiguousarray(np.asarray(inputs["g_final"], dtype=np.float32)[None, :])
        in_maps.append(m)
    res = run_bass_kernel_spmd(nc, in_maps, core_ids=list(range(8)))
    out = np.zeros((4, 8192, D), dtype=np.float32)
    for c in range(8):
        b, half = c // 2, c % 2
        out[b, half * 4096:(half + 1) * 4096] = np.asarray(res.results[c]["yout"], dtype=np.float32)
    return out
```

```python
import math
from contextlib import ExitStack

import ml_dtypes
import numpy as np
import concourse.bass as bass
import concourse.mybir as mybir
from concourse.bass_utils import run_bass_kernel_spmd

F32 = mybir.dt.float32
BF16 = mybir.dt.bfloat16
I32 = mybir.dt.int32
AF = mybir.ActivationFunctionType
ALU = mybir.AluOpType

ENGS = ("pe", "act", "dve", "pool", "sp")
D = 1024
NIN = 8716
TT = 512
NT = 16
LTOK = 8192
OWN0 = 8
C_Q, C_K, C_V, C_FL, C_GF, C_U, C_GS, C_QM, C_GM, C_GL = 0, 768, 1536, 2304, 2316, 3084, 3852, 4620, 5132, 5644
EPS = 1e-6
TWO_PI = 2.0 * math.pi


class Op:
    __slots__ = ("eng", "fn", "deps", "needs_inc", "sigval", "is_dma", "lane", "laneval")

    def __init__(self, eng, fn, is_dma=False):
        self.eng = eng
        self.fn = fn
        self.deps = []
        self.needs_inc = False
        self.sigval = None
        self.is_dma = is_dma
        self.lane = None
        self.laneval = None


class Prog:
    def __init__(self, nc, es, n_lanes=6):
        self.nc = nc
        self.n_lanes = n_lanes
        self.esem = {e: es.enter_context(nc.semaphore("s_" + e)) for e in ENGS}
        self.lsem = {}
        for e in ("sp", "act", "pool"):
            for i in range(n_lanes):
                self.lsem[(e, i)] = es.enter_context(nc.semaphore("l_%s%d" % (e, i)))
        self.ecnt = {e: 0 for e in ENGS}
        self.lane_rr = {e: 0 for e in ENGS}
        self.lane_cnt = {k: 0 for k in self.lsem}
        self.barrier = {}
        self._reset()

    def _reset(self):
        self.ops = {e: [] for e in ENGS}
        self.last_w = {}
        self.readers = {}
        self.lane_last = {}

    def _add(self, op, reads, writes):
        deps = []
        for r in reads:
            w = self.last_w.get(r)
            if w is not None:
                deps.append(w)
        for r in writes:
            w = self.last_w.get(r)
            if w is not None:
                deps.append(w)
            deps.extend(self.readers.get(r, ()))
        seen = set(id(d) for d in op.deps)
        for d in deps:
            if d is op or id(d) in seen:
                continue
            if op.eng == "pe" and d.eng == "pe" and not d.is_dma and not op.is_dma:
                continue
            seen.add(id(d))
            op.deps.append(d)
            if not d.is_dma:
                d.needs_inc = True
        for r in reads:
            self.readers.setdefault(r, []).append(op)
        for r in writes:
            self.last_w[r] = op
            self.readers[r] = []
        self.ops[op.eng].append(op)
        return op

    def c(self, eng, fn, reads=(), writes=()):
        return self._add(Op(eng, fn), reads, writes)

    def dma(self, eng, fn, reads=(), writes=()):
        op = Op(eng, fn, is_dma=True)
        lane = (eng, self.lane_rr[eng] % self.n_lanes)
        self.lane_rr[eng] += 1
        prev = self.lane_last.get(lane)
        self.lane_cnt[lane] += 1
        op.lane = lane
        op.laneval = 16 * self.lane_cnt[lane]
        if prev is not None:
            op.deps.append(prev)
        self.lane_last[lane] = op
        return self._add(op, reads, writes)

    def emit(self):
        nc = self.nc
        for e in ENGS:
            last = None
            for op in self.ops[e]:
                if not op.is_dma:
                    last = op
            if last is not None:
                last.needs_inc = True
            for op in self.ops[e]:
                if (not op.is_dma) and op.needs_inc:
                    self.ecnt[e] += 1
                    op.sigval = self.ecnt[e]
        barrier = dict(self.barrier)
        esem, lsem = self.esem, self.lsem

        def tok(d):
            if d.is_dma:
                return lsem[d.lane], d.laneval
            return esem[d.eng], d.sigval

        def run(e, h):
            waited = {}
            for s, v in barrier.values():
                if v > 0:
                    h.wait_ge(s, v)
                    waited[id(s)] = v
            for op in self.ops[e]:
                for d in op.deps:
                    s, v = tok(d)
                    if waited.get(id(s), 0) >= v:
                        continue
                    waited[id(s)] = v
                    h.wait_ge(s, v)
                inst = op.fn(h)
                if op.is_dma:
                    inst.then_inc(lsem[op.lane], 16)
                elif op.needs_inc:
                    inst.then_inc(esem[e], 1)
                import os as _os
                if _os.environ.get('DBGPRINT'):
                    print("OP", e, "dma" if op.is_dma else "c", "lane=%s val=%s" % (op.lane, op.laneval) if op.is_dma else "sig=%s" % op.sigval,
                          "deps=", [((d.lane, d.laneval) if d.is_dma else (d.eng, d.sigval)) for d in op.deps], type(inst).__name__)
            for lane, cnt in self.lane_cnt.items():
                if lane[0] == e and cnt > 0:
                    h.wait_ge(lsem[lane], 16 * cnt)

        with nc.Block() as block:
            @block.tensor
            def _(h):
                run("pe", h)

            @block.scalar
            def _(h):
                run("act", h)

            @block.vector
            def _(h):
                run("dve", h)

            @block.gpsimd
            def _(h):
                run("pool", h)

            @block.sync
            def _(h):
                run("sp", h)

        self.barrier = {}
        for e in ENGS:
            self.barrier[("e", e)] = (esem[e], self.ecnt[e])
        for lane, cnt in self.lane_cnt.items():
            self.barrier[("l", lane)] = (lsem[lane], 16 * cnt)
        self._reset()


def build_nc(dbg=False, upto=9):
    nc = bass.Bass("TRN2", target_bir_lowering=False)

    def din(name, shape, dt=F32):
        return nc.dram_tensor(name, list(shape), dt, kind="ExternalInput").ap()

    def dscr(name, shape, dt):
        return nc.dram_tensor(name, list(shape), dt, kind=("ExternalOutput" if dbg else "Internal")).ap()

    xa = din("xa", [LTOK, D])
    kmrow = din("kmrow", [NT, TT], BF16)
    mem = din("mem", [256, D])
    g_norm = din("g_norm", [1, D])
    g_mem_norm = din("g_mem_norm", [1, D])
    g_final = din("g_final", [1, D])
    w_in = din("w_in", [D, NIN])
    b_forget = din("b_forget", [1, 12])
    b_merge = din("b_merge", [1, 3072])
    w_mem_kv = din("w_mem_kv", [D, 1024])
    lam_re = din("lam_re", [48, 64])
    lam_im = din("lam_im", [48, 64])
    log_step = din("log_step", [1, 48])
    s5_b_re = din("s5_b_re", [48, 64, 16])
    s5_b_im = din("s5_b_im", [48, 64, 16])
    s5_c_re = din("s5_c_re", [48, 16, 64])
    s5_c_im = din("s5_c_im", [48, 16, 64])
    s5_d = din("s5_d", [1, 768])
    w_glu = din("w_glu", [768, 768])
    b_glu = din("b_glu", [1, 768])
    w_proj_fox = din("w_proj_fox", [768, D])
    w_proj_s5 = din("w_proj_s5", [768, D])
    w_proj_mem = din("w_proj_mem", [512, D])
    w_out = din("w_out", [D, D])
    yout = nc.dram_tensor("yout", [4096, D], F32, kind="ExternalOutput").ap()

    WB = dscr("WB", [128, 8, NIN], BF16)
    WKV = dscr("WKV", [128, 8, 1024], BF16)
    WGLU = dscr("WGLU", [128, 6, 768], BF16)
    WPF = dscr("WPF", [128, 6, D], BF16)
    WPS = dscr("WPS", [128, 6, D], BF16)
    WPM = dscr("WPM", [128, 4, D], BF16)
    WOUT = dscr("WOUT", [128, 8, D], BF16)
    BDS = dscr("BDS", [6, 128, 8, 128], BF16)
    WZS = dscr("WZS", [6, 128, 8, 2, 128], BF16)
    WYS = dscr("WYS", [6, 128, 4, 8, 2, 32], BF16)
    KT = dscr("KT", [12, 128, LTOK], BF16)
    QT = dscr("QT", [12, 128, 4096], BF16)
    VS = dscr("VS", [64, 128, 780], BF16)
    M0S = dscr("M0S", [8, 128, 8, TT], F32)
    G0S = dscr("G0S", [8, 128, 8, TT], BF16)
    SGFS = dscr("SGFS", [8, 128, 6, TT], BF16)
    YFS = dscr("YFS", [12, 64, 4096], BF16)

    es = ExitStack()
    with es:
        P = Prog(nc, es)

        def sb(stack, name, shape, dt):
            return stack.enter_context(nc.sbuf_tensor(name, list(shape), dt))

        def ps(stack, name, shape, dt=F32):
            return stack.enter_context(nc.psum_tensor(name, list(shape), dt))

        def act(out, in_, func, r, w, bias=None, scale=None, accum=None):
            kw = {}
            if bias is not None:
                kw["bias"] = bias
            if scale is not None:
                kw["scale"] = scale
            if accum is not None:
                kw["accum_out"] = accum
            return P.c("act", lambda h: h.activation(out=out, in_=in_, func=func, **kw), r, w)

        def tt(eng, out, in0, in1, op, r, w):
            return P.c(eng, lambda h: h.tensor_tensor(out=out, in0=in0, in1=in1, op=op), r, w)

        def ts(eng, out, in0, s1, s2, op0, op1, r, w):
            if op1 is None:
                return P.c(eng, lambda h: h.tensor_scalar(out=out, in0=in0, scalar1=s1, scalar2=None, op0=op0), r, w)
            return P.c(eng, lambda h: h.tensor_scalar(out=out, in0=in0, scalar1=s1, scalar2=s2, op0=op0, op1=op1), r, w)

        def stt(out, in0, scalar, in1, op0, op1, r, w):
            return P.c("dve", lambda h: h.scalar_tensor_tensor(out=out, in0=in0, scalar=scalar, in1=in1, op0=op0, op1=op1), r, w)

        def cp(eng, out, in_, r, w):
            if eng == "act":
                return P.c("act", lambda h: h.activation(out=out, in_=in_, func=AF.Identity), r, w)
            return P.c(eng, lambda h: h.tensor_copy(out=out, in_=in_), r, w)

        def mset(eng, ap, val, w):
            return P.c(eng, lambda h: h.memset(ap, val), (), w)

        def mmg(calls, r, w):
            def fn(h):
                inst = None
                for kw in calls:
                    inst = h.matmul(**kw)
                return inst
            return P.c("pe", fn, r, w)

        def tpg(calls, r, w):
            def fn(h):
                inst = None
                for (o, i, idn) in calls:
                    inst = h.transpose(o, i, idn)
                return inst
            return P.c("pe", fn, r, w)

        uq = [0]

        def U():
            uq[0] += 1
            return "u%d" % uq[0]

        def ld(out, in_, r, w, eng="sp", slow=False):
            if slow:
                return P.dma(eng, lambda h: h.dma_start(out=out, in_=in_, allow_slow_non_contiguous=True), r, w)
            return P.dma(eng, lambda h: h.dma_start(out=out, in_=in_), r, w)

        def rows_pattern(tile_ap, n, lo, hi, w):
            P.c("pool", lambda h: h.memset(tile_ap, 1.0), (), w)
            P.c("pool", lambda h: h.affine_select(out=tile_ap, in_=tile_ap, pattern=[[0, n]], compare_op=ALU.is_ge,
                                                  fill=0.0, base=-lo, channel_multiplier=1), w, w)
            P.c("pool", lambda h: h.affine_select(out=tile_ap, in_=tile_ap, pattern=[[0, n]], compare_op=ALU.is_ge,
                                                  fill=0.0, base=hi, channel_multiplier=-1), w, w)

        G = ExitStack()
        es.enter_context(G)
        ident_f = sb(G, "ident_f", [128, 128], F32)
        ident_b = sb(G, "ident_b", [128, 128], BF16)
        ones_f = sb(G, "ones_f", [128, 512], F32)
        ones_b = sb(G, "ones_b", [128, 128], BF16)
        maskT = sb(G, "maskT", [128, 128], BF16)
        gn = sb(G, "gn", [128, 8], F32)
        gmn = sb(G, "gmn", [128, 8], F32)
        bm = sb(G, "bm", [128, 24], F32)
        bglu = sb(G, "bglu", [128, 6], F32)
        SELK = sb(G, "SELK", [128, 12, 128], BF16)
        SELQ = sb(G, "SELQ", [128, 12, 128], BF16)
        qscale = sb(G, "qscale", [128, 1], F32)
        negb = sb(G, "negb", [128, 1], F32)
        nm1 = sb(G, "nm1", [128, 1], F32)
        nm2 = sb(G, "nm2", [128, 1], F32)
        WFL3 = sb(G, "WFL3", [128, 8, 76], BF16)
        MKT = sb(G, "MKT", [128, 4, 256], BF16)
        MV = sb(G, "MV", [128, 2, 512], BF16)
        A1R = sb(G, "A1R", [128, 24], F32)
        A1I = sb(G, "A1I", [128, 24], F32)
        A8R = sb(G, "A8R", [128, 24], F32)
        A8I = sb(G, "A8I", [128, 24], F32)
        APR = sb(G, "APR", [128, 8, 24], F32)
        API = sb(G, "API", [128, 8, 24], F32)
        Fcar = sb(G, "Fcar", [128, 1], F32)
        TB = sb(G, "TB", [128, 24, 2, 9], F32)
        ssq = sb(G, "ssq", [128, 8], F32)
        rstd = sb(G, "rstd", [128, 8], F32)

        mset("pool", ones_f[:], 1.0, ["ones_f"])
        mset("pool", ident_f[:], 1.0, ["ident_f"])
        P.c("pool", lambda h: h.affine_select(out=ident_f[:], in_=ident_f[:], pattern=[[-1, 128]], compare_op=ALU.is_equal,
                                              fill=0.0, base=0, channel_multiplier=1), ["ident_f"], ["ident_f"])
        cp("dve", ident_b[:], ident_f[:], ["ident_f"], ["ident_b"])
        cp("dve", ones_b[:], ones_f[:, 0:128], ["ones_f"], ["ones_b"])
        mset("pool", maskT[:], 1.0, ["maskT"])
        P.c("pool", lambda h: h.affine_select(out=maskT[:], in_=maskT[:], pattern=[[1, 128]], compare_op=ALU.is_ge,
                                              fill=0.0, base=0, channel_multiplier=-1), ["maskT"], ["maskT"])
        ld(gn[:], g_norm.rearrange("o (kt p) -> p (o kt)", p=128), [], ["gn"], slow=True)
        ld(gmn[:], g_mem_norm.rearrange("o (kt p) -> p (o kt)", p=128), [], ["gmn"], slow=True)
        ld(bm[:], b_merge.rearrange("o (c p) -> p (o c)", p=128), [], ["bm"], slow=True)
        ld(bglu[:], b_glu.rearrange("o (c p) -> p (o c)", p=128), [], ["bglu"], slow=True)
        mset("pool", qscale[:], 1.0, ["qscale"])
        mset("pool", qscale[0:64, :], 0.125, ["qscale"])
        mset("pool", negb[:], 0.0, ["negb"])
        for base in (0, 32, 64):
            ld(negb[base:base + 12, :], b_forget.rearrange("o h -> h o"), [], ["negb"], slow=True)
        ts("dve", negb[:], negb[:], -1.0, None, ALU.mult, None, ["negb"], ["negb"])
        mset("pool", nm1[:], 0.0, ["nm1"])
        mset("pool", nm1[32:64, :], -1.0, ["nm1"])
        mset("pool", nm1[64:96, :], -1.0, ["nm1"])
        mset("pool", nm2[:], 0.0, ["nm2"])
        mset("pool", nm2[64:96, :], -1.0, ["nm2"])
        mset("pool", Fcar[:], 0.0, ["Fcar"])
        mset("pool", TB[:], 0.0, ["TB"])
        mset("pool", SELK[:], 0.0, ["SELK"])
        mset("pool", SELQ[:], 0.0, ["SELQ"])
        for j, base in enumerate((0, 32, 64)):
            ts("dve", SELK[:, :, 96 + j], ident_f[:, base:base + 12], -1.0, None, ALU.mult, None, ["ident_f", "SELK"], ["SELK"])
            cp("dve", SELQ[:, :, 64 + j], ident_f[:, base:base + 12], ["ident_f", "SELQ"], ["SELQ"])
        for h_ in range(12):
            cp("dve", SELK[:, h_, 99:100], ident_f[:, 76:77], ["ident_f", "SELK"], ["SELK"])
            for c_ in (64, 65, 66):
                cp("dve", SELK[:, h_, c_:c_ + 1], ident_f[:, 96:97], ["ident_f", "SELK"], ["SELK"])
            for c_ in (96, 97, 98, 99):
                cp("dve", SELQ[:, h_, c_:c_ + 1], ident_f[:, 96:97], ["ident_f", "SELQ"], ["SELQ"])

        S0 = ExitStack()
        with S0:
            cv = [sb(S0, "cv%d" % i, [128, 8, 512], BF16) for i in range(2)]
            cvn = [0]

            def convert(src, nkt, ncols, dst, wkey=None):
                srcv = src.rearrange("(kt p) c -> p kt c", p=128)
                c0 = 0
                while c0 < ncols:
                    n = min(512, ncols - c0)
                    i = cvn[0] % 2
                    cvn[0] += 1
                    t = cv[i]
                    ld(t[:, 0:nkt, 0:n], srcv[:, :, c0:c0 + n], [], ["cv%d" % i], eng="pool")
                    ld(dst[:, :, c0:c0 + n], t[:, 0:nkt, 0:n], ["cv%d" % i], [wkey] if wkey else [U()], eng="sp")
                    c0 += n

            convert(w_in, 8, NIN, WB)
            convert(w_mem_kv, 8, 1024, WKV, "WKV")
            convert(w_glu, 6, 768, WGLU)
            convert(w_proj_fox, 6, D, WPF)
            convert(w_proj_s5, 6, D, WPS)
            convert(w_proj_mem, 4, D, WPM)
            convert(w_out, 8, D, WOUT)
            mset("pool", WFL3[:], 0.0, ["WFL3"])
            for base in (0, 32, 64):
                ld(WFL3[:, :, base:base + 12], w_in.rearrange("(kt p) c -> p kt c", p=128)[:, :, C_FL:C_FL + 12],
                   ["WFL3"], ["WFL3"], eng="pool")

            def s5t(name, shape, dt=F32):
                return sb(S0, name, shape, dt)
            LR = s5t("LR", [128, 24]); LI = s5t("LI", [128, 24]); LS = s5t("LS", [128, 24])
            ld(LR[:], lam_re.rearrange("(pr g2) p -> (g2 p) pr", g2=2), [], ["LR"], slow=True)
            ld(LI[:], lam_im.rearrange("(pr g2) p -> (g2 p) pr", g2=2), [], ["LI"], slow=True)
            for g2 in range(2):
                ld(LS[g2 * 64:(g2 + 1) * 64, :],
                   log_step.rearrange("o (pr g2) -> o g2 pr", g2=2)[:, g2, :].to_broadcast([64, 24]), [], ["LS"], slow=True)
            STP = s5t("STP", [128, 24]); MAG = s5t("MAG", [128, 24]); ANG = s5t("ANG", [128, 24])
            T0 = s5t("T0", [128, 24]); T1 = s5t("T1", [128, 24]); T2 = s5t("T2", [128, 24]); TI = s5t("TI", [128, 24], I32)
            ABR = s5t("ABR", [128, 24]); ABI = s5t("ABI", [128, 24]); SN = s5t("SN", [128, 24]); CS = s5t("CS", [128, 24])
            FR = s5t("FR", [128, 24]); FI = s5t("FI", [128, 24])
            act(STP[:], LS[:], AF.Exp, ["LS"], ["STP"])
            tt("dve", T0[:], LR[:], STP[:], ALU.mult, ["LR", "STP"], ["T0"])
            act(MAG[:], T0[:], AF.Exp, ["T0"], ["MAG"])
            tt("dve", ANG[:], LI[:], STP[:], ALU.mult, ["LI", "STP"], ["ANG"])

            def sin_of(dst, shift, key):
                ts("dve", T0[:], ANG[:], shift, None, ALU.add, None, ["ANG"], ["T0"])
                ts("dve", T1[:], T0[:], 1.0 / TWO_PI, 0.5, ALU.mult, ALU.add, ["T0"], ["T1"])
                cp("dve", TI[:], T1[:], ["T1"], ["TI"])
                cp("dve", T1[:], TI[:], ["TI"], ["T1"])
                stt(T2[:], T1[:], -TWO_PI, T0[:], ALU.mult, ALU.add, ["T1", "T0"], ["T2"])
                ts("dve", T1[:], T2[:], math.pi, -TWO_PI, ALU.is_gt, ALU.mult, ["T2"], ["T1"])
                tt("dve", T2[:], T2[:], T1[:], ALU.add, ["T2", "T1"], ["T2"])
                ts("dve", T1[:], T2[:], -math.pi, TWO_PI, ALU.is_lt, ALU.mult, ["T2"], ["T1"])
                tt("dve", T2[:], T2[:], T1[:], ALU.add, ["T2", "T1"], ["T2"])
                ts("dve", T2[:], T2[:], math.pi, -math.pi, ALU.min, ALU.max, ["T2"], ["T2"])
                act(dst[:], T2[:], AF.Sin, ["T2"], [key])
            sin_of(SN, 0.0, "SN")
            sin_of(CS, math.pi / 2.0, "CS")
            tt("dve", ABR[:], MAG[:], CS[:], ALU.mult, ["MAG", "CS"], ["ABR"])
            tt("dve", ABI[:], MAG[:], SN[:], ALU.mult, ["MAG", "SN"], ["ABI"])
            DEN = s5t("DEN", [128, 24]); NR = s5t("NR", [128, 24])
            tt("dve", T0[:], LR[:], LR[:], ALU.mult, ["LR"], ["T0"])
            tt("dve", T1[:], LI[:], LI[:], ALU.mult, ["LI"], ["T1"])
            tt("dve", DEN[:], T0[:], T1[:], ALU.add, ["T0", "T1"], ["DEN"])
            P.c("dve", lambda h: h.reciprocal(out=DEN[:], in_=DEN[:]), ["DEN"], ["DEN"])
            ts("dve", NR[:], ABR[:], -1.0, None, ALU.add, None, ["ABR"], ["NR"])
            tt("dve", T0[:], NR[:], LR[:], ALU.mult, ["NR", "LR"], ["T0"])
            tt("dve", T1[:], ABI[:], LI[:], ALU.mult, ["ABI", "LI"], ["T1"])
            tt("dve", T0[:], T0[:], T1[:], ALU.add, ["T0", "T1"], ["T0"])
            tt("dve", FR[:], T0[:], DEN[:], ALU.mult, ["T0", "DEN"], ["FR"])
            tt("dve", T0[:], ABI[:], LR[:], ALU.mult, ["ABI", "LR"], ["T0"])
            tt("dve", T1[:], NR[:], LI[:], ALU.mult, ["NR", "LI"], ["T1"])
            tt("dve", T0[:], T0[:], T1[:], ALU.subtract, ["T0", "T1"], ["T0"])
            tt("dve", FI[:], T0[:], DEN[:], ALU.mult, ["T0", "DEN"], ["FI"])
            PWR = s5t("PWR", [128, 9, 24]); PWI = s5t("PWI", [128, 9, 24])
            AWR = s5t("AWR", [128, 9, 24]); AWI = s5t("AWI", [128, 9, 24])

            def cmul(orr, oi, ar, ai, br, bi, rk, wk):
                tt("dve", T0[:], ar, br, ALU.mult, rk, ["T0"])
                tt("dve", T1[:], ai, bi, ALU.mult, rk, ["T1"])
                tt("dve", T2[:], ar, bi, ALU.mult, rk, ["T2"])
                tt("dve", orr, T0[:], T1[:], ALU.subtract, ["T0", "T1"], wk)
                tt("dve", T0[:], ai, br, ALU.mult, rk + wk, ["T0"])
                tt("dve", oi, T2[:], T0[:], ALU.add, ["T2", "T0"], wk)
            mset("pool", PWR[:, 0, :], 1.0, ["PW"])
            mset("pool", PWI[:, 0, :], 0.0, ["PW"])
            for k in range(1, 9):
                cmul(PWR[:, k, :], PWI[:, k, :], PWR[:, k - 1, :], PWI[:, k - 1, :], ABR[:], ABI[:], ["PW", "ABR", "ABI"], ["PW"])
            mset("pool", AWR[:, 0, :], 1.0, ["AW"])
            mset("pool", AWI[:, 0, :], 0.0, ["AW"])
            for k in range(1, 9):
                cmul(AWR[:, k, :], AWI[:, k, :], AWR[:, k - 1, :], AWI[:, k - 1, :], PWR[:, 8, :], PWI[:, 8, :], ["AW", "PW"], ["AW"])
            cp("dve", A1R[:], AWR[:, 1, :], ["AW"], ["A1"])
            cp("dve", A1I[:], AWI[:, 1, :], ["AW"], ["A1"])
            cp("dve", APR[:], AWR[:, 1:9, :], ["AW"], ["APR"])
            cp("dve", API[:], AWI[:, 1:9, :], ["AW"], ["API"])
            cp("dve", A8R[:], AWR[:, 8, :], ["AW"], ["A8"])
            cp("dve", A8I[:], AWI[:, 8, :], ["AW"], ["A8"])

            BRE = s5t("BRE", [128, 24, 16]); BIM = s5t("BIM", [128, 24, 16])
            ld(BRE[:], s5_b_re.rearrange("(pr g2) p h -> (g2 p) pr h", g2=2), [], ["BRE"])
            ld(BIM[:], s5_b_im.rearrange("(pr g2) p h -> (g2 p) pr h", g2=2), [], ["BIM"])
            BBR = s5t("BBR", [128, 24, 16]); BBI = s5t("BBI", [128, 24, 16])
            U0 = s5t("U0", [128, 24, 16]); U1 = s5t("U1", [128, 24, 16])
            frb = FR[:].unsqueeze(2).to_broadcast([128, 24, 16])
            fib = FI[:].unsqueeze(2).to_broadcast([128, 24, 16])
            tt("dve", U0[:], BRE[:], frb, ALU.mult, ["BRE", "FR"], ["U0"])
            tt("dve", U1[:], BIM[:], fib, ALU.mult, ["BIM", "FI"], ["U1"])
            tt("dve", BBR[:], U0[:], U1[:], ALU.subtract, ["U0", "U1"], ["BBR"])
            tt("dve", U0[:], BIM[:], frb, ALU.mult, ["BIM", "FR"], ["U0"])
            tt("dve", U1[:], BRE[:], fib, ALU.mult, ["BRE", "FI"], ["U1"])
            tt("dve", BBI[:], U0[:], U1[:], ALU.add, ["U0", "U1"], ["BBI"])
            BDR = s5t("BDR", [128, 24, 32]); BDI = s5t("BDI", [128, 24, 32])
            mset("pool", BDR[:], 0.0, ["BDR"]); mset("pool", BDI[:], 0.0, ["BDI"])
            for g2 in range(2):
                sl = slice(g2 * 64, (g2 + 1) * 64)
                cp("dve", BDR[sl, :, g2 * 16:(g2 + 1) * 16], BBR[sl, :, :], ["BBR", "BDR"], ["BDR"])
                cp("dve", BDI[sl, :, g2 * 16:(g2 + 1) * 16], BBI[sl, :, :], ["BBI", "BDI"], ["BDI"])
            BWR = s5t("BWR", [128, 24, 128]); BWI = s5t("BWI", [128, 24, 128])
            mset("pool", BWR[:], 0.0, ["BWR"]); mset("pool", BWI[:], 0.0, ["BWI"])
            for q4 in range(4):
                for o in range(6):
                    pr = 4 * o + q4
                    cp("dve", BWR[:, pr, 32 * q4:32 * q4 + 32], BDR[:, pr, :], ["BDR", "BWR"], ["BWR"])
                    cp("pool", BWI[:, pr, 32 * q4:32 * q4 + 32], BDI[:, pr, :], ["BDI", "BWI"], ["BWI"])
            CTR = s5t("CTR", [128, 24, 16]); CTI = s5t("CTI", [128, 24, 16])
            CN = s5t("CN", [128, 3, 2, 128])
            tp_ps = ps(S0, "tp_ps", [128, 512], F32)
            for ri, (csrc, cdst, key) in enumerate(((s5_c_re, CTR, "CTR"), (s5_c_im, CTI, "CTI"))):
                cv4 = csrc.rearrange("(pr g2) h p -> pr h g2 p", g2=2)
                for pr in range(24):
                    ld(CN[(pr % 8) * 16:(pr % 8) * 16 + 16, pr // 8, ri, :].rearrange("h (a p) -> h a p", a=2),
                       cv4[pr], ["CN%d" % ri], ["CN%d" % ri])
                for j in range(3):
                    tpg([(tp_ps[:, 0:128], CN[:, j, ri, :], ident_f[:])], ["CN%d" % ri, "ident_f", "tp_ps"], ["tp_ps"])
                    cp("act", cdst[:, 8 * j:8 * j + 8, :], tp_ps[:, 0:128].rearrange("p (a h) -> p a h", a=8), ["tp_ps"], [key])
            XR = s5t("XR", [128, 24, 32]); XI = s5t("XI", [128, 24, 32])
            V0 = s5t("V0", [128, 24, 32]); V1 = s5t("V1", [128, 24, 32])
            WZs = [s5t("WZs%d" % i, [128, 6, 2, 128], BF16) for i in range(2)]
            for s in range(8):
                k = 7 - s
                wzs, wzk = WZs[s % 2], "WZs%d" % (s % 2)
                pr_ = PWR[:, k, :].unsqueeze(2).to_broadcast([128, 24, 32])
                pi_ = PWI[:, k, :].unsqueeze(2).to_broadcast([128, 24, 32])
                tt("dve", V0[:], BDR[:], pr_, ALU.mult, ["BDR", "PW"], ["V0"])
                tt("dve", V1[:], BDI[:], pi_, ALU.mult, ["BDI", "PW"], ["V1"])
                tt("dve", XR[:], V0[:], V1[:], ALU.subtract, ["V0", "V1"], ["XR"])
                tt("dve", V0[:], BDI[:], pr_, ALU.mult, ["BDI", "PW"], ["V0"])
                tt("dve", V1[:], BDR[:], pi_, ALU.mult, ["BDR", "PW"], ["V1"])
                tt("dve", XI[:], V0[:], V1[:], ALU.add, ["V0", "V1"], ["XI"])
                for o in range(6):
                    for ri, X in enumerate((XR, XI)):
                        tpg([(tp_ps[:, 0:128], X[:, 4 * o:4 * o + 4, :].rearrange("p a b -> p (a b)"), ident_f[:])], ["XR", "XI", "ident_f", "tp_ps"], ["tp_ps"])
                        cp("act", wzs[:, o, ri, :], tp_ps[:, 0:128], ["tp_ps"], [wzk])
                for o in range(6):
                    ld(WZS[o, :, s, :, :], wzs[:, o, :, :], [wzk], [U()])
            QR = s5t("QR", [128, 24, 16]); QI = s5t("QI", [128, 24, 16])
            QBR = s5t("QBR", [128, 24, 32]); QBI = s5t("QBI", [128, 24, 32])
            WYk = [s5t("WYk%d" % i, [128, 24, 2, 32], BF16) for i in range(2)]
            BDj = [s5t("BDj%d" % i, [128, 6, 128], F32) for i in range(2)]
            BDjb = [s5t("BDjb%d" % i, [128, 6, 128], BF16) for i in range(2)]
            DSK = s5t("DSK", [128, 6], F32)
            ld(DSK[:], s5_d.rearrange("o (c p) -> p (o c)", p=128), [], ["DSK"], slow=True)
            mset("pool", QBR[:], 0.0, ["QBR"]); mset("pool", QBI[:], 0.0, ["QBI"])
            for i in range(2):
                mset("pool", BDj[i][:], 0.0, ["BDj%d" % i])
            bd_ps = ps(S0, "bd_ps", [128, 6, 32], F32)
            for k in range(9):
                pr_ = PWR[:, k, :].unsqueeze(2).to_broadcast([128, 24, 16])
                pi_ = PWI[:, k, :].unsqueeze(2).to_broadcast([128, 24, 16])
                tt("dve", U0[:], CTR[:], pr_, ALU.mult, ["CTR", "PW"], ["U0"])
                tt("dve", U1[:], CTI[:], pi_, ALU.mult, ["CTI", "PW"], ["U1"])
                tt("dve", QR[:], U0[:], U1[:], ALU.subtract, ["U0", "U1"], ["QR"])
                tt("dve", U0[:], CTR[:], pi_, ALU.mult, ["CTR", "PW"], ["U0"])
                tt("dve", U1[:], CTI[:], pr_, ALU.mult, ["CTI", "PW"], ["U1"])
                tt("dve", QI[:], U0[:], U1[:], ALU.add, ["U0", "U1"], ["QI"])
                for g2 in range(2):
                    sl = slice(g2 * 64, (g2 + 1) * 64)
                    cp("dve", QBR[sl, :, g2 * 16:(g2 + 1) * 16], QR[sl, :, :], ["QR", "QBR"], ["QBR"])
                    ts("dve", QBI[sl, :, g2 * 16:(g2 + 1) * 16], QI[sl, :, :], -1.0, None, ALU.mult, None, ["QI", "QBI"], ["QBI"])
                if k >= 1:
                    wyk, wykk = WYk[k % 2], "WYk%d" % (k % 2)
                    cp("act", wyk[:, :, 0, :], QBR[:], ["QBR"], [wykk])
                    cp("act", wyk[:, :, 1, :], QBI[:], ["QBI"], [wykk])
                    for o in range(6):
                        ld(WYS[o, :, :, k - 1, :, :], wyk[:, 4 * o:4 * o + 4, :, :], [wykk], [U()])
                if k <= 7:
                    bdj, bdk = BDj[k % 2], "BDj%d" % (k % 2)
                    bdjb, bdbk = BDjb[k % 2], "BDjb%d" % (k % 2)
                    for o in range(6):
                        calls = []
                        for q4 in range(4):
                            pr = 4 * o + q4
                            calls.append(dict(out=bd_ps[:, o, :], lhsT=BWR[:, pr, :], rhs=QBR[:, pr, :], start=(q4 == 0), stop=False))
                            calls.append(dict(out=bd_ps[:, o, :], lhsT=BWI[:, pr, :], rhs=QBI[:, pr, :], start=False, stop=(q4 == 3)))
                        mmg(calls, ["BWR", "BWI", "QBR", "QBI", "bd_ps"], ["bd_ps"])
                    for q4 in range(4):
                        sl = slice(32 * q4, 32 * q4 + 32)
                        cp("act", bdj[sl, :, 32 * q4:32 * q4 + 32], bd_ps[sl, :, :], ["bd_ps"], [bdk])
                    if k == 0:
                        for o in range(6):
                            stt(bdj[:, o, :], ident_f[:], DSK[:, o:o + 1], bdj[:, o, :], ALU.mult, ALU.add, ["ident_f", "DSK", bdk], [bdk])
                    cp("dve", bdjb[:], bdj[:], [bdk], [bdbk])
                    for o in range(6):
                        ld(BDS[o, :, k, :], bdjb[:, o, :], [bdbk], [U()])
            P.emit()
        if upto < 1:
            return nc
        S0b = ExitStack()
        with S0b:
            S0 = S0b

            def s5t(name, shape, dt=F32):
                return sb(S0b, name, shape, dt)
            mt_x = s5t("mt_x", [128, 2, D], F32)
            mt_n = s5t("mt_n", [128, 2, D], BF16)
            mt_j = s5t("mt_j", [128, D], BF16)
            MHT = s5t("MHT", [128, 8, 256], BF16)
            wkv_t = s5t("wkv_t", [128, 8, 1024], BF16)
            tpb_ps = ps(S0, "tpb_ps", [128, 512], BF16)
            mk_ps = ps(S0, "mk_ps", [128, 512], F32)
            ld(mt_x[:], mem.rearrange("(a p) d -> p a d", p=128), [], ["mt_x"])
            ld(wkv_t[:], WKV, ["WKV"], ["wkv_t"])
            for a in range(2):
                act(mt_j[:], mt_x[:, a, :], AF.Square, ["mt_x"], ["mt_j", "ssq"], accum=ssq[:, a:a + 1])
            ts("dve", rstd[:, 0:2], ssq[:, 0:2], 1.0 / D, EPS, ALU.mult, ALU.add, ["ssq"], ["rstd"])
            act(rstd[:, 0:2], rstd[:, 0:2], AF.Sqrt, ["rstd"], ["rstd"])
            P.c("dve", lambda h: h.reciprocal(out=rstd[:, 0:2], in_=rstd[:, 0:2]), ["rstd"], ["rstd"])
            for a in range(2):
                ts("dve", mt_n[:, a, :], mt_x[:, a, :], rstd[:, a:a + 1], None, ALU.mult, None, ["mt_x", "rstd"], ["mt_n"])
            for kt in range(8):
                tpg([(tpb_ps[:, a * 128:(a + 1) * 128], mt_n[:, a, kt * 128:(kt + 1) * 128], ident_b[:]) for a in range(2)],
                    ["mt_n", "ident_b", "tpb_ps"], ["tpb_ps"])
                ts("dve", MHT[:, kt, :], tpb_ps[:, 0:256], gmn[:, kt:kt + 1], None, ALU.mult, None, ["tpb_ps", "gmn"], ["MHT"])
            for hd in range(4):
                mmg([dict(out=mk_ps[:, 0:256], lhsT=wkv_t[:, kt, hd * 128:(hd + 1) * 128], rhs=MHT[:, kt, :],
                          start=(kt == 0), stop=(kt == 7)) for kt in range(8)], ["wkv_t", "MHT", "mk_ps"], ["mk_ps"])
                cp("act", MKT[:, hd, :], mk_ps[:, 0:256], ["mk_ps"], ["MKT"])
            for mt in range(2):
                mmg([dict(out=mk_ps[:, :], lhsT=MHT[:, kt, mt * 128:(mt + 1) * 128], rhs=wkv_t[:, kt, 512:1024],
                          start=(kt == 0), stop=(kt == 7)) for kt in range(8)], ["wkv_t", "MHT", "mk_ps"], ["mk_ps"])
                cp("act", MV[:, mt, :], mk_ps[:, :], ["mk_ps"], ["MV"])
            P.emit()

        if upto < 2:
            return nc
        S1 = ExitStack()
        with S1:
            xt = sb(S1, "xt", [128, 4, D], F32)
            xn = sb(S1, "xn", [128, 4, D], BF16)
            sqj = sb(S1, "sqj", [128, D], BF16)
            hT = sb(S1, "hT", [128, 8, TT], BF16)
            NWS = 3
            wsl = [sb(S1, "ws%d" % i, [128, 4096], BF16) for i in range(NWS)]
            SPt = [sb(S1, "SPt%d" % i, [128, TT], BF16) for i in range(2)]
            HIb = sb(S1, "HIb", [128, TT], BF16)
            MIDb = sb(S1, "MIDb", [128, TT], BF16)
            kst = [sb(S1, "kst%d" % i, [128, TT], BF16) for i in range(2)]
            Vst = sb(S1, "Vst", [128, 4, 12, 65], BF16)
            uT = sb(S1, "uT", [128, 6, TT], BF16)
            s5w = [(sb(S1, "s5bd%d" % i, [128, 8, 128], BF16), sb(S1, "s5wz%d" % i, [128, 8, 2, 128], BF16),
                    sb(S1, "s5wy%d" % i, [128, 4, 8, 2, 32], BF16)) for i in range(2)]
            Z = sb(S1, "Z", [128, 24, 2, 64], F32)
            Hb = sb(S1, "Hb", [128, 24, 2, 65], BF16)
            RT = [sb(S1, "RT%d" % i, [128, 24, 8], F32) for i in range(4)]
            RS = [sb(S1, "RS%d" % i, [128, 24], F32) for i in range(4)]
            tA = sb(S1, "tA", [128, TT], F32)
            tB = sb(S1, "tB", [128, TT], F32)
            tC = sb(S1, "tC", [128, TT], F32)
            Fe, Ff, R1 = tA, tB, tC
            YG = sb(S1, "YG", [128, 6, TT], BF16)
            SGS = sb(S1, "SGS", [128, 6, TT], BF16)
            YS = SGS
            QM = sb(S1, "QM", [128, 4, TT], BF16)
            SGM = sb(S1, "SGM", [128, 4, TT], BF16)
            PmT = sb(S1, "PmT", [128, 2, TT], BF16)
            YM = SGM
            M0 = [sb(S1, "M0_%d" % i, [128, TT], F32) for i in range(2)]
            G0st = [sb(S1, "G0st%d" % i, [128, TT], BF16) for i in range(2)]
            pj = [ps(S1, "pj%d" % i, [128, 512], F32) for i in range(2)]
            tpall = ps(S1, "tpall", [128, 1024], BF16)
            tpp = [tpall[:, 0:512], tpall[:, 0:512]]
            zpsb = [ps(S1, "zps%d" % i, [128, 512], F32) for i in range(4)]
            yps = ps(S1, "yps", [128, 512], F32)

            import os
            if not os.environ.get('SKIPM'):
                for i in range(2):
                    mset("pool", SPt[i][:], 0.0, ["SPt%d" % i])
                    mset("pool", SPt[i][96:97, :], 1.0, ["SPt%d" % i])
                mset("pool", Vst[:], 1.0, ["Vst"])

            wn = [0]
            pjn = [0]

            def wload(src3, nkt, ncols):
                i = wn[0] % NWS
                wn[0] += 1
                t = wsl[i][:, 0:nkt * ncols].rearrange("p (k c) -> p k c", k=nkt)
                ld(t, src3, [], ["ws%d" % i])
                return t, "ws%d" % i

            def nextpj():
                i = pjn[0] % 2
                pjn[0] += 1
                return pj[i], "pj%d" % i

            import os
            P1T = int(os.environ.get('P1T', NT)); P1S = int(os.environ.get('P1S', 99)); OWNX = int(os.environ.get('OWNX', OWN0))
            for it in range(P1T):
                own = it >= OWNX
                ot = it - OWNX
                sp_i = it % 2
                spt, spk = SPt[sp_i], "SPt%d" % sp_i
                ld(xt[:], xa[it * TT:(it + 1) * TT, :].rearrange("(a p) d -> p a d", p=128), [], ["xt"])
                if not os.environ.get('SKIPK'):
                    ld(spt[76:77, :], kmrow[it:it + 1, :], [], [spk], eng="pool")
                if P1S < -1:
                    continue
                for a in range(4):
                    act(sqj[:], xt[:, a, :], AF.Square, ["xt"], ["sqj", "ssq"], accum=ssq[:, a:a + 1])
                ts("dve", rstd[:, 0:4], ssq[:, 0:4], 1.0 / D, EPS, ALU.mult, ALU.add, ["ssq"], ["rstd"])
                act(rstd[:, 0:4], rstd[:, 0:4], AF.Sqrt, ["rstd"], ["rstd"])
                P.c("dve", lambda h: h.reciprocal(out=rstd[:, 0:4], in_=rstd[:, 0:4]), ["rstd"], ["rstd"])
                if os.environ.get('NOXN'):
                    continue
                XNV = int(os.environ.get('XNV', 0))
                for a in range(1 if XNV == 1 else 4):
                    if XNV == 2:
                        ts("dve", sqj[:], xt[:, a, :], rstd[:, a:a + 1], None, ALU.mult, None, ["xt", "rstd"], ["xn"])
                    elif XNV == 3:
                        ts("dve", xn[:, a, :], xt[:, a, :], 2.0, None, ALU.mult, None, ["xt", "rstd"], ["xn"])
                    elif XNV == 4:
                        act(xn[:, a, :], xt[:, a, :], AF.Identity, ["xt", "rstd"], ["xn"], scale=rstd[:, a:a + 1])
                    else:
                        ts("dve", xn[:, a, :], xt[:, a, :], rstd[:, a:a + 1], None, ALU.mult, None, ["xt", "rstd"], ["xn"])
                if P1S < 0:
                    continue
                for kt in range(8):
                    tp, tk = tpp[kt % 2], "tpp0"
                    tpg([(tp[:, a * 128:(a + 1) * 128], xn[:, a, kt * 128:(kt + 1) * 128], ident_b[:]) for a in range(4)],
                        ["xn", "ident_b", tk], [tk])
                    if kt % 2 == 0:
                        ts("dve", hT[:, kt, :], tp[:, :], gn[:, kt:kt + 1], None, ALU.mult, None, [tk, "gn"], ["hT"])
                    else:
                        act(hT[:, kt, :], tp[:, :], AF.Identity, [tk, "gn"], ["hT"], scale=gn[:, kt:kt + 1])
                if P1S < 2:
                    continue
                pjt, pk = nextpj()
                mmg([dict(out=pjt[0:76, :], lhsT=WFL3[:, kt, :], rhs=hT[:, kt, :], start=(kt == 0), stop=(kt == 7)) for kt in range(8)],
                    ["WFL3", "hT", pk], [pk])
                act(Fe[0:76, :], pjt[0:76, :], AF.Exp, [pk, "negb"], ["tA"], bias=negb[0:76, :], scale=-1.0)
                act(Fe[0:76, :], Fe[0:76, :], AF.Ln, ["tA"], ["tA"], bias=1.0)
                P.c("dve", lambda h: h.tensor_tensor_scan(out=Ff[0:76, :], data0=ones_f[0:76, :], data1=Fe[0:76, :],
                                                          initial=Fcar[0:76, :], op0=ALU.mult, op1=ALU.subtract),
                    ["ones_f", "tA", "Fcar"], ["tB"])
                cp("dve", Fcar[0:76, :], Ff[0:76, TT - 1:TT], ["tB"], ["Fcar"])
                cp("dve", HIb[0:76, :], Ff[0:76, :], ["tB"], ["HIb"])
                stt(R1[0:76, :], HIb[0:76, :], nm1[0:76, :], Ff[0:76, :], ALU.mult, ALU.add, ["HIb", "nm1", "tB"], ["tC"])
                tt("dve", MIDb[0:76, :], Ff[0:76, :], HIb[0:76, :], ALU.subtract, ["tB", "HIb"], ["MIDb"])
                stt(R1[0:76, :], MIDb[0:76, :], nm2[0:76, :], R1[0:76, :], ALU.mult, ALU.add, ["MIDb", "nm2", "tC"], ["tC"])
                cp("dve", spt[0:76, :], R1[0:76, :], ["tC"], [spk])
                if P1S < 3:
                    continue
                wk, wkk = None, None
                for h_ in range(12):
                    if h_ % 8 == 0:
                        n = min(512, 768 - h_ * 64)
                        wk, wkk = wload(WB[:, :, C_K + h_ * 64:C_K + h_ * 64 + n], 8, n)
                    pjt, pk = nextpj()
                    c0 = (h_ % 8) * 64
                    calls = [dict(out=pjt[:, :], lhsT=SELK[:, h_, :], rhs=spt[:, :], start=True, stop=False)]
                    calls += [dict(out=pjt[0:64, :], lhsT=wk[:, kt, c0:c0 + 64], rhs=hT[:, kt, :], start=False, stop=(kt == 7)) for kt in range(8)]
                    mmg(calls, ["SELK", spk, wkk, "hT", pk], [pk])
                    ks, kk = kst[h_ % 2], "kst%d" % (h_ % 2)
                    if h_ % 2 == 0:
                        cp("act", ks[:], pjt[:, :], [pk], [kk])
                        ld(KT[h_, :, it * TT:(it + 1) * TT], ks[:], [kk], [U()], eng="act")
                    else:
                        cp("dve", ks[:], pjt[:, :], [pk], [kk])
                        ld(KT[h_, :, it * TT:(it + 1) * TT], ks[:], [kk], [U()], eng="act")
                if P1S < 4:
                    continue
                if own:
                    for h_ in range(12):
                        if h_ % 8 == 0:
                            n = min(512, 768 - h_ * 64)
                            wk, wkk = wload(WB[:, :, C_Q + h_ * 64:C_Q + h_ * 64 + n], 8, n)
                        pjt, pk = nextpj()
                        c0 = (h_ % 8) * 64
                        calls = [dict(out=pjt[:, :], lhsT=SELQ[:, h_, :], rhs=spt[:, :], start=True, stop=False)]
                        calls += [dict(out=pjt[0:64, :], lhsT=wk[:, kt, c0:c0 + 64], rhs=hT[:, kt, :], start=False, stop=(kt == 7)) for kt in range(8)]
                        mmg(calls, ["SELQ", spk, wkk, "hT", pk], [pk])
                        ks, kk = kst[h_ % 2], "kst%d" % (h_ % 2)
                        act(ks[:], pjt[:, :], AF.Identity, [pk, "qscale"], [kk], scale=qscale[:, :])
                        ld(QT[h_, :, ot * TT:(ot + 1) * TT], ks[:], [kk], [U()], eng="act")
                if P1S < 5:
                    continue
                wv1, wv1k = wload(WB[:, :, C_V:C_V + 512], 8, 512)
                wv2, wv2k = wload(WB[:, :, C_V + 512:C_V + 768], 8, 256)
                for a in range(4):
                    p1, p1k = nextpj()
                    p2, p2k = nextpj()
                    mmg([dict(out=p1[:, :], lhsT=hT[:, kt, a * 128:(a + 1) * 128], rhs=wv1[:, kt, :], start=(kt == 0), stop=(kt == 7)) for kt in range(8)],
                        ["hT", wv1k, p1k], [p1k])
                    mmg([dict(out=p2[:, 0:256], lhsT=hT[:, kt, a * 128:(a + 1) * 128], rhs=wv2[:, kt, :], start=(kt == 0), stop=(kt == 7)) for kt in range(8)],
                        ["hT", wv2k, p2k], [p2k])
                    cp("act", Vst[:, a, 0:8, 0:64], p1[:, :].rearrange("p (h d) -> p h d", h=8), [p1k], ["Vst"])
                    cp("dve", Vst[:, a, 8:12, 0:64], p2[:, 0:256].rearrange("p (h d) -> p h d", h=4), [p2k], ["Vst"])
                ld(VS[4 * it:4 * it + 4].rearrange("a p f -> p a f"), Vst[:].rearrange("p a h d -> p a (h d)"), ["Vst"], [U()], eng="act")
                if P1S < 6:
                    continue
                for o in range(6):
                    if o % 4 == 0:
                        n = min(512, 768 - o * 128)
                        wk, wkk = wload(WB[:, :, C_U + o * 128:C_U + o * 128 + n], 8, n)
                    pjt, pk = nextpj()
                    c0 = (o % 4) * 128
                    mmg([dict(out=pjt[:, :], lhsT=wk[:, kt, c0:c0 + 128], rhs=hT[:, kt, :], start=(kt == 0), stop=(kt == 7)) for kt in range(8)],
                        [wkk, "hT", pk], [pk])
                    cp("act" if o % 2 == 0 else "dve", uT[:, o, :], pjt[:, :], [pk], ["uT"])
                if P1S < 7:
                    continue
                s5slots = []
                for o in range(6):
                    bd_t, wz_t, wy_t = s5w[o % 2]
                    sk = "s5w%d" % (o % 2)
                    ld(wz_t[:], WZS[o], [], [sk + "z"])
                    for q4 in range(4):
                        calls = []
                        for ri in range(2):
                            for s in range(8):
                                calls.append(dict(out=zpsb[q4][:, ri * 64:(ri + 1) * 64], lhsT=wz_t[32 * q4:32 * q4 + 32, s, ri, :],
                                                  rhs=uT[32 * q4:32 * q4 + 32, o, :].rearrange("p (c s) -> p c s", s=8)[:, :, s],
                                                  start=(s == 0), stop=(s == 7), tile_position=(32 * q4, 0)))
                        mmg(calls, [sk + "z", "uT", "zps%d" % q4], ["zps%d" % q4])
                        cp("act" if q4 % 2 == 0 else "dve", Z[:, 4 * o + q4, :, :], zpsb[q4][:, 0:128].rearrange("p (i c) -> p i c", i=2), ["zps%d" % q4], ["Z"])
                if P1S < 8:
                    continue
                if it > 0:
                    cp("pool", TB[:, :, :, 0], TB[:, :, :, 8], ["TB"], ["TB"])
                Zv = Z[:].rearrange("p r i (b j) -> p r i b j", j=8)

                def cmac(dre, dim_, sre, sim, mr, mi, T, key):
                    tt("pool", T[0], mr, sre, ALU.mult, [key, "TB"], ["RT0"])
                    tt("pool", T[1], mi, sim, ALU.mult, [key, "TB"], ["RT1"])
                    tt("pool", T[0], T[0], T[1], ALU.subtract, ["RT0", "RT1"], ["RT0"])
                    tt("pool", T[2], mr, sim, ALU.mult, [key, "TB"], ["RT2"])
                    tt("pool", T[3], mi, sre, ALU.mult, [key, "TB"], ["RT3"])
                    tt("pool", T[2], T[2], T[3], ALU.add, ["RT2", "RT3"], ["RT2"])
                    tt("pool", dre, dre, T[0], ALU.add, ["RT0", key], [key])
                    tt("pool", dim_, dim_, T[2], ALU.add, ["RT2", key], [key])
                RTv = [t[:] for t in RT]
                RSv = [t[:] for t in RS]
                for j in range(1, 8):
                    cmac(Zv[:, :, 0, :, j], Zv[:, :, 1, :, j], Zv[:, :, 0, :, j - 1], Zv[:, :, 1, :, j - 1],
                         A1R[:].unsqueeze(2).to_broadcast([128, 24, 8]), A1I[:].unsqueeze(2).to_broadcast([128, 24, 8]), RTv, "Z")
                for b_ in range(8):
                    cp("pool", TB[:, :, :, b_ + 1], Zv[:, :, :, b_, 7], ["Z", "TB"], ["TB"])
                    cmac(TB[:, :, 0, b_ + 1], TB[:, :, 1, b_ + 1], TB[:, :, 0, b_], TB[:, :, 1, b_], A8R[:], A8I[:], RSv, "TB")
                for j in range(8):
                    cmac(Zv[:, :, 0, :, j], Zv[:, :, 1, :, j], TB[:, :, 0, 0:8], TB[:, :, 1, 0:8],
                         APR[:, j, :].unsqueeze(2).to_broadcast([128, 24, 8]), API[:, j, :].unsqueeze(2).to_broadcast([128, 24, 8]), RTv, "Z")
                if own:
                    cp("pool", Hb[:, :, :, 0], TB[:, :, :, 0], ["TB"], ["Hb"])
                    cp("pool", Hb[:, :, :, 1:65], Z[:, :, :, :], ["Z"], ["Hb"])
                if not own:
                    continue
                if P1S < 9:
                    continue
                for o in range(6):
                    if o % 4 == 0:
                        n = min(512, 768 - o * 128)
                        wk, wkk = wload(WB[:, :, C_GS + o * 128:C_GS + o * 128 + n], 8, n)
                    pjt, pk = nextpj()
                    c0 = (o % 4) * 128
                    mmg([dict(out=pjt[:, :], lhsT=wk[:, kt, c0:c0 + 128], rhs=hT[:, kt, :], start=(kt == 0), stop=(kt == 7)) for kt in range(8)],
                        [wkk, "hT", pk], [pk])
                    act(SGS[:, o, :], pjt[:, :], AF.Silu, [pk], ["SGS"])
                if P1S < 10:
                    continue
                for o in range(6):
                    bd_t, wz_t, wy_t = s5w[o % 2]
                    sk = "s5w%d" % (o % 2)
                    ld(bd_t[:], BDS[o], [], [sk + "b"])
                    ld(wy_t[:], WYS[o], [], [sk + "y"])
                    yv = yps[:, :].rearrange("p (c s) -> p c s", s=8)
                    uv = uT[:, o, :].rearrange("p (c s) -> p c s", s=8)
                    calls = []
                    for j in range(8):
                        calls.append(dict(out=yv[:, :, j:8], lhsT=bd_t[:, j, :], rhs=uv[:, :, 0:8 - j], start=(j == 0), stop=False))
                    for q4 in range(4):
                        for tau in range(8):
                            for ri in range(2):
                                last = (q4 == 3 and tau == 7 and ri == 1)
                                calls.append(dict(out=yv[32 * q4:32 * q4 + 32, :, tau], lhsT=wy_t[:, q4, tau, ri, :],
                                                  rhs=Hb[:, 4 * o + q4, ri, 0:64], start=False, stop=last, tile_position=(0, 32 * q4)))
                    mmg(calls, [sk + "b", sk + "y", "uT", "Hb", "yps"], ["yps"])
                    act(tA[:], yps[:, :], AF.Square, ["yps"], ["tA"])
                    ts("dve", tA[:], tA[:], 0.044715, 1.0, ALU.mult, ALU.add, ["tA"], ["tA"])
                    tt("dve", tB[:], tA[:], yps[:, :], ALU.mult, ["tA", "yps"], ["tB"])
                    act(tA[:], tB[:], AF.Sigmoid, ["tB"], ["tA"], scale=1.5957691216057308)
                    tt("dve", YG[:, o, :], tA[:], yps[:, :], ALU.mult, ["tA", "yps"], ["YG"])
                if P1S < 11:
                    continue
                for half in range(2):
                    wg, wgk = wload(WGLU[:, :, half * 384:(half + 1) * 384], 6, 384)
                    for c3 in range(3):
                        cb = half * 3 + c3
                        pjt, pk = nextpj()
                        mmg([dict(out=pjt[:, :], lhsT=wg[:, kt, c3 * 128:(c3 + 1) * 128], rhs=YG[:, kt, :], start=(kt == 0), stop=(kt == 5)) for kt in range(6)],
                            [wgk, "YG", pk], [pk])
                        act(tA[:], pjt[:, :], AF.Sigmoid, [pk, "bglu"], ["tA"], bias=bglu[:, cb:cb + 1])
                        tt("dve", tB[:], tA[:], YG[:, cb, :], ALU.mult, ["tA", "YG"], ["tB"])
                        tt("dve", YS[:, cb, :], tB[:], SGS[:, cb, :], ALU.mult, ["tB", "SGS"], ["SGS"])
                if P1S < 12:
                    continue
                wk, wkk = wload(WB[:, :, C_QM:C_QM + 512], 8, 512)
                for hd in range(4):
                    pjt, pk = nextpj()
                    mmg([dict(out=pjt[:, :], lhsT=wk[:, kt, hd * 128:(hd + 1) * 128], rhs=hT[:, kt, :], start=(kt == 0), stop=(kt == 7)) for kt in range(8)],
                        [wkk, "hT", pk], [pk])
                    act(QM[:, hd, :], pjt[:, :], AF.Identity, [pk], ["QM"], scale=128.0 ** -0.5)
                wk, wkk = wload(WB[:, :, C_GM:C_GM + 512], 8, 512)
                for hd in range(4):
                    pjt, pk = nextpj()
                    mmg([dict(out=pjt[:, :], lhsT=wk[:, kt, hd * 128:(hd + 1) * 128], rhs=hT[:, kt, :], start=(kt == 0), stop=(kt == 7)) for kt in range(8)],
                        [wkk, "hT", pk], [pk])
                    act(SGM[:, hd, :], pjt[:, :], AF.Silu, [pk], ["SGM"])
                for hd in range(4):
                    for mt in range(2):
                        pjt, pk = nextpj()
                        mmg([dict(out=pjt[:, :], lhsT=MKT[:, hd, mt * 128:(mt + 1) * 128], rhs=QM[:, hd, :], start=True, stop=True)],
                            ["MKT", "QM", pk], [pk])
                        act(PmT[:, mt, :], pjt[:, :], AF.Exp, [pk], ["PmT%d" % mt])
                    pjt, pk = nextpj()
                    mmg([dict(out=pjt[:, :], lhsT=MV[:, mt, hd * 128:(hd + 1) * 128], rhs=PmT[:, mt, :], start=(mt == 0), stop=(mt == 1)) for mt in range(2)],
                        ["MV", "PmT0", "PmT1", pk], [pk])
                    axp, axk = nextpj()
                    mmg([dict(out=axp[:, :], lhsT=ones_b[:, :], rhs=PmT[:, mt, :], start=(mt == 0), stop=(mt == 1)) for mt in range(2)],
                        ["ones_b", "PmT0", "PmT1", axk], [axk])
                    P.c("dve", lambda h, axp=axp: h.reciprocal(out=tC[:], in_=axp[:, :]), [axk], ["tC"])
                    tt("dve", tB[:], tC[:], pjt[:, :], ALU.mult, ["tC", pk], ["tB"])
                    tt("dve", YM[:, hd, :], tB[:], SGM[:, hd, :], ALU.mult, ["tB", "SGM"], ["SGM"])
                if P1S < 13:
                    continue
                for cb in range(8):
                    si = wn[0] % NWS
                    wn[0] += 1
                    wfull = wsl[si][:, 0:26 * 128].rearrange("p (k c) -> p k c", k=26)
                    wpsk = wpmk = wg1k = wg2k = "ws%d" % si
                    wps_t, wpm_t, wg1, wg2 = wfull[:, 0:6, :], wfull[:, 6:10, :], wfull[:, 10:18, :], wfull[:, 18:26, :]
                    ld(wps_t, WPS[:, :, cb * 128:(cb + 1) * 128], [], [wpsk])
                    ld(wpm_t, WPM[:, :, cb * 128:(cb + 1) * 128], [wpsk], [wpsk])
                    ld(wg1, WB[:, :, C_GL + 1024 + cb * 128:C_GL + 1024 + (cb + 1) * 128], [wpsk], [wpsk])
                    ld(wg2, WB[:, :, C_GL + 2048 + cb * 128:C_GL + 2048 + (cb + 1) * 128], [wpsk], [wpsk])
                    c0 = 0
                    m0, m0k = M0[cb % 2], "M0_%d" % (cb % 2)
                    pjt, pk = nextpj()
                    mmg([dict(out=pjt[:, :], lhsT=wg1[:, kt, c0:c0 + 128], rhs=hT[:, kt, :], start=(kt == 0), stop=(kt == 7)) for kt in range(8)],
                        [wg1k, "hT", pk], [pk])
                    act(tA[:], pjt[:, :], AF.Sigmoid, [pk, "bm"], ["tA"], bias=bm[:, 8 + cb:9 + cb])
                    pjt, pk = nextpj()
                    mmg([dict(out=pjt[:, :], lhsT=wps_t[:, kt, c0:c0 + 128], rhs=YS[:, kt, :], start=(kt == 0), stop=(kt == 5)) for kt in range(6)],
                        [wpsk, "SGS", pk], [pk])
                    tt("dve", m0[:], tA[:], pjt[:, :], ALU.mult, ["tA", pk], [m0k])
                    pjt, pk = nextpj()
                    mmg([dict(out=pjt[:, :], lhsT=wg2[:, kt, c0:c0 + 128], rhs=hT[:, kt, :], start=(kt == 0), stop=(kt == 7)) for kt in range(8)],
                        [wg2k, "hT", pk], [pk])
                    act(tB[:], pjt[:, :], AF.Sigmoid, [pk, "bm"], ["tB"], bias=bm[:, 16 + cb:17 + cb])
                    pjt, pk = nextpj()
                    mmg([dict(out=pjt[:, :], lhsT=wpm_t[:, kt, c0:c0 + 128], rhs=YM[:, kt, :], start=(kt == 0), stop=(kt == 3)) for kt in range(4)],
                        [wpmk, "SGM", pk], [pk])
                    tt("dve", tC[:], tB[:], pjt[:, :], ALU.mult, ["tB", pk], ["tC"])
                    tt("dve", m0[:], m0[:], tC[:], ALU.add, [m0k, "tC"], [m0k])
                    ld(M0S[ot, :, cb, :], m0[:], [m0k], [U()], eng="act")
                if P1S < 14:
                    continue
                for cb in range(8):
                    if cb % 4 == 0:
                        wg0, wg0k = wload(WB[:, :, C_GL + cb * 128:C_GL + cb * 128 + 512], 8, 512)
                    c0 = (cb % 4) * 128
                    pjt, pk = nextpj()
                    mmg([dict(out=pjt[:, :], lhsT=wg0[:, kt, c0:c0 + 128], rhs=hT[:, kt, :], start=(kt == 0), stop=(kt == 7)) for kt in range(8)],
                        [wg0k, "hT", pk], [pk])
                    g0, g0k = G0st[cb % 2], "G0st%d" % (cb % 2)
                    act(g0[:], pjt[:, :], AF.Sigmoid, [pk, "bm"], [g0k], bias=bm[:, cb:cb + 1])
                    ld(G0S[ot, :, cb, :], g0[:], [g0k], [U()], eng="act")
                for o in range(6):
                    if o % 4 == 0:
                        n = min(512, 768 - o * 128)
                        wk, wkk = wload(WB[:, :, C_GF + o * 128:C_GF + o * 128 + n], 8, n)
                    pjt, pk = nextpj()
                    c0 = (o % 4) * 128
                    mmg([dict(out=pjt[:, :], lhsT=wk[:, kt, c0:c0 + 128], rhs=hT[:, kt, :], start=(kt == 0), stop=(kt == 7)) for kt in range(8)],
                        [wkk, "hT", pk], [pk])
                    ks, kk = kst[o % 2], "kst%d" % (o % 2)
                    act(ks[:], pjt[:, :], AF.Silu, [pk], [kk])
                    ld(SGFS[ot, :, o, :], ks[:], [kk], [U()], eng="act")
            P.emit()

        if upto < 3:
            return nc
        S2 = ExitStack()
        with S2:
            Kh = [sb(S2, "Kh%d" % i, [128, LTOK], BF16) for i in range(2)]
            Vh = [sb(S2, "Vh%d" % i, [128, 64, 65], BF16) for i in range(2)]
            Qh = [sb(S2, "Qh%d" % i, [128, 4096], BF16) for i in range(2)]
            NPT = 4
            Pt = [sb(S2, "Pt%d" % i, [128, TT], BF16) for i in range(NPT)]
            Osb = sb(S2, "Osb", [65, TT], F32)
            Rd = sb(S2, "Rd", [64, TT], F32)
            Yst = [sb(S2, "Yst%d" % i, [64, TT], BF16) for i in range(2)]
            SEL = sb(S2, "SEL", [65, 64], F32)
            sps = [ps(S2, "sps%d" % i, [128, 512], F32) for i in range(4)]
            ops_ = [ps(S2, "ops%d" % i, [128, 512], F32) for i in range(2)]
            dps = ps(S2, "dps", [128, 512], F32)
            mset("pool", SEL[:], 0.0, ["SEL"])
            mset("pool", SEL[64:65, :], 1.0, ["SEL"])
            LA = 2
            its = []
            for h_ in range(12):
                for qt in range(8):
                    nkb = 32 + 4 * qt + 4
                    order = list(range(32 + 4 * qt, nkb)) + list(range(0, 32 + 4 * qt))
                    for n_, kb in enumerate(order):
                        its.append((h_, qt, kb, n_ == 0, n_ == nkb - 1))

            def head_loads(h_):
                hs = h_ % 2
                kh, vh, qh = Kh[hs], Vh[hs], Qh[hs]
                kk, vk, qk = "Kh%d" % hs, "Vh%d" % hs, "Qh%d" % hs
                ld(qh[:], QT[h_], [], [qk])
                for c4 in range(4):
                    ld(kh[:, c4 * 2048:(c4 + 1) * 2048], KT[h_, :, c4 * 2048:(c4 + 1) * 2048], [qk], [kk])
                for c4 in range(4):
                    ld(vh[:, c4 * 16:(c4 + 1) * 16, :], VS[c4 * 16:(c4 + 1) * 16, :, h_ * 65:(h_ + 1) * 65].rearrange("b p d -> p b d"), [kk], [vk])

            def geom(i):
                h_, qt, kb, first, last = its[i]
                diag = kb - (32 + 4 * qt)
                c0 = max(0, diag) * 128
                return h_, qt, kb, first, last, diag, c0

            def emit_qk(i):
                h_, qt, kb, first, last, diag, c0 = geom(i)
                hs = h_ % 2
                st, sk = sps[i % 4], "sps%d" % (i % 4)
                mmg([dict(out=st[:, c0:TT], lhsT=Kh[hs][:, kb * 128:(kb + 1) * 128], rhs=Qh[hs][:, qt * TT + c0:(qt + 1) * TT], start=True, stop=True)],
                    ["Kh%d" % hs, "Qh%d" % hs, sk], [sk])

            pend = []

            def emit_rest(i):
                h_, qt, kb, first, last, diag, c0 = geom(i)
                hs = h_ % 2
                st, sk = sps[i % 4], "sps%d" % (i % 4)
                pt, ptk = Pt[i % NPT], "Pt%d" % (i % NPT)
                op_t, opk = ops_[(h_ * 8 + qt) % 2], "ops%d" % ((h_ * 8 + qt) % 2)
                act(pt[:, c0:TT], st[:, c0:TT], AF.Exp, [sk], [ptk])
                if diag >= 0:
                    tt("dve", pt[:, c0:c0 + 128], pt[:, c0:c0 + 128], maskT[:], ALU.mult, [ptk, "maskT"], [ptk])
                mmg([dict(out=op_t[0:65, c0:TT], lhsT=Vh[hs][:, kb, :], rhs=pt[:, c0:TT], start=first, stop=last)],
                    ["Vh%d" % hs, ptk, opk], [opk])
                if last:
                    cp("dve", Osb[:], op_t[0:65, :], [opk], ["Osb"])
                    pend.append((i + 3, h_, qt))

            def emit_norm(h_, qt):
                mmg([dict(out=dps[0:64, :], lhsT=SEL[:, :], rhs=Osb[:, :], start=True, stop=True)], ["SEL", "Osb", "dps"], ["dps"])
                P.c("dve", lambda h: h.reciprocal(out=Rd[:], in_=dps[0:64, :]), ["dps"], ["Rd"])
                ys, ysk = Yst[qt % 2], "Yst%d" % (qt % 2)
                tt("dve", ys[:], Osb[0:64, :], Rd[:], ALU.mult, ["Osb", "Rd"], [ysk])
                ld(YFS[h_, :, qt * TT:(qt + 1) * TT], ys[:], [ysk], [U()], eng="sp")

            head_loads(0)
            head_loads(1)
            NI = len(its)
            for i in range(min(LA, NI)):
                emit_qk(i)
            for i in range(NI):
                if i + LA < NI:
                    emit_qk(i + LA)
                emit_rest(i)
                while pend and pend[0][0] <= i:
                    _, ph, pq = pend.pop(0)
                    emit_norm(ph, pq)
                if its[i][4] and its[i][1] == 7 and its[i][0] + 2 < 12:
                    head_loads(its[i][0] + 2)
            while pend:
                _, ph, pq = pend.pop(0)
                emit_norm(ph, pq)
            P.emit()

        if upto < 4:
            return nc
        S3 = ExitStack()
        with S3:
            gfin = sb(S3, "gfin", [128, D], F32)
            ld(gfin[:], g_final.to_broadcast([128, D]), [], ["gfin"])
            wpf = sb(S3, "wpf", [128, 6, D], BF16)
            wo = sb(S3, "wo", [128, 8, D], BF16)
            yf = sb(S3, "yf", [128, 6, TT], BF16)
            sgf = sb(S3, "sgf", [128, 6, TT], BF16)
            yfg = sb(S3, "yfg", [128, 6, TT], BF16)
            g0t = sb(S3, "g0t", [128, 8, TT], BF16)
            m0t = sb(S3, "m0t", [128, 8, TT], F32)
            mg = sb(S3, "mg", [128, 8, TT], BF16)
            t3 = sb(S3, "t3", [128, TT], F32)
            x3 = sb(S3, "x3", [128, 4, D], F32)
            o3 = [sb(S3, "o3_%d" % i, [128, D], F32) for i in range(2)]
            j3 = sb(S3, "j3", [128, D], BF16)
            pf = [ps(S3, "pf%d" % i, [128, 512], F32) for i in range(2)]
            po = [ps(S3, "po%d" % i, [128, 512], F32) for i in range(4)]
            ld(wpf[:], WPF, [], ["wpf"])
            ld(wo[:], WOUT, [], ["wo"])
            for ot in range(8):
                for a2 in range(2):
                    ld(yf[a2 * 64:(a2 + 1) * 64, :, :], YFS[:, :, ot * TT:(ot + 1) * TT].rearrange("(c a) d t -> a d c t", a=2)[a2], ["yf"], ["yf"])
                ld(sgf[:], SGFS[ot], [], ["sgf"])
                ld(g0t[:], G0S[ot], [], ["g0t"])
                ld(m0t[:], M0S[ot], [], ["m0t"])
                ld(x3[:], xa[(OWN0 + ot) * TT:(OWN0 + ot + 1) * TT, :].rearrange("(a p) d -> p a d", p=128), [], ["x3"])
                for c in range(6):
                    tt("pool" if c % 2 else "dve", yfg[:, c, :], yf[:, c, :], sgf[:, c, :], ALU.mult, ["yf", "sgf"], ["yfg"])
                for cb in range(8):
                    pt_, pk = pf[cb % 2], "pf%d" % (cb % 2)
                    mmg([dict(out=pt_[:, :], lhsT=wpf[:, kt, cb * 128:(cb + 1) * 128], rhs=yfg[:, kt, :], start=(kt == 0), stop=(kt == 5)) for kt in range(6)],
                        ["wpf", "yfg", pk], [pk])
                    tt("dve", t3[:], g0t[:, cb, :], pt_[:, :], ALU.mult, ["g0t", pk], ["t3"])
                    tt("dve", mg[:, cb, :], t3[:], m0t[:, cb, :], ALU.add, ["t3", "m0t"], ["mg"])
                for a in range(4):
                    ob, obk = o3[a % 2], "o3_%d" % (a % 2)
                    for hf in range(2):
                        pt_, pk = po[(2 * a + hf) % 4], "po%d" % ((2 * a + hf) % 4)
                        mmg([dict(out=pt_[:, :], lhsT=mg[:, kt, a * 128:(a + 1) * 128], rhs=wo[:, kt, hf * 512:(hf + 1) * 512], start=(kt == 0), stop=(kt == 7)) for kt in range(8)],
                            ["mg", "wo", pk], [pk])
                        tt("dve", ob[:, hf * 512:(hf + 1) * 512], pt_[:, :], x3[:, a, hf * 512:(hf + 1) * 512], ALU.add, [pk, "x3"], [obk])
                    act(j3[:], ob[:], AF.Square, [obk], ["j3", "ssq"], accum=ssq[:, a:a + 1])
                    ts("dve", rstd[:, a:a + 1], ssq[:, a:a + 1], 1.0 / D, EPS, ALU.mult, ALU.add, ["ssq"], ["rstd"])
                    act(rstd[:, a:a + 1], rstd[:, a:a + 1], AF.Sqrt, ["rstd"], ["rstd"])
                    P.c("dve", lambda h, a=a: h.reciprocal(out=rstd[:, a:a + 1], in_=rstd[:, a:a + 1]), ["rstd"], ["rstd"])
                    stt(ob[:], ob[:], rstd[:, a:a + 1], gfin[:], ALU.mult, ALU.mult, [obk, "rstd", "gfin"], [obk])
                    ld(yout[ot * TT + a * 128:ot * TT + (a + 1) * 128, :], ob[:], [obk], [U()], eng="act")
            P.emit()
    return nc


_NC = None


def kernel(**inputs):
    global _NC
    if _NC is None:
        _NC = build_nc()
    nc = _NC
    x = np.ascontiguousarray(np.asarray(inputs["x"], dtype=np.float32))
    mem = np.asarray(inputs["mem"], dtype=np.float32)
    in_maps = []
    for c in range(8):
        b, half = c // 2, c % 2
        if half == 0:
            xa = np.concatenate([np.zeros((4096, D), np.float32), x[b, 0:4096]], axis=0)
        else:
            xa = x[b]
        km = np.zeros((NT, TT), dtype=ml_dtypes.bfloat16)
        if half == 0:
            km[0:OWN0, :] = -30000.0
        m = {"xa": np.ascontiguousarray(xa), "kmrow": km, "mem": np.ascontiguousarray(mem[b])}
        for name in ("g_norm", "g_mem_norm", "w_in", "b_forget", "b_merge", "w_mem_kv", "lam_re", "lam_im", "log_step",
                     "s5_b_re", "s5_b_im", "s5_c_re", "s5_c_im", "s5_d", "w_glu", "b_glu", "w_proj_fox", "w_proj_s5",
                     "w_proj_mem", "w_out"):
            a = np.asarray(inputs[name], dtype=np.float32)
            a = a[0]
            if a.ndim == 1:
                a = a[None, :]
            m[name] = np.ascontiguousarray(a)
        m["g_final"] = np.ascontiguousarray(np.asarray(inputs["g_final"], dtype=np.float32)[None, :])
        in_maps.append(m)
    res = run_bass_kernel_spmd(nc, in_maps, core_ids=list(range(8)))
    out = np.zeros((4, 8192, D), dtype=np.float32)
    for c in range(8):
        b, half = c // 2, c % 2
        out[b, half * 4096:(half + 1) * 4096] = np.asarray(res.results[c]["yout"], dtype=np.float32)
    return out
```

```python
import math
from contextlib import ExitStack

import ml_dtypes
import numpy as np
import concourse.bass as bass
import concourse.mybir as mybir
from concourse.bass_utils import run_bass_kernel_spmd

F32 = mybir.dt.float32
BF16 = mybir.dt.bfloat16
I32 = mybir.dt.int32
AF = mybir.ActivationFunctionType
ALU = mybir.AluOpType

ENGS = ("pe", "act", "dve", "pool", "sp")
import os as _os0
NOSELF = tuple(x for x in _os0.environ.get('NOSELF', '').split(',') if x)
D = 1024
NIN = 8716
TT = 512
NT = 16
LTOK = 8192
OWN0 = 8
C_Q, C_K, C_V, C_FL, C_GF, C_U, C_GS, C_QM, C_GM, C_GL = 0, 768, 1536, 2304, 2316, 3084, 3852, 4620, 5132, 5644
EPS = 1e-6
TWO_PI = 2.0 * math.pi


class Op:
    __slots__ = ("eng", "fn", "deps", "needs_inc", "sigval", "is_dma", "lane", "laneval")

    def __init__(self, eng, fn, is_dma=False):
        self.eng = eng
        self.fn = fn
        self.deps = []
        self.needs_inc = False
        self.sigval = None
        self.is_dma = is_dma
        self.lane = None
        self.laneval = None


class Prog:
    def __init__(self, nc, es, n_lanes=6):
        self.nc = nc
        self.n_lanes = n_lanes
        self.esem = {e: es.enter_context(nc.semaphore("s_" + e)) for e in ENGS}
        self.lsem = {}
        for e in ("sp", "act", "pool"):
            for i in range(n_lanes):
                self.lsem[(e, i)] = es.enter_context(nc.semaphore("l_%s%d" % (e, i)))
        self.ecnt = {e: 0 for e in ENGS}
        self.lane_rr = {e: 0 for e in ENGS}
        self.lane_cnt = {k: 0 for k in self.lsem}
        self.barrier = {}
        self._reset()

    def _reset(self):
        self.ops = {e: [] for e in ENGS}
        self.last_w = {}
        self.readers = {}
        self.lane_last = {}

    def _add(self, op, reads, writes):
        deps = []
        for r in reads:
            w = self.last_w.get(r)
            if w is not None:
                deps.append(w)
        for r in writes:
            w = self.last_w.get(r)
            if w is not None:
                deps.append(w)
            deps.extend(self.readers.get(r, ()))
        seen = set(id(d) for d in op.deps)
        for d in deps:
            if d is op or id(d) in seen:
                continue
            if op.eng == "pe" and d.eng == "pe" and not d.is_dma and not op.is_dma:
                continue
            if (not d.is_dma) and (not op.is_dma) and op.eng == d.eng and op.eng in NOSELF:
                continue
            seen.add(id(d))
            op.deps.append(d)
            if not d.is_dma:
                d.needs_inc = True
        for r in reads:
            self.readers.setdefault(r, []).append(op)
        for r in writes:
            self.last_w[r] = op
            self.readers[r] = []
        self.ops[op.eng].append(op)
        return op

    def c(self, eng, fn, reads=(), writes=()):
        return self._add(Op(eng, fn), reads, writes)

    def dma(self, eng, fn, reads=(), writes=()):
        op = Op(eng, fn, is_dma=True)
        lane = (eng, self.lane_rr[eng] % self.n_lanes)
        self.lane_rr[eng] += 1
        prev = self.lane_last.get(lane)
        self.lane_cnt[lane] += 1
        op.lane = lane
        op.laneval = 16 * self.lane_cnt[lane]
        if prev is not None:
            op.deps.append(prev)
        self.lane_last[lane] = op
        return self._add(op, reads, writes)

    def emit(self):
        nc = self.nc
        for e in ENGS:
            last = None
            for op in self.ops[e]:
                if not op.is_dma:
                    last = op
            if last is not None:
                last.needs_inc = True
            for op in self.ops[e]:
                if (not op.is_dma) and op.needs_inc:
                    self.ecnt[e] += 1
                    op.sigval = self.ecnt[e]
        barrier = dict(self.barrier)
        esem, lsem = self.esem, self.lsem

        def tok(d):
            if d.is_dma:
                return lsem[d.lane], d.laneval
            return esem[d.eng], d.sigval

        def run(e, h):
            waited = {}
            for s, v in barrier.values():
                if v > 0:
                    h.wait_ge(s, v)
                    waited[id(s)] = v
            for op in self.ops[e]:
                for d in op.deps:
                    s, v = tok(d)
                    if waited.get(id(s), 0) >= v:
                        continue
                    waited[id(s)] = v
                    h.wait_ge(s, v)
                inst = op.fn(h)
                if op.is_dma:
                    inst.then_inc(lsem[op.lane], 16)
                elif op.needs_inc:
                    inst.then_inc(esem[e], 1)
                import os as _os
                if _os.environ.get('DBGPRINT'):
                    print("OP", e, "dma" if op.is_dma else "c", "lane=%s val=%s" % (op.lane, op.laneval) if op.is_dma else "sig=%s" % op.sigval,
                          "deps=", [((d.lane, d.laneval) if d.is_dma else (d.eng, d.sigval)) for d in op.deps], type(inst).__name__)
            for lane, cnt in self.lane_cnt.items():
                if lane[0] == e and cnt > 0:
                    h.wait_ge(lsem[lane], 16 * cnt)

        with nc.Block() as block:
            @block.tensor
            def _(h):
                run("pe", h)

            @block.scalar
            def _(h):
                run("act", h)

            @block.vector
            def _(h):
                run("dve", h)

            @block.gpsimd
            def _(h):
                run("pool", h)

            @block.sync
            def _(h):
                run("sp", h)

        self.barrier = {}
        for e in ENGS:
            self.barrier[("e", e)] = (esem[e], self.ecnt[e])
        for lane, cnt in self.lane_cnt.items():
            self.barrier[("l", lane)] = (lsem[lane], 16 * cnt)
        self._reset()


def build_nc(dbg=False, upto=9):
    nc = bass.Bass("TRN2", target_bir_lowering=False)

    def din(name, shape, dt=F32):
        return nc.dram_tensor(name, list(shape), dt, kind="ExternalInput").ap()

    def dscr(name, shape, dt):
        return nc.dram_tensor(name, list(shape), dt, kind=("ExternalOutput" if dbg else "Internal")).ap()

    xa = din("xa", [LTOK, D])
    kmrow = din("kmrow", [NT, TT], BF16)
    mem = din("mem", [256, D])
    g_norm = din("g_norm", [1, D])
    g_mem_norm = din("g_mem_norm", [1, D])
    g_final = din("g_final", [1, D])
    w_in = din("w_in", [D, NIN])
    b_forget = din("b_forget", [1, 12])
    b_merge = din("b_merge", [1, 3072])
    w_mem_kv = din("w_mem_kv", [D, 1024])
    lam_re = din("lam_re", [48, 64])
    lam_im = din("lam_im", [48, 64])
    log_step = din("log_step", [1, 48])
    s5_b_re = din("s5_b_re", [48, 64, 16])
    s5_b_im = din("s5_b_im", [48, 64, 16])
    s5_c_re = din("s5_c_re", [48, 16, 64])
    s5_c_im = din("s5_c_im", [48, 16, 64])
    s5_d = din("s5_d", [1, 768])
    w_glu = din("w_glu", [768, 768])
    b_glu = din("b_glu", [1, 768])
    w_proj_fox = din("w_proj_fox", [768, D])
    w_proj_s5 = din("w_proj_s5", [768, D])
    w_proj_mem = din("w_proj_mem", [512, D])
    w_out = din("w_out", [D, D])
    yout = nc.dram_tensor("yout", [4096, D], F32, kind="ExternalOutput").ap()

    WB = dscr("WB", [128, 8, NIN], BF16)
    WKV = dscr("WKV", [128, 8, 1024], BF16)
    WGLU = dscr("WGLU", [128, 6, 768], BF16)
    WPF = dscr("WPF", [128, 6, D], BF16)
    WPS = dscr("WPS", [128, 6, D], BF16)
    WPM = dscr("WPM", [128, 4, D], BF16)
    WOUT = dscr("WOUT", [128, 8, D], BF16)
    BDS = dscr("BDS", [6, 128, 8, 128], BF16)
    WZS = dscr("WZS", [6, 128, 8, 2, 128], BF16)
    WYS = dscr("WYS", [6, 128, 4, 8, 2, 32], BF16)
    KT = dscr("KT", [12, 128, LTOK], BF16)
    QT = dscr("QT", [12, 128, 4096], BF16)
    VS = dscr("VS", [64, 128, 780], BF16)
    M0S = dscr("M0S", [8, 128, 8, TT], F32)
    G0S = dscr("G0S", [8, 128, 8, TT], BF16)
    SGFS = dscr("SGFS", [8, 128, 6, TT], BF16)
    YFS = dscr("YFS", [12, 64, 4096], BF16)

    es = ExitStack()
    with es:
        P = Prog(nc, es)

        def sb(stack, name, shape, dt):
            return stack.enter_context(nc.sbuf_tensor(name, list(shape), dt))

        def ps(stack, name, shape, dt=F32):
            return stack.enter_context(nc.psum_tensor(name, list(shape), dt))

        def act(out, in_, func, r, w, bias=None, scale=None, accum=None):
            kw = {}
            if bias is not None:
                kw["bias"] = bias
            if scale is not None:
                kw["scale"] = scale
            if accum is not None:
                kw["accum_out"] = accum
            return P.c("act", lambda h: h.activation(out=out, in_=in_, func=func, **kw), r, w)

        def tt(eng, out, in0, in1, op, r, w):
            return P.c(eng, lambda h: h.tensor_tensor(out=out, in0=in0, in1=in1, op=op), r, w)

        def ts(eng, out, in0, s1, s2, op0, op1, r, w):
            if op1 is None:
                return P.c(eng, lambda h: h.tensor_scalar(out=out, in0=in0, scalar1=s1, scalar2=None, op0=op0), r, w)
            return P.c(eng, lambda h: h.tensor_scalar(out=out, in0=in0, scalar1=s1, scalar2=s2, op0=op0, op1=op1), r, w)

        def stt(out, in0, scalar, in1, op0, op1, r, w):
            return P.c("dve", lambda h: h.scalar_tensor_tensor(out=out, in0=in0, scalar=scalar, in1=in1, op0=op0, op1=op1), r, w)

        def cp(eng, out, in_, r, w):
            if eng == "act":
                return P.c("act", lambda h: h.activation(out=out, in_=in_, func=AF.Identity), r, w)
            return P.c(eng, lambda h: h.tensor_copy(out=out, in_=in_), r, w)

        def mset(eng, ap, val, w):
            return P.c(eng, lambda h: h.memset(ap, val), (), w)

        def mmg(calls, r, w):
            def fn(h):
                inst = None
                for kw in calls:
                    inst = h.matmul(**kw)
                return inst
            return P.c("pe", fn, r, w)

        def tpg(calls, r, w):
            def fn(h):
                inst = None
                for (o, i, idn) in calls:
                    inst = h.transpose(o, i, idn)
                return inst
            return P.c("pe", fn, r, w)

        uq = [0]

        def U():
            uq[0] += 1
            return "u%d" % uq[0]

        def ld(out, in_, r, w, eng="sp", slow=False):
            if slow:
                return P.dma(eng, lambda h: h.dma_start(out=out, in_=in_, allow_slow_non_contiguous=True), r, w)
            return P.dma(eng, lambda h: h.dma_start(out=out, in_=in_), r, w)

        def rows_pattern(tile_ap, n, lo, hi, w):
            P.c("pool", lambda h: h.memset(tile_ap, 1.0), (), w)
            P.c("pool", lambda h: h.affine_select(out=tile_ap, in_=tile_ap, pattern=[[0, n]], compare_op=ALU.is_ge,
                                                  fill=0.0, base=-lo, channel_multiplier=1), w, w)
            P.c("pool", lambda h: h.affine_select(out=tile_ap, in_=tile_ap, pattern=[[0, n]], compare_op=ALU.is_ge,
                                                  fill=0.0, base=hi, channel_multiplier=-1), w, w)

        G = ExitStack()
        es.enter_context(G)
        ident_f = sb(G, "ident_f", [128, 128], F32)
        ident_b = sb(G, "ident_b", [128, 128], BF16)
        ones_f = sb(G, "ones_f", [128, 512], F32)
        ones_b = sb(G, "ones_b", [128, 128], BF16)
        maskT = sb(G, "maskT", [128, 128], BF16)
        gn = sb(G, "gn", [128, 8], F32)
        gmn = sb(G, "gmn", [128, 8], F32)
        bm = sb(G, "bm", [128, 24], F32)
        bglu = sb(G, "bglu", [128, 6], F32)
        SELK = sb(G, "SELK", [128, 12, 128], BF16)
        SELQ = sb(G, "SELQ", [128, 12, 128], BF16)
        qscale = sb(G, "qscale", [128, 1], F32)
        negb = sb(G, "negb", [128, 1], F32)
        nm1 = sb(G, "nm1", [128, 1], F32)
        nm2 = sb(G, "nm2", [128, 1], F32)
        WFL3 = sb(G, "WFL3", [128, 8, 76], BF16)
        MKT = sb(G, "MKT", [128, 4, 256], BF16)
        MV = sb(G, "MV", [128, 2, 512], BF16)
        A1R = sb(G, "A1R", [128, 24], F32)
        A1I = sb(G, "A1I", [128, 24], F32)
        A8R = sb(G, "A8R", [128, 24], F32)
        A8I = sb(G, "A8I", [128, 24], F32)
        APR = sb(G, "APR", [128, 8, 24], F32)
        API = sb(G, "API", [128, 8, 24], F32)
        Fcar = sb(G, "Fcar", [128, 1], F32)
        TB = sb(G, "TB", [128, 24, 2, 9], F32)
        ssq = sb(G, "ssq", [128, 8], F32)
        rstd = sb(G, "rstd", [128, 8], F32)

        mset("pool", ones_f[:], 1.0, ["ones_f"])
        mset("pool", ident_f[:], 1.0, ["ident_f"])
        P.c("pool", lambda h: h.affine_select(out=ident_f[:], in_=ident_f[:], pattern=[[-1, 128]], compare_op=ALU.is_equal,
                                              fill=0.0, base=0, channel_multiplier=1), ["ident_f"], ["ident_f"])
        cp("dve", ident_b[:], ident_f[:], ["ident_f"], ["ident_b"])
        cp("dve", ones_b[:], ones_f[:, 0:128], ["ones_f"], ["ones_b"])
        mset("pool", maskT[:], 1.0, ["maskT"])
        P.c("pool", lambda h: h.affine_select(out=maskT[:], in_=maskT[:], pattern=[[1, 128]], compare_op=ALU.is_ge,
                                              fill=0.0, base=0, channel_multiplier=-1), ["maskT"], ["maskT"])
        ld(gn[:], g_norm.rearrange("o (kt p) -> p (o kt)", p=128), [], ["gn"], slow=True)
        ld(gmn[:], g_mem_norm.rearrange("o (kt p) -> p (o kt)", p=128), [], ["gmn"], slow=True)
        ld(bm[:], b_merge.rearrange("o (c p) -> p (o c)", p=128), [], ["bm"], slow=True)
        ld(bglu[:], b_glu.rearrange("o (c p) -> p (o c)", p=128), [], ["bglu"], slow=True)
        mset("pool", qscale[:], 1.0, ["qscale"])
        mset("pool", qscale[0:64, :], 0.125, ["qscale"])
        mset("pool", negb[:], 0.0, ["negb"])
        for base in (0, 32, 64):
            ld(negb[base:base + 12, :], b_forget.rearrange("o h -> h o"), [], ["negb"], slow=True)
        ts("dve", negb[:], negb[:], -1.0, None, ALU.mult, None, ["negb"], ["negb"])
        mset("pool", nm1[:], 0.0, ["nm1"])
        mset("pool", nm1[32:64, :], -1.0, ["nm1"])
        mset("pool", nm1[64:96, :], -1.0, ["nm1"])
        mset("pool", nm2[:], 0.0, ["nm2"])
        mset("pool", nm2[64:96, :], -1.0, ["nm2"])
        mset("pool", Fcar[:], 0.0, ["Fcar"])
        mset("pool", TB[:], 0.0, ["TB"])
        mset("pool", SELK[:], 0.0, ["SELK"])
        mset("pool", SELQ[:], 0.0, ["SELQ"])
        for j, base in enumerate((0, 32, 64)):
            ts("dve", SELK[:, :, 96 + j], ident_f[:, base:base + 12], -1.0, None, ALU.mult, None, ["ident_f", "SELK"], ["SELK"])
            cp("dve", SELQ[:, :, 64 + j], ident_f[:, base:base + 12], ["ident_f", "SELQ"], ["SELQ"])
        for h_ in range(12):
            cp("dve", SELK[:, h_, 99:100], ident_f[:, 76:77], ["ident_f", "SELK"], ["SELK"])
            for c_ in (64, 65, 66):
                cp("dve", SELK[:, h_, c_:c_ + 1], ident_f[:, 96:97], ["ident_f", "SELK"], ["SELK"])
            for c_ in (96, 97, 98, 99):
                cp("dve", SELQ[:, h_, c_:c_ + 1], ident_f[:, 96:97], ["ident_f", "SELQ"], ["SELQ"])

        S0 = ExitStack()
        with S0:
            cv = [sb(S0, "cv%d" % i, [128, 8, 512], BF16) for i in range(2)]
            cvn = [0]

            def convert(src, nkt, ncols, dst, wkey=None):
                srcv = src.rearrange("(kt p) c -> p kt c", p=128)
                c0 = 0
                while c0 < ncols:
                    n = min(512, ncols - c0)
                    i = cvn[0] % 2
                    cvn[0] += 1
                    t = cv[i]
                    ld(t[:, 0:nkt, 0:n], srcv[:, :, c0:c0 + n], [], ["cv%d" % i], eng="pool")
                    ld(dst[:, :, c0:c0 + n], t[:, 0:nkt, 0:n], ["cv%d" % i], [wkey] if wkey else [U()], eng="sp")
                    c0 += n

            convert(w_in, 8, NIN, WB)
            convert(w_mem_kv, 8, 1024, WKV, "WKV")
            convert(w_glu, 6, 768, WGLU)
            convert(w_proj_fox, 6, D, WPF)
            convert(w_proj_s5, 6, D, WPS)
            convert(w_proj_mem, 4, D, WPM)
            convert(w_out, 8, D, WOUT)
            mset("pool", WFL3[:], 0.0, ["WFL3"])
            for base in (0, 32, 64):
                ld(WFL3[:, :, base:base + 12], w_in.rearrange("(kt p) c -> p kt c", p=128)[:, :, C_FL:C_FL + 12],
                   ["WFL3"], ["WFL3"], eng="pool")

            def s5t(name, shape, dt=F32):
                return sb(S0, name, shape, dt)
            LR = s5t("LR", [128, 24]); LI = s5t("LI", [128, 24]); LS = s5t("LS", [128, 24])
            ld(LR[:], lam_re.rearrange("(pr g2) p -> (g2 p) pr", g2=2), [], ["LR"], slow=True)
            ld(LI[:], lam_im.rearrange("(pr g2) p -> (g2 p) pr", g2=2), [], ["LI"], slow=True)
            for g2 in range(2):
                ld(LS[g2 * 64:(g2 + 1) * 64, :],
                   log_step.rearrange("o (pr g2) -> o g2 pr", g2=2)[:, g2, :].to_broadcast([64, 24]), [], ["LS"], slow=True)
            STP = s5t("STP", [128, 24]); MAG = s5t("MAG", [128, 24]); ANG = s5t("ANG", [128, 24])
            T0 = s5t("T0", [128, 24]); T1 = s5t("T1", [128, 24]); T2 = s5t("T2", [128, 24]); TI = s5t("TI", [128, 24], I32)
            ABR = s5t("ABR", [128, 24]); ABI = s5t("ABI", [128, 24]); SN = s5t("SN", [128, 24]); CS = s5t("CS", [128, 24])
            FR = s5t("FR", [128, 24]); FI = s5t("FI", [128, 24])
            act(STP[:], LS[:], AF.Exp, ["LS"], ["STP"])
            tt("dve", T0[:], LR[:], STP[:], ALU.mult, ["LR", "STP"], ["T0"])
            act(MAG[:], T0[:], AF.Exp, ["T0"], ["MAG"])
            tt("dve", ANG[:], LI[:], STP[:], ALU.mult, ["LI", "STP"], ["ANG"])

            def sin_of(dst, shift, key):
                ts("dve", T0[:], ANG[:], shift, None, ALU.add, None, ["ANG"], ["T0"])
                ts("dve", T1[:], T0[:], 1.0 / TWO_PI, 0.5, ALU.mult, ALU.add, ["T0"], ["T1"])
                cp("dve", TI[:], T1[:], ["T1"], ["TI"])
                cp("dve", T1[:], TI[:], ["TI"], ["T1"])
                stt(T2[:], T1[:], -TWO_PI, T0[:], ALU.mult, ALU.add, ["T1", "T0"], ["T2"])
                ts("dve", T1[:], T2[:], math.pi, -TWO_PI, ALU.is_gt, ALU.mult, ["T2"], ["T1"])
                tt("dve", T2[:], T2[:], T1[:], ALU.add, ["T2", "T1"], ["T2"])
                ts("dve", T1[:], T2[:], -math.pi, TWO_PI, ALU.is_lt, ALU.mult, ["T2"], ["T1"])
                tt("dve", T2[:], T2[:], T1[:], ALU.add, ["T2", "T1"], ["T2"])
                ts("dve", T2[:], T2[:], math.pi, -math.pi, ALU.min, ALU.max, ["T2"], ["T2"])
                act(dst[:], T2[:], AF.Sin, ["T2"], [key])
            sin_of(SN, 0.0, "SN")
            sin_of(CS, math.pi / 2.0, "CS")
            tt("dve", ABR[:], MAG[:], CS[:], ALU.mult, ["MAG", "CS"], ["ABR"])
            tt("dve", ABI[:], MAG[:], SN[:], ALU.mult, ["MAG", "SN"], ["ABI"])
            DEN = s5t("DEN", [128, 24]); NR = s5t("NR", [128, 24])
            tt("dve", T0[:], LR[:], LR[:], ALU.mult, ["LR"], ["T0"])
            tt("dve", T1[:], LI[:], LI[:], ALU.mult, ["LI"], ["T1"])
            tt("dve", DEN[:], T0[:], T1[:], ALU.add, ["T0", "T1"], ["DEN"])
            P.c("dve", lambda h: h.reciprocal(out=DEN[:], in_=DEN[:]), ["DEN"], ["DEN"])
            ts("dve", NR[:], ABR[:], -1.0, None, ALU.add, None, ["ABR"], ["NR"])
            tt("dve", T0[:], NR[:], LR[:], ALU.mult, ["NR", "LR"], ["T0"])
            tt("dve", T1[:], ABI[:], LI[:], ALU.mult, ["ABI", "LI"], ["T1"])
            tt("dve", T0[:], T0[:], T1[:], ALU.add, ["T0", "T1"], ["T0"])
            tt("dve", FR[:], T0[:], DEN[:], ALU.mult, ["T0", "DEN"], ["FR"])
            tt("dve", T0[:], ABI[:], LR[:], ALU.mult, ["ABI", "LR"], ["T0"])
            tt("dve", T1[:], NR[:], LI[:], ALU.mult, ["NR", "LI"], ["T1"])
            tt("dve", T0[:], T0[:], T1[:], ALU.subtract, ["T0", "T1"], ["T0"])
            tt("dve", FI[:], T0[:], DEN[:], ALU.mult, ["T0", "DEN"], ["FI"])
            PWR = s5t("PWR", [128, 9, 24]); PWI = s5t("PWI", [128, 9, 24])
            AWR = s5t("AWR", [128, 9, 24]); AWI = s5t("AWI", [128, 9, 24])

            def cmul(orr, oi, ar, ai, br, bi, rk, wk):
                tt("dve", T0[:], ar, br, ALU.mult, rk, ["T0"])
                tt("dve", T1[:], ai, bi, ALU.mult, rk, ["T1"])
                tt("dve", T2[:], ar, bi, ALU.mult, rk, ["T2"])
                tt("dve", orr, T0[:], T1[:], ALU.subtract, ["T0", "T1"], wk)
                tt("dve", T0[:], ai, br, ALU.mult, rk + wk, ["T0"])
                tt("dve", oi, T2[:], T0[:], ALU.add, ["T2", "T0"], wk)
            mset("pool", PWR[:, 0, :], 1.0, ["PW"])
            mset("pool", PWI[:, 0, :], 0.0, ["PW"])
            for k in range(1, 9):
                cmul(PWR[:, k, :], PWI[:, k, :], PWR[:, k - 1, :], PWI[:, k - 1, :], ABR[:], ABI[:], ["PW", "ABR", "ABI"], ["PW"])
            mset("pool", AWR[:, 0, :], 1.0, ["AW"])
            mset("pool", AWI[:, 0, :], 0.0, ["AW"])
            for k in range(1, 9):
                cmul(AWR[:, k, :], AWI[:, k, :], AWR[:, k - 1, :], AWI[:, k - 1, :], PWR[:, 8, :], PWI[:, 8, :], ["AW", "PW"], ["AW"])
            cp("dve", A1R[:], AWR[:, 1, :], ["AW"], ["A1"])
            cp("dve", A1I[:], AWI[:, 1, :], ["AW"], ["A1"])
            cp("dve", APR[:], AWR[:, 1:9, :], ["AW"], ["APR"])
            cp("dve", API[:], AWI[:, 1:9, :], ["AW"], ["API"])
            cp("dve", A8R[:], AWR[:, 8, :], ["AW"], ["A8"])
            cp("dve", A8I[:], AWI[:, 8, :], ["AW"], ["A8"])

            BRE = s5t("BRE", [128, 24, 16]); BIM = s5t("BIM", [128, 24, 16])
            ld(BRE[:], s5_b_re.rearrange("(pr g2) p h -> (g2 p) pr h", g2=2), [], ["BRE"])
            ld(BIM[:], s5_b_im.rearrange("(pr g2) p h -> (g2 p) pr h", g2=2), [], ["BIM"])
            BBR = s5t("BBR", [128, 24, 16]); BBI = s5t("BBI", [128, 24, 16])
            U0 = s5t("U0", [128, 24, 16]); U1 = s5t("U1", [128, 24, 16])
            frb = FR[:].unsqueeze(2).to_broadcast([128, 24, 16])
            fib = FI[:].unsqueeze(2).to_broadcast([128, 24, 16])
            tt("dve", U0[:], BRE[:], frb, ALU.mult, ["BRE", "FR"], ["U0"])
            tt("dve", U1[:], BIM[:], fib, ALU.mult, ["BIM", "FI"], ["U1"])
            tt("dve", BBR[:], U0[:], U1[:], ALU.subtract, ["U0", "U1"], ["BBR"])
            tt("dve", U0[:], BIM[:], frb, ALU.mult, ["BIM", "FR"], ["U0"])
            tt("dve", U1[:], BRE[:], fib, ALU.mult, ["BRE", "FI"], ["U1"])
            tt("dve", BBI[:], U0[:], U1[:], ALU.add, ["U0", "U1"], ["BBI"])
            BDR = s5t("BDR", [128, 24, 32]); BDI = s5t("BDI", [128, 24, 32])
            mset("pool", BDR[:], 0.0, ["BDR"]); mset("pool", BDI[:], 0.0, ["BDI"])
            for g2 in range(2):
                sl = slice(g2 * 64, (g2 + 1) * 64)
                cp("dve", BDR[sl, :, g2 * 16:(g2 + 1) * 16], BBR[sl, :, :], ["BBR", "BDR"], ["BDR"])
                cp("dve", BDI[sl, :, g2 * 16:(g2 + 1) * 16], BBI[sl, :, :], ["BBI", "BDI"], ["BDI"])
            BWR = s5t("BWR", [128, 24, 128]); BWI = s5t("BWI", [128, 24, 128])
            mset("pool", BWR[:], 0.0, ["BWR"]); mset("pool", BWI[:], 0.0, ["BWI"])
            for q4 in range(4):
                for o in range(6):
                    pr = 4 * o + q4
                    cp("dve", BWR[:, pr, 32 * q4:32 * q4 + 32], BDR[:, pr, :], ["BDR", "BWR"], ["BWR"])
                    cp("pool", BWI[:, pr, 32 * q4:32 * q4 + 32], BDI[:, pr, :], ["BDI", "BWI"], ["BWI"])
            CTR = s5t("CTR", [128, 24, 16]); CTI = s5t("CTI", [128, 24, 16])
            CN = s5t("CN", [128, 3, 2, 128])
            tp_ps = ps(S0, "tp_ps", [128, 512], F32)
            for ri, (csrc, cdst, key) in enumerate(((s5_c_re, CTR, "CTR"), (s5_c_im, CTI, "CTI"))):
                cv4 = csrc.rearrange("(pr g2) h p -> pr h g2 p", g2=2)
                for pr in range(24):
                    ld(CN[(pr % 8) * 16:(pr % 8) * 16 + 16, pr // 8, ri, :].rearrange("h (a p) -> h a p", a=2),
                       cv4[pr], ["CN%d" % ri], ["CN%d" % ri])
                for j in range(3):
                    tpg([(tp_ps[:, 0:128], CN[:, j, ri, :], ident_f[:])], ["CN%d" % ri, "ident_f", "tp_ps"], ["tp_ps"])
                    cp("act", cdst[:, 8 * j:8 * j + 8, :], tp_ps[:, 0:128].rearrange("p (a h) -> p a h", a=8), ["tp_ps"], [key])
            XR = s5t("XR", [128, 24, 32]); XI = s5t("XI", [128, 24, 32])
            V0 = s5t("V0", [128, 24, 32]); V1 = s5t("V1", [128, 24, 32])
            WZs = [s5t("WZs%d" % i, [128, 6, 2, 128], BF16) for i in range(2)]
            for s in range(8):
                k = 7 - s
                wzs, wzk = WZs[s % 2], "WZs%d" % (s % 2)
                pr_ = PWR[:, k, :].unsqueeze(2).to_broadcast([128, 24, 32])
                pi_ = PWI[:, k, :].unsqueeze(2).to_broadcast([128, 24, 32])
                tt("dve", V0[:], BDR[:], pr_, ALU.mult, ["BDR", "PW"], ["V0"])
                tt("dve", V1[:], BDI[:], pi_, ALU.mult, ["BDI", "PW"], ["V1"])
                tt("dve", XR[:], V0[:], V1[:], ALU.subtract, ["V0", "V1"], ["XR"])
                tt("dve", V0[:], BDI[:], pr_, ALU.mult, ["BDI", "PW"], ["V0"])
                tt("dve", V1[:], BDR[:], pi_, ALU.mult, ["BDR", "PW"], ["V1"])
                tt("dve", XI[:], V0[:], V1[:], ALU.add, ["V0", "V1"], ["XI"])
                for o in range(6):
                    for ri, X in enumerate((XR, XI)):
                        tpg([(tp_ps[:, 0:128], X[:, 4 * o:4 * o + 4, :].rearrange("p a b -> p (a b)"), ident_f[:])], ["XR", "XI", "ident_f", "tp_ps"], ["tp_ps"])
                        cp("act", wzs[:, o, ri, :], tp_ps[:, 0:128], ["tp_ps"], [wzk])
                for o in range(6):
                    ld(WZS[o, :, s, :, :], wzs[:, o, :, :], [wzk], [U()])
            QR = s5t("QR", [128, 24, 16]); QI = s5t("QI", [128, 24, 16])
            QBR = s5t("QBR", [128, 24, 32]); QBI = s5t("QBI", [128, 24, 32])
            WYk = [s5t("WYk%d" % i, [128, 24, 2, 32], BF16) for i in range(2)]
            BDj = [s5t("BDj%d" % i, [128, 6, 128], F32) for i in range(2)]
            BDjb = [s5t("BDjb%d" % i, [128, 6, 128], BF16) for i in range(2)]
            DSK = s5t("DSK", [128, 6], F32)
            ld(DSK[:], s5_d.rearrange("o (c p) -> p (o c)", p=128), [], ["DSK"], slow=True)
            mset("pool", QBR[:], 0.0, ["QBR"]); mset("pool", QBI[:], 0.0, ["QBI"])
            for i in range(2):
                mset("pool", BDj[i][:], 0.0, ["BDj%d" % i])
            bd_ps = ps(S0, "bd_ps", [128, 6, 32], F32)
            for k in range(9):
                pr_ = PWR[:, k, :].unsqueeze(2).to_broadcast([128, 24, 16])
                pi_ = PWI[:, k, :].unsqueeze(2).to_broadcast([128, 24, 16])
                tt("dve", U0[:], CTR[:], pr_, ALU.mult, ["CTR", "PW"], ["U0"])
                tt("dve", U1[:], CTI[:], pi_, ALU.mult, ["CTI", "PW"], ["U1"])
                tt("dve", QR[:], U0[:], U1[:], ALU.subtract, ["U0", "U1"], ["QR"])
                tt("dve", U0[:], CTR[:], pi_, ALU.mult, ["CTR", "PW"], ["U0"])
                tt("dve", U1[:], CTI[:], pr_, ALU.mult, ["CTI", "PW"], ["U1"])
                tt("dve", QI[:], U0[:], U1[:], ALU.add, ["U0", "U1"], ["QI"])
                for g2 in range(2):
                    sl = slice(g2 * 64, (g2 + 1) * 64)
                    cp("dve", QBR[sl, :, g2 * 16:(g2 + 1) * 16], QR[sl, :, :], ["QR", "QBR"], ["QBR"])
                    ts("dve", QBI[sl, :, g2 * 16:(g2 + 1) * 16], QI[sl, :, :], -1.0, None, ALU.mult, None, ["QI", "QBI"], ["QBI"])
                if k >= 1:
                    wyk, wykk = WYk[k % 2], "WYk%d" % (k % 2)
                    cp("act", wyk[:, :, 0, :], QBR[:], ["QBR"], [wykk])
                    cp("act", wyk[:, :, 1, :], QBI[:], ["QBI"], [wykk])
                    for o in range(6):
                        ld(WYS[o, :, :, k - 1, :, :], wyk[:, 4 * o:4 * o + 4, :, :], [wykk], [U()])
                if k <= 7:
                    bdj, bdk = BDj[k % 2], "BDj%d" % (k % 2)
                    bdjb, bdbk = BDjb[k % 2], "BDjb%d" % (k % 2)
                    for o in range(6):
                        calls = []
                        for q4 in range(4):
                            pr = 4 * o + q4
                            calls.append(dict(out=bd_ps[:, o, :], lhsT=BWR[:, pr, :], rhs=QBR[:, pr, :], start=(q4 == 0), stop=False))
                            calls.append(dict(out=bd_ps[:, o, :], lhsT=BWI[:, pr, :], rhs=QBI[:, pr, :], start=False, stop=(q4 == 3)))
                        mmg(calls, ["BWR", "BWI", "QBR", "QBI", "bd_ps"], ["bd_ps"])
                    for q4 in range(4):
                        sl = slice(32 * q4, 32 * q4 + 32)
                        cp("act", bdj[sl, :, 32 * q4:32 * q4 + 32], bd_ps[sl, :, :], ["bd_ps"], [bdk])
                    if k == 0:
                        for o in range(6):
                            stt(bdj[:, o, :], ident_f[:], DSK[:, o:o + 1], bdj[:, o, :], ALU.mult, ALU.add, ["ident_f", "DSK", bdk], [bdk])
                    cp("dve", bdjb[:], bdj[:], [bdk], [bdbk])
                    for o in range(6):
                        ld(BDS[o, :, k, :], bdjb[:, o, :], [bdbk], [U()])
            P.emit()
        if upto < 1:
            return nc
        S0b = ExitStack()
        with S0b:
            S0 = S0b

            def s5t(name, shape, dt=F32):
                return sb(S0b, name, shape, dt)
            mt_x = s5t("mt_x", [128, 2, D], F32)
            mt_n = s5t("mt_n", [128, 2, D], BF16)
            mt_j = s5t("mt_j", [128, D], BF16)
            MHT = s5t("MHT", [128, 8, 256], BF16)
            wkv_t = s5t("wkv_t", [128, 8, 1024], BF16)
            tpb_ps = ps(S0, "tpb_ps", [128, 512], BF16)
            mk_ps = ps(S0, "mk_ps", [128, 512], F32)
            ld(mt_x[:], mem.rearrange("(a p) d -> p a d", p=128), [], ["mt_x"])
            ld(wkv_t[:], WKV, ["WKV"], ["wkv_t"])
            for a in range(2):
                act(mt_j[:], mt_x[:, a, :], AF.Square, ["mt_x"], ["mt_j", "ssq"], accum=ssq[:, a:a + 1])
            ts("dve", rstd[:, 0:2], ssq[:, 0:2], 1.0 / D, EPS, ALU.mult, ALU.add, ["ssq"], ["rstd"])
            act(rstd[:, 0:2], rstd[:, 0:2], AF.Sqrt, ["rstd"], ["rstd"])
            P.c("dve", lambda h: h.reciprocal(out=rstd[:, 0:2], in_=rstd[:, 0:2]), ["rstd"], ["rstd"])
            for a in range(2):
                ts("dve", mt_n[:, a, :], mt_x[:, a, :], rstd[:, a:a + 1], None, ALU.mult, None, ["mt_x", "rstd"], ["mt_n"])
            for kt in range(8):
                tpg([(tpb_ps[:, a * 128:(a + 1) * 128], mt_n[:, a, kt * 128:(kt + 1) * 128], ident_b[:]) for a in range(2)],
                    ["mt_n", "ident_b", "tpb_ps"], ["tpb_ps"])
                ts("dve", MHT[:, kt, :], tpb_ps[:, 0:256], gmn[:, kt:kt + 1], None, ALU.mult, None, ["tpb_ps", "gmn"], ["MHT"])
            for hd in range(4):
                mmg([dict(out=mk_ps[:, 0:256], lhsT=wkv_t[:, kt, hd * 128:(hd + 1) * 128], rhs=MHT[:, kt, :],
                          start=(kt == 0), stop=(kt == 7)) for kt in range(8)], ["wkv_t", "MHT", "mk_ps"], ["mk_ps"])
                cp("act", MKT[:, hd, :], mk_ps[:, 0:256], ["mk_ps"], ["MKT"])
            for mt in range(2):
                mmg([dict(out=mk_ps[:, :], lhsT=MHT[:, kt, mt * 128:(mt + 1) * 128], rhs=wkv_t[:, kt, 512:1024],
                          start=(kt == 0), stop=(kt == 7)) for kt in range(8)], ["wkv_t", "MHT", "mk_ps"], ["mk_ps"])
                cp("act", MV[:, mt, :], mk_ps[:, :], ["mk_ps"], ["MV"])
            P.emit()

        if upto < 2:
            return nc
        S1 = ExitStack()
        with S1:
            xt = sb(S1, "xt", [128, 4, D], F32)
            xn = sb(S1, "xn", [128, 4, D], BF16)
            sqj = sb(S1, "sqj", [128, D], BF16)
            hT = sb(S1, "hT", [128, 8, TT], BF16)
            NWS = 3
            wsl = [sb(S1, "ws%d" % i, [128, 4096], BF16) for i in range(NWS)]
            SPt = [sb(S1, "SPt%d" % i, [128, TT], BF16) for i in range(2)]
            HIb = sb(S1, "HIb", [128, TT], BF16)
            MIDb = sb(S1, "MIDb", [128, TT], BF16)
            kst = [sb(S1, "kst%d" % i, [128, TT], BF16) for i in range(2)]
            Vst = sb(S1, "Vst", [128, 4, 12, 65], BF16)
            uT = sb(S1, "uT", [128, 6, TT], BF16)
            s5w = [(sb(S1, "s5bd%d" % i, [128, 8, 128], BF16), sb(S1, "s5wz%d" % i, [128, 8, 2, 128], BF16),
                    sb(S1, "s5wy%d" % i, [128, 4, 8, 2, 32], BF16)) for i in range(2)]
            Z = sb(S1, "Z", [128, 24, 2, 64], F32)
            Hb = sb(S1, "Hb", [128, 24, 2, 65], BF16)
            RT = [sb(S1, "RT%d" % i, [128, 24, 8], F32) for i in range(8)]
            RS = [sb(S1, "RS%d" % i, [128, 24], F32) for i in range(8)]
            tA = sb(S1, "tA", [128, TT], F32)
            tB = sb(S1, "tB", [128, TT], F32)
            tC = sb(S1, "tC", [128, TT], F32)
            Fe, Ff, R1 = tA, tB, tC
            YG = sb(S1, "YG", [128, 6, TT], BF16)
            SGS = sb(S1, "SGS", [128, 6, TT], BF16)
            YS = SGS
            QM = sb(S1, "QM", [128, 4, TT], BF16)
            SGM = sb(S1, "SGM", [128, 4, TT], BF16)
            PmT = sb(S1, "PmT", [128, 2, TT], BF16)
            YM = SGM
            M0 = [sb(S1, "M0_%d" % i, [128, TT], F32) for i in range(2)]
            G0st = [sb(S1, "G0st%d" % i, [128, TT], BF16) for i in range(2)]
            pj = [ps(S1, "pj%d" % i, [128, 512], F32) for i in range(2)]
            tpall = ps(S1, "tpall", [128, 1024], BF16)
            tpp = [tpall[:, 0:512], tpall[:, 0:512]]
            zpsb = [ps(S1, "zps%d" % i, [128, 512], F32) for i in range(4)]
            yps = ps(S1, "yps", [128, 512], F32)

            import os
            if not os.environ.get('SKIPM'):
                for i in range(2):
                    mset("pool", SPt[i][:], 0.0, ["SPt%d" % i])
                    mset("pool", SPt[i][96:97, :], 1.0, ["SPt%d" % i])
                mset("pool", Vst[:], 1.0, ["Vst"])

            wn = [0]
            pjn = [0]

            def wload(src3, nkt, ncols):
                i = wn[0] % NWS
                wn[0] += 1
                t = wsl[i][:, 0:nkt * ncols].rearrange("p (k c) -> p k c", k=nkt)
                if os.environ.get('WLDTINY'):
                    ld(t[:, 0:1, 0:16], src3[:, 0:1, 0:16], [], ["ws%d" % i])
                else:
                    ld(t, src3, [], ["ws%d" % i])
                return t, "ws%d" % i

            PJL = [(pj[0], "pj0"), (pj[1], "pj1"), (zpsb[0], "zps0"), (zpsb[1], "zps1"), (zpsb[2], "zps2"), (zpsb[3], "zps3"), (yps, "yps")]

            def nextpj():
                i = pjn[0] % len(PJL)
                pjn[0] += 1
                return PJL[i]

            import os
            P1T = int(os.environ.get('P1T', NT)); P1S = int(os.environ.get('P1S', 99)); OWNX = int(os.environ.get('OWNX', OWN0))
            for it in range(P1T):
                own = it >= OWNX
                ot = it - OWNX
                sp_i = it % 2
                spt, spk = SPt[sp_i], "SPt%d" % sp_i
                ld(xt[:], xa[it * TT:(it + 1) * TT, :].rearrange("(a p) d -> p a d", p=128), [], ["xt"])
                if not os.environ.get('SKIPK'):
                    ld(spt[76:77, :], kmrow[it:it + 1, :], [], [spk], eng="pool")
                if P1S < -1:
                    continue
                for a in range(4):
                    act(sqj[:], xt[:, a, :], AF.Square, ["xt"], ["sqj", "ssq"], accum=ssq[:, a:a + 1])
                ts("dve", rstd[:, 0:4], ssq[:, 0:4], 1.0 / D, EPS, ALU.mult, ALU.add, ["ssq"], ["rstd"])
                act(rstd[:, 0:4], rstd[:, 0:4], AF.Sqrt, ["rstd"], ["rstd"])
                P.c("dve", lambda h: h.reciprocal(out=rstd[:, 0:4], in_=rstd[:, 0:4]), ["rstd"], ["rstd"])
                if os.environ.get('NOXN'):
                    continue
                XNV = int(os.environ.get('XNV', 0))
                for a in range(1 if XNV == 1 else 4):
                    if XNV == 2:
                        ts("dve", sqj[:], xt[:, a, :], rstd[:, a:a + 1], None, ALU.mult, None, ["xt", "rstd"], ["xn"])
                    elif XNV == 3:
                        ts("dve", xn[:, a, :], xt[:, a, :], 2.0, None, ALU.mult, None, ["xt", "rstd"], ["xn"])
                    elif XNV == 4:
                        act(xn[:, a, :], xt[:, a, :], AF.Identity, ["xt", "rstd"], ["xn"], scale=rstd[:, a:a + 1])
                    else:
                        ts("dve", xn[:, a, :], xt[:, a, :], rstd[:, a:a + 1], None, ALU.mult, None, ["xt", "rstd"], ["xn"])
                if P1S < 0:
                    continue
                for kt in range(8):
                    tp, tk = tpp[kt % 2], "tpp0"
                    tpg([(tp[:, a * 128:(a + 1) * 128], xn[:, a, kt * 128:(kt + 1) * 128], ident_b[:]) for a in range(4)],
                        ["xn", "ident_b", tk], [tk])
                    if kt % 2 == 0:
                        ts("dve", hT[:, kt, :], tp[:, :], gn[:, kt:kt + 1], None, ALU.mult, None, [tk, "gn"], ["hT"])
                    else:
                        act(hT[:, kt, :], tp[:, :], AF.Identity, [tk, "gn"], ["hT"], scale=gn[:, kt:kt + 1])
                if P1S < 2:
                    continue
                pjt, pk = nextpj()
                mmg([dict(out=pjt[0:76, :], lhsT=WFL3[:, kt, :], rhs=hT[:, kt, :], start=(kt == 0), stop=(kt == 7)) for kt in range(8)],
                    ["WFL3", "hT", pk], [pk])
                act(Fe[0:76, :], pjt[0:76, :], AF.Exp, [pk, "negb"], ["tA"], bias=negb[0:76, :], scale=-1.0)
                act(Fe[0:76, :], Fe[0:76, :], AF.Ln, ["tA"], ["tA"], bias=1.0)
                P.c("dve", lambda h: h.tensor_tensor_scan(out=Ff[0:76, :], data0=ones_f[0:76, :], data1=Fe[0:76, :],
                                                          initial=Fcar[0:76, :], op0=ALU.mult, op1=ALU.subtract),
                    ["ones_f", "tA", "Fcar"], ["tB"])
                cp("dve", Fcar[0:76, :], Ff[0:76, TT - 1:TT], ["tB"], ["Fcar"])
                cp("dve", HIb[0:76, :], Ff[0:76, :], ["tB"], ["HIb"])
                stt(R1[0:76, :], HIb[0:76, :], nm1[0:76, :], Ff[0:76, :], ALU.mult, ALU.add, ["HIb", "nm1", "tB"], ["tC"])
                tt("dve", MIDb[0:76, :], Ff[0:76, :], HIb[0:76, :], ALU.subtract, ["tB", "HIb"], ["MIDb"])
                stt(R1[0:76, :], MIDb[0:76, :], nm2[0:76, :], R1[0:76, :], ALU.mult, ALU.add, ["MIDb", "nm2", "tC"], ["tC"])
                cp("dve", spt[0:76, :], R1[0:76, :], ["tC"], [spk])
                if P1S < 3:
                    continue
                wk, wkk = None, None
                for h_ in range(12):
                    if h_ % 8 == 0:
                        n = min(512, 768 - h_ * 64)
                        wk, wkk = wload(WB[:, :, C_K + h_ * 64:C_K + h_ * 64 + n], 8, n)
                    pjt, pk = nextpj()
                    c0 = (h_ % 8) * 64
                    calls = [dict(out=pjt[:, :], lhsT=SELK[:, h_, :], rhs=spt[:, :], start=True, stop=False)]
                    calls += [dict(out=pjt[0:64, :], lhsT=wk[:, kt, c0:c0 + 64], rhs=hT[:, kt, :], start=False, stop=(kt == 7)) for kt in range(8)]
                    mmg(calls, ["SELK", spk, wkk, "hT", pk], [pk])
                    ks, kk = kst[h_ % 2], "kst%d" % (h_ % 2)
                    if h_ % 2 == 0:
                        cp("act", ks[:], pjt[:, :], [pk], [kk])
                        ld(KT[h_, :, it * TT:(it + 1) * TT], ks[:], [kk], [U()], eng="act")
                    else:
                        cp("dve", ks[:], pjt[:, :], [pk], [kk])
                        ld(KT[h_, :, it * TT:(it + 1) * TT], ks[:], [kk], [U()], eng="act")
                if P1S < 4:
                    continue
                if own:
                    for h_ in range(12):
                        if h_ % 8 == 0:
                            n = min(512, 768 - h_ * 64)
                            wk, wkk = wload(WB[:, :, C_Q + h_ * 64:C_Q + h_ * 64 + n], 8, n)
                        pjt, pk = nextpj()
                        c0 = (h_ % 8) * 64
                        calls = [dict(out=pjt[:, :], lhsT=SELQ[:, h_, :], rhs=spt[:, :], start=True, stop=False)]
                        calls += [dict(out=pjt[0:64, :], lhsT=wk[:, kt, c0:c0 + 64], rhs=hT[:, kt, :], start=False, stop=(kt == 7)) for kt in range(8)]
                        mmg(calls, ["SELQ", spk, wkk, "hT", pk], [pk])
                        ks, kk = kst[h_ % 2], "kst%d" % (h_ % 2)
                        act(ks[:], pjt[:, :], AF.Identity, [pk, "qscale"], [kk], scale=qscale[:, :])
                        ld(QT[h_, :, ot * TT:(ot + 1) * TT], ks[:], [kk], [U()], eng="act")
                if P1S < 5:
                    continue
                wv1, wv1k = wload(WB[:, :, C_V:C_V + 512], 8, 512)
                wv2, wv2k = wload(WB[:, :, C_V + 512:C_V + 768], 8, 256)
                for a in range(4):
                    p1, p1k = nextpj()
                    p2, p2k = nextpj()
                    mmg([dict(out=p1[:, :], lhsT=hT[:, kt, a * 128:(a + 1) * 128], rhs=wv1[:, kt, :], start=(kt == 0), stop=(kt == 7)) for kt in range(8)],
                        ["hT", wv1k, p1k], [p1k])
                    mmg([dict(out=p2[:, 0:256], lhsT=hT[:, kt, a * 128:(a + 1) * 128], rhs=wv2[:, kt, :], start=(kt == 0), stop=(kt == 7)) for kt in range(8)],
                        ["hT", wv2k, p2k], [p2k])
                    cp("act", Vst[:, a, 0:8, 0:64], p1[:, :].rearrange("p (h d) -> p h d", h=8), [p1k], ["Vst"])
                    cp("dve", Vst[:, a, 8:12, 0:64], p2[:, 0:256].rearrange("p (h d) -> p h d", h=4), [p2k], ["Vst"])
                ld(VS[4 * it:4 * it + 4].rearrange("a p f -> p a f"), Vst[:].rearrange("p a h d -> p a (h d)"), ["Vst"], [U()], eng="act")
                if P1S < 6:
                    continue
                for o in range(6):
                    if o % 4 == 0:
                        n = min(512, 768 - o * 128)
                        wk, wkk = wload(WB[:, :, C_U + o * 128:C_U + o * 128 + n], 8, n)
                    pjt, pk = nextpj()
                    c0 = (o % 4) * 128
                    mmg([dict(out=pjt[:, :], lhsT=wk[:, kt, c0:c0 + 128], rhs=hT[:, kt, :], start=(kt == 0), stop=(kt == 7)) for kt in range(8)],
                        [wkk, "hT", pk], [pk])
                    cp("act" if o % 2 == 0 else "dve", uT[:, o, :], pjt[:, :], [pk], ["uT"])
                if P1S < 7:
                    continue
                s5slots = []
                for o in range(6):
                    bd_t, wz_t, wy_t = s5w[o % 2]
                    sk = "s5w%d" % (o % 2)
                    ld(wz_t[:], WZS[o], [], [sk + "z"])
                    for q4 in range(4):
                        calls = []
                        for ri in range(2):
                            for s in range(8):
                                calls.append(dict(out=zpsb[q4][:, ri * 64:(ri + 1) * 64], lhsT=wz_t[32 * q4:32 * q4 + 32, s, ri, :],
                                                  rhs=uT[32 * q4:32 * q4 + 32, o, :].rearrange("p (c s) -> p c s", s=8)[:, :, s],
                                                  start=(s == 0), stop=(s == 7), tile_position=(32 * q4, 0)))
                        mmg(calls, [sk + "z", "uT", "zps%d" % q4], ["zps%d" % q4])
                        cp("act" if q4 % 2 == 0 else "dve", Z[:, 4 * o + q4, :, :], zpsb[q4][:, 0:128].rearrange("p (i c) -> p i c", i=2), ["zps%d" % q4], ["Zj%d" % j for j in range(8)])
                if P1S < 8:
                    continue
                if it > 0:
                    cp(os.environ.get('RENG', 'dve'), TB[:, :, :, 0], TB[:, :, :, 8], ["TB"], ["TB"])
                Zv = Z[:].rearrange("p r i (b j) -> p r i b j", j=8)

                RENG = os.environ.get('RENG', 'dve')
                cmn = [0]

                def cmac(dre, dim_, sre, sim, mr, mi, Tsets, rkeys, wkey):
                    si_ = cmn[0] % len(Tsets)
                    cmn[0] += 1
                    T = Tsets[si_]
                    tk = ["RT%d_%d" % (si_, q) for q in range(4)]
                    tt(RENG, T[0], mr, sre, ALU.mult, rkeys, [tk[0]])
                    tt(RENG, T[1], mi, sim, ALU.mult, rkeys, [tk[1]])
                    tt(RENG, T[2], mr, sim, ALU.mult, rkeys, [tk[2]])
                    tt(RENG, T[3], mi, sre, ALU.mult, rkeys, [tk[3]])
                    tt(RENG, T[0], T[0], T[1], ALU.subtract, [tk[0], tk[1]], [tk[0]])
                    tt(RENG, T[2], T[2], T[3], ALU.add, [tk[2], tk[3]], [tk[2]])
                    tt(RENG, dre, dre, T[0], ALU.add, [tk[0], wkey], [wkey])
                    tt(RENG, dim_, dim_, T[2], ALU.add, [tk[2], wkey], [wkey])
                RTs = [[t[:] for t in RT[0:4]], [t[:] for t in RT[4:8]]]
                RSs = [[t[:] for t in RS[0:4]], [t[:] for t in RS[4:8]]]
                zkeys = ["Zj%d" % j for j in range(8)]
                a1r = A1R[:].unsqueeze(2).to_broadcast([128, 24, 8])
                a1i = A1I[:].unsqueeze(2).to_broadcast([128, 24, 8])
                for j in range(1, 8):
                    cmac(Zv[:, :, 0, :, j], Zv[:, :, 1, :, j], Zv[:, :, 0, :, j - 1], Zv[:, :, 1, :, j - 1], a1r, a1i, RTs, [zkeys[j - 1]], zkeys[j])
                for b_ in range(8):
                    cp(RENG, TB[:, :, :, b_ + 1], Zv[:, :, :, b_, 7], [zkeys[7], "TB"], ["TB"])
                    cmac(TB[:, :, 0, b_ + 1], TB[:, :, 1, b_ + 1], TB[:, :, 0, b_], TB[:, :, 1, b_], A8R[:], A8I[:], RSs, ["TB"], "TB")
                for j in range(8):
                    cmac(Zv[:, :, 0, :, j], Zv[:, :, 1, :, j], TB[:, :, 0, 0:8], TB[:, :, 1, 0:8],
                         APR[:, j, :].unsqueeze(2).to_broadcast([128, 24, 8]), API[:, j, :].unsqueeze(2).to_broadcast([128, 24, 8]), RTs, ["TB"], zkeys[j])
                if own:
                    cp(RENG, Hb[:, :, :, 0], TB[:, :, :, 0], ["TB"], ["Hb"])
                    cp(RENG, Hb[:, :, :, 1:65], Z[:, :, :, :], zkeys, ["Hb"])
                if not own:
                    continue
                if P1S < 9:
                    continue
                for o in range(6):
                    if o % 4 == 0:
                        n = min(512, 768 - o * 128)
                        wk, wkk = wload(WB[:, :, C_GS + o * 128:C_GS + o * 128 + n], 8, n)
                    pjt, pk = nextpj()
                    c0 = (o % 4) * 128
                    mmg([dict(out=pjt[:, :], lhsT=wk[:, kt, c0:c0 + 128], rhs=hT[:, kt, :], start=(kt == 0), stop=(kt == 7)) for kt in range(8)],
                        [wkk, "hT", pk], [pk])
                    act(SGS[:, o, :], pjt[:, :], AF.Silu, [pk], ["SGS"])
                if P1S < 12:
                    continue
                wk, wkk = wload(WB[:, :, C_QM:C_QM + 512], 8, 512)
                for hd in range(4):
                    pjt, pk = nextpj()
                    mmg([dict(out=pjt[:, :], lhsT=wk[:, kt, hd * 128:(hd + 1) * 128], rhs=hT[:, kt, :], start=(kt == 0), stop=(kt == 7)) for kt in range(8)],
                        [wkk, "hT", pk], [pk])
                    act(QM[:, hd, :], pjt[:, :], AF.Identity, [pk], ["QM"], scale=128.0 ** -0.5)
                wk, wkk = wload(WB[:, :, C_GM:C_GM + 512], 8, 512)
                for hd in range(4):
                    pjt, pk = nextpj()
                    mmg([dict(out=pjt[:, :], lhsT=wk[:, kt, hd * 128:(hd + 1) * 128], rhs=hT[:, kt, :], start=(kt == 0), stop=(kt == 7)) for kt in range(8)],
                        [wkk, "hT", pk], [pk])
                    act(SGM[:, hd, :], pjt[:, :], AF.Silu, [pk], ["SGM"])
                for hd in range(4):
                    for mt in range(2):
                        pjt, pk = nextpj()
                        mmg([dict(out=pjt[:, :], lhsT=MKT[:, hd, mt * 128:(mt + 1) * 128], rhs=QM[:, hd, :], start=True, stop=True)],
                            ["MKT", "QM", pk], [pk])
                        act(PmT[:, mt, :], pjt[:, :], AF.Exp, [pk], ["PmT%d" % mt])
                    pjt, pk = nextpj()
                    mmg([dict(out=pjt[:, :], lhsT=MV[:, mt, hd * 128:(hd + 1) * 128], rhs=PmT[:, mt, :], start=(mt == 0), stop=(mt == 1)) for mt in range(2)],
                        ["MV", "PmT0", "PmT1", pk], [pk])
                    axp, axk = nextpj()
                    mmg([dict(out=axp[:, :], lhsT=ones_b[:, :], rhs=PmT[:, mt, :], start=(mt == 0), stop=(mt == 1)) for mt in range(2)],
                        ["ones_b", "PmT0", "PmT1", axk], [axk])
                    P.c("dve", lambda h, axp=axp: h.reciprocal(out=tC[:], in_=axp[:, :]), [axk], ["tC"])
                    tt("dve", tB[:], tC[:], pjt[:, :], ALU.mult, ["tC", pk], ["tB"])
                    tt("dve", YM[:, hd, :], tB[:], SGM[:, hd, :], ALU.mult, ["tB", "SGM"], ["SGM"])
                if P1S < 14:
                    continue
                for cb in range(8):
                    if cb % 4 == 0:
                        wg0, wg0k = wload(WB[:, :, C_GL + cb * 128:C_GL + cb * 128 + 512], 8, 512)
                    c0 = (cb % 4) * 128
                    pjt, pk = nextpj()
                    mmg([dict(out=pjt[:, :], lhsT=wg0[:, kt, c0:c0 + 128], rhs=hT[:, kt, :], start=(kt == 0), stop=(kt == 7)) for kt in range(8)],
                        [wg0k, "hT", pk], [pk])
                    g0, g0k = G0st[cb % 2], "G0st%d" % (cb % 2)
                    act(g0[:], pjt[:, :], AF.Sigmoid, [pk, "bm"], [g0k], bias=bm[:, cb:cb + 1])
                    ld(G0S[ot, :, cb, :], g0[:], [g0k], [U()], eng="act")
                for o in range(6):
                    if o % 4 == 0:
                        n = min(512, 768 - o * 128)
                        wk, wkk = wload(WB[:, :, C_GF + o * 128:C_GF + o * 128 + n], 8, n)
                    pjt, pk = nextpj()
                    c0 = (o % 4) * 128
                    mmg([dict(out=pjt[:, :], lhsT=wk[:, kt, c0:c0 + 128], rhs=hT[:, kt, :], start=(kt == 0), stop=(kt == 7)) for kt in range(8)],
                        [wkk, "hT", pk], [pk])
                    ks, kk = kst[o % 2], "kst%d" % (o % 2)
                    act(ks[:], pjt[:, :], AF.Silu, [pk], [kk])
                    ld(SGFS[ot, :, o, :], ks[:], [kk], [U()], eng="act")
                if P1S < 10:
                    continue
                for o in range(6):
                    bd_t, wz_t, wy_t = s5w[o % 2]
                    sk = "s5w%d" % (o % 2)
                    ld(bd_t[:], BDS[o], [], [sk + "b"])
                    ld(wy_t[:], WYS[o], [], [sk + "y"])
                    yv = yps[:, :].rearrange("p (c s) -> p c s", s=8)
                    uv = uT[:, o, :].rearrange("p (c s) -> p c s", s=8)
                    calls = []
                    for j in range(8):
                        calls.append(dict(out=yv[:, :, j:8], lhsT=bd_t[:, j, :], rhs=uv[:, :, 0:8 - j], start=(j == 0), stop=False))
                    for q4 in range(4):
                        for tau in range(8):
                            for ri in range(2):
                                last = (q4 == 3 and tau == 7 and ri == 1)
                                calls.append(dict(out=yv[32 * q4:32 * q4 + 32, :, tau], lhsT=wy_t[:, q4, tau, ri, :],
                                                  rhs=Hb[:, 4 * o + q4, ri, 0:64], start=False, stop=last, tile_position=(0, 32 * q4)))
                    mmg(calls, [sk + "b", sk + "y", "uT", "Hb", "yps"], ["yps"])
                    act(tA[:], yps[:, :], AF.Square, ["yps"], ["tA"])
                    ts("dve", tA[:], tA[:], 0.044715, 1.0, ALU.mult, ALU.add, ["tA"], ["tA"])
                    tt("dve", tB[:], tA[:], yps[:, :], ALU.mult, ["tA", "yps"], ["tB"])
                    act(tA[:], tB[:], AF.Sigmoid, ["tB"], ["tA"], scale=1.5957691216057308)
                    tt("dve", YG[:, o, :], tA[:], yps[:, :], ALU.mult, ["tA", "yps"], ["YG"])
                if P1S < 11:
                    continue
                for half in range(2):
                    wg, wgk = wload(WGLU[:, :, half * 384:(half + 1) * 384], 6, 384)
                    for c3 in range(3):
                        cb = half * 3 + c3
                        pjt, pk = nextpj()
                        mmg([dict(out=pjt[:, :], lhsT=wg[:, kt, c3 * 128:(c3 + 1) * 128], rhs=YG[:, kt, :], start=(kt == 0), stop=(kt == 5)) for kt in range(6)],
                            [wgk, "YG", pk], [pk])
                        act(tA[:], pjt[:, :], AF.Sigmoid, [pk, "bglu"], ["tA"], bias=bglu[:, cb:cb + 1])
                        tt("dve", tB[:], tA[:], YG[:, cb, :], ALU.mult, ["tA", "YG"], ["tB"])
                        tt("dve", YS[:, cb, :], tB[:], SGS[:, cb, :], ALU.mult, ["tB", "SGS"], ["SGS"])
                if P1S < 13:
                    continue
                for cb in range(8):
                    si = wn[0] % NWS
                    wn[0] += 1
                    wfull = wsl[si][:, 0:26 * 128].rearrange("p (k c) -> p k c", k=26)
                    wpsk = wpmk = wg1k = wg2k = "ws%d" % si
                    wps_t, wpm_t, wg1, wg2 = wfull[:, 0:6, :], wfull[:, 6:10, :], wfull[:, 10:18, :], wfull[:, 18:26, :]
                    ld(wps_t, WPS[:, :, cb * 128:(cb + 1) * 128], [], [wpsk])
                    ld(wpm_t, WPM[:, :, cb * 128:(cb + 1) * 128], [wpsk], [wpsk])
                    ld(wg1, WB[:, :, C_GL + 1024 + cb * 128:C_GL + 1024 + (cb + 1) * 128], [wpsk], [wpsk])
                    ld(wg2, WB[:, :, C_GL + 2048 + cb * 128:C_GL + 2048 + (cb + 1) * 128], [wpsk], [wpsk])
                    c0 = 0
                    m0, m0k = M0[cb % 2], "M0_%d" % (cb % 2)
                    pjt, pk = nextpj()
                    mmg([dict(out=pjt[:, :], lhsT=wg1[:, kt, c0:c0 + 128], rhs=hT[:, kt, :], start=(kt == 0), stop=(kt == 7)) for kt in range(8)],
                        [wg1k, "hT", pk], [pk])
                    act(tA[:], pjt[:, :], AF.Sigmoid, [pk, "bm"], ["tA"], bias=bm[:, 8 + cb:9 + cb])
                    pjt, pk = nextpj()
                    mmg([dict(out=pjt[:, :], lhsT=wps_t[:, kt, c0:c0 + 128], rhs=YS[:, kt, :], start=(kt == 0), stop=(kt == 5)) for kt in range(6)],
                        [wpsk, "SGS", pk], [pk])
                    tt("dve", m0[:], tA[:], pjt[:, :], ALU.mult, ["tA", pk], [m0k])
                    pjt, pk = nextpj()
                    mmg([dict(out=pjt[:, :], lhsT=wg2[:, kt, c0:c0 + 128], rhs=hT[:, kt, :], start=(kt == 0), stop=(kt == 7)) for kt in range(8)],
                        [wg2k, "hT", pk], [pk])
                    act(tB[:], pjt[:, :], AF.Sigmoid, [pk, "bm"], ["tB"], bias=bm[:, 16 + cb:17 + cb])
                    pjt, pk = nextpj()
                    mmg([dict(out=pjt[:, :], lhsT=wpm_t[:, kt, c0:c0 + 128], rhs=YM[:, kt, :], start=(kt == 0), stop=(kt == 3)) for kt in range(4)],
                        [wpmk, "SGM", pk], [pk])
                    tt("dve", tC[:], tB[:], pjt[:, :], ALU.mult, ["tB", pk], ["tC"])
                    tt("dve", m0[:], m0[:], tC[:], ALU.add, [m0k, "tC"], [m0k])
                    ld(M0S[ot, :, cb, :], m0[:], [m0k], [U()], eng="act")
            P.emit()

        if upto < 3:
            return nc
        S2 = ExitStack()
        with S2:
            Kh = [sb(S2, "Kh%d" % i, [128, LTOK], BF16) for i in range(2)]
            Vh = [sb(S2, "Vh%d" % i, [128, 64, 65], BF16) for i in range(2)]
            Qh = [sb(S2, "Qh%d" % i, [128, 4096], BF16) for i in range(2)]
            NPT = 4
            Pt = [sb(S2, "Pt%d" % i, [128, TT], BF16) for i in range(NPT)]
            Osb = sb(S2, "Osb", [65, TT], F32)
            Rd = sb(S2, "Rd", [64, TT], F32)
            Yst = [sb(S2, "Yst%d" % i, [64, TT], BF16) for i in range(2)]
            SEL = sb(S2, "SEL", [65, 64], F32)
            sps = [ps(S2, "sps%d" % i, [128, 512], F32) for i in range(4)]
            ops_ = [ps(S2, "ops%d" % i, [128, 512], F32) for i in range(2)]
            dps = ps(S2, "dps", [128, 512], F32)
            mset("pool", SEL[:], 0.0, ["SEL"])
            mset("pool", SEL[64:65, :], 1.0, ["SEL"])
            LA = 2
            its = []
            for h_ in range(12):
                for qt in range(8):
                    nkb = 32 + 4 * qt + 4
                    order = list(range(32 + 4 * qt, nkb)) + list(range(0, 32 + 4 * qt))
                    for n_, kb in enumerate(order):
                        its.append((h_, qt, kb, n_ == 0, n_ == nkb - 1))

            def head_loads(h_):
                hs = h_ % 2
                kh, vh, qh = Kh[hs], Vh[hs], Qh[hs]
                kk, vk, qk = "Kh%d" % hs, "Vh%d" % hs, "Qh%d" % hs
                ld(qh[:], QT[h_], [], [qk])
                for c4 in range(4):
                    ld(kh[:, c4 * 2048:(c4 + 1) * 2048], KT[h_, :, c4 * 2048:(c4 + 1) * 2048], [qk], [kk])
                for c4 in range(4):
                    ld(vh[:, c4 * 16:(c4 + 1) * 16, :], VS[c4 * 16:(c4 + 1) * 16, :, h_ * 65:(h_ + 1) * 65].rearrange("b p d -> p b d"), [kk], [vk])

            def geom(i):
                h_, qt, kb, first, last = its[i]
                diag = kb - (32 + 4 * qt)
                c0 = max(0, diag) * 128
                return h_, qt, kb, first, last, diag, c0

            def emit_qk(i):
                h_, qt, kb, first, last, diag, c0 = geom(i)
                hs = h_ % 2
                st, sk = sps[i % 4], "sps%d" % (i % 4)
                mmg([dict(out=st[:, c0:TT], lhsT=Kh[hs][:, kb * 128:(kb + 1) * 128], rhs=Qh[hs][:, qt * TT + c0:(qt + 1) * TT], start=True, stop=True)],
                    ["Kh%d" % hs, "Qh%d" % hs, sk], [sk])

            pend = []

            def emit_rest(i):
                h_, qt, kb, first, last, diag, c0 = geom(i)
                hs = h_ % 2
                st, sk = sps[i % 4], "sps%d" % (i % 4)
                pt, ptk = Pt[i % NPT], "Pt%d" % (i % NPT)
                op_t, opk = ops_[(h_ * 8 + qt) % 2], "ops%d" % ((h_ * 8 + qt) % 2)
                act(pt[:, c0:TT], st[:, c0:TT], AF.Exp, [sk], [ptk])
                if diag >= 0:
                    tt("dve", pt[:, c0:c0 + 128], pt[:, c0:c0 + 128], maskT[:], ALU.mult, [ptk, "maskT"], [ptk])
                mmg([dict(out=op_t[0:65, c0:TT], lhsT=Vh[hs][:, kb, :], rhs=pt[:, c0:TT], start=first, stop=last)],
                    ["Vh%d" % hs, ptk, opk], [opk])
                if last:
                    cp("dve", Osb[:], op_t[0:65, :], [opk], ["Osb"])
                    pend.append((i + 3, h_, qt))

            def emit_norm(h_, qt):
                mmg([dict(out=dps[0:64, :], lhsT=SEL[:, :], rhs=Osb[:, :], start=True, stop=True)], ["SEL", "Osb", "dps"], ["dps"])
                P.c("dve", lambda h: h.reciprocal(out=Rd[:], in_=dps[0:64, :]), ["dps"], ["Rd"])
                ys, ysk = Yst[qt % 2], "Yst%d" % (qt % 2)
                tt("dve", ys[:], Osb[0:64, :], Rd[:], ALU.mult, ["Osb", "Rd"], [ysk])
                ld(YFS[h_, :, qt * TT:(qt + 1) * TT], ys[:], [ysk], [U()], eng="sp")

            head_loads(0)
            head_loads(1)
            NI = len(its)
            for i in range(min(LA, NI)):
                emit_qk(i)
            for i in range(NI):
                if i + LA < NI:
                    emit_qk(i + LA)
                emit_rest(i)
                while pend and pend[0][0] <= i:
                    _, ph, pq = pend.pop(0)
                    emit_norm(ph, pq)
                if its[i][4] and its[i][1] == 7 and its[i][0] + 2 < 12:
                    head_loads(its[i][0] + 2)
            while pend:
                _, ph, pq = pend.pop(0)
                emit_norm(ph, pq)
            P.emit()

        if upto < 4:
            return nc
        S3 = ExitStack()
        with S3:
            gfin = sb(S3, "gfin", [128, D], F32)
            ld(gfin[:], g_final.to_broadcast([128, D]), [], ["gfin"])
            wpf = sb(S3, "wpf", [128, 6, D], BF16)
            wo = sb(S3, "wo", [128, 8, D], BF16)
            yf = sb(S3, "yf", [128, 6, TT], BF16)
            sgf = sb(S3, "sgf", [128, 6, TT], BF16)
            yfg = sb(S3, "yfg", [128, 6, TT], BF16)
            g0t = sb(S3, "g0t", [128, 8, TT], BF16)
            m0t = sb(S3, "m0t", [128, 8, TT], F32)
            mg = sb(S3, "mg", [128, 8, TT], BF16)
            t3 = sb(S3, "t3", [128, TT], F32)
            x3 = sb(S3, "x3", [128, 4, D], F32)
            o3 = [sb(S3, "o3_%d" % i, [128, D], F32) for i in range(2)]
            j3 = sb(S3, "j3", [128, D], BF16)
            pf = [ps(S3, "pf%d" % i, [128, 512], F32) for i in range(2)]
            po = [ps(S3, "po%d" % i, [128, 512], F32) for i in range(4)]
            ld(wpf[:], WPF, [], ["wpf"])
            ld(wo[:], WOUT, [], ["wo"])
            for ot in range(8):
                for a2 in range(2):
                    ld(yf[a2 * 64:(a2 + 1) * 64, :, :], YFS[:, :, ot * TT:(ot + 1) * TT].rearrange("(c a) d t -> a d c t", a=2)[a2], ["yf"], ["yf"])
                ld(sgf[:], SGFS[ot], [], ["sgf"])
                ld(g0t[:], G0S[ot], [], ["g0t"])
                ld(m0t[:], M0S[ot], [], ["m0t"])
                ld(x3[:], xa[(OWN0 + ot) * TT:(OWN0 + ot + 1) * TT, :].rearrange("(a p) d -> p a d", p=128), [], ["x3"])
                for c in range(6):
                    tt("pool" if c % 2 else "dve", yfg[:, c, :], yf[:, c, :], sgf[:, c, :], ALU.mult, ["yf", "sgf"], ["yfg"])
                for cb in range(8):
                    pt_, pk = pf[cb % 2], "pf%d" % (cb % 2)
                    mmg([dict(out=pt_[:, :], lhsT=wpf[:, kt, cb * 128:(cb + 1) * 128], rhs=yfg[:, kt, :], start=(kt == 0), stop=(kt == 5)) for kt in range(6)],
                        ["wpf", "yfg", pk], [pk])
                    tt("dve", t3[:], g0t[:, cb, :], pt_[:, :], ALU.mult, ["g0t", pk], ["t3"])
                    tt("dve", mg[:, cb, :], t3[:], m0t[:, cb, :], ALU.add, ["t3", "m0t"], ["mg"])
                for a in range(4):
                    ob, obk = o3[a % 2], "o3_%d" % (a % 2)
                    for hf in range(2):
                        pt_, pk = po[(2 * a + hf) % 4], "po%d" % ((2 * a + hf) % 4)
                        mmg([dict(out=pt_[:, :], lhsT=mg[:, kt, a * 128:(a + 1) * 128], rhs=wo[:, kt, hf * 512:(hf + 1) * 512], start=(kt == 0), stop=(kt == 7)) for kt in range(8)],
                            ["mg", "wo", pk], [pk])
                        tt("dve", ob[:, hf * 512:(hf + 1) * 512], pt_[:, :], x3[:, a, hf * 512:(hf + 1) * 512], ALU.add, [pk, "x3"], [obk])
                    act(j3[:], ob[:], AF.Square, [obk], ["j3", "ssq"], accum=ssq[:, a:a + 1])
                    ts("dve", rstd[:, a:a + 1], ssq[:, a:a + 1], 1.0 / D, EPS, ALU.mult, ALU.add, ["ssq"], ["rstd"])
                    act(rstd[:, a:a + 1], rstd[:, a:a + 1], AF.Sqrt, ["rstd"], ["rstd"])
                    P.c("dve", lambda h, a=a: h.reciprocal(out=rstd[:, a:a + 1], in_=rstd[:, a:a + 1]), ["rstd"], ["rstd"])
                    stt(ob[:], ob[:], rstd[:, a:a + 1], gfin[:], ALU.mult, ALU.mult, [obk, "rstd", "gfin"], [obk])
                    ld(yout[ot * TT + a * 128:ot * TT + (a + 1) * 128, :], ob[:], [obk], [U()], eng="act")
            P.emit()
    return nc


_NC = None


def kernel(**inputs):
    global _NC
    if _NC is None:
        _NC = build_nc()
    nc = _NC
    x = np.ascontiguousarray(np.asarray(inputs["x"], dtype=np.float32))
    mem = np.asarray(inputs["mem"], dtype=np.float32)
    in_maps = []
    for c in range(8):
        b, half = c // 2, c % 2
        if half == 0:
            xa = np.concatenate([np.zeros((4096, D), np.float32), x[b, 0:4096]], axis=0)
        else:
            xa = x[b]
        km = np.zeros((NT, TT), dtype=ml_dtypes.bfloat16)
        if half == 0:
            km[0:OWN0, :] = -30000.0
        m = {"xa": np.ascontiguousarray(xa), "kmrow": km, "mem": np.ascontiguousarray(mem[b])}
        for name in ("g_norm", "g_mem_norm", "w_in", "b_forget", "b_merge", "w_mem_kv", "lam_re", "lam_im", "log_step",
                     "s5_b_re", "s5_b_im", "s5_c_re", "s5_c_im", "s5_d", "w_glu", "b_glu", "w_proj_fox", "w_proj_s5",
                     "w_proj_mem", "w_out"):
            a = np.asarray(inputs[name], dtype=np.float32)
            a = a[0]
            if a.ndim == 1:
                a = a[None, :]
            m[name] = np.ascontiguousarray(a)
        m["g_final"] = np.ascontiguousarray(np.asarray(inputs["g_final"], dtype=np.float32)[None, :])
        in_maps.append(m)
    res = run_bass_kernel_spmd(nc, in_maps, core_ids=list(range(8)))
    out = np.zeros((4, 8192, D), dtype=np.float32)
    for c in range(8):
        b, half = c // 2, c % 2
        out[b, half * 4096:(half + 1) * 4096] = np.asarray(res.results[c]["yout"], dtype=np.float32)
    return out
```

```python
import math
from contextlib import ExitStack

import ml_dtypes
import numpy as np
import concourse.bass as bass
import concourse.mybir as mybir
from concourse.bass_utils import run_bass_kernel_spmd

F32 = mybir.dt.float32
BF16 = mybir.dt.bfloat16
I32 = mybir.dt.int32
AF = mybir.ActivationFunctionType
ALU = mybir.AluOpType

ENGS = ("pe", "act", "dve", "pool", "sp")
import os as _os0
NOSELF = tuple(x for x in _os0.environ.get('NOSELF', '').split(',') if x)
D = 1024
NIN = 8716
TT = 512
NT = 16
LTOK = 8192
OWN0 = 8
C_Q, C_K, C_V, C_FL, C_GF, C_U, C_GS, C_QM, C_GM, C_GL = 0, 768, 1536, 2304, 2316, 3084, 3852, 4620, 5132, 5644
EPS = 1e-6
TWO_PI = 2.0 * math.pi


class Op:
    __slots__ = ("eng", "fn", "deps", "needs_inc", "sigval", "is_dma", "lane", "laneval")

    def __init__(self, eng, fn, is_dma=False):
        self.eng = eng
        self.fn = fn
        self.deps = []
        self.needs_inc = False
        self.sigval = None
        self.is_dma = is_dma
        self.lane = None
        self.laneval = None


class Prog:
    def __init__(self, nc, es, n_lanes=6):
        self.nc = nc
        self.n_lanes = n_lanes
        self.esem = {e: es.enter_context(nc.semaphore("s_" + e)) for e in ENGS}
        self.lsem = {}
        for e in ("sp", "act", "pool"):
            for i in range(n_lanes):
                self.lsem[(e, i)] = es.enter_context(nc.semaphore("l_%s%d" % (e, i)))
        self.ecnt = {e: 0 for e in ENGS}
        self.lane_rr = {e: 0 for e in ENGS}
        self.lane_cnt = {k: 0 for k in self.lsem}
        self.barrier = {}
        self._reset()

    def _reset(self):
        self.ops = {e: [] for e in ENGS}
        self.last_w = {}
        self.readers = {}
        self.lane_last = {}

    def _add(self, op, reads, writes):
        deps = []
        for r in reads:
            w = self.last_w.get(r)
            if w is not None:
                deps.append(w)
        for r in writes:
            w = self.last_w.get(r)
            if w is not None:
                deps.append(w)
            deps.extend(self.readers.get(r, ()))
        seen = set(id(d) for d in op.deps)
        for d in deps:
            if d is op or id(d) in seen:
                continue
            if op.eng == "pe" and d.eng == "pe" and not d.is_dma and not op.is_dma:
                continue
            if (not d.is_dma) and (not op.is_dma) and op.eng == d.eng and op.eng in NOSELF:
                continue
            seen.add(id(d))
            op.deps.append(d)
            if not d.is_dma:
                d.needs_inc = True
        for r in reads:
            self.readers.setdefault(r, []).append(op)
        for r in writes:
            self.last_w[r] = op
            self.readers[r] = []
        self.ops[op.eng].append(op)
        return op

    def c(self, eng, fn, reads=(), writes=()):
        return self._add(Op(eng, fn), reads, writes)

    def dma(self, eng, fn, reads=(), writes=()):
        op = Op(eng, fn, is_dma=True)
        lane = (eng, self.lane_rr[eng] % self.n_lanes)
        self.lane_rr[eng] += 1
        prev = self.lane_last.get(lane)
        self.lane_cnt[lane] += 1
        op.lane = lane
        op.laneval = 16 * self.lane_cnt[lane]
        if prev is not None:
            op.deps.append(prev)
        self.lane_last[lane] = op
        return self._add(op, reads, writes)

    def emit(self):
        nc = self.nc
        for e in ENGS:
            last = None
            for op in self.ops[e]:
                if not op.is_dma:
                    last = op
            if last is not None:
                last.needs_inc = True
            for op in self.ops[e]:
                if (not op.is_dma) and op.needs_inc:
                    self.ecnt[e] += 1
                    op.sigval = self.ecnt[e]
        barrier = dict(self.barrier)
        esem, lsem = self.esem, self.lsem

        def tok(d):
            if d.is_dma:
                return lsem[d.lane], d.laneval
            return esem[d.eng], d.sigval

        def run(e, h):
            waited = {}
            for s, v in barrier.values():
                if v > 0:
                    h.wait_ge(s, v)
                    waited[id(s)] = v
            for op in self.ops[e]:
                for d in op.deps:
                    s, v = tok(d)
                    if waited.get(id(s), 0) >= v:
                        continue
                    waited[id(s)] = v
                    h.wait_ge(s, v)
                inst = op.fn(h)
                if op.is_dma:
                    inst.then_inc(lsem[op.lane], 16)
                elif op.needs_inc:
                    inst.then_inc(esem[e], 1)
                import os as _os
                if _os.environ.get('DBGPRINT'):
                    print("OP", e, "dma" if op.is_dma else "c", "lane=%s val=%s" % (op.lane, op.laneval) if op.is_dma else "sig=%s" % op.sigval,
                          "deps=", [((d.lane, d.laneval) if d.is_dma else (d.eng, d.sigval)) for d in op.deps], type(inst).__name__)
            for lane, cnt in self.lane_cnt.items():
                if lane[0] == e and cnt > 0:
                    h.wait_ge(lsem[lane], 16 * cnt)

        with nc.Block() as block:
            @block.tensor
            def _(h):
                run("pe", h)

            @block.scalar
            def _(h):
                run("act", h)

            @block.vector
            def _(h):
                run("dve", h)

            @block.gpsimd
            def _(h):
                run("pool", h)

            @block.sync
            def _(h):
                run("sp", h)

        self.barrier = {}
        for e in ENGS:
            self.barrier[("e", e)] = (esem[e], self.ecnt[e])
        for lane, cnt in self.lane_cnt.items():
            self.barrier[("l", lane)] = (lsem[lane], 16 * cnt)
        self._reset()


def build_nc(dbg=False, upto=9):
    nc = bass.Bass("TRN2", target_bir_lowering=False)

    def din(name, shape, dt=F32):
        return nc.dram_tensor(name, list(shape), dt, kind="ExternalInput").ap()

    def dscr(name, shape, dt):
        return nc.dram_tensor(name, list(shape), dt, kind=("ExternalOutput" if dbg else "Internal")).ap()

    xa = din("xa", [LTOK, D])
    kmrow = din("kmrow", [NT, TT], BF16)
    mem = din("mem", [256, D])
    g_norm = din("g_norm", [1, D])
    g_mem_norm = din("g_mem_norm", [1, D])
    g_final = din("g_final", [1, D])
    w_in = din("w_in", [D, NIN])
    b_forget = din("b_forget", [1, 12])
    b_merge = din("b_merge", [1, 3072])
    w_mem_kv = din("w_mem_kv", [D, 1024])
    lam_re = din("lam_re", [48, 64])
    lam_im = din("lam_im", [48, 64])
    log_step = din("log_step", [1, 48])
    s5_b_re = din("s5_b_re", [48, 64, 16])
    s5_b_im = din("s5_b_im", [48, 64, 16])
    s5_c_re = din("s5_c_re", [48, 16, 64])
    s5_c_im = din("s5_c_im", [48, 16, 64])
    s5_d = din("s5_d", [1, 768])
    w_glu = din("w_glu", [768, 768])
    b_glu = din("b_glu", [1, 768])
    w_proj_fox = din("w_proj_fox", [768, D])
    w_proj_s5 = din("w_proj_s5", [768, D])
    w_proj_mem = din("w_proj_mem", [512, D])
    w_out = din("w_out", [D, D])
    yout = nc.dram_tensor("yout", [4096, D], F32, kind="ExternalOutput").ap()

    WB = dscr("WB", [128, 8, NIN], BF16)
    WKV = dscr("WKV", [128, 8, 1024], BF16)
    WGLU = dscr("WGLU", [128, 6, 768], BF16)
    WPF = dscr("WPF", [128, 6, D], BF16)
    WPS = dscr("WPS", [128, 6, D], BF16)
    WPM = dscr("WPM", [128, 4, D], BF16)
    WOUT = dscr("WOUT", [128, 8, D], BF16)
    BDS = dscr("BDS", [6, 128, 8, 128], BF16)
    WZS = dscr("WZS", [6, 128, 8, 2, 128], BF16)
    WYS = dscr("WYS", [6, 128, 4, 8, 2, 32], BF16)
    KT = dscr("KT", [12, 128, LTOK], BF16)
    QT = dscr("QT", [12, 128, 4096], BF16)
    VS = dscr("VS", [64, 128, 780], BF16)
    M0S = dscr("M0S", [8, 128, 8, TT], F32)
    G0S = dscr("G0S", [8, 128, 8, TT], BF16)
    SGFS = dscr("SGFS", [8, 128, 6, TT], BF16)
    YFS = dscr("YFS", [12, 64, 4096], BF16)

    es = ExitStack()
    with es:
        P = Prog(nc, es)

        def sb(stack, name, shape, dt):
            return stack.enter_context(nc.sbuf_tensor(name, list(shape), dt))

        def ps(stack, name, shape, dt=F32):
            return stack.enter_context(nc.psum_tensor(name, list(shape), dt))

        def act(out, in_, func, r, w, bias=None, scale=None, accum=None):
            kw = {}
            if bias is not None:
                kw["bias"] = bias
            if scale is not None:
                kw["scale"] = scale
            if accum is not None:
                kw["accum_out"] = accum
            return P.c("act", lambda h: h.activation(out=out, in_=in_, func=func, **kw), r, w)

        def tt(eng, out, in0, in1, op, r, w):
            return P.c(eng, lambda h: h.tensor_tensor(out=out, in0=in0, in1=in1, op=op), r, w)

        def ts(eng, out, in0, s1, s2, op0, op1, r, w):
            if op1 is None:
                return P.c(eng, lambda h: h.tensor_scalar(out=out, in0=in0, scalar1=s1, scalar2=None, op0=op0), r, w)
            return P.c(eng, lambda h: h.tensor_scalar(out=out, in0=in0, scalar1=s1, scalar2=s2, op0=op0, op1=op1), r, w)

        def stt(out, in0, scalar, in1, op0, op1, r, w):
            return P.c("dve", lambda h: h.scalar_tensor_tensor(out=out, in0=in0, scalar=scalar, in1=in1, op0=op0, op1=op1), r, w)

        def cp(eng, out, in_, r, w):
            if eng == "act":
                return P.c("act", lambda h: h.activation(out=out, in_=in_, func=AF.Identity), r, w)
            return P.c(eng, lambda h: h.tensor_copy(out=out, in_=in_), r, w)

        def mset(eng, ap, val, w):
            return P.c(eng, lambda h: h.memset(ap, val), (), w)

        def mmg(calls, r, w):
            def fn(h):
                inst = None
                for kw in calls:
                    inst = h.matmul(**kw)
                return inst
            return P.c("pe", fn, r, w)

        def tpg(calls, r, w):
            def fn(h):
                inst = None
                for (o, i, idn) in calls:
                    inst = h.transpose(o, i, idn)
                return inst
            return P.c("pe", fn, r, w)

        uq = [0]

        def U():
            uq[0] += 1
            return "u%d" % uq[0]

        def ld(out, in_, r, w, eng="sp", slow=False):
            if slow:
                return P.dma(eng, lambda h: h.dma_start(out=out, in_=in_, allow_slow_non_contiguous=True), r, w)
            return P.dma(eng, lambda h: h.dma_start(out=out, in_=in_), r, w)

        def rows_pattern(tile_ap, n, lo, hi, w):
            P.c("pool", lambda h: h.memset(tile_ap, 1.0), (), w)
            P.c("pool", lambda h: h.affine_select(out=tile_ap, in_=tile_ap, pattern=[[0, n]], compare_op=ALU.is_ge,
                                                  fill=0.0, base=-lo, channel_multiplier=1), w, w)
            P.c("pool", lambda h: h.affine_select(out=tile_ap, in_=tile_ap, pattern=[[0, n]], compare_op=ALU.is_ge,
                                                  fill=0.0, base=hi, channel_multiplier=-1), w, w)

        G = ExitStack()
        es.enter_context(G)
        ident_f = sb(G, "ident_f", [128, 128], F32)
        ident_b = sb(G, "ident_b", [128, 128], BF16)
        ones_f = sb(G, "ones_f", [128, 512], F32)
        ones_b = sb(G, "ones_b", [128, 128], BF16)
        maskT = sb(G, "maskT", [128, 128], BF16)
        gn = sb(G, "gn", [128, 8], F32)
        gmn = sb(G, "gmn", [128, 8], F32)
        bm = sb(G, "bm", [128, 24], F32)
        bglu = sb(G, "bglu", [128, 6], F32)
        SELK = sb(G, "SELK", [128, 12, 128], BF16)
        SELQ = sb(G, "SELQ", [128, 12, 128], BF16)
        qscale = sb(G, "qscale", [128, 1], F32)
        negb = sb(G, "negb", [128, 1], F32)
        nm1 = sb(G, "nm1", [128, 1], F32)
        nm2 = sb(G, "nm2", [128, 1], F32)
        WFL3 = sb(G, "WFL3", [128, 8, 76], BF16)
        MKT = sb(G, "MKT", [128, 4, 256], BF16)
        MV = sb(G, "MV", [128, 2, 512], BF16)
        A1R = sb(G, "A1R", [128, 24], F32)
        A1I = sb(G, "A1I", [128, 24], F32)
        A8R = sb(G, "A8R", [128, 24], F32)
        A8I = sb(G, "A8I", [128, 24], F32)
        APR = sb(G, "APR", [128, 8, 24], F32)
        API = sb(G, "API", [128, 8, 24], F32)
        Fcar = sb(G, "Fcar", [128, 1], F32)
        TB = sb(G, "TB", [128, 24, 2, 9], F32)
        ssq = sb(G, "ssq", [128, 8], F32)
        rstd = sb(G, "rstd", [128, 8], F32)

        mset("pool", ones_f[:], 1.0, ["ones_f"])
        mset("pool", ident_f[:], 1.0, ["ident_f"])
        P.c("pool", lambda h: h.affine_select(out=ident_f[:], in_=ident_f[:], pattern=[[-1, 128]], compare_op=ALU.is_equal,
                                              fill=0.0, base=0, channel_multiplier=1), ["ident_f"], ["ident_f"])
        cp("dve", ident_b[:], ident_f[:], ["ident_f"], ["ident_b"])
        cp("dve", ones_b[:], ones_f[:, 0:128], ["ones_f"], ["ones_b"])
        mset("pool", maskT[:], 1.0, ["maskT"])
        P.c("pool", lambda h: h.affine_select(out=maskT[:], in_=maskT[:], pattern=[[1, 128]], compare_op=ALU.is_ge,
                                              fill=0.0, base=0, channel_multiplier=-1), ["maskT"], ["maskT"])
        ld(gn[:], g_norm.rearrange("o (kt p) -> p (o kt)", p=128), [], ["gn"], slow=True)
        ld(gmn[:], g_mem_norm.rearrange("o (kt p) -> p (o kt)", p=128), [], ["gmn"], slow=True)
        ld(bm[:], b_merge.rearrange("o (c p) -> p (o c)", p=128), [], ["bm"], slow=True)
        ld(bglu[:], b_glu.rearrange("o (c p) -> p (o c)", p=128), [], ["bglu"], slow=True)
        mset("pool", qscale[:], 1.0, ["qscale"])
        mset("pool", qscale[0:64, :], 0.125, ["qscale"])
        mset("pool", negb[:], 0.0, ["negb"])
        for base in (0, 32, 64):
            ld(negb[base:base + 12, :], b_forget.rearrange("o h -> h o"), [], ["negb"], slow=True)
        ts("dve", negb[:], negb[:], -1.0, None, ALU.mult, None, ["negb"], ["negb"])
        mset("pool", nm1[:], 0.0, ["nm1"])
        mset("pool", nm1[32:64, :], -1.0, ["nm1"])
        mset("pool", nm1[64:96, :], -1.0, ["nm1"])
        mset("pool", nm2[:], 0.0, ["nm2"])
        mset("pool", nm2[64:96, :], -1.0, ["nm2"])
        mset("pool", Fcar[:], 0.0, ["Fcar"])
        mset("pool", TB[:], 0.0, ["TB"])
        mset("pool", SELK[:], 0.0, ["SELK"])
        mset("pool", SELQ[:], 0.0, ["SELQ"])
        for j, base in enumerate((0, 32, 64)):
            ts("dve", SELK[:, :, 96 + j], ident_f[:, base:base + 12], -1.0, None, ALU.mult, None, ["ident_f", "SELK"], ["SELK"])
            cp("dve", SELQ[:, :, 64 + j], ident_f[:, base:base + 12], ["ident_f", "SELQ"], ["SELQ"])
        for h_ in range(12):
            cp("dve", SELK[:, h_, 99:100], ident_f[:, 76:77], ["ident_f", "SELK"], ["SELK"])
            for c_ in (64, 65, 66):
                cp("dve", SELK[:, h_, c_:c_ + 1], ident_f[:, 96:97], ["ident_f", "SELK"], ["SELK"])
            for c_ in (96, 97, 98, 99):
                cp("dve", SELQ[:, h_, c_:c_ + 1], ident_f[:, 96:97], ["ident_f", "SELQ"], ["SELQ"])

        S0 = ExitStack()
        with S0:
            cv = [sb(S0, "cv%d" % i, [128, 8, 512], BF16) for i in range(2)]
            cvn = [0]

            def convert(src, nkt, ncols, dst, wkey=None):
                srcv = src.rearrange("(kt p) c -> p kt c", p=128)
                c0 = 0
                while c0 < ncols:
                    n = min(512, ncols - c0)
                    i = cvn[0] % 2
                    cvn[0] += 1
                    t = cv[i]
                    ld(t[:, 0:nkt, 0:n], srcv[:, :, c0:c0 + n], [], ["cv%d" % i], eng="pool")
                    ld(dst[:, :, c0:c0 + n], t[:, 0:nkt, 0:n], ["cv%d" % i], [wkey] if wkey else [U()], eng="sp")
                    c0 += n

            convert(w_in, 8, NIN, WB)
            convert(w_mem_kv, 8, 1024, WKV, "WKV")
            convert(w_glu, 6, 768, WGLU)
            convert(w_proj_fox, 6, D, WPF)
            convert(w_proj_s5, 6, D, WPS)
            convert(w_proj_mem, 4, D, WPM)
            convert(w_out, 8, D, WOUT)
            mset("pool", WFL3[:], 0.0, ["WFL3"])
            for base in (0, 32, 64):
                ld(WFL3[:, :, base:base + 12], w_in.rearrange("(kt p) c -> p kt c", p=128)[:, :, C_FL:C_FL + 12],
                   ["WFL3"], ["WFL3"], eng="pool")

            def s5t(name, shape, dt=F32):
                return sb(S0, name, shape, dt)
            LR = s5t("LR", [128, 24]); LI = s5t("LI", [128, 24]); LS = s5t("LS", [128, 24])
            ld(LR[:], lam_re.rearrange("(pr g2) p -> (g2 p) pr", g2=2), [], ["LR"], slow=True)
            ld(LI[:], lam_im.rearrange("(pr g2) p -> (g2 p) pr", g2=2), [], ["LI"], slow=True)
            for g2 in range(2):
                ld(LS[g2 * 64:(g2 + 1) * 64, :],
                   log_step.rearrange("o (pr g2) -> o g2 pr", g2=2)[:, g2, :].to_broadcast([64, 24]), [], ["LS"], slow=True)
            STP = s5t("STP", [128, 24]); MAG = s5t("MAG", [128, 24]); ANG = s5t("ANG", [128, 24])
            T0 = s5t("T0", [128, 24]); T1 = s5t("T1", [128, 24]); T2 = s5t("T2", [128, 24]); TI = s5t("TI", [128, 24], I32)
            ABR = s5t("ABR", [128, 24]); ABI = s5t("ABI", [128, 24]); SN = s5t("SN", [128, 24]); CS = s5t("CS", [128, 24])
            FR = s5t("FR", [128, 24]); FI = s5t("FI", [128, 24])
            act(STP[:], LS[:], AF.Exp, ["LS"], ["STP"])
            tt("dve", T0[:], LR[:], STP[:], ALU.mult, ["LR", "STP"], ["T0"])
            act(MAG[:], T0[:], AF.Exp, ["T0"], ["MAG"])
            tt("dve", ANG[:], LI[:], STP[:], ALU.mult, ["LI", "STP"], ["ANG"])

            def sin_of(dst, shift, key):
                ts("dve", T0[:], ANG[:], shift, None, ALU.add, None, ["ANG"], ["T0"])
                ts("dve", T1[:], T0[:], 1.0 / TWO_PI, 0.5, ALU.mult, ALU.add, ["T0"], ["T1"])
                cp("dve", TI[:], T1[:], ["T1"], ["TI"])
                cp("dve", T1[:], TI[:], ["TI"], ["T1"])
                stt(T2[:], T1[:], -TWO_PI, T0[:], ALU.mult, ALU.add, ["T1", "T0"], ["T2"])
                ts("dve", T1[:], T2[:], math.pi, -TWO_PI, ALU.is_gt, ALU.mult, ["T2"], ["T1"])
                tt("dve", T2[:], T2[:], T1[:], ALU.add, ["T2", "T1"], ["T2"])
                ts("dve", T1[:], T2[:], -math.pi, TWO_PI, ALU.is_lt, ALU.mult, ["T2"], ["T1"])
                tt("dve", T2[:], T2[:], T1[:], ALU.add, ["T2", "T1"], ["T2"])
                ts("dve", T2[:], T2[:], math.pi, -math.pi, ALU.min, ALU.max, ["T2"], ["T2"])
                act(dst[:], T2[:], AF.Sin, ["T2"], [key])
            sin_of(SN, 0.0, "SN")
            sin_of(CS, math.pi / 2.0, "CS")
            tt("dve", ABR[:], MAG[:], CS[:], ALU.mult, ["MAG", "CS"], ["ABR"])
            tt("dve", ABI[:], MAG[:], SN[:], ALU.mult, ["MAG", "SN"], ["ABI"])
            DEN = s5t("DEN", [128, 24]); NR = s5t("NR", [128, 24])
            tt("dve", T0[:], LR[:], LR[:], ALU.mult, ["LR"], ["T0"])
            tt("dve", T1[:], LI[:], LI[:], ALU.mult, ["LI"], ["T1"])
            tt("dve", DEN[:], T0[:], T1[:], ALU.add, ["T0", "T1"], ["DEN"])
            P.c("dve", lambda h: h.reciprocal(out=DEN[:], in_=DEN[:]), ["DEN"], ["DEN"])
            ts("dve", NR[:], ABR[:], -1.0, None, ALU.add, None, ["ABR"], ["NR"])
            tt("dve", T0[:], NR[:], LR[:], ALU.mult, ["NR", "LR"], ["T0"])
            tt("dve", T1[:], ABI[:], LI[:], ALU.mult, ["ABI", "LI"], ["T1"])
            tt("dve", T0[:], T0[:], T1[:], ALU.add, ["T0", "T1"], ["T0"])
            tt("dve", FR[:], T0[:], DEN[:], ALU.mult, ["T0", "DEN"], ["FR"])
            tt("dve", T0[:], ABI[:], LR[:], ALU.mult, ["ABI", "LR"], ["T0"])
            tt("dve", T1[:], NR[:], LI[:], ALU.mult, ["NR", "LI"], ["T1"])
            tt("dve", T0[:], T0[:], T1[:], ALU.subtract, ["T0", "T1"], ["T0"])
            tt("dve", FI[:], T0[:], DEN[:], ALU.mult, ["T0", "DEN"], ["FI"])
            PWR = s5t("PWR", [128, 9, 24]); PWI = s5t("PWI", [128, 9, 24])
            AWR = s5t("AWR", [128, 9, 24]); AWI = s5t("AWI", [128, 9, 24])

            def cmul(orr, oi, ar, ai, br, bi, rk, wk):
                tt("dve", T0[:], ar, br, ALU.mult, rk, ["T0"])
                tt("dve", T1[:], ai, bi, ALU.mult, rk, ["T1"])
                tt("dve", T2[:], ar, bi, ALU.mult, rk, ["T2"])
                tt("dve", orr, T0[:], T1[:], ALU.subtract, ["T0", "T1"], wk)
                tt("dve", T0[:], ai, br, ALU.mult, rk + wk, ["T0"])
                tt("dve", oi, T2[:], T0[:], ALU.add, ["T2", "T0"], wk)
            mset("pool", PWR[:, 0, :], 1.0, ["PW"])
            mset("pool", PWI[:, 0, :], 0.0, ["PW"])
            for k in range(1, 9):
                cmul(PWR[:, k, :], PWI[:, k, :], PWR[:, k - 1, :], PWI[:, k - 1, :], ABR[:], ABI[:], ["PW", "ABR", "ABI"], ["PW"])
            mset("pool", AWR[:, 0, :], 1.0, ["AW"])
            mset("pool", AWI[:, 0, :], 0.0, ["AW"])
            for k in range(1, 9):
                cmul(AWR[:, k, :], AWI[:, k, :], AWR[:, k - 1, :], AWI[:, k - 1, :], PWR[:, 8, :], PWI[:, 8, :], ["AW", "PW"], ["AW"])
            cp("dve", A1R[:], AWR[:, 1, :], ["AW"], ["A1"])
            cp("dve", A1I[:], AWI[:, 1, :], ["AW"], ["A1"])
            cp("dve", APR[:], AWR[:, 1:9, :], ["AW"], ["APR"])
            cp("dve", API[:], AWI[:, 1:9, :], ["AW"], ["API"])
            cp("dve", A8R[:], AWR[:, 8, :], ["AW"], ["A8"])
            cp("dve", A8I[:], AWI[:, 8, :], ["AW"], ["A8"])

            BRE = s5t("BRE", [128, 24, 16]); BIM = s5t("BIM", [128, 24, 16])
            ld(BRE[:], s5_b_re.rearrange("(pr g2) p h -> (g2 p) pr h", g2=2), [], ["BRE"])
            ld(BIM[:], s5_b_im.rearrange("(pr g2) p h -> (g2 p) pr h", g2=2), [], ["BIM"])
            BBR = s5t("BBR", [128, 24, 16]); BBI = s5t("BBI", [128, 24, 16])
            U0 = s5t("U0", [128, 24, 16]); U1 = s5t("U1", [128, 24, 16])
            frb = FR[:].unsqueeze(2).to_broadcast([128, 24, 16])
            fib = FI[:].unsqueeze(2).to_broadcast([128, 24, 16])
            tt("dve", U0[:], BRE[:], frb, ALU.mult, ["BRE", "FR"], ["U0"])
            tt("dve", U1[:], BIM[:], fib, ALU.mult, ["BIM", "FI"], ["U1"])
            tt("dve", BBR[:], U0[:], U1[:], ALU.subtract, ["U0", "U1"], ["BBR"])
            tt("dve", U0[:], BIM[:], frb, ALU.mult, ["BIM", "FR"], ["U0"])
            tt("dve", U1[:], BRE[:], fib, ALU.mult, ["BRE", "FI"], ["U1"])
            tt("dve", BBI[:], U0[:], U1[:], ALU.add, ["U0", "U1"], ["BBI"])
            BDR = s5t("BDR", [128, 24, 32]); BDI = s5t("BDI", [128, 24, 32])
            mset("pool", BDR[:], 0.0, ["BDR"]); mset("pool", BDI[:], 0.0, ["BDI"])
            for g2 in range(2):
                sl = slice(g2 * 64, (g2 + 1) * 64)
                cp("dve", BDR[sl, :, g2 * 16:(g2 + 1) * 16], BBR[sl, :, :], ["BBR", "BDR"], ["BDR"])
                cp("dve", BDI[sl, :, g2 * 16:(g2 + 1) * 16], BBI[sl, :, :], ["BBI", "BDI"], ["BDI"])
            BWR = s5t("BWR", [128, 24, 128]); BWI = s5t("BWI", [128, 24, 128])
            mset("pool", BWR[:], 0.0, ["BWR"]); mset("pool", BWI[:], 0.0, ["BWI"])
            for q4 in range(4):
                for o in range(6):
                    pr = 4 * o + q4
                    cp("dve", BWR[:, pr, 32 * q4:32 * q4 + 32], BDR[:, pr, :], ["BDR", "BWR"], ["BWR"])
                    cp("pool", BWI[:, pr, 32 * q4:32 * q4 + 32], BDI[:, pr, :], ["BDI", "BWI"], ["BWI"])
            CTR = s5t("CTR", [128, 24, 16]); CTI = s5t("CTI", [128, 24, 16])
            CN = s5t("CN", [128, 3, 2, 128])
            tp_ps = ps(S0, "tp_ps", [128, 512], F32)
            for ri, (csrc, cdst, key) in enumerate(((s5_c_re, CTR, "CTR"), (s5_c_im, CTI, "CTI"))):
                cv4 = csrc.rearrange("(pr g2) h p -> pr h g2 p", g2=2)
                for pr in range(24):
                    ld(CN[(pr % 8) * 16:(pr % 8) * 16 + 16, pr // 8, ri, :].rearrange("h (a p) -> h a p", a=2),
                       cv4[pr], ["CN%d" % ri], ["CN%d" % ri])
                for j in range(3):
                    tpg([(tp_ps[:, 0:128], CN[:, j, ri, :], ident_f[:])], ["CN%d" % ri, "ident_f", "tp_ps"], ["tp_ps"])
                    cp("act", cdst[:, 8 * j:8 * j + 8, :], tp_ps[:, 0:128].rearrange("p (a h) -> p a h", a=8), ["tp_ps"], [key])
            XR = s5t("XR", [128, 24, 32]); XI = s5t("XI", [128, 24, 32])
            V0 = s5t("V0", [128, 24, 32]); V1 = s5t("V1", [128, 24, 32])
            WZs = [s5t("WZs%d" % i, [128, 6, 2, 128], BF16) for i in range(2)]
            for s in range(8):
                k = 7 - s
                wzs, wzk = WZs[s % 2], "WZs%d" % (s % 2)
                pr_ = PWR[:, k, :].unsqueeze(2).to_broadcast([128, 24, 32])
                pi_ = PWI[:, k, :].unsqueeze(2).to_broadcast([128, 24, 32])
                tt("dve", V0[:], BDR[:], pr_, ALU.mult, ["BDR", "PW"], ["V0"])
                tt("dve", V1[:], BDI[:], pi_, ALU.mult, ["BDI", "PW"], ["V1"])
                tt("dve", XR[:], V0[:], V1[:], ALU.subtract, ["V0", "V1"], ["XR"])
                tt("dve", V0[:], BDI[:], pr_, ALU.mult, ["BDI", "PW"], ["V0"])
                tt("dve", V1[:], BDR[:], pi_, ALU.mult, ["BDR", "PW"], ["V1"])
                tt("dve", XI[:], V0[:], V1[:], ALU.add, ["V0", "V1"], ["XI"])
                for o in range(6):
                    for ri, X in enumerate((XR, XI)):
                        tpg([(tp_ps[:, 0:128], X[:, 4 * o:4 * o + 4, :].rearrange("p a b -> p (a b)"), ident_f[:])], ["XR", "XI", "ident_f", "tp_ps"], ["tp_ps"])
                        cp("act", wzs[:, o, ri, :], tp_ps[:, 0:128], ["tp_ps"], [wzk])
                for o in range(6):
                    ld(WZS[o, :, s, :, :], wzs[:, o, :, :], [wzk], [U()])
            QR = s5t("QR", [128, 24, 16]); QI = s5t("QI", [128, 24, 16])
            QBR = s5t("QBR", [128, 24, 32]); QBI = s5t("QBI", [128, 24, 32])
            WYk = [s5t("WYk%d" % i, [128, 24, 2, 32], BF16) for i in range(2)]
            BDj = [s5t("BDj%d" % i, [128, 6, 128], F32) for i in range(2)]
            BDjb = [s5t("BDjb%d" % i, [128, 6, 128], BF16) for i in range(2)]
            DSK = s5t("DSK", [128, 6], F32)
            ld(DSK[:], s5_d.rearrange("o (c p) -> p (o c)", p=128), [], ["DSK"], slow=True)
            mset("pool", QBR[:], 0.0, ["QBR"]); mset("pool", QBI[:], 0.0, ["QBI"])
            for i in range(2):
                mset("pool", BDj[i][:], 0.0, ["BDj%d" % i])
            bd_ps = ps(S0, "bd_ps", [128, 6, 32], F32)
            for k in range(9):
                pr_ = PWR[:, k, :].unsqueeze(2).to_broadcast([128, 24, 16])
                pi_ = PWI[:, k, :].unsqueeze(2).to_broadcast([128, 24, 16])
                tt("dve", U0[:], CTR[:], pr_, ALU.mult, ["CTR", "PW"], ["U0"])
                tt("dve", U1[:], CTI[:], pi_, ALU.mult, ["CTI", "PW"], ["U1"])
                tt("dve", QR[:], U0[:], U1[:], ALU.subtract, ["U0", "U1"], ["QR"])
                tt("dve", U0[:], CTR[:], pi_, ALU.mult, ["CTR", "PW"], ["U0"])
                tt("dve", U1[:], CTI[:], pr_, ALU.mult, ["CTI", "PW"], ["U1"])
                tt("dve", QI[:], U0[:], U1[:], ALU.add, ["U0", "U1"], ["QI"])
                for g2 in range(2):
                    sl = slice(g2 * 64, (g2 + 1) * 64)
                    cp("dve", QBR[sl, :, g2 * 16:(g2 + 1) * 16], QR[sl, :, :], ["QR", "QBR"], ["QBR"])
                    ts("dve", QBI[sl, :, g2 * 16:(g2 + 1) * 16], QI[sl, :, :], -1.0, None, ALU.mult, None, ["QI", "QBI"], ["QBI"])
                if k >= 1:
                    wyk, wykk = WYk[k % 2], "WYk%d" % (k % 2)
                    cp("act", wyk[:, :, 0, :], QBR[:], ["QBR"], [wykk])
                    cp("act", wyk[:, :, 1, :], QBI[:], ["QBI"], [wykk])
                    for o in range(6):
                        ld(WYS[o, :, :, k - 1, :, :], wyk[:, 4 * o:4 * o + 4, :, :], [wykk], [U()])
                if k <= 7:
                    bdj, bdk = BDj[k % 2], "BDj%d" % (k % 2)
                    bdjb, bdbk = BDjb[k % 2], "BDjb%d" % (k % 2)
                    for o in range(6):
                        calls = []
                        for q4 in range(4):
                            pr = 4 * o + q4
                            calls.append(dict(out=bd_ps[:, o, :], lhsT=BWR[:, pr, :], rhs=QBR[:, pr, :], start=(q4 == 0), stop=False))
                            calls.append(dict(out=bd_ps[:, o, :], lhsT=BWI[:, pr, :], rhs=QBI[:, pr, :], start=False, stop=(q4 == 3)))
                        mmg(calls, ["BWR", "BWI", "QBR", "QBI", "bd_ps"], ["bd_ps"])
                    for q4 in range(4):
                        sl = slice(32 * q4, 32 * q4 + 32)
                        cp("act", bdj[sl, :, 32 * q4:32 * q4 + 32], bd_ps[sl, :, :], ["bd_ps"], [bdk])
                    if k == 0:
                        for o in range(6):
                            stt(bdj[:, o, :], ident_f[:], DSK[:, o:o + 1], bdj[:, o, :], ALU.mult, ALU.add, ["ident_f", "DSK", bdk], [bdk])
                    cp("dve", bdjb[:], bdj[:], [bdk], [bdbk])
                    for o in range(6):
                        ld(BDS[o, :, k, :], bdjb[:, o, :], [bdbk], [U()])
            P.emit()
        if upto < 1:
            return nc
        S0b = ExitStack()
        with S0b:
            S0 = S0b

            def s5t(name, shape, dt=F32):
                return sb(S0b, name, shape, dt)
            mt_x = s5t("mt_x", [128, 2, D], F32)
            mt_n = s5t("mt_n", [128, 2, D], BF16)
            mt_j = s5t("mt_j", [128, D], BF16)
            MHT = s5t("MHT", [128, 8, 256], BF16)
            wkv_t = s5t("wkv_t", [128, 8, 1024], BF16)
            tpb_ps = ps(S0, "tpb_ps", [128, 512], BF16)
            mk_ps = ps(S0, "mk_ps", [128, 512], F32)
            ld(mt_x[:], mem.rearrange("(a p) d -> p a d", p=128), [], ["mt_x"])
            ld(wkv_t[:], WKV, ["WKV"], ["wkv_t"])
            for a in range(2):
                act(mt_j[:], mt_x[:, a, :], AF.Square, ["mt_x"], ["mt_j", "ssq"], accum=ssq[:, a:a + 1])
            ts("dve", rstd[:, 0:2], ssq[:, 0:2], 1.0 / D, EPS, ALU.mult, ALU.add, ["ssq"], ["rstd"])
            act(rstd[:, 0:2], rstd[:, 0:2], AF.Sqrt, ["rstd"], ["rstd"])
            P.c("dve", lambda h: h.reciprocal(out=rstd[:, 0:2], in_=rstd[:, 0:2]), ["rstd"], ["rstd"])
            for a in range(2):
                ts("dve", mt_n[:, a, :], mt_x[:, a, :], rstd[:, a:a + 1], None, ALU.mult, None, ["mt_x", "rstd"], ["mt_n"])
            for kt in range(8):
                tpg([(tpb_ps[:, a * 128:(a + 1) * 128], mt_n[:, a, kt * 128:(kt + 1) * 128], ident_b[:]) for a in range(2)],
                    ["mt_n", "ident_b", "tpb_ps"], ["tpb_ps"])
                ts("dve", MHT[:, kt, :], tpb_ps[:, 0:256], gmn[:, kt:kt + 1], None, ALU.mult, None, ["tpb_ps", "gmn"], ["MHT"])
            for hd in range(4):
                mmg([dict(out=mk_ps[:, 0:256], lhsT=wkv_t[:, kt, hd * 128:(hd + 1) * 128], rhs=MHT[:, kt, :],
                          start=(kt == 0), stop=(kt == 7)) for kt in range(8)], ["wkv_t", "MHT", "mk_ps"], ["mk_ps"])
                cp("act", MKT[:, hd, :], mk_ps[:, 0:256], ["mk_ps"], ["MKT"])
            for mt in range(2):
                mmg([dict(out=mk_ps[:, :], lhsT=MHT[:, kt, mt * 128:(mt + 1) * 128], rhs=wkv_t[:, kt, 512:1024],
                          start=(kt == 0), stop=(kt == 7)) for kt in range(8)], ["wkv_t", "MHT", "mk_ps"], ["mk_ps"])
                cp("act", MV[:, mt, :], mk_ps[:, :], ["mk_ps"], ["MV"])
            P.emit()

        if upto < 2:
            return nc
        S1 = ExitStack()
        with S1:
            xt = sb(S1, "xt", [128, 4, D], F32)
            xn = sb(S1, "xn", [128, 4, D], BF16)
            sqj = sb(S1, "sqj", [128, D], BF16)
            hT = sb(S1, "hT", [128, 8, TT], BF16)
            NWS = 3
            wsl = [sb(S1, "ws%d" % i, [128, 4096], BF16) for i in range(NWS)]
            SPt = [sb(S1, "SPt%d" % i, [128, TT], BF16) for i in range(2)]
            HIb = sb(S1, "HIb", [128, TT], BF16)
            MIDb = sb(S1, "MIDb", [128, TT], BF16)
            kst = [sb(S1, "kst%d" % i, [128, TT], BF16) for i in range(2)]
            Vst = sb(S1, "Vst", [128, 4, 12, 65], BF16)
            uT = sb(S1, "uT", [128, 6, TT], BF16)
            s5w = [(sb(S1, "s5bd%d" % i, [128, 8, 128], BF16), sb(S1, "s5wz%d" % i, [128, 8, 2, 128], BF16),
                    sb(S1, "s5wy%d" % i, [128, 4, 8, 2, 32], BF16)) for i in range(2)]
            Z = sb(S1, "Z", [128, 24, 2, 64], F32)
            Hb = sb(S1, "Hb", [128, 24, 2, 65], BF16)
            RT = [sb(S1, "RT%d" % i, [128, 24, 8], F32) for i in range(8)]
            RS = [sb(S1, "RS%d" % i, [128, 24], F32) for i in range(8)]
            tA = sb(S1, "tA", [128, TT], F32)
            tB = sb(S1, "tB", [128, TT], F32)
            tC = sb(S1, "tC", [128, TT], F32)
            Fe, Ff, R1 = tA, tB, tC
            YG = sb(S1, "YG", [128, 6, TT], BF16)
            SGS = sb(S1, "SGS", [128, 6, TT], BF16)
            YS = SGS
            QM = sb(S1, "QM", [128, 4, TT], BF16)
            SGM = sb(S1, "SGM", [128, 4, TT], BF16)
            PmT = sb(S1, "PmT", [128, 2, TT], BF16)
            YM = SGM
            M0 = [sb(S1, "M0_%d" % i, [128, TT], F32) for i in range(2)]
            G0st = [sb(S1, "G0st%d" % i, [128, TT], BF16) for i in range(2)]
            pj = [ps(S1, "pj%d" % i, [128, 512], F32) for i in range(2)]
            tpall = ps(S1, "tpall", [128, 1024], BF16)
            tpp = [tpall[:, 0:512], tpall[:, 0:512]]
            zpsb = [ps(S1, "zps%d" % i, [128, 512], F32) for i in range(4)]
            yps = ps(S1, "yps", [128, 512], F32)

            import os
            if not os.environ.get('SKIPM'):
                for i in range(2):
                    mset("pool", SPt[i][:], 0.0, ["SPt%d" % i])
                    mset("pool", SPt[i][96:97, :], 1.0, ["SPt%d" % i])
                mset("pool", Vst[:], 1.0, ["Vst"])

            wn = [0]
            pjn = [0]

            def wload(src3, nkt, ncols):
                i = wn[0] % NWS
                wn[0] += 1
                t = wsl[i][:, 0:nkt * ncols].rearrange("p (k c) -> p k c", k=nkt)
                if os.environ.get('WLDTINY'):
                    ld(t[:, 0:1, 0:16], src3[:, 0:1, 0:16], [], ["ws%d" % i])
                else:
                    ld(t, src3, [], ["ws%d" % i])
                return t, "ws%d" % i

            PJL = [(pj[0], "pj0"), (pj[1], "pj1"), (zpsb[0], "zps0"), (zpsb[1], "zps1"), (zpsb[2], "zps2"), (zpsb[3], "zps3"), (yps, "yps")]

            def nextpj():
                i = pjn[0] % len(PJL)
                pjn[0] += 1
                return PJL[i]

            import os
            P1T = int(os.environ.get('P1T', NT)); P1S = int(os.environ.get('P1S', 99)); OWNX = int(os.environ.get('OWNX', OWN0))
            for it in range(P1T):
                own = it >= OWNX
                ot = it - OWNX
                sp_i = it % 2
                spt, spk = SPt[sp_i], "SPt%d" % sp_i
                if it == 0:
                    ld(xt[:], xa[it * TT:(it + 1) * TT, :].rearrange("(a p) d -> p a d", p=128), [], ["xt"])
                if not os.environ.get('SKIPK'):
                    ld(spt[76:77, :], kmrow[it:it + 1, :], [], [spk], eng="pool")
                if P1S < -1:
                    continue
                for a in range(4):
                    act(sqj[:], xt[:, a, :], AF.Square, ["xt"], ["sqj", "ssq"], accum=ssq[:, a:a + 1])
                ts("dve", rstd[:, 0:4], ssq[:, 0:4], 1.0 / D, EPS, ALU.mult, ALU.add, ["ssq"], ["rstd"])
                act(rstd[:, 0:4], rstd[:, 0:4], AF.Sqrt, ["rstd"], ["rstd"])
                P.c("dve", lambda h: h.reciprocal(out=rstd[:, 0:4], in_=rstd[:, 0:4]), ["rstd"], ["rstd"])
                if os.environ.get('NOXN'):
                    continue
                XNV = int(os.environ.get('XNV', 0))
                for a in range(1 if XNV == 1 else 4):
                    if XNV == 2:
                        ts("dve", sqj[:], xt[:, a, :], rstd[:, a:a + 1], None, ALU.mult, None, ["xt", "rstd"], ["xn"])
                    elif XNV == 3:
                        ts("dve", xn[:, a, :], xt[:, a, :], 2.0, None, ALU.mult, None, ["xt", "rstd"], ["xn"])
                    elif XNV == 4:
                        act(xn[:, a, :], xt[:, a, :], AF.Identity, ["xt", "rstd"], ["xn"], scale=rstd[:, a:a + 1])
                    else:
                        ts("dve", xn[:, a, :], xt[:, a, :], rstd[:, a:a + 1], None, ALU.mult, None, ["xt", "rstd"], ["xn"])
                if it + 1 < P1T:
                    ld(xt[:], xa[(it + 1) * TT:(it + 2) * TT, :].rearrange("(a p) d -> p a d", p=128), [], ["xt"])
                if P1S < 0:
                    continue
                for kt in range(8):
                    tp, tk = tpp[kt % 2], "tpp0"
                    tpg([(tp[:, a * 128:(a + 1) * 128], xn[:, a, kt * 128:(kt + 1) * 128], ident_b[:]) for a in range(4)],
                        ["xn", "ident_b", tk], [tk])
                    if kt % 2 == 0:
                        ts("dve", hT[:, kt, :], tp[:, :], gn[:, kt:kt + 1], None, ALU.mult, None, [tk, "gn"], ["hT"])
                    else:
                        act(hT[:, kt, :], tp[:, :], AF.Identity, [tk, "gn"], ["hT"], scale=gn[:, kt:kt + 1])
                if P1S < 2:
                    continue
                pjt, pk = nextpj()
                mmg([dict(out=pjt[0:76, :], lhsT=WFL3[:, kt, :], rhs=hT[:, kt, :], start=(kt == 0), stop=(kt == 7)) for kt in range(8)],
                    ["WFL3", "hT", pk], [pk])
                act(Fe[0:76, :], pjt[0:76, :], AF.Exp, [pk, "negb"], ["tA"], bias=negb[0:76, :], scale=-1.0)
                act(Fe[0:76, :], Fe[0:76, :], AF.Ln, ["tA"], ["tA"], bias=1.0)
                P.c("dve", lambda h: h.tensor_tensor_scan(out=Ff[0:76, :], data0=ones_f[0:76, :], data1=Fe[0:76, :],
                                                          initial=Fcar[0:76, :], op0=ALU.mult, op1=ALU.subtract),
                    ["ones_f", "tA", "Fcar"], ["tB"])
                cp("dve", Fcar[0:76, :], Ff[0:76, TT - 1:TT], ["tB"], ["Fcar"])
                cp("dve", HIb[0:76, :], Ff[0:76, :], ["tB"], ["HIb"])
                stt(R1[0:76, :], HIb[0:76, :], nm1[0:76, :], Ff[0:76, :], ALU.mult, ALU.add, ["HIb", "nm1", "tB"], ["tC"])
                tt("dve", MIDb[0:76, :], Ff[0:76, :], HIb[0:76, :], ALU.subtract, ["tB", "HIb"], ["MIDb"])
                stt(R1[0:76, :], MIDb[0:76, :], nm2[0:76, :], R1[0:76, :], ALU.mult, ALU.add, ["MIDb", "nm2", "tC"], ["tC"])
                cp("dve", spt[0:76, :], R1[0:76, :], ["tC"], [spk])
                if P1S < 5:
                    continue
                wv1, wv1k = wload(WB[:, :, C_V:C_V + 512], 8, 512)
                wv2, wv2k = wload(WB[:, :, C_V + 512:C_V + 768], 8, 256)
                for a in range(4):
                    p1, p1k = nextpj()
                    p2, p2k = nextpj()
                    mmg([dict(out=p1[:, :], lhsT=hT[:, kt, a * 128:(a + 1) * 128], rhs=wv1[:, kt, :], start=(kt == 0), stop=(kt == 7)) for kt in range(8)],
                        ["hT", wv1k, p1k], [p1k])
                    mmg([dict(out=p2[:, 0:256], lhsT=hT[:, kt, a * 128:(a + 1) * 128], rhs=wv2[:, kt, :], start=(kt == 0), stop=(kt == 7)) for kt in range(8)],
                        ["hT", wv2k, p2k], [p2k])
                    cp("act", Vst[:, a, 0:8, 0:64], p1[:, :].rearrange("p (h d) -> p h d", h=8), [p1k], ["Vst"])
                    cp("dve", Vst[:, a, 8:12, 0:64], p2[:, 0:256].rearrange("p (h d) -> p h d", h=4), [p2k], ["Vst"])
                ld(VS[4 * it:4 * it + 4].rearrange("a p f -> p a f"), Vst[:].rearrange("p a h d -> p a (h d)"), ["Vst"], [U()], eng="act")
                if P1S < 6:
                    continue
                for o in range(6):
                    if o % 4 == 0:
                        n = min(512, 768 - o * 128)
                        wk, wkk = wload(WB[:, :, C_U + o * 128:C_U + o * 128 + n], 8, n)
                    pjt, pk = nextpj()
                    c0 = (o % 4) * 128
                    mmg([dict(out=pjt[:, :], lhsT=wk[:, kt, c0:c0 + 128], rhs=hT[:, kt, :], start=(kt == 0), stop=(kt == 7)) for kt in range(8)],
                        [wkk, "hT", pk], [pk])
                    cp("act" if o % 2 == 0 else "dve", uT[:, o, :], pjt[:, :], [pk], ["uT"])
                if P1S < 3:
                    continue
                wk, wkk = None, None
                for h_ in range(12):
                    if h_ % 8 == 0:
                        n = min(512, 768 - h_ * 64)
                        wk, wkk = wload(WB[:, :, C_K + h_ * 64:C_K + h_ * 64 + n], 8, n)
                    pjt, pk = nextpj()
                    c0 = (h_ % 8) * 64
                    calls = [dict(out=pjt[:, :], lhsT=SELK[:, h_, :], rhs=spt[:, :], start=True, stop=False)]
                    calls += [dict(out=pjt[0:64, :], lhsT=wk[:, kt, c0:c0 + 64], rhs=hT[:, kt, :], start=False, stop=(kt == 7)) for kt in range(8)]
                    mmg(calls, ["SELK", spk, wkk, "hT", pk], [pk])
                    ks, kk = kst[h_ % 2], "kst%d" % (h_ % 2)
                    if h_ % 2 == 0:
                        cp("act", ks[:], pjt[:, :], [pk], [kk])
                        ld(KT[h_, :, it * TT:(it + 1) * TT], ks[:], [kk], [U()], eng="act")
                    else:
                        cp("dve", ks[:], pjt[:, :], [pk], [kk])
                        ld(KT[h_, :, it * TT:(it + 1) * TT], ks[:], [kk], [U()], eng="act")
                if P1S < 4:
                    continue
                if own:
                    for h_ in range(12):
                        if h_ % 8 == 0:
                            n = min(512, 768 - h_ * 64)
                            wk, wkk = wload(WB[:, :, C_Q + h_ * 64:C_Q + h_ * 64 + n], 8, n)
                        pjt, pk = nextpj()
                        c0 = (h_ % 8) * 64
                        calls = [dict(out=pjt[:, :], lhsT=SELQ[:, h_, :], rhs=spt[:, :], start=True, stop=False)]
                        calls += [dict(out=pjt[0:64, :], lhsT=wk[:, kt, c0:c0 + 64], rhs=hT[:, kt, :], start=False, stop=(kt == 7)) for kt in range(8)]
                        mmg(calls, ["SELQ", spk, wkk, "hT", pk], [pk])
                        ks, kk = kst[h_ % 2], "kst%d" % (h_ % 2)
                        act(ks[:], pjt[:, :], AF.Identity, [pk, "qscale"], [kk], scale=qscale[:, :])
                        ld(QT[h_, :, ot * TT:(ot + 1) * TT], ks[:], [kk], [U()], eng="act")
                if P1S < 7:
                    continue
                s5slots = []
                for o in range(6):
                    bd_t, wz_t, wy_t = s5w[o % 2]
                    sk = "s5w%d" % (o % 2)
                    ld(wz_t[:], WZS[o], [], [sk + "z"])
                    calls = []
                    for ri in range(2):
                        for s in range(8):
                            for q4 in range(4):
                                calls.append(dict(out=zpsb[q4][:, ri * 64:(ri + 1) * 64], lhsT=wz_t[32 * q4:32 * q4 + 32, s, ri, :],
                                                  rhs=uT[32 * q4:32 * q4 + 32, o, :].rearrange("p (c s) -> p c s", s=8)[:, :, s],
                                                  start=(s == 0), stop=(s == 7), tile_position=(32 * q4, 0)))
                    mmg(calls, [sk + "z", "uT"] + ["zps%d" % q for q in range(4)], ["zps%d" % q for q in range(4)])
                    for q4 in range(4):
                        cp("act" if q4 % 2 == 0 else "dve", Z[:, 4 * o + q4, :, :], zpsb[q4][:, 0:128].rearrange("p (i c) -> p i c", i=2), ["zps%d" % q4], ["Zj%d" % j for j in range(8)])
                if P1S < 8:
                    continue
                if it > 0:
                    cp(os.environ.get('RENG', 'dve'), TB[:, :, :, 0], TB[:, :, :, 8], ["TB"], ["TB"])
                Zv = Z[:].rearrange("p r i (b j) -> p r i b j", j=8)

                RENG = os.environ.get('RENG', 'dve')
                cmn = [0]

                def cmac(dre, dim_, sre, sim, mr, mi, Tsets, rkeys, wkey):
                    si_ = cmn[0] % len(Tsets)
                    cmn[0] += 1
                    T = Tsets[si_]
                    tk = ["RT%d_%d" % (si_, q) for q in range(4)]
                    tt(RENG, T[0], mr, sre, ALU.mult, rkeys, [tk[0]])
                    tt(RENG, T[1], mi, sim, ALU.mult, rkeys, [tk[1]])
                    tt(RENG, T[2], mr, sim, ALU.mult, rkeys, [tk[2]])
                    tt(RENG, T[3], mi, sre, ALU.mult, rkeys, [tk[3]])
                    tt(RENG, T[0], T[0], T[1], ALU.subtract, [tk[0], tk[1]], [tk[0]])
                    tt(RENG, T[2], T[2], T[3], ALU.add, [tk[2], tk[3]], [tk[2]])
                    tt(RENG, dre, dre, T[0], ALU.add, [tk[0], wkey], [wkey])
                    tt(RENG, dim_, dim_, T[2], ALU.add, [tk[2], wkey], [wkey])
                RTs = [[t[:] for t in RT[0:4]], [t[:] for t in RT[4:8]]]
                RSs = [[t[:] for t in RS[0:4]], [t[:] for t in RS[4:8]]]
                zkeys = ["Zj%d" % j for j in range(8)]
                a1r = A1R[:].unsqueeze(2).to_broadcast([128, 24, 8])
                a1i = A1I[:].unsqueeze(2).to_broadcast([128, 24, 8])
                for j in range(1, 8):
                    cmac(Zv[:, :, 0, :, j], Zv[:, :, 1, :, j], Zv[:, :, 0, :, j - 1], Zv[:, :, 1, :, j - 1], a1r, a1i, RTs, [zkeys[j - 1]], zkeys[j])
                for b_ in range(8):
                    cp(RENG, TB[:, :, :, b_ + 1], Zv[:, :, :, b_, 7], [zkeys[7], "TB"], ["TB"])
                    cmac(TB[:, :, 0, b_ + 1], TB[:, :, 1, b_ + 1], TB[:, :, 0, b_], TB[:, :, 1, b_], A8R[:], A8I[:], RSs, ["TB"], "TB")
                for j in range(8):
                    cmac(Zv[:, :, 0, :, j], Zv[:, :, 1, :, j], TB[:, :, 0, 0:8], TB[:, :, 1, 0:8],
                         APR[:, j, :].unsqueeze(2).to_broadcast([128, 24, 8]), API[:, j, :].unsqueeze(2).to_broadcast([128, 24, 8]), RTs, ["TB"], zkeys[j])
                if own:
                    cp(RENG, Hb[:, :, :, 0], TB[:, :, :, 0], ["TB"], ["Hb"])
                    cp(RENG, Hb[:, :, :, 1:65], Z[:, :, :, :], zkeys, ["Hb"])
                if not own:
                    continue
                if P1S < 9:
                    continue
                for o in range(6):
                    if o % 4 == 0:
                        n = min(512, 768 - o * 128)
                        wk, wkk = wload(WB[:, :, C_GS + o * 128:C_GS + o * 128 + n], 8, n)
                    pjt, pk = nextpj()
                    c0 = (o % 4) * 128
                    mmg([dict(out=pjt[:, :], lhsT=wk[:, kt, c0:c0 + 128], rhs=hT[:, kt, :], start=(kt == 0), stop=(kt == 7)) for kt in range(8)],
                        [wkk, "hT", pk], [pk])
                    act(SGS[:, o, :], pjt[:, :], AF.Silu, [pk], ["SGS"])
                if P1S < 12:
                    continue
                wk, wkk = wload(WB[:, :, C_QM:C_QM + 512], 8, 512)
                for hd in range(4):
                    pjt, pk = nextpj()
                    mmg([dict(out=pjt[:, :], lhsT=wk[:, kt, hd * 128:(hd + 1) * 128], rhs=hT[:, kt, :], start=(kt == 0), stop=(kt == 7)) for kt in range(8)],
                        [wkk, "hT", pk], [pk])
                    act(QM[:, hd, :], pjt[:, :], AF.Identity, [pk], ["QM"], scale=128.0 ** -0.5)
                wk, wkk = wload(WB[:, :, C_GM:C_GM + 512], 8, 512)
                for hd in range(4):
                    pjt, pk = nextpj()
                    mmg([dict(out=pjt[:, :], lhsT=wk[:, kt, hd * 128:(hd + 1) * 128], rhs=hT[:, kt, :], start=(kt == 0), stop=(kt == 7)) for kt in range(8)],
                        [wkk, "hT", pk], [pk])
                    act(SGM[:, hd, :], pjt[:, :], AF.Silu, [pk], ["SGM"])
                for hd in range(4):
                    for mt in range(2):
                        pjt, pk = nextpj()
                        mmg([dict(out=pjt[:, :], lhsT=MKT[:, hd, mt * 128:(mt + 1) * 128], rhs=QM[:, hd, :], start=True, stop=True)],
                            ["MKT", "QM", pk], [pk])
                        act(PmT[:, mt, :], pjt[:, :], AF.Exp, [pk], ["PmT%d" % mt])
                    pjt, pk = nextpj()
                    mmg([dict(out=pjt[:, :], lhsT=MV[:, mt, hd * 128:(hd + 1) * 128], rhs=PmT[:, mt, :], start=(mt == 0), stop=(mt == 1)) for mt in range(2)],
                        ["MV", "PmT0", "PmT1", pk], [pk])
                    axp, axk = nextpj()
                    mmg([dict(out=axp[:, :], lhsT=ones_b[:, :], rhs=PmT[:, mt, :], start=(mt == 0), stop=(mt == 1)) for mt in range(2)],
                        ["ones_b", "PmT0", "PmT1", axk], [axk])
                    P.c("dve", lambda h, axp=axp: h.reciprocal(out=tC[:], in_=axp[:, :]), [axk], ["tC"])
                    tt("dve", tB[:], tC[:], pjt[:, :], ALU.mult, ["tC", pk], ["tB"])
                    tt("dve", YM[:, hd, :], tB[:], SGM[:, hd, :], ALU.mult, ["tB", "SGM"], ["SGM"])
                if P1S < 14:
                    continue
                for cb in range(8):
                    if cb % 4 == 0:
                        wg0, wg0k = wload(WB[:, :, C_GL + cb * 128:C_GL + cb * 128 + 512], 8, 512)
                    c0 = (cb % 4) * 128
                    pjt, pk = nextpj()
                    mmg([dict(out=pjt[:, :], lhsT=wg0[:, kt, c0:c0 + 128], rhs=hT[:, kt, :], start=(kt == 0), stop=(kt == 7)) for kt in range(8)],
                        [wg0k, "hT", pk], [pk])
                    g0, g0k = G0st[cb % 2], "G0st%d" % (cb % 2)
                    act(g0[:], pjt[:, :], AF.Sigmoid, [pk, "bm"], [g0k], bias=bm[:, cb:cb + 1])
                    ld(G0S[ot, :, cb, :], g0[:], [g0k], [U()], eng="act")
                for o in range(6):
                    if o % 4 == 0:
                        n = min(512, 768 - o * 128)
                        wk, wkk = wload(WB[:, :, C_GF + o * 128:C_GF + o * 128 + n], 8, n)
                    pjt, pk = nextpj()
                    c0 = (o % 4) * 128
                    mmg([dict(out=pjt[:, :], lhsT=wk[:, kt, c0:c0 + 128], rhs=hT[:, kt, :], start=(kt == 0), stop=(kt == 7)) for kt in range(8)],
                        [wkk, "hT", pk], [pk])
                    ks, kk = kst[o % 2], "kst%d" % (o % 2)
                    act(ks[:], pjt[:, :], AF.Silu, [pk], [kk])
                    ld(SGFS[ot, :, o, :], ks[:], [kk], [U()], eng="act")
                if P1S < 10:
                    continue
                for o in range(6):
                    bd_t, wz_t, wy_t = s5w[o % 2]
                    sk = "s5w%d" % (o % 2)
                    ld(bd_t[:], BDS[o], [], [sk + "b"])
                    ld(wy_t[:], WYS[o], [], [sk + "y"])
                    yv = yps[:, :].rearrange("p (c s) -> p c s", s=8)
                    uv = uT[:, o, :].rearrange("p (c s) -> p c s", s=8)
                    calls = []
                    for j in range(8):
                        calls.append(dict(out=yv[:, :, j:8], lhsT=bd_t[:, j, :], rhs=uv[:, :, 0:8 - j], start=(j == 0), stop=False))
                    for tau in range(8):
                        for ri in range(2):
                            for q4 in range(4):
                                last = (q4 == 3 and tau == 7 and ri == 1)
                                calls.append(dict(out=yv[32 * q4:32 * q4 + 32, :, tau], lhsT=wy_t[:, q4, tau, ri, :],
                                                  rhs=Hb[:, 4 * o + q4, ri, 0:64], start=False, stop=last, tile_position=(0, 32 * q4)))
                    mmg(calls, [sk + "b", sk + "y", "uT", "Hb", "yps"], ["yps"])
                    act(tA[:], yps[:, :], AF.Square, ["yps"], ["tA"])
                    ts("dve", tA[:], tA[:], 0.044715, 1.0, ALU.mult, ALU.add, ["tA"], ["tA"])
                    tt("dve", tB[:], tA[:], yps[:, :], ALU.mult, ["tA", "yps"], ["tB"])
                    act(tA[:], tB[:], AF.Sigmoid, ["tB"], ["tA"], scale=1.5957691216057308)
                    tt("dve", YG[:, o, :], tA[:], yps[:, :], ALU.mult, ["tA", "yps"], ["YG"])
                if P1S < 11:
                    continue
                for half in range(2):
                    wg, wgk = wload(WGLU[:, :, half * 384:(half + 1) * 384], 6, 384)
                    for c3 in range(3):
                        cb = half * 3 + c3
                        pjt, pk = nextpj()
                        mmg([dict(out=pjt[:, :], lhsT=wg[:, kt, c3 * 128:(c3 + 1) * 128], rhs=YG[:, kt, :], start=(kt == 0), stop=(kt == 5)) for kt in range(6)],
                            [wgk, "YG", pk], [pk])
                        act(tA[:], pjt[:, :], AF.Sigmoid, [pk, "bglu"], ["tA"], bias=bglu[:, cb:cb + 1])
                        tt("dve", tB[:], tA[:], YG[:, cb, :], ALU.mult, ["tA", "YG"], ["tB"])
                        tt("dve", YS[:, cb, :], tB[:], SGS[:, cb, :], ALU.mult, ["tB", "SGS"], ["SGS"])
                if P1S < 13:
                    continue
                for cb in range(8):
                    si = wn[0] % NWS
                    wn[0] += 1
                    wfull = wsl[si][:, 0:26 * 128].rearrange("p (k c) -> p k c", k=26)
                    wpsk = wpmk = wg1k = wg2k = "ws%d" % si
                    wps_t, wpm_t, wg1, wg2 = wfull[:, 0:6, :], wfull[:, 6:10, :], wfull[:, 10:18, :], wfull[:, 18:26, :]
                    ld(wps_t, WPS[:, :, cb * 128:(cb + 1) * 128], [], [wpsk])
                    ld(wpm_t, WPM[:, :, cb * 128:(cb + 1) * 128], [wpsk], [wpsk])
                    ld(wg1, WB[:, :, C_GL + 1024 + cb * 128:C_GL + 1024 + (cb + 1) * 128], [wpsk], [wpsk])
                    ld(wg2, WB[:, :, C_GL + 2048 + cb * 128:C_GL + 2048 + (cb + 1) * 128], [wpsk], [wpsk])
                    c0 = 0
                    m0, m0k = M0[cb % 2], "M0_%d" % (cb % 2)
                    pjt, pk = nextpj()
                    mmg([dict(out=pjt[:, :], lhsT=wg1[:, kt, c0:c0 + 128], rhs=hT[:, kt, :], start=(kt == 0), stop=(kt == 7)) for kt in range(8)],
                        [wg1k, "hT", pk], [pk])
                    act(tA[:], pjt[:, :], AF.Sigmoid, [pk, "bm"], ["tA"], bias=bm[:, 8 + cb:9 + cb])
                    pjt, pk = nextpj()
                    mmg([dict(out=pjt[:, :], lhsT=wps_t[:, kt, c0:c0 + 128], rhs=YS[:, kt, :], start=(kt == 0), stop=(kt == 5)) for kt in range(6)],
                        [wpsk, "SGS", pk], [pk])
                    tt("dve", m0[:], tA[:], pjt[:, :], ALU.mult, ["tA", pk], [m0k])
                    pjt, pk = nextpj()
                    mmg([dict(out=pjt[:, :], lhsT=wg2[:, kt, c0:c0 + 128], rhs=hT[:, kt, :], start=(kt == 0), stop=(kt == 7)) for kt in range(8)],
                        [wg2k, "hT", pk], [pk])
                    act(tB[:], pjt[:, :], AF.Sigmoid, [pk, "bm"], ["tB"], bias=bm[:, 16 + cb:17 + cb])
                    pjt, pk = nextpj()
                    mmg([dict(out=pjt[:, :], lhsT=wpm_t[:, kt, c0:c0 + 128], rhs=YM[:, kt, :], start=(kt == 0), stop=(kt == 3)) for kt in range(4)],
                        [wpmk, "SGM", pk], [pk])
                    tt("dve", tC[:], tB[:], pjt[:, :], ALU.mult, ["tB", pk], ["tC"])
                    tt("dve", m0[:], m0[:], tC[:], ALU.add, [m0k, "tC"], [m0k])
                    ld(M0S[ot, :, cb, :], m0[:], [m0k], [U()], eng="act")
            P.emit()

        if upto < 3:
            return nc
        S2 = ExitStack()
        with S2:
            Kh = [sb(S2, "Kh%d" % i, [128, LTOK], BF16) for i in range(2)]
            Vh = [sb(S2, "Vh%d" % i, [128, 64, 65], BF16) for i in range(2)]
            Qh = [sb(S2, "Qh%d" % i, [128, 4096], BF16) for i in range(2)]
            NPT = 4
            Pt = [sb(S2, "Pt%d" % i, [128, TT], BF16) for i in range(NPT)]
            Osb = sb(S2, "Osb", [65, TT], F32)
            Rd = sb(S2, "Rd", [64, TT], F32)
            Yst = [sb(S2, "Yst%d" % i, [64, TT], BF16) for i in range(2)]
            SEL = sb(S2, "SEL", [65, 64], F32)
            sps = [ps(S2, "sps%d" % i, [128, 512], F32) for i in range(4)]
            ops_ = [ps(S2, "ops%d" % i, [128, 512], F32) for i in range(2)]
            dps = ps(S2, "dps", [128, 512], F32)
            mset("pool", SEL[:], 0.0, ["SEL"])
            mset("pool", SEL[64:65, :], 1.0, ["SEL"])
            LA = 2
            its = []
            for h_ in range(12):
                for qt in range(8):
                    nkb = 32 + 4 * qt + 4
                    order = list(range(32 + 4 * qt, nkb)) + list(range(0, 32 + 4 * qt))
                    for n_, kb in enumerate(order):
                        its.append((h_, qt, kb, n_ == 0, n_ == nkb - 1))

            def head_loads(h_):
                hs = h_ % 2
                kh, vh, qh = Kh[hs], Vh[hs], Qh[hs]
                kk, vk, qk = "Kh%d" % hs, "Vh%d" % hs, "Qh%d" % hs
                ld(qh[:], QT[h_], [], [qk])
                for c4 in range(4):
                    ld(kh[:, c4 * 2048:(c4 + 1) * 2048], KT[h_, :, c4 * 2048:(c4 + 1) * 2048], [qk], [kk])
                for c4 in range(4):
                    ld(vh[:, c4 * 16:(c4 + 1) * 16, :], VS[c4 * 16:(c4 + 1) * 16, :, h_ * 65:(h_ + 1) * 65].rearrange("b p d -> p b d"), [kk], [vk])

            def geom(i):
                h_, qt, kb, first, last = its[i]
                diag = kb - (32 + 4 * qt)
                c0 = max(0, diag) * 128
                return h_, qt, kb, first, last, diag, c0

            def emit_qk(i):
                h_, qt, kb, first, last, diag, c0 = geom(i)
                hs = h_ % 2
                st, sk = sps[i % 4], "sps%d" % (i % 4)
                mmg([dict(out=st[:, c0:TT], lhsT=Kh[hs][:, kb * 128:(kb + 1) * 128], rhs=Qh[hs][:, qt * TT + c0:(qt + 1) * TT], start=True, stop=True)],
                    ["Kh%d" % hs, "Qh%d" % hs, sk], [sk])

            pend = []

            def emit_rest(i):
                h_, qt, kb, first, last, diag, c0 = geom(i)
                hs = h_ % 2
                st, sk = sps[i % 4], "sps%d" % (i % 4)
                pt, ptk = Pt[i % NPT], "Pt%d" % (i % NPT)
                op_t, opk = ops_[(h_ * 8 + qt) % 2], "ops%d" % ((h_ * 8 + qt) % 2)
                act(pt[:, c0:TT], st[:, c0:TT], AF.Exp, [sk], [ptk])
                if diag >= 0:
                    tt("dve", pt[:, c0:c0 + 128], pt[:, c0:c0 + 128], maskT[:], ALU.mult, [ptk, "maskT"], [ptk])
                mmg([dict(out=op_t[0:65, c0:TT], lhsT=Vh[hs][:, kb, :], rhs=pt[:, c0:TT], start=first, stop=last)],
                    ["Vh%d" % hs, ptk, opk], [opk])
                if last:
                    cp("dve", Osb[:], op_t[0:65, :], [opk], ["Osb"])
                    pend.append((i + 3, h_, qt))

            def emit_norm(h_, qt):
                mmg([dict(out=dps[0:64, :], lhsT=SEL[:, :], rhs=Osb[:, :], start=True, stop=True)], ["SEL", "Osb", "dps"], ["dps"])
                P.c("dve", lambda h: h.reciprocal(out=Rd[:], in_=dps[0:64, :]), ["dps"], ["Rd"])
                ys, ysk = Yst[qt % 2], "Yst%d" % (qt % 2)
                tt("dve", ys[:], Osb[0:64, :], Rd[:], ALU.mult, ["Osb", "Rd"], [ysk])
                ld(YFS[h_, :, qt * TT:(qt + 1) * TT], ys[:], [ysk], [U()], eng="sp")

            head_loads(0)
            head_loads(1)
            NI = len(its)
            for i in range(min(LA, NI)):
                emit_qk(i)
            for i in range(NI):
                if i + LA < NI:
                    emit_qk(i + LA)
                emit_rest(i)
                while pend and pend[0][0] <= i:
                    _, ph, pq = pend.pop(0)
                    emit_norm(ph, pq)
                if its[i][4] and its[i][1] == 7 and its[i][0] + 2 < 12:
                    head_loads(its[i][0] + 2)
            while pend:
                _, ph, pq = pend.pop(0)
                emit_norm(ph, pq)
            P.emit()

        if upto < 4:
            return nc
        S3 = ExitStack()
        with S3:
            gfin = sb(S3, "gfin", [128, D], F32)
            ld(gfin[:], g_final.to_broadcast([128, D]), [], ["gfin"])
            wpf = sb(S3, "wpf", [128, 6, D], BF16)
            wo = sb(S3, "wo", [128, 8, D], BF16)
            yf = sb(S3, "yf", [128, 6, TT], BF16)
            sgf = sb(S3, "sgf", [128, 6, TT], BF16)
            yfg = sb(S3, "yfg", [128, 6, TT], BF16)
            g0t = sb(S3, "g0t", [128, 8, TT], BF16)
            m0t = sb(S3, "m0t", [128, 8, TT], F32)
            mg = sb(S3, "mg", [128, 8, TT], BF16)
            t3 = sb(S3, "t3", [128, TT], F32)
            x3 = sb(S3, "x3", [128, 4, D], F32)
            o3 = [sb(S3, "o3_%d" % i, [128, D], F32) for i in range(2)]
            j3 = sb(S3, "j3", [128, D], BF16)
            pf = [ps(S3, "pf%d" % i, [128, 512], F32) for i in range(2)]
            po = [ps(S3, "po%d" % i, [128, 512], F32) for i in range(4)]
            ld(wpf[:], WPF, [], ["wpf"])
            ld(wo[:], WOUT, [], ["wo"])
            for ot in range(8):
                for a2 in range(2):
                    ld(yf[a2 * 64:(a2 + 1) * 64, :, :], YFS[:, :, ot * TT:(ot + 1) * TT].rearrange("(c a) d t -> a d c t", a=2)[a2], ["yf"], ["yf"])
                ld(sgf[:], SGFS[ot], [], ["sgf"])
                ld(g0t[:], G0S[ot], [], ["g0t"])
                ld(m0t[:], M0S[ot], [], ["m0t"])
                ld(x3[:], xa[(OWN0 + ot) * TT:(OWN0 + ot + 1) * TT, :].rearrange("(a p) d -> p a d", p=128), [], ["x3"])
                for c in range(6):
                    tt("pool" if c % 2 else "dve", yfg[:, c, :], yf[:, c, :], sgf[:, c, :], ALU.mult, ["yf", "sgf"], ["yfg"])
                for cb in range(8):
                    pt_, pk = pf[cb % 2], "pf%d" % (cb % 2)
                    mmg([dict(out=pt_[:, :], lhsT=wpf[:, kt, cb * 128:(cb + 1) * 128], rhs=yfg[:, kt, :], start=(kt == 0), stop=(kt == 5)) for kt in range(6)],
                        ["wpf", "yfg", pk], [pk])
                    tt("dve", t3[:], g0t[:, cb, :], pt_[:, :], ALU.mult, ["g0t", pk], ["t3"])
                    tt("dve", mg[:, cb, :], t3[:], m0t[:, cb, :], ALU.add, ["t3", "m0t"], ["mg"])
                for a in range(4):
                    ob, obk = o3[a % 2], "o3_%d" % (a % 2)
                    for hf in range(2):
                        pt_, pk = po[(2 * a + hf) % 4], "po%d" % ((2 * a + hf) % 4)
                        mmg([dict(out=pt_[:, :], lhsT=mg[:, kt, a * 128:(a + 1) * 128], rhs=wo[:, kt, hf * 512:(hf + 1) * 512], start=(kt == 0), stop=(kt == 7)) for kt in range(8)],
                            ["mg", "wo", pk], [pk])
                        tt("dve", ob[:, hf * 512:(hf + 1) * 512], pt_[:, :], x3[:, a, hf * 512:(hf + 1) * 512], ALU.add, [pk, "x3"], [obk])
                    act(j3[:], ob[:], AF.Square, [obk], ["j3", "ssq"], accum=ssq[:, a:a + 1])
                    ts("dve", rstd[:, a:a + 1], ssq[:, a:a + 1], 1.0 / D, EPS, ALU.mult, ALU.add, ["ssq"], ["rstd"])
                    act(rstd[:, a:a + 1], rstd[:, a:a + 1], AF.Sqrt, ["rstd"], ["rstd"])
                    P.c("dve", lambda h, a=a: h.reciprocal(out=rstd[:, a:a + 1], in_=rstd[:, a:a + 1]), ["rstd"], ["rstd"])
                    stt(ob[:], ob[:], rstd[:, a:a + 1], gfin[:], ALU.mult, ALU.mult, [obk, "rstd", "gfin"], [obk])
                    ld(yout[ot * TT + a * 128:ot * TT + (a + 1) * 128, :], ob[:], [obk], [U()], eng="act")
            P.emit()
    return nc


_NC = None


def kernel(**inputs):
    global _NC
    if _NC is None:
        _NC = build_nc()
    nc = _NC
    x = np.ascontiguousarray(np.asarray(inputs["x"], dtype=np.float32))
    mem = np.asarray(inputs["mem"], dtype=np.float32)
    in_maps = []
    for c in range(8):
        b, half = c // 2, c % 2
        if half == 0:
            xa = np.concatenate([np.zeros((4096, D), np.float32), x[b, 0:4096]], axis=0)
        else:
            xa = x[b]
        km = np.zeros((NT, TT), dtype=ml_dtypes.bfloat16)
        if half == 0:
            km[0:OWN0, :] = -30000.0
        m = {"xa": np.ascontiguousarray(xa), "kmrow": km, "mem": np.ascontiguousarray(mem[b])}
        for name in ("g_norm", "g_mem_norm", "w_in", "b_forget", "b_merge", "w_mem_kv", "lam_re", "lam_im", "log_step",
                     "s5_b_re", "s5_b_im", "s5_c_re", "s5_c_im", "s5_d", "w_glu", "b_glu", "w_proj_fox", "w_proj_s5",
                     "w_proj_mem", "w_out"):
            a = np.asarray(inputs[name], dtype=np.float32)
            a = a[0]
            if a.ndim == 1:
                a = a[None, :]
            m[name] = np.ascontiguousarray(a)
        m["g_final"] = np.ascontiguousarray(np.asarray(inputs["g_final"], dtype=np.float32)[None, :])
        in_maps.append(m)
    res = run_bass_kernel_spmd(nc, in_maps, core_ids=list(range(8)))
    out = np.zeros((4, 8192, D), dtype=np.float32)
    for c in range(8):
        b, half = c // 2, c % 2
        out[b, half * 4096:(half + 1) * 4096] = np.asarray(res.results[c]["yout"], dtype=np.float32)
    return out
```

```python
import math
from contextlib import ExitStack

import ml_dtypes
import numpy as np
import concourse.bass as bass
import concourse.mybir as mybir
from concourse.bass_utils import run_bass_kernel_spmd

F32 = mybir.dt.float32
BF16 = mybir.dt.bfloat16
I32 = mybir.dt.int32
AF = mybir.ActivationFunctionType
ALU = mybir.AluOpType

ENGS = ("pe", "act", "dve", "pool", "sp")
import os as _os0
NOSELF = tuple(x for x in _os0.environ.get('NOSELF', '').split(',') if x)
D = 1024
NIN = 8716
TT = 512
NT = 16
LTOK = 8192
OWN0 = 8
C_Q, C_K, C_V, C_FL, C_GF, C_U, C_GS, C_QM, C_GM, C_GL = 0, 768, 1536, 2304, 2316, 3084, 3852, 4620, 5132, 5644
EPS = 1e-6
TWO_PI = 2.0 * math.pi


class Op:
    __slots__ = ("eng", "fn", "deps", "needs_inc", "sigval", "is_dma", "lane", "laneval")

    def __init__(self, eng, fn, is_dma=False):
        self.eng = eng
        self.fn = fn
        self.deps = []
        self.needs_inc = False
        self.sigval = None
        self.is_dma = is_dma
        self.lane = None
        self.laneval = None


class Prog:
    def __init__(self, nc, es, n_lanes=6):
        self.nc = nc
        self.n_lanes = n_lanes
        self.esem = {e: es.enter_context(nc.semaphore("s_" + e)) for e in ENGS}
        self.lsem = {}
        for e in ("sp", "act", "pool"):
            for i in range(n_lanes):
                self.lsem[(e, i)] = es.enter_context(nc.semaphore("l_%s%d" % (e, i)))
        self.ecnt = {e: 0 for e in ENGS}
        self.lane_rr = {e: 0 for e in ENGS}
        self.lane_cnt = {k: 0 for k in self.lsem}
        self.barrier = {}
        self._reset()

    def _reset(self):
        self.ops = {e: [] for e in ENGS}
        self.last_w = {}
        self.readers = {}
        self.lane_last = {}

    def _add(self, op, reads, writes):
        deps = []
        for r in reads:
            w = self.last_w.get(r)
            if w is not None:
                deps.append(w)
        for r in writes:
            w = self.last_w.get(r)
            if w is not None:
                deps.append(w)
            deps.extend(self.readers.get(r, ()))
        seen = set(id(d) for d in op.deps)
        for d in deps:
            if d is op or id(d) in seen:
                continue
            if op.eng == "pe" and d.eng == "pe" and not d.is_dma and not op.is_dma:
                continue
            if (not d.is_dma) and (not op.is_dma) and op.eng == d.eng and op.eng in NOSELF:
                continue
            seen.add(id(d))
            op.deps.append(d)
            if not d.is_dma:
                d.needs_inc = True
        for r in reads:
            self.readers.setdefault(r, []).append(op)
        for r in writes:
            self.last_w[r] = op
            self.readers[r] = []
        self.ops[op.eng].append(op)
        return op

    def c(self, eng, fn, reads=(), writes=()):
        return self._add(Op(eng, fn), reads, writes)

    def dma(self, eng, fn, reads=(), writes=()):
        op = Op(eng, fn, is_dma=True)
        lane = (eng, self.lane_rr[eng] % self.n_lanes)
        self.lane_rr[eng] += 1
        prev = self.lane_last.get(lane)
        self.lane_cnt[lane] += 1
        op.lane = lane
        op.laneval = 16 * self.lane_cnt[lane]
        if prev is not None:
            op.deps.append(prev)
        self.lane_last[lane] = op
        return self._add(op, reads, writes)

    def emit(self):
        nc = self.nc
        for e in ENGS:
            last = None
            for op in self.ops[e]:
                if not op.is_dma:
                    last = op
            if last is not None:
                last.needs_inc = True
            for op in self.ops[e]:
                if (not op.is_dma) and op.needs_inc:
                    self.ecnt[e] += 1
                    op.sigval = self.ecnt[e]
        barrier = dict(self.barrier)
        esem, lsem = self.esem, self.lsem

        def tok(d):
            if d.is_dma:
                return lsem[d.lane], d.laneval
            return esem[d.eng], d.sigval

        def run(e, h):
            waited = {}
            for s, v in barrier.values():
                if v > 0:
                    h.wait_ge(s, v)
                    waited[id(s)] = v
            for op in self.ops[e]:
                for d in op.deps:
                    s, v = tok(d)
                    if waited.get(id(s), 0) >= v:
                        continue
                    waited[id(s)] = v
                    h.wait_ge(s, v)
                inst = op.fn(h)
                if op.is_dma:
                    inst.then_inc(lsem[op.lane], 16)
                elif op.needs_inc:
                    inst.then_inc(esem[e], 1)
                import os as _os
                if _os.environ.get('DBGPRINT'):
                    print("OP", e, "dma" if op.is_dma else "c", "lane=%s val=%s" % (op.lane, op.laneval) if op.is_dma else "sig=%s" % op.sigval,
                          "deps=", [((d.lane, d.laneval) if d.is_dma else (d.eng, d.sigval)) for d in op.deps], type(inst).__name__)
            for lane, cnt in self.lane_cnt.items():
                if lane[0] == e and cnt > 0:
                    h.wait_ge(lsem[lane], 16 * cnt)

        with nc.Block() as block:
            @block.tensor
            def _(h):
                run("pe", h)

            @block.scalar
            def _(h):
                run("act", h)

            @block.vector
            def _(h):
                run("dve", h)

            @block.gpsimd
            def _(h):
                run("pool", h)

            @block.sync
            def _(h):
                run("sp", h)

        self.barrier = {}
        for e in ENGS:
            self.barrier[("e", e)] = (esem[e], self.ecnt[e])
        for lane, cnt in self.lane_cnt.items():
            self.barrier[("l", lane)] = (lsem[lane], 16 * cnt)
        self._reset()


def build_nc(dbg=False, upto=9):
    nc = bass.Bass("TRN2", target_bir_lowering=False)

    def din(name, shape, dt=F32):
        return nc.dram_tensor(name, list(shape), dt, kind="ExternalInput").ap()

    def dscr(name, shape, dt):
        return nc.dram_tensor(name, list(shape), dt, kind=("ExternalOutput" if dbg else "Internal")).ap()

    xa = din("xa", [LTOK, D])
    kmrow = din("kmrow", [NT, TT], BF16)
    mem = din("mem", [256, D])
    g_norm = din("g_norm", [1, D])
    g_mem_norm = din("g_mem_norm", [1, D])
    g_final = din("g_final", [1, D])
    w_in = din("w_in", [D, NIN])
    b_forget = din("b_forget", [1, 12])
    b_merge = din("b_merge", [1, 3072])
    w_mem_kv = din("w_mem_kv", [D, 1024])
    lam_re = din("lam_re", [48, 64])
    lam_im = din("lam_im", [48, 64])
    log_step = din("log_step", [1, 48])
    s5_b_re = din("s5_b_re", [48, 64, 16])
    s5_b_im = din("s5_b_im", [48, 64, 16])
    s5_c_re = din("s5_c_re", [48, 16, 64])
    s5_c_im = din("s5_c_im", [48, 16, 64])
    s5_d = din("s5_d", [1, 768])
    w_glu = din("w_glu", [768, 768])
    b_glu = din("b_glu", [1, 768])
    w_proj_fox = din("w_proj_fox", [768, D])
    w_proj_s5 = din("w_proj_s5", [768, D])
    w_proj_mem = din("w_proj_mem", [512, D])
    w_out = din("w_out", [D, D])
    yout = nc.dram_tensor("yout", [4096, D], F32, kind="ExternalOutput").ap()

    WB = dscr("WB", [128, 8, NIN], BF16)
    WKV = dscr("WKV", [128, 8, 1024], BF16)
    WGLU = dscr("WGLU", [128, 6, 768], BF16)
    WPF = dscr("WPF", [128, 6, D], BF16)
    WPS = dscr("WPS", [128, 6, D], BF16)
    WPM = dscr("WPM", [128, 4, D], BF16)
    WOUT = dscr("WOUT", [128, 8, D], BF16)
    BDS = dscr("BDS", [6, 128, 8, 128], BF16)
    WZS = dscr("WZS", [6, 128, 8, 2, 128], BF16)
    WYS = dscr("WYS", [6, 128, 4, 8, 2, 32], BF16)
    KT = dscr("KT", [12, 128, LTOK], BF16)
    QT = dscr("QT", [12, 128, 4096], BF16)
    VS = dscr("VS", [64, 128, 780], BF16)
    M0S = dscr("M0S", [8, 128, 8, TT], F32)
    G0S = dscr("G0S", [8, 128, 8, TT], BF16)
    SGFS = dscr("SGFS", [8, 128, 6, TT], BF16)
    YFS = dscr("YFS", [12, 64, 4096], BF16)

    es = ExitStack()
    with es:
        P = Prog(nc, es)

        def sb(stack, name, shape, dt):
            return stack.enter_context(nc.sbuf_tensor(name, list(shape), dt))

        def ps(stack, name, shape, dt=F32):
            return stack.enter_context(nc.psum_tensor(name, list(shape), dt))

        def act(out, in_, func, r, w, bias=None, scale=None, accum=None):
            kw = {}
            if bias is not None:
                kw["bias"] = bias
            if scale is not None:
                kw["scale"] = scale
            if accum is not None:
                kw["accum_out"] = accum
            return P.c("act", lambda h: h.activation(out=out, in_=in_, func=func, **kw), r, w)

        def tt(eng, out, in0, in1, op, r, w):
            return P.c(eng, lambda h: h.tensor_tensor(out=out, in0=in0, in1=in1, op=op), r, w)

        def ts(eng, out, in0, s1, s2, op0, op1, r, w):
            if op1 is None:
                return P.c(eng, lambda h: h.tensor_scalar(out=out, in0=in0, scalar1=s1, scalar2=None, op0=op0), r, w)
            return P.c(eng, lambda h: h.tensor_scalar(out=out, in0=in0, scalar1=s1, scalar2=s2, op0=op0, op1=op1), r, w)

        def stt(out, in0, scalar, in1, op0, op1, r, w):
            return P.c("dve", lambda h: h.scalar_tensor_tensor(out=out, in0=in0, scalar=scalar, in1=in1, op0=op0, op1=op1), r, w)

        def cp(eng, out, in_, r, w):
            if eng == "act":
                return P.c("act", lambda h: h.activation(out=out, in_=in_, func=AF.Identity), r, w)
            return P.c(eng, lambda h: h.tensor_copy(out=out, in_=in_), r, w)

        def mset(eng, ap, val, w):
            return P.c(eng, lambda h: h.memset(ap, val), (), w)

        def mmg(calls, r, w):
            def fn(h):
                inst = None
                for kw in calls:
                    inst = h.matmul(**kw)
                return inst
            return P.c("pe", fn, r, w)

        def tpg(calls, r, w):
            def fn(h):
                inst = None
                for (o, i, idn) in calls:
                    inst = h.transpose(o, i, idn)
                return inst
            return P.c("pe", fn, r, w)

        uq = [0]

        def U():
            uq[0] += 1
            return "u%d" % uq[0]

        def ld(out, in_, r, w, eng="sp", slow=False):
            if slow:
                return P.dma(eng, lambda h: h.dma_start(out=out, in_=in_, allow_slow_non_contiguous=True), r, w)
            return P.dma(eng, lambda h: h.dma_start(out=out, in_=in_), r, w)

        def rows_pattern(tile_ap, n, lo, hi, w):
            P.c("pool", lambda h: h.memset(tile_ap, 1.0), (), w)
            P.c("pool", lambda h: h.affine_select(out=tile_ap, in_=tile_ap, pattern=[[0, n]], compare_op=ALU.is_ge,
                                                  fill=0.0, base=-lo, channel_multiplier=1), w, w)
            P.c("pool", lambda h: h.affine_select(out=tile_ap, in_=tile_ap, pattern=[[0, n]], compare_op=ALU.is_ge,
                                                  fill=0.0, base=hi, channel_multiplier=-1), w, w)

        G = ExitStack()
        es.enter_context(G)
        ident_f = sb(G, "ident_f", [128, 128], F32)
        ident_b = sb(G, "ident_b", [128, 128], BF16)
        ones_f = sb(G, "ones_f", [128, 512], F32)
        ones_b = sb(G, "ones_b", [128, 128], BF16)
        maskT = sb(G, "maskT", [128, 128], BF16)
        gn = sb(G, "gn", [128, 8], F32)
        gmn = sb(G, "gmn", [128, 8], F32)
        bm = sb(G, "bm", [128, 24], F32)
        bglu = sb(G, "bglu", [128, 6], F32)
        SELK = sb(G, "SELK", [128, 12, 128], BF16)
        SELQ = sb(G, "SELQ", [128, 12, 128], BF16)
        qscale = sb(G, "qscale", [128, 1], F32)
        negb = sb(G, "negb", [128, 1], F32)
        nm1 = sb(G, "nm1", [128, 1], F32)
        nm2 = sb(G, "nm2", [128, 1], F32)
        WFL3 = sb(G, "WFL3", [128, 8, 76], BF16)
        MKT = sb(G, "MKT", [128, 4, 256], BF16)
        MV = sb(G, "MV", [128, 2, 512], BF16)
        A1R = sb(G, "A1R", [128, 24], F32)
        A1I = sb(G, "A1I", [128, 24], F32)
        A8R = sb(G, "A8R", [128, 24], F32)
        A8I = sb(G, "A8I", [128, 24], F32)
        APR = sb(G, "APR", [128, 8, 24], F32)
        API = sb(G, "API", [128, 8, 24], F32)
        Fcar = sb(G, "Fcar", [128, 1], F32)
        TB = sb(G, "TB", [128, 24, 2, 9], F32)
        ssq = sb(G, "ssq", [128, 8], F32)
        rstd = sb(G, "rstd", [128, 8], F32)

        mset("pool", ones_f[:], 1.0, ["ones_f"])
        mset("pool", ident_f[:], 1.0, ["ident_f"])
        P.c("pool", lambda h: h.affine_select(out=ident_f[:], in_=ident_f[:], pattern=[[-1, 128]], compare_op=ALU.is_equal,
                                              fill=0.0, base=0, channel_multiplier=1), ["ident_f"], ["ident_f"])
        cp("dve", ident_b[:], ident_f[:], ["ident_f"], ["ident_b"])
        cp("dve", ones_b[:], ones_f[:, 0:128], ["ones_f"], ["ones_b"])
        mset("pool", maskT[:], 1.0, ["maskT"])
        P.c("pool", lambda h: h.affine_select(out=maskT[:], in_=maskT[:], pattern=[[1, 128]], compare_op=ALU.is_ge,
                                              fill=0.0, base=0, channel_multiplier=-1), ["maskT"], ["maskT"])
        ld(gn[:], g_norm.rearrange("o (kt p) -> p (o kt)", p=128), [], ["gn"], slow=True)
        ld(gmn[:], g_mem_norm.rearrange("o (kt p) -> p (o kt)", p=128), [], ["gmn"], slow=True)
        ld(bm[:], b_merge.rearrange("o (c p) -> p (o c)", p=128), [], ["bm"], slow=True)
        ld(bglu[:], b_glu.rearrange("o (c p) -> p (o c)", p=128), [], ["bglu"], slow=True)
        mset("pool", qscale[:], 1.0, ["qscale"])
        mset("pool", qscale[0:64, :], 0.125, ["qscale"])
        mset("pool", negb[:], 0.0, ["negb"])
        for base in (0, 32, 64):
            ld(negb[base:base + 12, :], b_forget.rearrange("o h -> h o"), [], ["negb"], slow=True)
        ts("dve", negb[:], negb[:], -1.0, None, ALU.mult, None, ["negb"], ["negb"])
        mset("pool", nm1[:], 0.0, ["nm1"])
        mset("pool", nm1[32:64, :], -1.0, ["nm1"])
        mset("pool", nm1[64:96, :], -1.0, ["nm1"])
        mset("pool", nm2[:], 0.0, ["nm2"])
        mset("pool", nm2[64:96, :], -1.0, ["nm2"])
        mset("pool", Fcar[:], 0.0, ["Fcar"])
        mset("pool", TB[:], 0.0, ["TB"])
        mset("pool", SELK[:], 0.0, ["SELK"])
        mset("pool", SELQ[:], 0.0, ["SELQ"])
        for j, base in enumerate((0, 32, 64)):
            ts("dve", SELK[:, :, 96 + j], ident_f[:, base:base + 12], -1.0, None, ALU.mult, None, ["ident_f", "SELK"], ["SELK"])
            cp("dve", SELQ[:, :, 64 + j], ident_f[:, base:base + 12], ["ident_f", "SELQ"], ["SELQ"])
        for h_ in range(12):
            cp("dve", SELK[:, h_, 99:100], ident_f[:, 76:77], ["ident_f", "SELK"], ["SELK"])
            for c_ in (64, 65, 66):
                cp("dve", SELK[:, h_, c_:c_ + 1], ident_f[:, 96:97], ["ident_f", "SELK"], ["SELK"])
            for c_ in (96, 97, 98, 99):
                cp("dve", SELQ[:, h_, c_:c_ + 1], ident_f[:, 96:97], ["ident_f", "SELQ"], ["SELQ"])

        S0 = ExitStack()
        with S0:
            cv = [sb(S0, "cv%d" % i, [128, 8, 512], BF16) for i in range(2)]
            cvn = [0]

            def convert(src, nkt, ncols, dst, wkey=None):
                srcv = src.rearrange("(kt p) c -> p kt c", p=128)
                c0 = 0
                while c0 < ncols:
                    n = min(512, ncols - c0)
                    i = cvn[0] % 2
                    cvn[0] += 1
                    t = cv[i]
                    ld(t[:, 0:nkt, 0:n], srcv[:, :, c0:c0 + n], [], ["cv%d" % i], eng="pool")
                    ld(dst[:, :, c0:c0 + n], t[:, 0:nkt, 0:n], ["cv%d" % i], [wkey] if wkey else [U()], eng="sp")
                    c0 += n

            convert(w_in, 8, NIN, WB)
            convert(w_mem_kv, 8, 1024, WKV, "WKV")
            convert(w_glu, 6, 768, WGLU)
            convert(w_proj_fox, 6, D, WPF)
            convert(w_proj_s5, 6, D, WPS)
            convert(w_proj_mem, 4, D, WPM)
            convert(w_out, 8, D, WOUT)
            mset("pool", WFL3[:], 0.0, ["WFL3"])
            for base in (0, 32, 64):
                ld(WFL3[:, :, base:base + 12], w_in.rearrange("(kt p) c -> p kt c", p=128)[:, :, C_FL:C_FL + 12],
                   ["WFL3"], ["WFL3"], eng="pool")

            def s5t(name, shape, dt=F32):
                return sb(S0, name, shape, dt)
            LR = s5t("LR", [128, 24]); LI = s5t("LI", [128, 24]); LS = s5t("LS", [128, 24])
            ld(LR[:], lam_re.rearrange("(pr g2) p -> (g2 p) pr", g2=2), [], ["LR"], slow=True)
            ld(LI[:], lam_im.rearrange("(pr g2) p -> (g2 p) pr", g2=2), [], ["LI"], slow=True)
            for g2 in range(2):
                ld(LS[g2 * 64:(g2 + 1) * 64, :],
                   log_step.rearrange("o (pr g2) -> o g2 pr", g2=2)[:, g2, :].to_broadcast([64, 24]), [], ["LS"], slow=True)
            STP = s5t("STP", [128, 24]); MAG = s5t("MAG", [128, 24]); ANG = s5t("ANG", [128, 24])
            T0 = s5t("T0", [128, 24]); T1 = s5t("T1", [128, 24]); T2 = s5t("T2", [128, 24]); TI = s5t("TI", [128, 24], I32)
            ABR = s5t("ABR", [128, 24]); ABI = s5t("ABI", [128, 24]); SN = s5t("SN", [128, 24]); CS = s5t("CS", [128, 24])
            FR = s5t("FR", [128, 24]); FI = s5t("FI", [128, 24])
            act(STP[:], LS[:], AF.Exp, ["LS"], ["STP"])
            tt("dve", T0[:], LR[:], STP[:], ALU.mult, ["LR", "STP"], ["T0"])
            act(MAG[:], T0[:], AF.Exp, ["T0"], ["MAG"])
            tt("dve", ANG[:], LI[:], STP[:], ALU.mult, ["LI", "STP"], ["ANG"])

            def sin_of(dst, shift, key):
                ts("dve", T0[:], ANG[:], shift, None, ALU.add, None, ["ANG"], ["T0"])
                ts("dve", T1[:], T0[:], 1.0 / TWO_PI, 0.5, ALU.mult, ALU.add, ["T0"], ["T1"])
                cp("dve", TI[:], T1[:], ["T1"], ["TI"])
                cp("dve", T1[:], TI[:], ["TI"], ["T1"])
                stt(T2[:], T1[:], -TWO_PI, T0[:], ALU.mult, ALU.add, ["T1", "T0"], ["T2"])
                ts("dve", T1[:], T2[:], math.pi, -TWO_PI, ALU.is_gt, ALU.mult, ["T2"], ["T1"])
                tt("dve", T2[:], T2[:], T1[:], ALU.add, ["T2", "T1"], ["T2"])
                ts("dve", T1[:], T2[:], -math.pi, TWO_PI, ALU.is_lt, ALU.mult, ["T2"], ["T1"])
                tt("dve", T2[:], T2[:], T1[:], ALU.add, ["T2", "T1"], ["T2"])
                ts("dve", T2[:], T2[:], math.pi, -math.pi, ALU.min, ALU.max, ["T2"], ["T2"])
                act(dst[:], T2[:], AF.Sin, ["T2"], [key])
            sin_of(SN, 0.0, "SN")
            sin_of(CS, math.pi / 2.0, "CS")
            tt("dve", ABR[:], MAG[:], CS[:], ALU.mult, ["MAG", "CS"], ["ABR"])
            tt("dve", ABI[:], MAG[:], SN[:], ALU.mult, ["MAG", "SN"], ["ABI"])
            DEN = s5t("DEN", [128, 24]); NR = s5t("NR", [128, 24])
            tt("dve", T0[:], LR[:], LR[:], ALU.mult, ["LR"], ["T0"])
            tt("dve", T1[:], LI[:], LI[:], ALU.mult, ["LI"], ["T1"])
            tt("dve", DEN[:], T0[:], T1[:], ALU.add, ["T0", "T1"], ["DEN"])
            P.c("dve", lambda h: h.reciprocal(out=DEN[:], in_=DEN[:]), ["DEN"], ["DEN"])
            ts("dve", NR[:], ABR[:], -1.0, None, ALU.add, None, ["ABR"], ["NR"])
            tt("dve", T0[:], NR[:], LR[:], ALU.mult, ["NR", "LR"], ["T0"])
            tt("dve", T1[:], ABI[:], LI[:], ALU.mult, ["ABI", "LI"], ["T1"])
            tt("dve", T0[:], T0[:], T1[:], ALU.add, ["T0", "T1"], ["T0"])
            tt("dve", FR[:], T0[:], DEN[:], ALU.mult, ["T0", "DEN"], ["FR"])
            tt("dve", T0[:], ABI[:], LR[:], ALU.mult, ["ABI", "LR"], ["T0"])
            tt("dve", T1[:], NR[:], LI[:], ALU.mult, ["NR", "LI"], ["T1"])
            tt("dve", T0[:], T0[:], T1[:], ALU.subtract, ["T0", "T1"], ["T0"])
            tt("dve", FI[:], T0[:], DEN[:], ALU.mult, ["T0", "DEN"], ["FI"])
            PWR = s5t("PWR", [128, 9, 24]); PWI = s5t("PWI", [128, 9, 24])
            AWR = s5t("AWR", [128, 9, 24]); AWI = s5t("AWI", [128, 9, 24])

            def cmul(orr, oi, ar, ai, br, bi, rk, wk):
                tt("dve", T0[:], ar, br, ALU.mult, rk, ["T0"])
                tt("dve", T1[:], ai, bi, ALU.mult, rk, ["T1"])
                tt("dve", T2[:], ar, bi, ALU.mult, rk, ["T2"])
                tt("dve", orr, T0[:], T1[:], ALU.subtract, ["T0", "T1"], wk)
                tt("dve", T0[:], ai, br, ALU.mult, rk + wk, ["T0"])
                tt("dve", oi, T2[:], T0[:], ALU.add, ["T2", "T0"], wk)
            mset("pool", PWR[:, 0, :], 1.0, ["PW"])
            mset("pool", PWI[:, 0, :], 0.0, ["PW"])
            for k in range(1, 9):
                cmul(PWR[:, k, :], PWI[:, k, :], PWR[:, k - 1, :], PWI[:, k - 1, :], ABR[:], ABI[:], ["PW", "ABR", "ABI"], ["PW"])
            mset("pool", AWR[:, 0, :], 1.0, ["AW"])
            mset("pool", AWI[:, 0, :], 0.0, ["AW"])
            for k in range(1, 9):
                cmul(AWR[:, k, :], AWI[:, k, :], AWR[:, k - 1, :], AWI[:, k - 1, :], PWR[:, 8, :], PWI[:, 8, :], ["AW", "PW"], ["AW"])
            cp("dve", A1R[:], AWR[:, 1, :], ["AW"], ["A1"])
            cp("dve", A1I[:], AWI[:, 1, :], ["AW"], ["A1"])
            cp("dve", APR[:], AWR[:, 1:9, :], ["AW"], ["APR"])
            cp("dve", API[:], AWI[:, 1:9, :], ["AW"], ["API"])
            cp("dve", A8R[:], AWR[:, 8, :], ["AW"], ["A8"])
            cp("dve", A8I[:], AWI[:, 8, :], ["AW"], ["A8"])

            BRE = s5t("BRE", [128, 24, 16]); BIM = s5t("BIM", [128, 24, 16])
            ld(BRE[:], s5_b_re.rearrange("(pr g2) p h -> (g2 p) pr h", g2=2), [], ["BRE"])
            ld(BIM[:], s5_b_im.rearrange("(pr g2) p h -> (g2 p) pr h", g2=2), [], ["BIM"])
            BBR = s5t("BBR", [128, 24, 16]); BBI = s5t("BBI", [128, 24, 16])
            U0 = s5t("U0", [128, 24, 16]); U1 = s5t("U1", [128, 24, 16])
            frb = FR[:].unsqueeze(2).to_broadcast([128, 24, 16])
            fib = FI[:].unsqueeze(2).to_broadcast([128, 24, 16])
            tt("dve", U0[:], BRE[:], frb, ALU.mult, ["BRE", "FR"], ["U0"])
            tt("dve", U1[:], BIM[:], fib, ALU.mult, ["BIM", "FI"], ["U1"])
            tt("dve", BBR[:], U0[:], U1[:], ALU.subtract, ["U0", "U1"], ["BBR"])
            tt("dve", U0[:], BIM[:], frb, ALU.mult, ["BIM", "FR"], ["U0"])
            tt("dve", U1[:], BRE[:], fib, ALU.mult, ["BRE", "FI"], ["U1"])
            tt("dve", BBI[:], U0[:], U1[:], ALU.add, ["U0", "U1"], ["BBI"])
            BDR = s5t("BDR", [128, 24, 32]); BDI = s5t("BDI", [128, 24, 32])
            mset("pool", BDR[:], 0.0, ["BDR"]); mset("pool", BDI[:], 0.0, ["BDI"])
            for g2 in range(2):
                sl = slice(g2 * 64, (g2 + 1) * 64)
                cp("dve", BDR[sl, :, g2 * 16:(g2 + 1) * 16], BBR[sl, :, :], ["BBR", "BDR"], ["BDR"])
                cp("dve", BDI[sl, :, g2 * 16:(g2 + 1) * 16], BBI[sl, :, :], ["BBI", "BDI"], ["BDI"])
            BWR = s5t("BWR", [128, 24, 128]); BWI = s5t("BWI", [128, 24, 128])
            mset("pool", BWR[:], 0.0, ["BWR"]); mset("pool", BWI[:], 0.0, ["BWI"])
            for q4 in range(4):
                for o in range(6):
                    pr = 4 * o + q4
                    cp("dve", BWR[:, pr, 32 * q4:32 * q4 + 32], BDR[:, pr, :], ["BDR", "BWR"], ["BWR"])
                    cp("pool", BWI[:, pr, 32 * q4:32 * q4 + 32], BDI[:, pr, :], ["BDI", "BWI"], ["BWI"])
            CTR = s5t("CTR", [128, 24, 16]); CTI = s5t("CTI", [128, 24, 16])
            CN = s5t("CN", [128, 3, 2, 128])
            tp_ps = ps(S0, "tp_ps", [128, 512], F32)
            for ri, (csrc, cdst, key) in enumerate(((s5_c_re, CTR, "CTR"), (s5_c_im, CTI, "CTI"))):
                cv4 = csrc.rearrange("(pr g2) h p -> pr h g2 p", g2=2)
                for pr in range(24):
                    ld(CN[(pr % 8) * 16:(pr % 8) * 16 + 16, pr // 8, ri, :].rearrange("h (a p) -> h a p", a=2),
                       cv4[pr], ["CN%d" % ri], ["CN%d" % ri])
                for j in range(3):
                    tpg([(tp_ps[:, 0:128], CN[:, j, ri, :], ident_f[:])], ["CN%d" % ri, "ident_f", "tp_ps"], ["tp_ps"])
                    cp("act", cdst[:, 8 * j:8 * j + 8, :], tp_ps[:, 0:128].rearrange("p (a h) -> p a h", a=8), ["tp_ps"], [key])
            XR = s5t("XR", [128, 24, 32]); XI = s5t("XI", [128, 24, 32])
            V0 = s5t("V0", [128, 24, 32]); V1 = s5t("V1", [128, 24, 32])
            WZs = [s5t("WZs%d" % i, [128, 6, 2, 128], BF16) for i in range(2)]
            for s in range(8):
                k = 7 - s
                wzs, wzk = WZs[s % 2], "WZs%d" % (s % 2)
                pr_ = PWR[:, k, :].unsqueeze(2).to_broadcast([128, 24, 32])
                pi_ = PWI[:, k, :].unsqueeze(2).to_broadcast([128, 24, 32])
                tt("dve", V0[:], BDR[:], pr_, ALU.mult, ["BDR", "PW"], ["V0"])
                tt("dve", V1[:], BDI[:], pi_, ALU.mult, ["BDI", "PW"], ["V1"])
                tt("dve", XR[:], V0[:], V1[:], ALU.subtract, ["V0", "V1"], ["XR"])
                tt("dve", V0[:], BDI[:], pr_, ALU.mult, ["BDI", "PW"], ["V0"])
                tt("dve", V1[:], BDR[:], pi_, ALU.mult, ["BDR", "PW"], ["V1"])
                tt("dve", XI[:], V0[:], V1[:], ALU.add, ["V0", "V1"], ["XI"])
                for o in range(6):
                    for ri, X in enumerate((XR, XI)):
                        tpg([(tp_ps[:, 0:128], X[:, 4 * o:4 * o + 4, :].rearrange("p a b -> p (a b)"), ident_f[:])], ["XR", "XI", "ident_f", "tp_ps"], ["tp_ps"])
                        cp("act", wzs[:, o, ri, :], tp_ps[:, 0:128], ["tp_ps"], [wzk])
                for o in range(6):
                    ld(WZS[o, :, s, :, :], wzs[:, o, :, :], [wzk], [U()])
            QR = s5t("QR", [128, 24, 16]); QI = s5t("QI", [128, 24, 16])
            QBR = s5t("QBR", [128, 24, 32]); QBI = s5t("QBI", [128, 24, 32])
            WYk = [s5t("WYk%d" % i, [128, 24, 2, 32], BF16) for i in range(2)]
            BDj = [s5t("BDj%d" % i, [128, 6, 128], F32) for i in range(2)]
            BDjb = [s5t("BDjb%d" % i, [128, 6, 128], BF16) for i in range(2)]
            DSK = s5t("DSK", [128, 6], F32)
            ld(DSK[:], s5_d.rearrange("o (c p) -> p (o c)", p=128), [], ["DSK"], slow=True)
            mset("pool", QBR[:], 0.0, ["QBR"]); mset("pool", QBI[:], 0.0, ["QBI"])
            for i in range(2):
                mset("pool", BDj[i][:], 0.0, ["BDj%d" % i])
            bd_ps = ps(S0, "bd_ps", [128, 6, 32], F32)
            for k in range(9):
                pr_ = PWR[:, k, :].unsqueeze(2).to_broadcast([128, 24, 16])
                pi_ = PWI[:, k, :].unsqueeze(2).to_broadcast([128, 24, 16])
                tt("dve", U0[:], CTR[:], pr_, ALU.mult, ["CTR", "PW"], ["U0"])
                tt("dve", U1[:], CTI[:], pi_, ALU.mult, ["CTI", "PW"], ["U1"])
                tt("dve", QR[:], U0[:], U1[:], ALU.subtract, ["U0", "U1"], ["QR"])
                tt("dve", U0[:], CTR[:], pi_, ALU.mult, ["CTR", "PW"], ["U0"])
                tt("dve", U1[:], CTI[:], pr_, ALU.mult, ["CTI", "PW"], ["U1"])
                tt("dve", QI[:], U0[:], U1[:], ALU.add, ["U0", "U1"], ["QI"])
                for g2 in range(2):
                    sl = slice(g2 * 64, (g2 + 1) * 64)
                    cp("dve", QBR[sl, :, g2 * 16:(g2 + 1) * 16], QR[sl, :, :], ["QR", "QBR"], ["QBR"])
                    ts("dve", QBI[sl, :, g2 * 16:(g2 + 1) * 16], QI[sl, :, :], -1.0, None, ALU.mult, None, ["QI", "QBI"], ["QBI"])
                if k >= 1:
                    wyk, wykk = WYk[k % 2], "WYk%d" % (k % 2)
                    cp("act", wyk[:, :, 0, :], QBR[:], ["QBR"], [wykk])
                    cp("act", wyk[:, :, 1, :], QBI[:], ["QBI"], [wykk])
                    for o in range(6):
                        ld(WYS[o, :, :, k - 1, :, :], wyk[:, 4 * o:4 * o + 4, :, :], [wykk], [U()])
                if k <= 7:
                    bdj, bdk = BDj[k % 2], "BDj%d" % (k % 2)
                    bdjb, bdbk = BDjb[k % 2], "BDjb%d" % (k % 2)
                    for o in range(6):
                        calls = []
                        for q4 in range(4):
                            pr = 4 * o + q4
                            calls.append(dict(out=bd_ps[:, o, :], lhsT=BWR[:, pr, :], rhs=QBR[:, pr, :], start=(q4 == 0), stop=False))
                            calls.append(dict(out=bd_ps[:, o, :], lhsT=BWI[:, pr, :], rhs=QBI[:, pr, :], start=False, stop=(q4 == 3)))
                        mmg(calls, ["BWR", "BWI", "QBR", "QBI", "bd_ps"], ["bd_ps"])
                    for q4 in range(4):
                        sl = slice(32 * q4, 32 * q4 + 32)
                        cp("act", bdj[sl, :, 32 * q4:32 * q4 + 32], bd_ps[sl, :, :], ["bd_ps"], [bdk])
                    if k == 0:
                        for o in range(6):
                            stt(bdj[:, o, :], ident_f[:], DSK[:, o:o + 1], bdj[:, o, :], ALU.mult, ALU.add, ["ident_f", "DSK", bdk], [bdk])
                    cp("dve", bdjb[:], bdj[:], [bdk], [bdbk])
                    for o in range(6):
                        ld(BDS[o, :, k, :], bdjb[:, o, :], [bdbk], [U()])
            P.emit()
        if upto < 1:
            return nc
        S0b = ExitStack()
        with S0b:
            S0 = S0b

            def s5t(name, shape, dt=F32):
                return sb(S0b, name, shape, dt)
            mt_x = s5t("mt_x", [128, 2, D], F32)
            mt_n = s5t("mt_n", [128, 2, D], BF16)
            mt_j = s5t("mt_j", [128, D], BF16)
            MHT = s5t("MHT", [128, 8, 256], BF16)
            wkv_t = s5t("wkv_t", [128, 8, 1024], BF16)
            tpb_ps = ps(S0, "tpb_ps", [128, 512], BF16)
            mk_ps = ps(S0, "mk_ps", [128, 512], F32)
            ld(mt_x[:], mem.rearrange("(a p) d -> p a d", p=128), [], ["mt_x"])
            ld(wkv_t[:], WKV, ["WKV"], ["wkv_t"])
            for a in range(2):
                act(mt_j[:], mt_x[:, a, :], AF.Square, ["mt_x"], ["mt_j", "ssq"], accum=ssq[:, a:a + 1])
            ts("dve", rstd[:, 0:2], ssq[:, 0:2], 1.0 / D, EPS, ALU.mult, ALU.add, ["ssq"], ["rstd"])
            act(rstd[:, 0:2], rstd[:, 0:2], AF.Sqrt, ["rstd"], ["rstd"])
            P.c("dve", lambda h: h.reciprocal(out=rstd[:, 0:2], in_=rstd[:, 0:2]), ["rstd"], ["rstd"])
            for a in range(2):
                ts("dve", mt_n[:, a, :], mt_x[:, a, :], rstd[:, a:a + 1], None, ALU.mult, None, ["mt_x", "rstd"], ["mt_n"])
            for kt in range(8):
                tpg([(tpb_ps[:, a * 128:(a + 1) * 128], mt_n[:, a, kt * 128:(kt + 1) * 128], ident_b[:]) for a in range(2)],
                    ["mt_n", "ident_b", "tpb_ps"], ["tpb_ps"])
                ts("dve", MHT[:, kt, :], tpb_ps[:, 0:256], gmn[:, kt:kt + 1], None, ALU.mult, None, ["tpb_ps", "gmn"], ["MHT"])
            for hd in range(4):
                mmg([dict(out=mk_ps[:, 0:256], lhsT=wkv_t[:, kt, hd * 128:(hd + 1) * 128], rhs=MHT[:, kt, :],
                          start=(kt == 0), stop=(kt == 7)) for kt in range(8)], ["wkv_t", "MHT", "mk_ps"], ["mk_ps"])
                cp("act", MKT[:, hd, :], mk_ps[:, 0:256], ["mk_ps"], ["MKT"])
            for mt in range(2):
                mmg([dict(out=mk_ps[:, :], lhsT=MHT[:, kt, mt * 128:(mt + 1) * 128], rhs=wkv_t[:, kt, 512:1024],
                          start=(kt == 0), stop=(kt == 7)) for kt in range(8)], ["wkv_t", "MHT", "mk_ps"], ["mk_ps"])
                cp("act", MV[:, mt, :], mk_ps[:, :], ["mk_ps"], ["MV"])
            P.emit()

        if upto < 2:
            return nc
        S1 = ExitStack()
        with S1:
            xt = sb(S1, "xt", [128, 4, D], F32)
            xn = sb(S1, "xn", [128, 4, D], BF16)
            sqj = sb(S1, "sqj", [128, D], BF16)
            hTs = [sb(S1, "hT%d" % i, [128, 8, TT], BF16) for i in range(2)]
            NWS = 3
            wsl = [sb(S1, "ws%d" % i, [128, 4096], BF16) for i in range(NWS)]
            SPt = [sb(S1, "SPt%d" % i, [128, TT], BF16) for i in range(2)]
            HIb = sb(S1, "HIb", [128, TT], BF16)
            MIDb = sb(S1, "MIDb", [128, TT], BF16)
            kst = [sb(S1, "kst%d" % i, [128, TT], BF16) for i in range(2)]
            Vst = sb(S1, "Vst", [128, 4, 12, 65], BF16)
            uT = sb(S1, "uT", [128, 6, TT], BF16)
            s5w = [(sb(S1, "s5bd%d" % i, [128, 8, 128], BF16), sb(S1, "s5wz%d" % i, [128, 8, 2, 128], BF16),
                    sb(S1, "s5wy%d" % i, [128, 4, 8, 2, 32], BF16)) for i in range(2)]
            Z = sb(S1, "Z", [128, 24, 2, 64], F32)
            Hb = sb(S1, "Hb", [128, 24, 2, 65], BF16)
            RT = [sb(S1, "RT%d" % i, [128, 24, 8], F32) for i in range(8)]
            RS = [sb(S1, "RS%d" % i, [128, 24], F32) for i in range(8)]
            tA = sb(S1, "tA", [128, TT], F32)
            tB = sb(S1, "tB", [128, TT], F32)
            tC = sb(S1, "tC", [128, TT], F32)
            Fe = sb(S1, "Fe", [76, TT], F32)
            Ff = sb(S1, "Ff", [76, TT], F32)
            R1 = sb(S1, "R1", [76, TT], F32)
            YG = sb(S1, "YG", [128, 6, TT], BF16)
            SGS = sb(S1, "SGS", [128, 6, TT], BF16)
            YS = SGS
            QM = sb(S1, "QM", [128, 4, TT], BF16)
            SGM = sb(S1, "SGM", [128, 4, TT], BF16)
            PmT = sb(S1, "PmT", [128, 2, TT], BF16)
            YM = SGM
            M0 = [sb(S1, "M0_%d" % i, [128, TT], F32) for i in range(2)]
            G0st = [sb(S1, "G0st%d" % i, [128, TT], BF16) for i in range(2)]
            pj = [ps(S1, "pj%d" % i, [128, 512], F32) for i in range(2)]
            tpall = ps(S1, "tpall", [128, 1024], BF16)
            tpp = [tpall[:, 0:512], tpall[:, 0:512]]
            zpsb = [ps(S1, "zps%d" % i, [128, 512], F32) for i in range(4)]
            yps = ps(S1, "yps", [128, 512], F32)

            import os
            if not os.environ.get('SKIPM'):
                for i in range(2):
                    mset("pool", SPt[i][:], 0.0, ["SPt%d" % i])
                    mset("pool", SPt[i][96:97, :], 1.0, ["SPt%d" % i])
                mset("pool", Vst[:], 1.0, ["Vst"])

            wn = [0]
            pjn = [0]

            def wload(src3, nkt, ncols):
                i = wn[0] % NWS
                wn[0] += 1
                t = wsl[i][:, 0:nkt * ncols].rearrange("p (k c) -> p k c", k=nkt)
                if os.environ.get('WLDTINY'):
                    ld(t[:, 0:1, 0:16], src3[:, 0:1, 0:16], [], ["ws%d" % i])
                else:
                    ld(t, src3, [], ["ws%d" % i])
                return t, "ws%d" % i

            PJL = [(pj[0], "pj0"), (pj[1], "pj1"), (zpsb[0], "zps0"), (zpsb[1], "zps1"), (zpsb[2], "zps2"), (zpsb[3], "zps3"), (yps, "yps")]

            def nextpj():
                i = pjn[0] % len(PJL)
                pjn[0] += 1
                return PJL[i]

            import os
            P1T = int(os.environ.get('P1T', NT)); P1S = int(os.environ.get('P1S', 99)); OWNX = int(os.environ.get('OWNX', OWN0))
            def prep(it):
                hTn, hkn = hTs[it % 2], "hT%d" % (it % 2)
                sptn, spkn = SPt[it % 2], "SPt%d" % (it % 2)
                if it == 0:
                    ld(xt[:], xa[it * TT:(it + 1) * TT, :].rearrange("(a p) d -> p a d", p=128), [], ["xt"])
                ld(sptn[76:77, :], kmrow[it:it + 1, :], [], [spkn], eng="pool")
                for a in range(4):
                    act(sqj[:], xt[:, a, :], AF.Square, ["xt"], ["sqj", "ssq"], accum=ssq[:, a:a + 1])
                ts("dve", rstd[:, 0:4], ssq[:, 0:4], 1.0 / D, EPS, ALU.mult, ALU.add, ["ssq"], ["rstd"])
                act(rstd[:, 0:4], rstd[:, 0:4], AF.Sqrt, ["rstd"], ["rstd"])
                P.c("dve", lambda h: h.reciprocal(out=rstd[:, 0:4], in_=rstd[:, 0:4]), ["rstd"], ["rstd"])
                for a in range(4):
                    ts("dve", xn[:, a, :], xt[:, a, :], rstd[:, a:a + 1], None, ALU.mult, None, ["xt", "rstd"], ["xn"])
                if it + 1 < P1T:
                    ld(xt[:], xa[(it + 1) * TT:(it + 2) * TT, :].rearrange("(a p) d -> p a d", p=128), [], ["xt"])
                for kt in range(8):
                    tp, tk = tpp[kt % 2], "tpp0"
                    tpg([(tp[:, a * 128:(a + 1) * 128], xn[:, a, kt * 128:(kt + 1) * 128], ident_b[:]) for a in range(4)],
                        ["xn", "ident_b", tk], [tk])
                    if kt % 2 == 0:
                        ts("dve", hTn[:, kt, :], tp[:, :], gn[:, kt:kt + 1], None, ALU.mult, None, [tk, "gn"], [hkn])
                    else:
                        act(hTn[:, kt, :], tp[:, :], AF.Identity, [tk, "gn"], [hkn], scale=gn[:, kt:kt + 1])
                pjt, pk = nextpj()
                mmg([dict(out=pjt[0:76, :], lhsT=WFL3[:, kt, :], rhs=hTn[:, kt, :], start=(kt == 0), stop=(kt == 7)) for kt in range(8)],
                    ["WFL3", hkn, pk], [pk])
                act(Fe[0:76, :], pjt[0:76, :], AF.Exp, [pk, "negb"], ["Fe"], bias=negb[0:76, :], scale=-1.0)
                act(Fe[0:76, :], Fe[0:76, :], AF.Ln, ["Fe"], ["Fe"], bias=1.0)
                P.c("dve", lambda h: h.tensor_tensor_scan(out=Ff[0:76, :], data0=ones_f[0:76, :], data1=Fe[0:76, :],
                                                          initial=Fcar[0:76, :], op0=ALU.mult, op1=ALU.subtract),
                    ["ones_f", "Fe", "Fcar"], ["Ff"])
                cp("dve", Fcar[0:76, :], Ff[0:76, TT - 1:TT], ["Ff"], ["Fcar"])
                cp("dve", HIb[0:76, :], Ff[0:76, :], ["Ff"], ["HIb"])
                stt(R1[0:76, :], HIb[0:76, :], nm1[0:76, :], Ff[0:76, :], ALU.mult, ALU.add, ["HIb", "nm1", "Ff"], ["R1"])
                tt("dve", MIDb[0:76, :], Ff[0:76, :], HIb[0:76, :], ALU.subtract, ["Ff", "HIb"], ["MIDb"])
                stt(R1[0:76, :], MIDb[0:76, :], nm2[0:76, :], R1[0:76, :], ALU.mult, ALU.add, ["MIDb", "nm2", "R1"], ["R1"])
                cp("dve", sptn[0:76, :], R1[0:76, :], ["R1"], [spkn])

            prep(0)
            for it in range(P1T):
                own = it >= OWNX
                ot = it - OWNX
                sp_i = it % 2
                spt, spk = SPt[sp_i], "SPt%d" % sp_i
                hT, hk = hTs[it % 2], "hT%d" % (it % 2)
                if P1S < 5:
                    continue
                wv1, wv1k = wload(WB[:, :, C_V:C_V + 512], 8, 512)
                wv2, wv2k = wload(WB[:, :, C_V + 512:C_V + 768], 8, 256)
                for a in range(4):
                    p1, p1k = nextpj()
                    p2, p2k = nextpj()
                    mmg([dict(out=p1[:, :], lhsT=hT[:, kt, a * 128:(a + 1) * 128], rhs=wv1[:, kt, :], start=(kt == 0), stop=(kt == 7)) for kt in range(8)],
                        [hk, wv1k, p1k], [p1k])
                    mmg([dict(out=p2[:, 0:256], lhsT=hT[:, kt, a * 128:(a + 1) * 128], rhs=wv2[:, kt, :], start=(kt == 0), stop=(kt == 7)) for kt in range(8)],
                        [hk, wv2k, p2k], [p2k])
                    cp("act", Vst[:, a, 0:8, 0:64], p1[:, :].rearrange("p (h d) -> p h d", h=8), [p1k], ["Vst"])
                    cp("dve", Vst[:, a, 8:12, 0:64], p2[:, 0:256].rearrange("p (h d) -> p h d", h=4), [p2k], ["Vst"])
                ld(VS[4 * it:4 * it + 4].rearrange("a p f -> p a f"), Vst[:].rearrange("p a h d -> p a (h d)"), ["Vst"], [U()], eng="act")
                if P1S < 6:
                    continue
                for o in range(6):
                    if o % 4 == 0:
                        n = min(512, 768 - o * 128)
                        wk, wkk = wload(WB[:, :, C_U + o * 128:C_U + o * 128 + n], 8, n)
                    pjt, pk = nextpj()
                    c0 = (o % 4) * 128
                    mmg([dict(out=pjt[:, :], lhsT=wk[:, kt, c0:c0 + 128], rhs=hT[:, kt, :], start=(kt == 0), stop=(kt == 7)) for kt in range(8)],
                        [wkk, hk, pk], [pk])
                    cp("act" if o % 2 == 0 else "dve", uT[:, o, :], pjt[:, :], [pk], ["uT"])
                if P1S < 3:
                    continue
                wk, wkk = None, None
                for h_ in range(12):
                    if h_ % 8 == 0:
                        n = min(512, 768 - h_ * 64)
                        wk, wkk = wload(WB[:, :, C_K + h_ * 64:C_K + h_ * 64 + n], 8, n)
                    pjt, pk = nextpj()
                    c0 = (h_ % 8) * 64
                    calls = [dict(out=pjt[64:128, :], lhsT=SELK[:, h_, 64:128], rhs=spt[:, :], start=True, stop=True)]
                    calls += [dict(out=pjt[0:64, :], lhsT=wk[:, kt, c0:c0 + 64], rhs=hT[:, kt, :], start=(kt == 0), stop=(kt == 7)) for kt in range(8)]
                    mmg(calls, ["SELK", spk, wkk, hk, pk], [pk])
                    ks, kk = kst[h_ % 2], "kst%d" % (h_ % 2)
                    if h_ % 2 == 0:
                        cp("act", ks[:], pjt[:, :], [pk], [kk])
                        ld(KT[h_, :, it * TT:(it + 1) * TT], ks[:], [kk], [U()], eng="act")
                    else:
                        cp("dve", ks[:], pjt[:, :], [pk], [kk])
                        ld(KT[h_, :, it * TT:(it + 1) * TT], ks[:], [kk], [U()], eng="act")
                if P1S < 4:
                    continue
                if own:
                    for h_ in range(12):
                        if h_ % 8 == 0:
                            n = min(512, 768 - h_ * 64)
                            wk, wkk = wload(WB[:, :, C_Q + h_ * 64:C_Q + h_ * 64 + n], 8, n)
                        pjt, pk = nextpj()
                        c0 = (h_ % 8) * 64
                        calls = [dict(out=pjt[64:128, :], lhsT=SELQ[:, h_, 64:128], rhs=spt[:, :], start=True, stop=True)]
                        calls += [dict(out=pjt[0:64, :], lhsT=wk[:, kt, c0:c0 + 64], rhs=hT[:, kt, :], start=(kt == 0), stop=(kt == 7)) for kt in range(8)]
                        mmg(calls, ["SELQ", spk, wkk, hk, pk], [pk])
                        ks, kk = kst[h_ % 2], "kst%d" % (h_ % 2)
                        act(ks[:], pjt[:, :], AF.Identity, [pk, "qscale"], [kk], scale=qscale[:, :])
                        ld(QT[h_, :, ot * TT:(ot + 1) * TT], ks[:], [kk], [U()], eng="act")
                if P1S < 7:
                    continue
                s5slots = []
                for o in range(6):
                    bd_t, wz_t, wy_t = s5w[o % 2]
                    sk = "s5w%d" % (o % 2)
                    ld(wz_t[:], WZS[o], [], [sk + "z"])
                    calls = []
                    for ri in range(2):
                        for s in range(8):
                            for q4 in range(4):
                                calls.append(dict(out=zpsb[q4][:, ri * 64:(ri + 1) * 64], lhsT=wz_t[32 * q4:32 * q4 + 32, s, ri, :],
                                                  rhs=uT[32 * q4:32 * q4 + 32, o, :].rearrange("p (c s) -> p c s", s=8)[:, :, s],
                                                  start=(s == 0), stop=(s == 7), tile_position=(32 * q4, 0)))
                    mmg(calls, [sk + "z", "uT"] + ["zps%d" % q for q in range(4)], ["zps%d" % q for q in range(4)])
                    for q4 in range(4):
                        cp("act" if q4 % 2 == 0 else "dve", Z[:, 4 * o + q4, :, :], zpsb[q4][:, 0:128].rearrange("p (i c) -> p i c", i=2), ["zps%d" % q4], ["Zj%d" % j for j in range(8)])
                if it + 1 < P1T:
                    prep(it + 1)
                if P1S < 8:
                    continue
                if it > 0:
                    cp(os.environ.get('RENG', 'dve'), TB[:, :, :, 0], TB[:, :, :, 8], ["TB"], ["TB"])
                Zv = Z[:].rearrange("p r i (b j) -> p r i b j", j=8)

                RENG = os.environ.get('RENG', 'dve')
                cmn = [0]

                def cmac(dre, dim_, sre, sim, mr, mi, Tsets, rkeys, wkey):
                    si_ = cmn[0] % len(Tsets)
                    cmn[0] += 1
                    T = Tsets[si_]
                    tk = ["RT%d_%d" % (si_, q) for q in range(4)]
                    tt(RENG, T[0], mr, sre, ALU.mult, rkeys, [tk[0]])
                    tt(RENG, T[1], mi, sim, ALU.mult, rkeys, [tk[1]])
                    tt(RENG, T[2], mr, sim, ALU.mult, rkeys, [tk[2]])
                    tt(RENG, T[3], mi, sre, ALU.mult, rkeys, [tk[3]])
                    tt(RENG, T[0], T[0], T[1], ALU.subtract, [tk[0], tk[1]], [tk[0]])
                    tt(RENG, T[2], T[2], T[3], ALU.add, [tk[2], tk[3]], [tk[2]])
                    tt(RENG, dre, dre, T[0], ALU.add, [tk[0], wkey], [wkey])
                    tt(RENG, dim_, dim_, T[2], ALU.add, [tk[2], wkey], [wkey])
                RTs = [[t[:] for t in RT[0:4]], [t[:] for t in RT[4:8]]]
                RSs = [[t[:] for t in RS[0:4]], [t[:] for t in RS[4:8]]]
                zkeys = ["Zj%d" % j for j in range(8)]
                a1r = A1R[:].unsqueeze(2).to_broadcast([128, 24, 8])
                a1i = A1I[:].unsqueeze(2).to_broadcast([128, 24, 8])
                for j in range(1, 8):
                    cmac(Zv[:, :, 0, :, j], Zv[:, :, 1, :, j], Zv[:, :, 0, :, j - 1], Zv[:, :, 1, :, j - 1], a1r, a1i, RTs, [zkeys[j - 1]], zkeys[j])
                for b_ in range(8):
                    cp(RENG, TB[:, :, :, b_ + 1], Zv[:, :, :, b_, 7], [zkeys[7], "TB"], ["TB"])
                    cmac(TB[:, :, 0, b_ + 1], TB[:, :, 1, b_ + 1], TB[:, :, 0, b_], TB[:, :, 1, b_], A8R[:], A8I[:], RSs, ["TB"], "TB")
                for j in range(8):
                    cmac(Zv[:, :, 0, :, j], Zv[:, :, 1, :, j], TB[:, :, 0, 0:8], TB[:, :, 1, 0:8],
                         APR[:, j, :].unsqueeze(2).to_broadcast([128, 24, 8]), API[:, j, :].unsqueeze(2).to_broadcast([128, 24, 8]), RTs, ["TB"], zkeys[j])
                if own:
                    cp(RENG, Hb[:, :, :, 0], TB[:, :, :, 0], ["TB"], ["Hb"])
                    cp(RENG, Hb[:, :, :, 1:65], Z[:, :, :, :], zkeys, ["Hb"])
                if not own:
                    continue
                if P1S < 9:
                    continue
                for o in range(6):
                    if o % 4 == 0:
                        n = min(512, 768 - o * 128)
                        wk, wkk = wload(WB[:, :, C_GS + o * 128:C_GS + o * 128 + n], 8, n)
                    pjt, pk = nextpj()
                    c0 = (o % 4) * 128
                    mmg([dict(out=pjt[:, :], lhsT=wk[:, kt, c0:c0 + 128], rhs=hT[:, kt, :], start=(kt == 0), stop=(kt == 7)) for kt in range(8)],
                        [wkk, hk, pk], [pk])
                    act(SGS[:, o, :], pjt[:, :], AF.Silu, [pk], ["SGS"])
                if P1S < 12:
                    continue
                wk, wkk = wload(WB[:, :, C_QM:C_QM + 512], 8, 512)
                for hd in range(4):
                    pjt, pk = nextpj()
                    mmg([dict(out=pjt[:, :], lhsT=wk[:, kt, hd * 128:(hd + 1) * 128], rhs=hT[:, kt, :], start=(kt == 0), stop=(kt == 7)) for kt in range(8)],
                        [wkk, hk, pk], [pk])
                    act(QM[:, hd, :], pjt[:, :], AF.Identity, [pk], ["QM"], scale=128.0 ** -0.5)
                wk, wkk = wload(WB[:, :, C_GM:C_GM + 512], 8, 512)
                for hd in range(4):
                    pjt, pk = nextpj()
                    mmg([dict(out=pjt[:, :], lhsT=wk[:, kt, hd * 128:(hd + 1) * 128], rhs=hT[:, kt, :], start=(kt == 0), stop=(kt == 7)) for kt in range(8)],
                        [wkk, hk, pk], [pk])
                    act(SGM[:, hd, :], pjt[:, :], AF.Silu, [pk], ["SGM"])
                for hd in range(4):
                    for mt in range(2):
                        pjt, pk = nextpj()
                        mmg([dict(out=pjt[:, :], lhsT=MKT[:, hd, mt * 128:(mt + 1) * 128], rhs=QM[:, hd, :], start=True, stop=True)],
                            ["MKT", "QM", pk], [pk])
                        act(PmT[:, mt, :], pjt[:, :], AF.Exp, [pk], ["PmT%d" % mt])
                    pjt, pk = nextpj()
                    mmg([dict(out=pjt[:, :], lhsT=MV[:, mt, hd * 128:(hd + 1) * 128], rhs=PmT[:, mt, :], start=(mt == 0), stop=(mt == 1)) for mt in range(2)],
                        ["MV", "PmT0", "PmT1", pk], [pk])
                    axp, axk = nextpj()
                    mmg([dict(out=axp[:, :], lhsT=ones_b[:, :], rhs=PmT[:, mt, :], start=(mt == 0), stop=(mt == 1)) for mt in range(2)],
                        ["ones_b", "PmT0", "PmT1", axk], [axk])
                    P.c("dve", lambda h, axp=axp: h.reciprocal(out=tC[:], in_=axp[:, :]), [axk], ["tC"])
                    tt("dve", tB[:], tC[:], pjt[:, :], ALU.mult, ["tC", pk], ["tB"])
                    tt("dve", YM[:, hd, :], tB[:], SGM[:, hd, :], ALU.mult, ["tB", "SGM"], ["SGM"])
                if P1S < 14:
                    continue
                for cb in range(8):
                    if cb % 4 == 0:
                        wg0, wg0k = wload(WB[:, :, C_GL + cb * 128:C_GL + cb * 128 + 512], 8, 512)
                    c0 = (cb % 4) * 128
                    pjt, pk = nextpj()
                    mmg([dict(out=pjt[:, :], lhsT=wg0[:, kt, c0:c0 + 128], rhs=hT[:, kt, :], start=(kt == 0), stop=(kt == 7)) for kt in range(8)],
                        [wg0k, hk, pk], [pk])
                    g0, g0k = G0st[cb % 2], "G0st%d" % (cb % 2)
                    act(g0[:], pjt[:, :], AF.Sigmoid, [pk, "bm"], [g0k], bias=bm[:, cb:cb + 1])
                    ld(G0S[ot, :, cb, :], g0[:], [g0k], [U()], eng="act")
                for o in range(6):
                    if o % 4 == 0:
                        n = min(512, 768 - o * 128)
                        wk, wkk = wload(WB[:, :, C_GF + o * 128:C_GF + o * 128 + n], 8, n)
                    pjt, pk = nextpj()
                    c0 = (o % 4) * 128
                    mmg([dict(out=pjt[:, :], lhsT=wk[:, kt, c0:c0 + 128], rhs=hT[:, kt, :], start=(kt == 0), stop=(kt == 7)) for kt in range(8)],
                        [wkk, hk, pk], [pk])
                    ks, kk = kst[o % 2], "kst%d" % (o % 2)
                    act(ks[:], pjt[:, :], AF.Silu, [pk], [kk])
                    ld(SGFS[ot, :, o, :], ks[:], [kk], [U()], eng="act")
                if P1S < 10:
                    continue
                for o in range(6):
                    bd_t, wz_t, wy_t = s5w[o % 2]
                    sk = "s5w%d" % (o % 2)
                    ld(bd_t[:], BDS[o], [], [sk + "b"])
                    ld(wy_t[:], WYS[o], [], [sk + "y"])
                    yv = yps[:, :].rearrange("p (c s) -> p c s", s=8)
                    uv = uT[:, o, :].rearrange("p (c s) -> p c s", s=8)
                    calls = []
                    for j in range(8):
                        calls.append(dict(out=yv[:, :, j:8], lhsT=bd_t[:, j, :], rhs=uv[:, :, 0:8 - j], start=(j == 0), stop=False))
                    for tau in range(8):
                        for ri in range(2):
                            for q4 in range(4):
                                last = (tau == 7 and ri == 1)
                                calls.append(dict(out=yv[32 * q4:32 * q4 + 32, :, tau], lhsT=wy_t[:, q4, tau, ri, :],
                                                  rhs=Hb[:, 4 * o + q4, ri, 0:64], start=False, stop=last, tile_position=(0, 32 * q4)))
                    mmg(calls, [sk + "b", sk + "y", "uT", "Hb", "yps"], ["yps"])
                    act(tA[:], yps[:, :], AF.Square, ["yps"], ["tA"])
                    ts("dve", tA[:], tA[:], 0.044715, 1.0, ALU.mult, ALU.add, ["tA"], ["tA"])
                    tt("dve", tB[:], tA[:], yps[:, :], ALU.mult, ["tA", "yps"], ["tB"])
                    act(tA[:], tB[:], AF.Sigmoid, ["tB"], ["tA"], scale=1.5957691216057308)
                    tt("dve", YG[:, o, :], tA[:], yps[:, :], ALU.mult, ["tA", "yps"], ["YG"])
                if P1S < 11:
                    continue
                for half in range(2):
                    wg, wgk = wload(WGLU[:, :, half * 384:(half + 1) * 384], 6, 384)
                    for c3 in range(3):
                        cb = half * 3 + c3
                        pjt, pk = nextpj()
                        mmg([dict(out=pjt[:, :], lhsT=wg[:, kt, c3 * 128:(c3 + 1) * 128], rhs=YG[:, kt, :], start=(kt == 0), stop=(kt == 5)) for kt in range(6)],
                            [wgk, "YG", pk], [pk])
                        act(tA[:], pjt[:, :], AF.Sigmoid, [pk, "bglu"], ["tA"], bias=bglu[:, cb:cb + 1])
                        tt("dve", tB[:], tA[:], YG[:, cb, :], ALU.mult, ["tA", "YG"], ["tB"])
                        tt("dve", YS[:, cb, :], tB[:], SGS[:, cb, :], ALU.mult, ["tB", "SGS"], ["SGS"])
                if P1S < 13:
                    continue
                for cb in range(8):
                    si = wn[0] % NWS
                    wn[0] += 1
                    wfull = wsl[si][:, 0:26 * 128].rearrange("p (k c) -> p k c", k=26)
                    wpsk = wpmk = wg1k = wg2k = "ws%d" % si
                    wps_t, wpm_t, wg1, wg2 = wfull[:, 0:6, :], wfull[:, 6:10, :], wfull[:, 10:18, :], wfull[:, 18:26, :]
                    ld(wps_t, WPS[:, :, cb * 128:(cb + 1) * 128], [], [wpsk])
                    ld(wpm_t, WPM[:, :, cb * 128:(cb + 1) * 128], [wpsk], [wpsk])
                    ld(wg1, WB[:, :, C_GL + 1024 + cb * 128:C_GL + 1024 + (cb + 1) * 128], [wpsk], [wpsk])
                    ld(wg2, WB[:, :, C_GL + 2048 + cb * 128:C_GL + 2048 + (cb + 1) * 128], [wpsk], [wpsk])
                    c0 = 0
                    m0, m0k = M0[cb % 2], "M0_%d" % (cb % 2)
                    pjt, pk = nextpj()
                    mmg([dict(out=pjt[:, :], lhsT=wg1[:, kt, c0:c0 + 128], rhs=hT[:, kt, :], start=(kt == 0), stop=(kt == 7)) for kt in range(8)],
                        [wg1k, hk, pk], [pk])
                    act(tA[:], pjt[:, :], AF.Sigmoid, [pk, "bm"], ["tA"], bias=bm[:, 8 + cb:9 + cb])
                    pjt, pk = nextpj()
                    mmg([dict(out=pjt[:, :], lhsT=wps_t[:, kt, c0:c0 + 128], rhs=YS[:, kt, :], start=(kt == 0), stop=(kt == 5)) for kt in range(6)],
                        [wpsk, "SGS", pk], [pk])
                    tt("dve", m0[:], tA[:], pjt[:, :], ALU.mult, ["tA", pk], [m0k])
                    pjt, pk = nextpj()
                    mmg([dict(out=pjt[:, :], lhsT=wg2[:, kt, c0:c0 + 128], rhs=hT[:, kt, :], start=(kt == 0), stop=(kt == 7)) for kt in range(8)],
                        [wg2k, hk, pk], [pk])
                    act(tB[:], pjt[:, :], AF.Sigmoid, [pk, "bm"], ["tB"], bias=bm[:, 16 + cb:17 + cb])
                    pjt, pk = nextpj()
                    mmg([dict(out=pjt[:, :], lhsT=wpm_t[:, kt, c0:c0 + 128], rhs=YM[:, kt, :], start=(kt == 0), stop=(kt == 3)) for kt in range(4)],
                        [wpmk, "SGM", pk], [pk])
                    tt("dve", tC[:], tB[:], pjt[:, :], ALU.mult, ["tB", pk], ["tC"])
                    tt("dve", m0[:], m0[:], tC[:], ALU.add, [m0k, "tC"], [m0k])
                    ld(M0S[ot, :, cb, :], m0[:], [m0k], [U()], eng="act")
            P.emit()

        if upto < 3:
            return nc
        S2 = ExitStack()
        with S2:
            Kh = [sb(S2, "Kh%d" % i, [128, LTOK], BF16) for i in range(2)]
            Vh = [sb(S2, "Vh%d" % i, [128, 64, 65], BF16) for i in range(2)]
            Qh = [sb(S2, "Qh%d" % i, [128, 4096], BF16) for i in range(2)]
            NPT = 4
            Pt = [sb(S2, "Pt%d" % i, [128, TT], BF16) for i in range(NPT)]
            Osb = sb(S2, "Osb", [65, TT], F32)
            Rd = sb(S2, "Rd", [64, TT], F32)
            Yst = [sb(S2, "Yst%d" % i, [64, TT], BF16) for i in range(2)]
            SEL = sb(S2, "SEL", [65, 64], F32)
            sps = [ps(S2, "sps%d" % i, [128, 512], F32) for i in range(4)]
            ops_ = [ps(S2, "ops%d" % i, [128, 512], F32) for i in range(2)]
            dps = ps(S2, "dps", [128, 512], F32)
            mset("pool", SEL[:], 0.0, ["SEL"])
            mset("pool", SEL[64:65, :], 1.0, ["SEL"])
            LA = 2
            its = []
            for h_ in range(12):
                for qt in range(8):
                    nkb = 32 + 4 * qt + 4
                    order = list(range(32 + 4 * qt, nkb)) + list(range(0, 32 + 4 * qt))
                    for n_, kb in enumerate(order):
                        its.append((h_, qt, kb, n_ == 0, n_ == nkb - 1))

            def head_loads(h_):
                hs = h_ % 2
                kh, vh, qh = Kh[hs], Vh[hs], Qh[hs]
                kk, vk, qk = "Kh%d" % hs, "Vh%d" % hs, "Qh%d" % hs
                ld(qh[:], QT[h_], [], [qk])
                for c4 in range(4):
                    ld(kh[:, c4 * 2048:(c4 + 1) * 2048], KT[h_, :, c4 * 2048:(c4 + 1) * 2048], [qk], [kk])
                for c4 in range(4):
                    ld(vh[:, c4 * 16:(c4 + 1) * 16, :], VS[c4 * 16:(c4 + 1) * 16, :, h_ * 65:(h_ + 1) * 65].rearrange("b p d -> p b d"), [kk], [vk])

            def geom(i):
                h_, qt, kb, first, last = its[i]
                diag = kb - (32 + 4 * qt)
                c0 = max(0, diag) * 128
                return h_, qt, kb, first, last, diag, c0

            def emit_qk(i):
                h_, qt, kb, first, last, diag, c0 = geom(i)
                hs = h_ % 2
                st, sk = sps[i % 4], "sps%d" % (i % 4)
                mmg([dict(out=st[:, c0:TT], lhsT=Kh[hs][:, kb * 128:(kb + 1) * 128], rhs=Qh[hs][:, qt * TT + c0:(qt + 1) * TT], start=True, stop=True)],
                    ["Kh%d" % hs, "Qh%d" % hs, sk], [sk])

            pend = []

            def emit_rest(i):
                h_, qt, kb, first, last, diag, c0 = geom(i)
                hs = h_ % 2
                st, sk = sps[i % 4], "sps%d" % (i % 4)
                pt, ptk = Pt[i % NPT], "Pt%d" % (i % NPT)
                op_t, opk = ops_[(h_ * 8 + qt) % 2], "ops%d" % ((h_ * 8 + qt) % 2)
                act(pt[:, c0:TT], st[:, c0:TT], AF.Exp, [sk], [ptk])
                if diag >= 0:
                    tt("dve", pt[:, c0:c0 + 128], pt[:, c0:c0 + 128], maskT[:], ALU.mult, [ptk, "maskT"], [ptk])
                mmg([dict(out=op_t[0:65, c0:TT], lhsT=Vh[hs][:, kb, :], rhs=pt[:, c0:TT], start=first, stop=last)],
                    ["Vh%d" % hs, ptk, opk], [opk])
                if last:
                    cp("dve", Osb[:], op_t[0:65, :], [opk], ["Osb"])
                    pend.append((i + 3, h_, qt))

            def emit_norm(h_, qt):
                mmg([dict(out=dps[0:64, :], lhsT=SEL[:, :], rhs=Osb[:, :], start=True, stop=True)], ["SEL", "Osb", "dps"], ["dps"])
                P.c("dve", lambda h: h.reciprocal(out=Rd[:], in_=dps[0:64, :]), ["dps"], ["Rd"])
                ys, ysk = Yst[qt % 2], "Yst%d" % (qt % 2)
                tt("dve", ys[:], Osb[0:64, :], Rd[:], ALU.mult, ["Osb", "Rd"], [ysk])
                ld(YFS[h_, :, qt * TT:(qt + 1) * TT], ys[:], [ysk], [U()], eng="sp")

            head_loads(0)
            head_loads(1)
            NI = len(its)
            for i in range(min(LA, NI)):
                emit_qk(i)
            for i in range(NI):
                if i + LA < NI:
                    emit_qk(i + LA)
                emit_rest(i)
                while pend and pend[0][0] <= i:
                    _, ph, pq = pend.pop(0)
                    emit_norm(ph, pq)
                if its[i][4] and its[i][1] == 7 and its[i][0] + 2 < 12:
                    head_loads(its[i][0] + 2)
            while pend:
                _, ph, pq = pend.pop(0)
                emit_norm(ph, pq)
            P.emit()

        if upto < 4:
            return nc
        S3 = ExitStack()
        with S3:
            gfin = sb(S3, "gfin", [128, D], F32)
            ld(gfin[:], g_final.to_broadcast([128, D]), [], ["gfin"])
            wpf = sb(S3, "wpf", [128, 6, D], BF16)
            wo = sb(S3, "wo", [128, 8, D], BF16)
            yf = sb(S3, "yf", [128, 6, TT], BF16)
            sgf = sb(S3, "sgf", [128, 6, TT], BF16)
            yfg = sb(S3, "yfg", [128, 6, TT], BF16)
            g0t = sb(S3, "g0t", [128, 8, TT], BF16)
            m0t = sb(S3, "m0t", [128, 8, TT], F32)
            mg = sb(S3, "mg", [128, 8, TT], BF16)
            t3 = sb(S3, "t3", [128, TT], F32)
            x3 = sb(S3, "x3", [128, 4, D], F32)
            o3 = [sb(S3, "o3_%d" % i, [128, D], F32) for i in range(2)]
            j3 = sb(S3, "j3", [128, D], BF16)
            pf = [ps(S3, "pf%d" % i, [128, 512], F32) for i in range(2)]
            po = [ps(S3, "po%d" % i, [128, 512], F32) for i in range(4)]
            ld(wpf[:], WPF, [], ["wpf"])
            ld(wo[:], WOUT, [], ["wo"])
            for ot in range(8):
                for a2 in range(2):
                    ld(yf[a2 * 64:(a2 + 1) * 64, :, :], YFS[:, :, ot * TT:(ot + 1) * TT].rearrange("(c a) d t -> a d c t", a=2)[a2], ["yf"], ["yf"])
                ld(sgf[:], SGFS[ot], [], ["sgf"])
                ld(g0t[:], G0S[ot], [], ["g0t"])
                ld(m0t[:], M0S[ot], [], ["m0t"])
                ld(x3[:], xa[(OWN0 + ot) * TT:(OWN0 + ot + 1) * TT, :].rearrange("(a p) d -> p a d", p=128), [], ["x3"])
                for c in range(6):
                    tt("pool" if c % 2 else "dve", yfg[:, c, :], yf[:, c, :], sgf[:, c, :], ALU.mult, ["yf", "sgf"], ["yfg"])
                for cb in range(8):
                    pt_, pk = pf[cb % 2], "pf%d" % (cb % 2)
                    mmg([dict(out=pt_[:, :], lhsT=wpf[:, kt, cb * 128:(cb + 1) * 128], rhs=yfg[:, kt, :], start=(kt == 0), stop=(kt == 5)) for kt in range(6)],
                        ["wpf", "yfg", pk], [pk])
                    tt("dve", t3[:], g0t[:, cb, :], pt_[:, :], ALU.mult, ["g0t", pk], ["t3"])
                    tt("dve", mg[:, cb, :], t3[:], m0t[:, cb, :], ALU.add, ["t3", "m0t"], ["mg"])
                for a in range(4):
                    ob, obk = o3[a % 2], "o3_%d" % (a % 2)
                    for hf in range(2):
                        pt_, pk = po[(2 * a + hf) % 4], "po%d" % ((2 * a + hf) % 4)
                        mmg([dict(out=pt_[:, :], lhsT=mg[:, kt, a * 128:(a + 1) * 128], rhs=wo[:, kt, hf * 512:(hf + 1) * 512], start=(kt == 0), stop=(kt == 7)) for kt in range(8)],
                            ["mg", "wo", pk], [pk])
                        tt("dve", ob[:, hf * 512:(hf + 1) * 512], pt_[:, :], x3[:, a, hf * 512:(hf + 1) * 512], ALU.add, [pk, "x3"], [obk])
                    act(j3[:], ob[:], AF.Square, [obk], ["j3", "ssq"], accum=ssq[:, a:a + 1])
                    ts("dve", rstd[:, a:a + 1], ssq[:, a:a + 1], 1.0 / D, EPS, ALU.mult, ALU.add, ["ssq"], ["rstd"])
                    act(rstd[:, a:a + 1], rstd[:, a:a + 1], AF.Sqrt, ["rstd"], ["rstd"])
                    P.c("dve", lambda h, a=a: h.reciprocal(out=rstd[:, a:a + 1], in_=rstd[:, a:a + 1]), ["rstd"], ["rstd"])
                    stt(ob[:], ob[:], rstd[:, a:a + 1], gfin[:], ALU.mult, ALU.mult, [obk, "rstd", "gfin"], [obk])
                    ld(yout[ot * TT + a * 128:ot * TT + (a + 1) * 128, :], ob[:], [obk], [U()], eng="act")
            P.emit()
    return nc


_NC = None


def kernel(**inputs):
    global _NC
    if _NC is None:
        _NC = build_nc()
    nc = _NC
    x = np.ascontiguousarray(np.asarray(inputs["x"], dtype=np.float32))
    mem = np.asarray(inputs["mem"], dtype=np.float32)
    in_maps = []
    for c in range(8):
        b, half = c // 2, c % 2
        if half == 0:
            xa = np.concatenate([np.zeros((4096, D), np.float32), x[b, 0:4096]], axis=0)
        else:
            xa = x[b]
        km = np.zeros((NT, TT), dtype=ml_dtypes.bfloat16)
        if half == 0:
            km[0:OWN0, :] = -30000.0
        m = {"xa": np.ascontiguousarray(xa), "kmrow": km, "mem": np.ascontiguousarray(mem[b])}
        for name in ("g_norm", "g_mem_norm", "w_in", "b_forget", "b_merge", "w_mem_kv", "lam_re", "lam_im", "log_step",
                     "s5_b_re", "s5_b_im", "s5_c_re", "s5_c_im", "s5_d", "w_glu", "b_glu", "w_proj_fox", "w_proj_s5",
                     "w_proj_mem", "w_out"):
            a = np.asarray(inputs[name], dtype=np.float32)
            a = a[0]
            if a.ndim == 1:
                a = a[None, :]
            m[name] = np.ascontiguousarray(a)
        m["g_final"] = np.ascontiguousarray(np.asarray(inputs["g_final"], dtype=np.float32)[None, :])
        in_maps.append(m)
    res = run_bass_kernel_spmd(nc, in_maps, core_ids=list(range(8)))
    out = np.zeros((4, 8192, D), dtype=np.float32)
    for c in range(8):
        b, half = c // 2, c % 2
        out[b, half * 4096:(half + 1) * 4096] = np.asarray(res.results[c]["yout"], dtype=np.float32)
    return out
```

```python
import math
from contextlib import ExitStack

import ml_dtypes
import numpy as np
import concourse.bass as bass
import concourse.mybir as mybir
from concourse.bass_utils import run_bass_kernel_spmd

F32 = mybir.dt.float32
BF16 = mybir.dt.bfloat16
I32 = mybir.dt.int32
AF = mybir.ActivationFunctionType
ALU = mybir.AluOpType

ENGS = ("pe", "act", "dve", "pool", "sp")
import os as _os0
NOSELF = tuple(x for x in _os0.environ.get('NOSELF', '').split(',') if x)
D = 1024
NIN = 8716
TT = 512
NT = 16
LTOK = 8192
OWN0 = 8
C_Q, C_K, C_V, C_FL, C_GF, C_U, C_GS, C_QM, C_GM, C_GL = 0, 768, 1536, 2304, 2316, 3084, 3852, 4620, 5132, 5644
EPS = 1e-6
TWO_PI = 2.0 * math.pi


class Op:
    __slots__ = ("eng", "fn", "deps", "needs_inc", "sigval", "is_dma", "lane", "laneval")

    def __init__(self, eng, fn, is_dma=False):
        self.eng = eng
        self.fn = fn
        self.deps = []
        self.needs_inc = False
        self.sigval = None
        self.is_dma = is_dma
        self.lane = None
        self.laneval = None


class Prog:
    def __init__(self, nc, es, n_lanes=6):
        self.nc = nc
        self.n_lanes = n_lanes
        self.esem = {e: es.enter_context(nc.semaphore("s_" + e)) for e in ENGS}
        self.lsem = {}
        for e in ("sp", "act", "pool"):
            for i in range(n_lanes):
                self.lsem[(e, i)] = es.enter_context(nc.semaphore("l_%s%d" % (e, i)))
        self.ecnt = {e: 0 for e in ENGS}
        self.lane_rr = {e: 0 for e in ENGS}
        self.lane_cnt = {k: 0 for k in self.lsem}
        self.barrier = {}
        self._reset()

    def _reset(self):
        self.ops = {e: [] for e in ENGS}
        self.last_w = {}
        self.readers = {}
        self.lane_last = {}

    def _add(self, op, reads, writes):
        deps = []
        for r in reads:
            w = self.last_w.get(r)
            if w is not None:
                deps.append(w)
        for r in writes:
            w = self.last_w.get(r)
            if w is not None:
                deps.append(w)
            deps.extend(self.readers.get(r, ()))
        seen = set(id(d) for d in op.deps)
        for d in deps:
            if d is op or id(d) in seen:
                continue
            if op.eng == "pe" and d.eng == "pe" and not d.is_dma and not op.is_dma:
                continue
            if (not d.is_dma) and (not op.is_dma) and op.eng == d.eng and op.eng in NOSELF:
                continue
            seen.add(id(d))
            op.deps.append(d)
            if not d.is_dma:
                d.needs_inc = True
        for r in reads:
            self.readers.setdefault(r, []).append(op)
        for r in writes:
            self.last_w[r] = op
            self.readers[r] = []
        self.ops[op.eng].append(op)
        return op

    def c(self, eng, fn, reads=(), writes=()):
        return self._add(Op(eng, fn), reads, writes)

    def dma(self, eng, fn, reads=(), writes=()):
        op = Op(eng, fn, is_dma=True)
        lane = (eng, self.lane_rr[eng] % self.n_lanes)
        self.lane_rr[eng] += 1
        prev = self.lane_last.get(lane)
        self.lane_cnt[lane] += 1
        op.lane = lane
        op.laneval = 16 * self.lane_cnt[lane]
        if prev is not None:
            op.deps.append(prev)
        self.lane_last[lane] = op
        return self._add(op, reads, writes)

    def emit(self):
        nc = self.nc
        for e in ENGS:
            last = None
            for op in self.ops[e]:
                if not op.is_dma:
                    last = op
            if last is not None:
                last.needs_inc = True
            for op in self.ops[e]:
                if (not op.is_dma) and op.needs_inc:
                    self.ecnt[e] += 1
                    op.sigval = self.ecnt[e]
        barrier = dict(self.barrier)
        esem, lsem = self.esem, self.lsem

        def tok(d):
            if d.is_dma:
                return lsem[d.lane], d.laneval
            return esem[d.eng], d.sigval

        def run(e, h):
            waited = {}
            for s, v in barrier.values():
                if v > 0:
                    h.wait_ge(s, v)
                    waited[id(s)] = v
            for op in self.ops[e]:
                for d in op.deps:
                    s, v = tok(d)
                    if waited.get(id(s), 0) >= v:
                        continue
                    waited[id(s)] = v
                    h.wait_ge(s, v)
                inst = op.fn(h)
                if op.is_dma:
                    inst.then_inc(lsem[op.lane], 16)
                elif op.needs_inc:
                    inst.then_inc(esem[e], 1)
                import os as _os
                if _os.environ.get('DBGPRINT'):
                    print("OP", e, "dma" if op.is_dma else "c", "lane=%s val=%s" % (op.lane, op.laneval) if op.is_dma else "sig=%s" % op.sigval,
                          "deps=", [((d.lane, d.laneval) if d.is_dma else (d.eng, d.sigval)) for d in op.deps], type(inst).__name__)
            for lane, cnt in self.lane_cnt.items():
                if lane[0] == e and cnt > 0:
                    h.wait_ge(lsem[lane], 16 * cnt)

        with nc.Block() as block:
            @block.tensor
            def _(h):
                run("pe", h)

            @block.scalar
            def _(h):
                run("act", h)

            @block.vector
            def _(h):
                run("dve", h)

            @block.gpsimd
            def _(h):
                run("pool", h)

            @block.sync
            def _(h):
                run("sp", h)

        self.barrier = {}
        for e in ENGS:
            self.barrier[("e", e)] = (esem[e], self.ecnt[e])
        for lane, cnt in self.lane_cnt.items():
            self.barrier[("l", lane)] = (lsem[lane], 16 * cnt)
        self._reset()


def build_nc(dbg=False, upto=9):
    nc = bass.Bass("TRN2", target_bir_lowering=False)

    def din(name, shape, dt=F32):
        return nc.dram_tensor(name, list(shape), dt, kind="ExternalInput").ap()

    def dscr(name, shape, dt):
        return nc.dram_tensor(name, list(shape), dt, kind=("ExternalOutput" if dbg else "Internal")).ap()

    xa = din("xa", [LTOK, D])
    kmrow = din("kmrow", [NT, TT], BF16)
    mem = din("mem", [256, D])
    g_norm = din("g_norm", [1, D])
    g_mem_norm = din("g_mem_norm", [1, D])
    g_final = din("g_final", [1, D])
    w_in = din("w_in", [D, NIN])
    b_forget = din("b_forget", [1, 12])
    b_merge = din("b_merge", [1, 3072])
    w_mem_kv = din("w_mem_kv", [D, 1024])
    lam_re = din("lam_re", [48, 64])
    lam_im = din("lam_im", [48, 64])
    log_step = din("log_step", [1, 48])
    s5_b_re = din("s5_b_re", [48, 64, 16])
    s5_b_im = din("s5_b_im", [48, 64, 16])
    s5_c_re = din("s5_c_re", [48, 16, 64])
    s5_c_im = din("s5_c_im", [48, 16, 64])
    s5_d = din("s5_d", [1, 768])
    w_glu = din("w_glu", [768, 768])
    b_glu = din("b_glu", [1, 768])
    w_proj_fox = din("w_proj_fox", [768, D])
    w_proj_s5 = din("w_proj_s5", [768, D])
    w_proj_mem = din("w_proj_mem", [512, D])
    w_out = din("w_out", [D, D])
    yout = nc.dram_tensor("yout", [4096, D], F32, kind="ExternalOutput").ap()

    WB = dscr("WB", [128, 8, NIN], BF16)
    WKV = dscr("WKV", [128, 8, 1024], BF16)
    WGLU = dscr("WGLU", [128, 6, 768], BF16)
    WPF = dscr("WPF", [128, 6, D], BF16)
    WPS = dscr("WPS", [128, 6, D], BF16)
    WPM = dscr("WPM", [128, 4, D], BF16)
    WOUT = dscr("WOUT", [128, 8, D], BF16)
    BDS = dscr("BDS", [6, 128, 8, 128], BF16)
    WZS = dscr("WZS", [6, 128, 8, 2, 128], BF16)
    WYS = dscr("WYS", [6, 128, 4, 8, 2, 32], BF16)
    KT = dscr("KT", [12, 128, LTOK], BF16)
    QT = dscr("QT", [12, 128, 4096], BF16)
    VS = dscr("VS", [64, 128, 780], BF16)
    M0S = dscr("M0S", [8, 128, 8, TT], F32)
    G0S = dscr("G0S", [8, 128, 8, TT], BF16)
    SGFS = dscr("SGFS", [8, 128, 6, TT], BF16)
    YFS = dscr("YFS", [12, 64, 4096], BF16)

    es = ExitStack()
    with es:
        P = Prog(nc, es)

        def sb(stack, name, shape, dt):
            return stack.enter_context(nc.sbuf_tensor(name, list(shape), dt))

        def ps(stack, name, shape, dt=F32):
            return stack.enter_context(nc.psum_tensor(name, list(shape), dt))

        def act(out, in_, func, r, w, bias=None, scale=None, accum=None):
            kw = {}
            if bias is not None:
                kw["bias"] = bias
            if scale is not None:
                kw["scale"] = scale
            if accum is not None:
                kw["accum_out"] = accum
            return P.c("act", lambda h: h.activation(out=out, in_=in_, func=func, **kw), r, w)

        def tt(eng, out, in0, in1, op, r, w):
            return P.c(eng, lambda h: h.tensor_tensor(out=out, in0=in0, in1=in1, op=op), r, w)

        def ts(eng, out, in0, s1, s2, op0, op1, r, w):
            if op1 is None:
                return P.c(eng, lambda h: h.tensor_scalar(out=out, in0=in0, scalar1=s1, scalar2=None, op0=op0), r, w)
            return P.c(eng, lambda h: h.tensor_scalar(out=out, in0=in0, scalar1=s1, scalar2=s2, op0=op0, op1=op1), r, w)

        def stt(out, in0, scalar, in1, op0, op1, r, w):
            return P.c("dve", lambda h: h.scalar_tensor_tensor(out=out, in0=in0, scalar=scalar, in1=in1, op0=op0, op1=op1), r, w)

        def cp(eng, out, in_, r, w):
            if eng == "act":
                return P.c("act", lambda h: h.activation(out=out, in_=in_, func=AF.Identity), r, w)
            return P.c(eng, lambda h: h.tensor_copy(out=out, in_=in_), r, w)

        def mset(eng, ap, val, w):
            return P.c(eng, lambda h: h.memset(ap, val), (), w)

        def mmg(calls, r, w):
            def fn(h):
                inst = None
                for kw in calls:
                    inst = h.matmul(**kw)
                return inst
            return P.c("pe", fn, r, w)

        def tpg(calls, r, w):
            def fn(h):
                inst = None
                for (o, i, idn) in calls:
                    inst = h.transpose(o, i, idn)
                return inst
            return P.c("pe", fn, r, w)

        uq = [0]

        def U():
            uq[0] += 1
            return "u%d" % uq[0]

        def ld(out, in_, r, w, eng="sp", slow=False):
            if slow:
                return P.dma(eng, lambda h: h.dma_start(out=out, in_=in_, allow_slow_non_contiguous=True), r, w)
            return P.dma(eng, lambda h: h.dma_start(out=out, in_=in_), r, w)

        def rows_pattern(tile_ap, n, lo, hi, w):
            P.c("pool", lambda h: h.memset(tile_ap, 1.0), (), w)
            P.c("pool", lambda h: h.affine_select(out=tile_ap, in_=tile_ap, pattern=[[0, n]], compare_op=ALU.is_ge,
                                                  fill=0.0, base=-lo, channel_multiplier=1), w, w)
            P.c("pool", lambda h: h.affine_select(out=tile_ap, in_=tile_ap, pattern=[[0, n]], compare_op=ALU.is_ge,
                                                  fill=0.0, base=hi, channel_multiplier=-1), w, w)

        G = ExitStack()
        es.enter_context(G)
        ident_f = sb(G, "ident_f", [128, 128], F32)
        ident_b = sb(G, "ident_b", [128, 128], BF16)
        ones_f = sb(G, "ones_f", [128, 512], F32)
        ones_b = sb(G, "ones_b", [128, 128], BF16)
        maskT = sb(G, "maskT", [128, 128], BF16)
        gn = sb(G, "gn", [128, 8], F32)
        gmn = sb(G, "gmn", [128, 8], F32)
        bm = sb(G, "bm", [128, 24], F32)
        bglu = sb(G, "bglu", [128, 6], F32)
        SELK = sb(G, "SELK", [128, 12, 128], BF16)
        SELQ = sb(G, "SELQ", [128, 12, 128], BF16)
        qscale = sb(G, "qscale", [128, 1], F32)
        negb = sb(G, "negb", [128, 1], F32)
        nm1 = sb(G, "nm1", [128, 1], F32)
        nm2 = sb(G, "nm2", [128, 1], F32)
        WFL3 = sb(G, "WFL3", [128, 8, 76], BF16)
        MKT = sb(G, "MKT", [128, 4, 256], BF16)
        MV = sb(G, "MV", [128, 2, 512], BF16)
        A1R = sb(G, "A1R", [128, 24], F32)
        A1I = sb(G, "A1I", [128, 24], F32)
        A8R = sb(G, "A8R", [128, 24], F32)
        A8I = sb(G, "A8I", [128, 24], F32)
        APR = sb(G, "APR", [128, 8, 24], F32)
        API = sb(G, "API", [128, 8, 24], F32)
        Fcar = sb(G, "Fcar", [128, 1], F32)
        TB = sb(G, "TB", [128, 24, 2, 9], F32)
        ssq = sb(G, "ssq", [128, 8], F32)
        rstd = sb(G, "rstd", [128, 8], F32)

        mset("pool", ones_f[:], 1.0, ["ones_f"])
        mset("pool", ident_f[:], 1.0, ["ident_f"])
        P.c("pool", lambda h: h.affine_select(out=ident_f[:], in_=ident_f[:], pattern=[[-1, 128]], compare_op=ALU.is_equal,
                                              fill=0.0, base=0, channel_multiplier=1), ["ident_f"], ["ident_f"])
        cp("dve", ident_b[:], ident_f[:], ["ident_f"], ["ident_b"])
        cp("dve", ones_b[:], ones_f[:, 0:128], ["ones_f"], ["ones_b"])
        mset("pool", maskT[:], 1.0, ["maskT"])
        P.c("pool", lambda h: h.affine_select(out=maskT[:], in_=maskT[:], pattern=[[1, 128]], compare_op=ALU.is_ge,
                                              fill=0.0, base=0, channel_multiplier=-1), ["maskT"], ["maskT"])
        ld(gn[:], g_norm.rearrange("o (kt p) -> p (o kt)", p=128), [], ["gn"], slow=True)
        ld(gmn[:], g_mem_norm.rearrange("o (kt p) -> p (o kt)", p=128), [], ["gmn"], slow=True)
        ld(bm[:], b_merge.rearrange("o (c p) -> p (o c)", p=128), [], ["bm"], slow=True)
        ld(bglu[:], b_glu.rearrange("o (c p) -> p (o c)", p=128), [], ["bglu"], slow=True)
        mset("pool", qscale[:], 1.0, ["qscale"])
        mset("pool", qscale[0:64, :], 0.125, ["qscale"])
        mset("pool", negb[:], 0.0, ["negb"])
        for base in (0, 32, 64):
            ld(negb[base:base + 12, :], b_forget.rearrange("o h -> h o"), [], ["negb"], slow=True)
        ts("dve", negb[:], negb[:], -1.0, None, ALU.mult, None, ["negb"], ["negb"])
        mset("pool", nm1[:], 0.0, ["nm1"])
        mset("pool", nm1[32:64, :], -1.0, ["nm1"])
        mset("pool", nm1[64:96, :], -1.0, ["nm1"])
        mset("pool", nm2[:], 0.0, ["nm2"])
        mset("pool", nm2[64:96, :], -1.0, ["nm2"])
        mset("pool", Fcar[:], 0.0, ["Fcar"])
        mset("pool", TB[:], 0.0, ["TB"])
        mset("pool", SELK[:], 0.0, ["SELK"])
        mset("pool", SELQ[:], 0.0, ["SELQ"])
        for j, base in enumerate((0, 32, 64)):
            ts("dve", SELK[:, :, 96 + j], ident_f[:, base:base + 12], -1.0, None, ALU.mult, None, ["ident_f", "SELK"], ["SELK"])
            cp("dve", SELQ[:, :, 64 + j], ident_f[:, base:base + 12], ["ident_f", "SELQ"], ["SELQ"])
        for h_ in range(12):
            cp("dve", SELK[:, h_, 99:100], ident_f[:, 76:77], ["ident_f", "SELK"], ["SELK"])
            for c_ in (64, 65, 66):
                cp("dve", SELK[:, h_, c_:c_ + 1], ident_f[:, 96:97], ["ident_f", "SELK"], ["SELK"])
            for c_ in (96, 97, 98, 99):
                cp("dve", SELQ[:, h_, c_:c_ + 1], ident_f[:, 96:97], ["ident_f", "SELQ"], ["SELQ"])

        S0 = ExitStack()
        with S0:
            cv = [sb(S0, "cv%d" % i, [128, 8, 512], BF16) for i in range(2)]
            cvn = [0]

            def convert(src, nkt, ncols, dst, wkey=None):
                srcv = src.rearrange("(kt p) c -> p kt c", p=128)
                c0 = 0
                while c0 < ncols:
                    n = min(512, ncols - c0)
                    i = cvn[0] % 2
                    cvn[0] += 1
                    t = cv[i]
                    ld(t[:, 0:nkt, 0:n], srcv[:, :, c0:c0 + n], [], ["cv%d" % i], eng="pool")
                    ld(dst[:, :, c0:c0 + n], t[:, 0:nkt, 0:n], ["cv%d" % i], [wkey] if wkey else [U()], eng="sp")
                    c0 += n

            convert(w_in, 8, NIN, WB)
            convert(w_mem_kv, 8, 1024, WKV, "WKV")
            convert(w_glu, 6, 768, WGLU)
            convert(w_proj_fox, 6, D, WPF)
            convert(w_proj_s5, 6, D, WPS)
            convert(w_proj_mem, 4, D, WPM)
            convert(w_out, 8, D, WOUT)
            mset("pool", WFL3[:], 0.0, ["WFL3"])
            for base in (0, 32, 64):
                ld(WFL3[:, :, base:base + 12], w_in.rearrange("(kt p) c -> p kt c", p=128)[:, :, C_FL:C_FL + 12],
                   ["WFL3"], ["WFL3"], eng="pool")

            def s5t(name, shape, dt=F32):
                return sb(S0, name, shape, dt)
            LR = s5t("LR", [128, 24]); LI = s5t("LI", [128, 24]); LS = s5t("LS", [128, 24])
            ld(LR[:], lam_re.rearrange("(pr g2) p -> (g2 p) pr", g2=2), [], ["LR"], slow=True)
            ld(LI[:], lam_im.rearrange("(pr g2) p -> (g2 p) pr", g2=2), [], ["LI"], slow=True)
            for g2 in range(2):
                ld(LS[g2 * 64:(g2 + 1) * 64, :],
                   log_step.rearrange("o (pr g2) -> o g2 pr", g2=2)[:, g2, :].to_broadcast([64, 24]), [], ["LS"], slow=True)
            STP = s5t("STP", [128, 24]); MAG = s5t("MAG", [128, 24]); ANG = s5t("ANG", [128, 24])
            T0 = s5t("T0", [128, 24]); T1 = s5t("T1", [128, 24]); T2 = s5t("T2", [128, 24]); TI = s5t("TI", [128, 24], I32)
            ABR = s5t("ABR", [128, 24]); ABI = s5t("ABI", [128, 24]); SN = s5t("SN", [128, 24]); CS = s5t("CS", [128, 24])
            FR = s5t("FR", [128, 24]); FI = s5t("FI", [128, 24])
            act(STP[:], LS[:], AF.Exp, ["LS"], ["STP"])
            tt("dve", T0[:], LR[:], STP[:], ALU.mult, ["LR", "STP"], ["T0"])
            act(MAG[:], T0[:], AF.Exp, ["T0"], ["MAG"])
            tt("dve", ANG[:], LI[:], STP[:], ALU.mult, ["LI", "STP"], ["ANG"])

            def sin_of(dst, shift, key):
                ts("dve", T0[:], ANG[:], shift, None, ALU.add, None, ["ANG"], ["T0"])
                ts("dve", T1[:], T0[:], 1.0 / TWO_PI, 0.5, ALU.mult, ALU.add, ["T0"], ["T1"])
                cp("dve", TI[:], T1[:], ["T1"], ["TI"])
                cp("dve", T1[:], TI[:], ["TI"], ["T1"])
                stt(T2[:], T1[:], -TWO_PI, T0[:], ALU.mult, ALU.add, ["T1", "T0"], ["T2"])
                ts("dve", T1[:], T2[:], math.pi, -TWO_PI, ALU.is_gt, ALU.mult, ["T2"], ["T1"])
                tt("dve", T2[:], T2[:], T1[:], ALU.add, ["T2", "T1"], ["T2"])
                ts("dve", T1[:], T2[:], -math.pi, TWO_PI, ALU.is_lt, ALU.mult, ["T2"], ["T1"])
                tt("dve", T2[:], T2[:], T1[:], ALU.add, ["T2", "T1"], ["T2"])
                ts("dve", T2[:], T2[:], math.pi, -math.pi, ALU.min, ALU.max, ["T2"], ["T2"])
                act(dst[:], T2[:], AF.Sin, ["T2"], [key])
            sin_of(SN, 0.0, "SN")
            sin_of(CS, math.pi / 2.0, "CS")
            tt("dve", ABR[:], MAG[:], CS[:], ALU.mult, ["MAG", "CS"], ["ABR"])
            tt("dve", ABI[:], MAG[:], SN[:], ALU.mult, ["MAG", "SN"], ["ABI"])
            DEN = s5t("DEN", [128, 24]); NR = s5t("NR", [128, 24])
            tt("dve", T0[:], LR[:], LR[:], ALU.mult, ["LR"], ["T0"])
            tt("dve", T1[:], LI[:], LI[:], ALU.mult, ["LI"], ["T1"])
            tt("dve", DEN[:], T0[:], T1[:], ALU.add, ["T0", "T1"], ["DEN"])
            P.c("dve", lambda h: h.reciprocal(out=DEN[:], in_=DEN[:]), ["DEN"], ["DEN"])
            ts("dve", NR[:], ABR[:], -1.0, None, ALU.add, None, ["ABR"], ["NR"])
            tt("dve", T0[:], NR[:], LR[:], ALU.mult, ["NR", "LR"], ["T0"])
            tt("dve", T1[:], ABI[:], LI[:], ALU.mult, ["ABI", "LI"], ["T1"])
            tt("dve", T0[:], T0[:], T1[:], ALU.add, ["T0", "T1"], ["T0"])
            tt("dve", FR[:], T0[:], DEN[:], ALU.mult, ["T0", "DEN"], ["FR"])
            tt("dve", T0[:], ABI[:], LR[:], ALU.mult, ["ABI", "LR"], ["T0"])
            tt("dve", T1[:], NR[:], LI[:], ALU.mult, ["NR", "LI"], ["T1"])
            tt("dve", T0[:], T0[:], T1[:], ALU.subtract, ["T0", "T1"], ["T0"])
            tt("dve", FI[:], T0[:], DEN[:], ALU.mult, ["T0", "DEN"], ["FI"])
            PWR = s5t("PWR", [128, 9, 24]); PWI = s5t("PWI", [128, 9, 24])
            AWR = s5t("AWR", [128, 9, 24]); AWI = s5t("AWI", [128, 9, 24])

            def cmul(orr, oi, ar, ai, br, bi, rk, wk):
                tt("dve", T0[:], ar, br, ALU.mult, rk, ["T0"])
                tt("dve", T1[:], ai, bi, ALU.mult, rk, ["T1"])
                tt("dve", T2[:], ar, bi, ALU.mult, rk, ["T2"])
                tt("dve", orr, T0[:], T1[:], ALU.subtract, ["T0", "T1"], wk)
                tt("dve", T0[:], ai, br, ALU.mult, rk + wk, ["T0"])
                tt("dve", oi, T2[:], T0[:], ALU.add, ["T2", "T0"], wk)
            mset("pool", PWR[:, 0, :], 1.0, ["PW"])
            mset("pool", PWI[:, 0, :], 0.0, ["PW"])
            for k in range(1, 9):
                cmul(PWR[:, k, :], PWI[:, k, :], PWR[:, k - 1, :], PWI[:, k - 1, :], ABR[:], ABI[:], ["PW", "ABR", "ABI"], ["PW"])
            mset("pool", AWR[:, 0, :], 1.0, ["AW"])
            mset("pool", AWI[:, 0, :], 0.0, ["AW"])
            for k in range(1, 9):
                cmul(AWR[:, k, :], AWI[:, k, :], AWR[:, k - 1, :], AWI[:, k - 1, :], PWR[:, 8, :], PWI[:, 8, :], ["AW", "PW"], ["AW"])
            cp("dve", A1R[:], AWR[:, 1, :], ["AW"], ["A1"])
            cp("dve", A1I[:], AWI[:, 1, :], ["AW"], ["A1"])
            cp("dve", APR[:], AWR[:, 1:9, :], ["AW"], ["APR"])
            cp("dve", API[:], AWI[:, 1:9, :], ["AW"], ["API"])
            cp("dve", A8R[:], AWR[:, 8, :], ["AW"], ["A8"])
            cp("dve", A8I[:], AWI[:, 8, :], ["AW"], ["A8"])

            BRE = s5t("BRE", [128, 24, 16]); BIM = s5t("BIM", [128, 24, 16])
            ld(BRE[:], s5_b_re.rearrange("(pr g2) p h -> (g2 p) pr h", g2=2), [], ["BRE"])
            ld(BIM[:], s5_b_im.rearrange("(pr g2) p h -> (g2 p) pr h", g2=2), [], ["BIM"])
            BBR = s5t("BBR", [128, 24, 16]); BBI = s5t("BBI", [128, 24, 16])
            U0 = s5t("U0", [128, 24, 16]); U1 = s5t("U1", [128, 24, 16])
            frb = FR[:].unsqueeze(2).to_broadcast([128, 24, 16])
            fib = FI[:].unsqueeze(2).to_broadcast([128, 24, 16])
            tt("dve", U0[:], BRE[:], frb, ALU.mult, ["BRE", "FR"], ["U0"])
            tt("dve", U1[:], BIM[:], fib, ALU.mult, ["BIM", "FI"], ["U1"])
            tt("dve", BBR[:], U0[:], U1[:], ALU.subtract, ["U0", "U1"], ["BBR"])
            tt("dve", U0[:], BIM[:], frb, ALU.mult, ["BIM", "FR"], ["U0"])
            tt("dve", U1[:], BRE[:], fib, ALU.mult, ["BRE", "FI"], ["U1"])
            tt("dve", BBI[:], U0[:], U1[:], ALU.add, ["U0", "U1"], ["BBI"])
            BDR = s5t("BDR", [128, 24, 32]); BDI = s5t("BDI", [128, 24, 32])
            mset("pool", BDR[:], 0.0, ["BDR"]); mset("pool", BDI[:], 0.0, ["BDI"])
            for g2 in range(2):
                sl = slice(g2 * 64, (g2 + 1) * 64)
                cp("dve", BDR[sl, :, g2 * 16:(g2 + 1) * 16], BBR[sl, :, :], ["BBR", "BDR"], ["BDR"])
                cp("dve", BDI[sl, :, g2 * 16:(g2 + 1) * 16], BBI[sl, :, :], ["BBI", "BDI"], ["BDI"])
            BWR = s5t("BWR", [128, 24, 128]); BWI = s5t("BWI", [128, 24, 128])
            mset("pool", BWR[:], 0.0, ["BWR"]); mset("pool", BWI[:], 0.0, ["BWI"])
            for q4 in range(4):
                for o in range(6):
                    pr = 4 * o + q4
                    cp("dve", BWR[:, pr, 32 * q4:32 * q4 + 32], BDR[:, pr, :], ["BDR", "BWR"], ["BWR"])
                    cp("pool", BWI[:, pr, 32 * q4:32 * q4 + 32], BDI[:, pr, :], ["BDI", "BWI"], ["BWI"])
            CTR = s5t("CTR", [128, 24, 16]); CTI = s5t("CTI", [128, 24, 16])
            CN = s5t("CN", [128, 3, 2, 128])
            tp_ps = ps(S0, "tp_ps", [128, 512], F32)
            for ri, (csrc, cdst, key) in enumerate(((s5_c_re, CTR, "CTR"), (s5_c_im, CTI, "CTI"))):
                cv4 = csrc.rearrange("(pr g2) h p -> pr h g2 p", g2=2)
                for pr in range(24):
                    ld(CN[(pr % 8) * 16:(pr % 8) * 16 + 16, pr // 8, ri, :].rearrange("h (a p) -> h a p", a=2),
                       cv4[pr], ["CN%d" % ri], ["CN%d" % ri])
                for j in range(3):
                    tpg([(tp_ps[:, 0:128], CN[:, j, ri, :], ident_f[:])], ["CN%d" % ri, "ident_f", "tp_ps"], ["tp_ps"])
                    cp("act", cdst[:, 8 * j:8 * j + 8, :], tp_ps[:, 0:128].rearrange("p (a h) -> p a h", a=8), ["tp_ps"], [key])
            XR = s5t("XR", [128, 24, 32]); XI = s5t("XI", [128, 24, 32])
            V0 = s5t("V0", [128, 24, 32]); V1 = s5t("V1", [128, 24, 32])
            WZs = [s5t("WZs%d" % i, [128, 6, 2, 128], BF16) for i in range(2)]
            for s in range(8):
                k = 7 - s
                wzs, wzk = WZs[s % 2], "WZs%d" % (s % 2)
                pr_ = PWR[:, k, :].unsqueeze(2).to_broadcast([128, 24, 32])
                pi_ = PWI[:, k, :].unsqueeze(2).to_broadcast([128, 24, 32])
                tt("dve", V0[:], BDR[:], pr_, ALU.mult, ["BDR", "PW"], ["V0"])
                tt("dve", V1[:], BDI[:], pi_, ALU.mult, ["BDI", "PW"], ["V1"])
                tt("dve", XR[:], V0[:], V1[:], ALU.subtract, ["V0", "V1"], ["XR"])
                tt("dve", V0[:], BDI[:], pr_, ALU.mult, ["BDI", "PW"], ["V0"])
                tt("dve", V1[:], BDR[:], pi_, ALU.mult, ["BDR", "PW"], ["V1"])
                tt("dve", XI[:], V0[:], V1[:], ALU.add, ["V0", "V1"], ["XI"])
                for o in range(6):
                    for ri, X in enumerate((XR, XI)):
                        tpg([(tp_ps[:, 0:128], X[:, 4 * o:4 * o + 4, :].rearrange("p a b -> p (a b)"), ident_f[:])], ["XR", "XI", "ident_f", "tp_ps"], ["tp_ps"])
                        cp("act", wzs[:, o, ri, :], tp_ps[:, 0:128], ["tp_ps"], [wzk])
                for o in range(6):
                    ld(WZS[o, :, s, :, :], wzs[:, o, :, :], [wzk], [U()])
            QR = s5t("QR", [128, 24, 16]); QI = s5t("QI", [128, 24, 16])
            QBR = s5t("QBR", [128, 24, 32]); QBI = s5t("QBI", [128, 24, 32])
            WYk = [s5t("WYk%d" % i, [128, 24, 2, 32], BF16) for i in range(2)]
            BDj = [s5t("BDj%d" % i, [128, 6, 128], F32) for i in range(2)]
            BDjb = [s5t("BDjb%d" % i, [128, 6, 128], BF16) for i in range(2)]
            DSK = s5t("DSK", [128, 6], F32)
            ld(DSK[:], s5_d.rearrange("o (c p) -> p (o c)", p=128), [], ["DSK"], slow=True)
            mset("pool", QBR[:], 0.0, ["QBR"]); mset("pool", QBI[:], 0.0, ["QBI"])
            for i in range(2):
                mset("pool", BDj[i][:], 0.0, ["BDj%d" % i])
            bd_ps = ps(S0, "bd_ps", [128, 6, 32], F32)
            for k in range(9):
                pr_ = PWR[:, k, :].unsqueeze(2).to_broadcast([128, 24, 16])
                pi_ = PWI[:, k, :].unsqueeze(2).to_broadcast([128, 24, 16])
                tt("dve", U0[:], CTR[:], pr_, ALU.mult, ["CTR", "PW"], ["U0"])
                tt("dve", U1[:], CTI[:], pi_, ALU.mult, ["CTI", "PW"], ["U1"])
                tt("dve", QR[:], U0[:], U1[:], ALU.subtract, ["U0", "U1"], ["QR"])
                tt("dve", U0[:], CTR[:], pi_, ALU.mult, ["CTR", "PW"], ["U0"])
                tt("dve", U1[:], CTI[:], pr_, ALU.mult, ["CTI", "PW"], ["U1"])
                tt("dve", QI[:], U0[:], U1[:], ALU.add, ["U0", "U1"], ["QI"])
                for g2 in range(2):
                    sl = slice(g2 * 64, (g2 + 1) * 64)
                    cp("dve", QBR[sl, :, g2 * 16:(g2 + 1) * 16], QR[sl, :, :], ["QR", "QBR"], ["QBR"])
                    ts("dve", QBI[sl, :, g2 * 16:(g2 + 1) * 16], QI[sl, :, :], -1.0, None, ALU.mult, None, ["QI", "QBI"], ["QBI"])
                if k >= 1:
                    wyk, wykk = WYk[k % 2], "WYk%d" % (k % 2)
                    cp("act", wyk[:, :, 0, :], QBR[:], ["QBR"], [wykk])
                    cp("act", wyk[:, :, 1, :], QBI[:], ["QBI"], [wykk])
                    for o in range(6):
                        ld(WYS[o, :, :, k - 1, :, :], wyk[:, 4 * o:4 * o + 4, :, :], [wykk], [U()])
                if k <= 7:
                    bdj, bdk = BDj[k % 2], "BDj%d" % (k % 2)
                    bdjb, bdbk = BDjb[k % 2], "BDjb%d" % (k % 2)
                    for o in range(6):
                        calls = []
                        for q4 in range(4):
                            pr = 4 * o + q4
                            calls.append(dict(out=bd_ps[:, o, :], lhsT=BWR[:, pr, :], rhs=QBR[:, pr, :], start=(q4 == 0), stop=False))
                            calls.append(dict(out=bd_ps[:, o, :], lhsT=BWI[:, pr, :], rhs=QBI[:, pr, :], start=False, stop=(q4 == 3)))
                        mmg(calls, ["BWR", "BWI", "QBR", "QBI", "bd_ps"], ["bd_ps"])
                    for q4 in range(4):
                        sl = slice(32 * q4, 32 * q4 + 32)
                        cp("act", bdj[sl, :, 32 * q4:32 * q4 + 32], bd_ps[sl, :, :], ["bd_ps"], [bdk])
                    if k == 0:
                        for o in range(6):
                            stt(bdj[:, o, :], ident_f[:], DSK[:, o:o + 1], bdj[:, o, :], ALU.mult, ALU.add, ["ident_f", "DSK", bdk], [bdk])
                    cp("dve", bdjb[:], bdj[:], [bdk], [bdbk])
                    for o in range(6):
                        ld(BDS[o, :, k, :], bdjb[:, o, :], [bdbk], [U()])
            P.emit()
        if upto < 1:
            return nc
        S0b = ExitStack()
        with S0b:
            S0 = S0b

            def s5t(name, shape, dt=F32):
                return sb(S0b, name, shape, dt)
            mt_x = s5t("mt_x", [128, 2, D], F32)
            mt_n = s5t("mt_n", [128, 2, D], BF16)
            mt_j = s5t("mt_j", [128, D], BF16)
            MHT = s5t("MHT", [128, 8, 256], BF16)
            wkv_t = s5t("wkv_t", [128, 8, 1024], BF16)
            tpb_ps = ps(S0, "tpb_ps", [128, 512], BF16)
            mk_ps = ps(S0, "mk_ps", [128, 512], F32)
            ld(mt_x[:], mem.rearrange("(a p) d -> p a d", p=128), [], ["mt_x"])
            ld(wkv_t[:], WKV, ["WKV"], ["wkv_t"])
            for a in range(2):
                act(mt_j[:], mt_x[:, a, :], AF.Square, ["mt_x"], ["mt_j", "ssq"], accum=ssq[:, a:a + 1])
            ts("dve", rstd[:, 0:2], ssq[:, 0:2], 1.0 / D, EPS, ALU.mult, ALU.add, ["ssq"], ["rstd"])
            act(rstd[:, 0:2], rstd[:, 0:2], AF.Sqrt, ["rstd"], ["rstd"])
            P.c("dve", lambda h: h.reciprocal(out=rstd[:, 0:2], in_=rstd[:, 0:2]), ["rstd"], ["rstd"])
            for a in range(2):
                ts("dve", mt_n[:, a, :], mt_x[:, a, :], rstd[:, a:a + 1], None, ALU.mult, None, ["mt_x", "rstd"], ["mt_n"])
            for kt in range(8):
                tpg([(tpb_ps[:, a * 128:(a + 1) * 128], mt_n[:, a, kt * 128:(kt + 1) * 128], ident_b[:]) for a in range(2)],
                    ["mt_n", "ident_b", "tpb_ps"], ["tpb_ps"])
                ts("dve", MHT[:, kt, :], tpb_ps[:, 0:256], gmn[:, kt:kt + 1], None, ALU.mult, None, ["tpb_ps", "gmn"], ["MHT"])
            for hd in range(4):
                mmg([dict(out=mk_ps[:, 0:256], lhsT=wkv_t[:, kt, hd * 128:(hd + 1) * 128], rhs=MHT[:, kt, :],
                          start=(kt == 0), stop=(kt == 7)) for kt in range(8)], ["wkv_t", "MHT", "mk_ps"], ["mk_ps"])
                cp("act", MKT[:, hd, :], mk_ps[:, 0:256], ["mk_ps"], ["MKT"])
            for mt in range(2):
                mmg([dict(out=mk_ps[:, :], lhsT=MHT[:, kt, mt * 128:(mt + 1) * 128], rhs=wkv_t[:, kt, 512:1024],
                          start=(kt == 0), stop=(kt == 7)) for kt in range(8)], ["wkv_t", "MHT", "mk_ps"], ["mk_ps"])
                cp("act", MV[:, mt, :], mk_ps[:, :], ["mk_ps"], ["MV"])
            P.emit()

        if upto < 2:
            return nc
        S1 = ExitStack()
        with S1:
            xt = sb(S1, "xt", [128, 4, D], F32)
            xn = sb(S1, "xn", [128, 4, D], BF16)
            sqj = sb(S1, "sqj", [128, D], BF16)
            hTs = [sb(S1, "hT%d" % i, [128, 8, TT], BF16) for i in range(2)]
            NWS = 3
            wsl = [sb(S1, "ws%d" % i, [128, 4096], BF16) for i in range(NWS)]
            SPt = [sb(S1, "SPt%d" % i, [128, TT], BF16) for i in range(2)]
            HIb = sb(S1, "HIb", [128, TT], BF16)
            MIDb = sb(S1, "MIDb", [128, TT], BF16)
            kst = [sb(S1, "kst%d" % i, [128, TT], BF16) for i in range(2)]
            Vst = sb(S1, "Vst", [128, 4, 12, 65], BF16)
            uT = sb(S1, "uT", [128, 6, TT], BF16)
            s5w = [(sb(S1, "s5bd%d" % i, [128, 8, 128], BF16), sb(S1, "s5wz%d" % i, [128, 8, 2, 128], BF16),
                    sb(S1, "s5wy%d" % i, [128, 4, 8, 2, 32], BF16)) for i in range(2)]
            Z = sb(S1, "Z", [128, 24, 2, 64], F32)
            Hb = sb(S1, "Hb", [128, 24, 2, 65], BF16)
            RT = [sb(S1, "RT%d" % i, [128, 24, 8], F32) for i in range(8)]
            RS = [sb(S1, "RS%d" % i, [128, 24], F32) for i in range(8)]
            tA = sb(S1, "tA", [128, TT], F32)
            tB = sb(S1, "tB", [128, TT], F32)
            tC = sb(S1, "tC", [128, TT], F32)
            Fe = sb(S1, "Fe", [76, TT], F32)
            Ff = sb(S1, "Ff", [76, TT], F32)
            R1 = sb(S1, "R1", [76, TT], F32)
            YG = sb(S1, "YG", [128, 6, TT], BF16)
            SGS = sb(S1, "SGS", [128, 6, TT], BF16)
            YS = SGS
            QM = sb(S1, "QM", [128, 4, TT], BF16)
            SGM = sb(S1, "SGM", [128, 4, TT], BF16)
            PmT = sb(S1, "PmT", [128, 2, TT], BF16)
            YM = SGM
            M0 = [sb(S1, "M0_%d" % i, [128, TT], F32) for i in range(2)]
            G0st = [sb(S1, "G0st%d" % i, [128, TT], BF16) for i in range(2)]
            pj = [ps(S1, "pj%d" % i, [128, 512], F32) for i in range(2)]
            tpall = ps(S1, "tpall", [128, 1024], BF16)
            tpp = [tpall[:, 0:512], tpall[:, 0:512]]
            zpsb = [ps(S1, "zps%d" % i, [128, 512], F32) for i in range(4)]
            yps = ps(S1, "yps", [128, 512], F32)

            import os
            if not os.environ.get('SKIPM'):
                for i in range(2):
                    mset("pool", SPt[i][:], 0.0, ["SPt%d" % i])
                    mset("pool", SPt[i][96:97, :], 1.0, ["SPt%d" % i])
                mset("pool", Vst[:], 1.0, ["Vst"])

            wn = [0]
            pjn = [0]

            def wload(src3, nkt, ncols):
                i = wn[0] % NWS
                wn[0] += 1
                t = wsl[i][:, 0:nkt * ncols].rearrange("p (k c) -> p k c", k=nkt)
                if os.environ.get('WLDTINY'):
                    ld(t[:, 0:1, 0:16], src3[:, 0:1, 0:16], [], ["ws%d" % i])
                else:
                    ld(t, src3, [], ["ws%d" % i])
                return t, "ws%d" % i

            PJL = [(pj[0], "pj0"), (pj[1], "pj1"), (zpsb[0], "zps0"), (zpsb[1], "zps1"), (zpsb[2], "zps2"), (zpsb[3], "zps3"), (yps, "yps")]

            def nextpj():
                i = pjn[0] % len(PJL)
                pjn[0] += 1
                return PJL[i]

            import os
            P1T = int(os.environ.get('P1T', NT)); P1S = int(os.environ.get('P1S', 99)); OWNX = int(os.environ.get('OWNX', OWN0))
            def prep(it):
                hTn, hkn = hTs[it % 2], "hT%d" % (it % 2)
                sptn, spkn = SPt[it % 2], "SPt%d" % (it % 2)
                if it == 0:
                    ld(xt[:], xa[it * TT:(it + 1) * TT, :].rearrange("(a p) d -> p a d", p=128), [], ["xt"])
                ld(sptn[76:77, :], kmrow[it:it + 1, :], [], [spkn], eng="pool")
                for a in range(4):
                    act(sqj[:], xt[:, a, :], AF.Square, ["xt"], ["sqj", "ssq"], accum=ssq[:, a:a + 1])
                ts("dve", rstd[:, 0:4], ssq[:, 0:4], 1.0 / D, EPS, ALU.mult, ALU.add, ["ssq"], ["rstd"])
                act(rstd[:, 0:4], rstd[:, 0:4], AF.Sqrt, ["rstd"], ["rstd"])
                P.c("dve", lambda h: h.reciprocal(out=rstd[:, 0:4], in_=rstd[:, 0:4]), ["rstd"], ["rstd"])
                for a in range(4):
                    ts("dve", xn[:, a, :], xt[:, a, :], rstd[:, a:a + 1], None, ALU.mult, None, ["xt", "rstd"], ["xn"])
                if it + 1 < P1T:
                    ld(xt[:], xa[(it + 1) * TT:(it + 2) * TT, :].rearrange("(a p) d -> p a d", p=128), [], ["xt"])
                for kt in range(8):
                    tp, tk = tpp[kt % 2], "tpp0"
                    tpg([(tp[:, a * 128:(a + 1) * 128], xn[:, a, kt * 128:(kt + 1) * 128], ident_b[:]) for a in range(4)],
                        ["xn", "ident_b", tk], [tk])
                    if kt % 2 == 0:
                        ts("dve", hTn[:, kt, :], tp[:, :], gn[:, kt:kt + 1], None, ALU.mult, None, [tk, "gn"], [hkn])
                    else:
                        act(hTn[:, kt, :], tp[:, :], AF.Identity, [tk, "gn"], [hkn], scale=gn[:, kt:kt + 1])
                pjt, pk = nextpj()
                mmg([dict(out=pjt[0:76, :], lhsT=WFL3[:, kt, :], rhs=hTn[:, kt, :], start=(kt == 0), stop=(kt == 7)) for kt in range(8)],
                    ["WFL3", hkn, pk], [pk])
                act(Fe[0:76, :], pjt[0:76, :], AF.Exp, [pk, "negb"], ["Fe"], bias=negb[0:76, :], scale=-1.0)
                act(Fe[0:76, :], Fe[0:76, :], AF.Ln, ["Fe"], ["Fe"], bias=1.0)
                P.c("dve", lambda h: h.tensor_tensor_scan(out=Ff[0:76, :], data0=ones_f[0:76, :], data1=Fe[0:76, :],
                                                          initial=Fcar[0:76, :], op0=ALU.mult, op1=ALU.subtract),
                    ["ones_f", "Fe", "Fcar"], ["Ff"])
                cp("dve", Fcar[0:76, :], Ff[0:76, TT - 1:TT], ["Ff"], ["Fcar"])
                cp("dve", HIb[0:76, :], Ff[0:76, :], ["Ff"], ["HIb"])
                stt(R1[0:76, :], HIb[0:76, :], nm1[0:76, :], Ff[0:76, :], ALU.mult, ALU.add, ["HIb", "nm1", "Ff"], ["R1"])
                tt("dve", MIDb[0:76, :], Ff[0:76, :], HIb[0:76, :], ALU.subtract, ["Ff", "HIb"], ["MIDb"])
                stt(R1[0:76, :], MIDb[0:76, :], nm2[0:76, :], R1[0:76, :], ALU.mult, ALU.add, ["MIDb", "nm2", "R1"], ["R1"])
                cp("dve", sptn[0:76, :], R1[0:76, :], ["R1"], [spkn])

            prep(0)
            for it in range(P1T):
                own = it >= OWNX
                ot = it - OWNX
                sp_i = it % 2
                spt, spk = SPt[sp_i], "SPt%d" % sp_i
                hT, hk = hTs[it % 2], "hT%d" % (it % 2)
                if P1S < 5:
                    continue
                wv1, wv1k = wload(WB[:, :, C_V:C_V + 512], 8, 512)
                wv2, wv2k = wload(WB[:, :, C_V + 512:C_V + 768], 8, 256)
                for a in range(4):
                    p1, p1k = nextpj()
                    p2, p2k = nextpj()
                    mmg([dict(out=p1[:, :], lhsT=hT[:, kt, a * 128:(a + 1) * 128], rhs=wv1[:, kt, :], start=(kt == 0), stop=(kt == 7)) for kt in range(8)],
                        [hk, wv1k, p1k], [p1k])
                    mmg([dict(out=p2[:, 0:256], lhsT=hT[:, kt, a * 128:(a + 1) * 128], rhs=wv2[:, kt, :], start=(kt == 0), stop=(kt == 7)) for kt in range(8)],
                        [hk, wv2k, p2k], [p2k])
                    cp("act", Vst[:, a, 0:8, 0:64], p1[:, :].rearrange("p (h d) -> p h d", h=8), [p1k], ["Vst"])
                    cp("dve", Vst[:, a, 8:12, 0:64], p2[:, 0:256].rearrange("p (h d) -> p h d", h=4), [p2k], ["Vst"])
                ld(VS[4 * it:4 * it + 4].rearrange("a p f -> p a f"), Vst[:].rearrange("p a h d -> p a (h d)"), ["Vst"], [U()], eng="act")
                if P1S < 6:
                    continue
                for o in range(6):
                    if o % 4 == 0:
                        n = min(512, 768 - o * 128)
                        wk, wkk = wload(WB[:, :, C_U + o * 128:C_U + o * 128 + n], 8, n)
                    pjt, pk = nextpj()
                    c0 = (o % 4) * 128
                    mmg([dict(out=pjt[:, :], lhsT=wk[:, kt, c0:c0 + 128], rhs=hT[:, kt, :], start=(kt == 0), stop=(kt == 7)) for kt in range(8)],
                        [wkk, hk, pk], [pk])
                    cp("act" if o % 2 == 0 else "dve", uT[:, o, :].rearrange("p (s c) -> p c s", s=8), pjt[:, :].rearrange("p (c s) -> p c s", s=8), [pk], ["uT"])
                if P1S < 3:
                    continue
                wk, wkk = None, None
                for h_ in range(12):
                    if h_ % 8 == 0:
                        n = min(512, 768 - h_ * 64)
                        wk, wkk = wload(WB[:, :, C_K + h_ * 64:C_K + h_ * 64 + n], 8, n)
                    pjt, pk = nextpj()
                    c0 = (h_ % 8) * 64
                    calls = [dict(out=pjt[64:128, :], lhsT=SELK[:, h_, 64:128], rhs=spt[:, :], start=True, stop=True)]
                    calls += [dict(out=pjt[0:64, :], lhsT=wk[:, kt, c0:c0 + 64], rhs=hT[:, kt, :], start=(kt == 0), stop=(kt == 7)) for kt in range(8)]
                    mmg(calls, ["SELK", spk, wkk, hk, pk], [pk])
                    ks, kk = kst[h_ % 2], "kst%d" % (h_ % 2)
                    if h_ % 2 == 0:
                        cp("act", ks[:], pjt[:, :], [pk], [kk])
                        ld(KT[h_, :, it * TT:(it + 1) * TT], ks[:], [kk], [U()], eng="act")
                    else:
                        cp("dve", ks[:], pjt[:, :], [pk], [kk])
                        ld(KT[h_, :, it * TT:(it + 1) * TT], ks[:], [kk], [U()], eng="act")
                if P1S < 4:
                    continue
                if own:
                    for h_ in range(12):
                        if h_ % 8 == 0:
                            n = min(512, 768 - h_ * 64)
                            wk, wkk = wload(WB[:, :, C_Q + h_ * 64:C_Q + h_ * 64 + n], 8, n)
                        pjt, pk = nextpj()
                        c0 = (h_ % 8) * 64
                        calls = [dict(out=pjt[64:128, :], lhsT=SELQ[:, h_, 64:128], rhs=spt[:, :], start=True, stop=True)]
                        calls += [dict(out=pjt[0:64, :], lhsT=wk[:, kt, c0:c0 + 64], rhs=hT[:, kt, :], start=(kt == 0), stop=(kt == 7)) for kt in range(8)]
                        mmg(calls, ["SELQ", spk, wkk, hk, pk], [pk])
                        ks, kk = kst[h_ % 2], "kst%d" % (h_ % 2)
                        act(ks[:], pjt[:, :], AF.Identity, [pk, "qscale"], [kk], scale=qscale[:, :])
                        ld(QT[h_, :, ot * TT:(ot + 1) * TT], ks[:], [kk], [U()], eng="act")
                if P1S < 7:
                    continue
                s5slots = []
                for o in range(6):
                    bd_t, wz_t, wy_t = s5w[o % 2]
                    sk = "s5w%d" % (o % 2)
                    ld(wz_t[:], WZS[o], [], [sk + "z"])
                    calls = []
                    for ri in range(2):
                        for s in range(8):
                            for q4 in range(4):
                                calls.append(dict(out=zpsb[q4][:, ri * 64:(ri + 1) * 64], lhsT=wz_t[32 * q4:32 * q4 + 32, s, ri, :],
                                                  rhs=uT[32 * q4:32 * q4 + 32, o, s * 64:(s + 1) * 64],
                                                  start=(s == 0), stop=(s == 7), tile_position=(32 * q4, 0)))
                    mmg(calls, [sk + "z", "uT"] + ["zps%d" % q for q in range(4)], ["zps%d" % q for q in range(4)])
                    for q4 in range(4):
                        cp("act" if q4 % 2 == 0 else "dve", Z[:, 4 * o + q4, :, :], zpsb[q4][:, 0:128].rearrange("p (i c) -> p i c", i=2), ["zps%d" % q4], ["Zj%d" % j for j in range(8)])
                if it + 1 < P1T:
                    prep(it + 1)
                if P1S < 8:
                    continue
                if it > 0:
                    cp(os.environ.get('RENG', 'pool'), TB[:, :, :, 0], TB[:, :, :, 8], ["TB"], ["TB"])
                Zv = Z[:].rearrange("p r i (b j) -> p r i b j", j=8)

                RENG = os.environ.get('RENG', 'pool')
                cmn = [0]

                def cmac(dre, dim_, sre, sim, mr, mi, Tsets, rkeys, wkey):
                    si_ = cmn[0] % len(Tsets)
                    cmn[0] += 1
                    T = Tsets[si_]
                    tk = ["RT%d_%d" % (si_, q) for q in range(4)]
                    tt(RENG, T[0], mr, sre, ALU.mult, rkeys, [tk[0]])
                    tt(RENG, T[1], mi, sim, ALU.mult, rkeys, [tk[1]])
                    tt(RENG, T[2], mr, sim, ALU.mult, rkeys, [tk[2]])
                    tt(RENG, T[3], mi, sre, ALU.mult, rkeys, [tk[3]])
                    tt(RENG, T[0], T[0], T[1], ALU.subtract, [tk[0], tk[1]], [tk[0]])
                    tt(RENG, T[2], T[2], T[3], ALU.add, [tk[2], tk[3]], [tk[2]])
                    tt(RENG, dre, dre, T[0], ALU.add, [tk[0], wkey], [wkey])
                    tt(RENG, dim_, dim_, T[2], ALU.add, [tk[2], wkey], [wkey])
                RTs = [[t[:] for t in RT[0:4]], [t[:] for t in RT[4:8]]]
                RSs = [[t[:] for t in RS[0:4]], [t[:] for t in RS[4:8]]]
                zkeys = ["Zj%d" % j for j in range(8)]
                a1r = A1R[:].unsqueeze(2).to_broadcast([128, 24, 8])
                a1i = A1I[:].unsqueeze(2).to_broadcast([128, 24, 8])
                for j in range(1, 8):
                    cmac(Zv[:, :, 0, :, j], Zv[:, :, 1, :, j], Zv[:, :, 0, :, j - 1], Zv[:, :, 1, :, j - 1], a1r, a1i, RTs, [zkeys[j - 1]], zkeys[j])
                for b_ in range(8):
                    cp(RENG, TB[:, :, :, b_ + 1], Zv[:, :, :, b_, 7], [zkeys[7], "TB"], ["TB"])
                    cmac(TB[:, :, 0, b_ + 1], TB[:, :, 1, b_ + 1], TB[:, :, 0, b_], TB[:, :, 1, b_], A8R[:], A8I[:], RSs, ["TB"], "TB")
                for j in range(8):
                    cmac(Zv[:, :, 0, :, j], Zv[:, :, 1, :, j], TB[:, :, 0, 0:8], TB[:, :, 1, 0:8],
                         APR[:, j, :].unsqueeze(2).to_broadcast([128, 24, 8]), API[:, j, :].unsqueeze(2).to_broadcast([128, 24, 8]), RTs, ["TB"], zkeys[j])
                if own:
                    cp(RENG, Hb[:, :, :, 0], TB[:, :, :, 0], ["TB"], ["Hb"])
                    cp(RENG, Hb[:, :, :, 1:65], Z[:, :, :, :], zkeys, ["Hb"])
                if not own:
                    continue
                if P1S < 9:
                    continue
                for o in range(6):
                    if o % 4 == 0:
                        n = min(512, 768 - o * 128)
                        wk, wkk = wload(WB[:, :, C_GS + o * 128:C_GS + o * 128 + n], 8, n)
                    pjt, pk = nextpj()
                    c0 = (o % 4) * 128
                    mmg([dict(out=pjt[:, :], lhsT=wk[:, kt, c0:c0 + 128], rhs=hT[:, kt, :], start=(kt == 0), stop=(kt == 7)) for kt in range(8)],
                        [wkk, hk, pk], [pk])
                    act(SGS[:, o, :], pjt[:, :], AF.Silu, [pk], ["SGS"])
                if P1S < 12:
                    continue
                wk, wkk = wload(WB[:, :, C_QM:C_QM + 512], 8, 512)
                for hd in range(4):
                    pjt, pk = nextpj()
                    mmg([dict(out=pjt[:, :], lhsT=wk[:, kt, hd * 128:(hd + 1) * 128], rhs=hT[:, kt, :], start=(kt == 0), stop=(kt == 7)) for kt in range(8)],
                        [wkk, hk, pk], [pk])
                    act(QM[:, hd, :], pjt[:, :], AF.Identity, [pk], ["QM"], scale=128.0 ** -0.5)
                wk, wkk = wload(WB[:, :, C_GM:C_GM + 512], 8, 512)
                for hd in range(4):
                    pjt, pk = nextpj()
                    mmg([dict(out=pjt[:, :], lhsT=wk[:, kt, hd * 128:(hd + 1) * 128], rhs=hT[:, kt, :], start=(kt == 0), stop=(kt == 7)) for kt in range(8)],
                        [wkk, hk, pk], [pk])
                    act(SGM[:, hd, :], pjt[:, :], AF.Silu, [pk], ["SGM"])
                for hd in range(4):
                    for mt in range(2):
                        pjt, pk = nextpj()
                        mmg([dict(out=pjt[:, :], lhsT=MKT[:, hd, mt * 128:(mt + 1) * 128], rhs=QM[:, hd, :], start=True, stop=True)],
                            ["MKT", "QM", pk], [pk])
                        act(PmT[:, mt, :], pjt[:, :], AF.Exp, [pk], ["PmT%d" % mt])
                    pjt, pk = nextpj()
                    mmg([dict(out=pjt[:, :], lhsT=MV[:, mt, hd * 128:(hd + 1) * 128], rhs=PmT[:, mt, :], start=(mt == 0), stop=(mt == 1)) for mt in range(2)],
                        ["MV", "PmT0", "PmT1", pk], [pk])
                    axp, axk = nextpj()
                    mmg([dict(out=axp[:, :], lhsT=ones_b[:, :], rhs=PmT[:, mt, :], start=(mt == 0), stop=(mt == 1)) for mt in range(2)],
                        ["ones_b", "PmT0", "PmT1", axk], [axk])
                    P.c("dve", lambda h, axp=axp: h.reciprocal(out=tC[:], in_=axp[:, :]), [axk], ["tC"])
                    tt("dve", tB[:], tC[:], pjt[:, :], ALU.mult, ["tC", pk], ["tB"])
                    tt("dve", YM[:, hd, :], tB[:], SGM[:, hd, :], ALU.mult, ["tB", "SGM"], ["SGM"])
                if P1S < 14:
                    continue
                for cb in range(8):
                    if cb % 4 == 0:
                        wg0, wg0k = wload(WB[:, :, C_GL + cb * 128:C_GL + cb * 128 + 512], 8, 512)
                    c0 = (cb % 4) * 128
                    pjt, pk = nextpj()
                    mmg([dict(out=pjt[:, :], lhsT=wg0[:, kt, c0:c0 + 128], rhs=hT[:, kt, :], start=(kt == 0), stop=(kt == 7)) for kt in range(8)],
                        [wg0k, hk, pk], [pk])
                    g0, g0k = G0st[cb % 2], "G0st%d" % (cb % 2)
                    act(g0[:], pjt[:, :], AF.Sigmoid, [pk, "bm"], [g0k], bias=bm[:, cb:cb + 1])
                    ld(G0S[ot, :, cb, :], g0[:], [g0k], [U()], eng="act")
                for o in range(6):
                    if o % 4 == 0:
                        n = min(512, 768 - o * 128)
                        wk, wkk = wload(WB[:, :, C_GF + o * 128:C_GF + o * 128 + n], 8, n)
                    pjt, pk = nextpj()
                    c0 = (o % 4) * 128
                    mmg([dict(out=pjt[:, :], lhsT=wk[:, kt, c0:c0 + 128], rhs=hT[:, kt, :], start=(kt == 0), stop=(kt == 7)) for kt in range(8)],
                        [wkk, hk, pk], [pk])
                    ks, kk = kst[o % 2], "kst%d" % (o % 2)
                    act(ks[:], pjt[:, :], AF.Silu, [pk], [kk])
                    ld(SGFS[ot, :, o, :], ks[:], [kk], [U()], eng="act")
                if P1S < 10:
                    continue
                for o in range(6):
                    bd_t, wz_t, wy_t = s5w[o % 2]
                    sk = "s5w%d" % (o % 2)
                    ld(bd_t[:], BDS[o], [], [sk + "b"])
                    ld(wy_t[:], WYS[o], [], [sk + "y"])
                    calls = []
                    for j in range(8):
                        calls.append(dict(out=yps[:, j * 64:512], lhsT=bd_t[:, j, :], rhs=uT[:, o, 0:(8 - j) * 64], start=(j == 0), stop=False))
                    for tau in range(8):
                        for ri in range(2):
                            for q4 in range(4):
                                last = (tau == 7 and ri == 1)
                                calls.append(dict(out=yps[32 * q4:32 * q4 + 32, tau * 64:(tau + 1) * 64], lhsT=wy_t[:, q4, tau, ri, :],
                                                  rhs=Hb[:, 4 * o + q4, ri, 0:64], start=False, stop=last, tile_position=(0, 32 * q4)))
                    mmg(calls, [sk + "b", sk + "y", "uT", "Hb", "yps"], ["yps"])
                    ypv = yps[:, :].rearrange("p (s c) -> p c s", s=8)
                    v3 = lambda ap_: ap_.rearrange("p (c s) -> p c s", s=8)
                    act(v3(tA[:]), ypv, AF.Square, ["yps"], ["tA"])
                    ts("dve", tA[:], tA[:], 0.044715, 1.0, ALU.mult, ALU.add, ["tA"], ["tA"])
                    tt("dve", v3(tB[:]), v3(tA[:]), ypv, ALU.mult, ["tA", "yps"], ["tB"])
                    act(tA[:], tB[:], AF.Sigmoid, ["tB"], ["tA"], scale=1.5957691216057308)
                    tt("dve", v3(YG[:, o, :]), v3(tA[:]), ypv, ALU.mult, ["tA", "yps"], ["YG"])
                if P1S < 11:
                    continue
                for half in range(2):
                    wg, wgk = wload(WGLU[:, :, half * 384:(half + 1) * 384], 6, 384)
                    for c3 in range(3):
                        cb = half * 3 + c3
                        pjt, pk = nextpj()
                        mmg([dict(out=pjt[:, :], lhsT=wg[:, kt, c3 * 128:(c3 + 1) * 128], rhs=YG[:, kt, :], start=(kt == 0), stop=(kt == 5)) for kt in range(6)],
                            [wgk, "YG", pk], [pk])
                        act(tA[:], pjt[:, :], AF.Sigmoid, [pk, "bglu"], ["tA"], bias=bglu[:, cb:cb + 1])
                        tt("dve", tB[:], tA[:], YG[:, cb, :], ALU.mult, ["tA", "YG"], ["tB"])
                        tt("dve", YS[:, cb, :], tB[:], SGS[:, cb, :], ALU.mult, ["tB", "SGS"], ["SGS"])
                if P1S < 13:
                    continue
                for cb in range(8):
                    si = wn[0] % NWS
                    wn[0] += 1
                    wfull = wsl[si][:, 0:26 * 128].rearrange("p (k c) -> p k c", k=26)
                    wpsk = wpmk = wg1k = wg2k = "ws%d" % si
                    wps_t, wpm_t, wg1, wg2 = wfull[:, 0:6, :], wfull[:, 6:10, :], wfull[:, 10:18, :], wfull[:, 18:26, :]
                    ld(wps_t, WPS[:, :, cb * 128:(cb + 1) * 128], [], [wpsk])
                    ld(wpm_t, WPM[:, :, cb * 128:(cb + 1) * 128], [wpsk], [wpsk])
                    ld(wg1, WB[:, :, C_GL + 1024 + cb * 128:C_GL + 1024 + (cb + 1) * 128], [wpsk], [wpsk])
                    ld(wg2, WB[:, :, C_GL + 2048 + cb * 128:C_GL + 2048 + (cb + 1) * 128], [wpsk], [wpsk])
                    c0 = 0
                    m0, m0k = M0[cb % 2], "M0_%d" % (cb % 2)
                    pjt, pk = nextpj()
                    mmg([dict(out=pjt[:, :], lhsT=wg1[:, kt, c0:c0 + 128], rhs=hT[:, kt, :], start=(kt == 0), stop=(kt == 7)) for kt in range(8)],
                        [wg1k, hk, pk], [pk])
                    act(tA[:], pjt[:, :], AF.Sigmoid, [pk, "bm"], ["tA"], bias=bm[:, 8 + cb:9 + cb])
                    pjt, pk = nextpj()
                    mmg([dict(out=pjt[:, :], lhsT=wps_t[:, kt, c0:c0 + 128], rhs=YS[:, kt, :], start=(kt == 0), stop=(kt == 5)) for kt in range(6)],
                        [wpsk, "SGS", pk], [pk])
                    tt("dve", m0[:], tA[:], pjt[:, :], ALU.mult, ["tA", pk], [m0k])
                    pjt, pk = nextpj()
                    mmg([dict(out=pjt[:, :], lhsT=wg2[:, kt, c0:c0 + 128], rhs=hT[:, kt, :], start=(kt == 0), stop=(kt == 7)) for kt in range(8)],
                        [wg2k, hk, pk], [pk])
                    act(tB[:], pjt[:, :], AF.Sigmoid, [pk, "bm"], ["tB"], bias=bm[:, 16 + cb:17 + cb])
                    pjt, pk = nextpj()
                    mmg([dict(out=pjt[:, :], lhsT=wpm_t[:, kt, c0:c0 + 128], rhs=YM[:, kt, :], start=(kt == 0), stop=(kt == 3)) for kt in range(4)],
                        [wpmk, "SGM", pk], [pk])
                    tt("dve", tC[:], tB[:], pjt[:, :], ALU.mult, ["tB", pk], ["tC"])
                    tt("dve", m0[:], m0[:], tC[:], ALU.add, [m0k, "tC"], [m0k])
                    ld(M0S[ot, :, cb, :], m0[:], [m0k], [U()], eng="act")
            P.emit()

        if upto < 3:
            return nc
        S2 = ExitStack()
        with S2:
            Kh = [sb(S2, "Kh%d" % i, [128, LTOK], BF16) for i in range(2)]
            Vh = [sb(S2, "Vh%d" % i, [128, 64, 65], BF16) for i in range(2)]
            Qh = [sb(S2, "Qh%d" % i, [128, 4096], BF16) for i in range(2)]
            NPT = 4
            Pt = [sb(S2, "Pt%d" % i, [128, TT], BF16) for i in range(NPT)]
            Osb = sb(S2, "Osb", [65, TT], F32)
            Rd = sb(S2, "Rd", [64, TT], F32)
            Yst = [sb(S2, "Yst%d" % i, [64, TT], BF16) for i in range(2)]
            SEL = sb(S2, "SEL", [65, 64], F32)
            sps = [ps(S2, "sps%d" % i, [128, 512], F32) for i in range(4)]
            ops_ = [ps(S2, "ops%d" % i, [128, 512], F32) for i in range(2)]
            dps = ps(S2, "dps", [128, 512], F32)
            mset("pool", SEL[:], 0.0, ["SEL"])
            mset("pool", SEL[64:65, :], 1.0, ["SEL"])
            LA = 2
            its = []
            for h_ in range(12):
                for qt in range(8):
                    nkb = 32 + 4 * qt + 4
                    order = list(range(32 + 4 * qt, nkb)) + list(range(0, 32 + 4 * qt))
                    for n_, kb in enumerate(order):
                        its.append((h_, qt, kb, n_ == 0, n_ == nkb - 1))

            def head_loads(h_):
                hs = h_ % 2
                kh, vh, qh = Kh[hs], Vh[hs], Qh[hs]
                kk, vk, qk = "Kh%d" % hs, "Vh%d" % hs, "Qh%d" % hs
                ld(qh[:], QT[h_], [], [qk])
                for c4 in range(4):
                    ld(kh[:, c4 * 2048:(c4 + 1) * 2048], KT[h_, :, c4 * 2048:(c4 + 1) * 2048], [qk], [kk])
                for c4 in range(4):
                    ld(vh[:, c4 * 16:(c4 + 1) * 16, :], VS[c4 * 16:(c4 + 1) * 16, :, h_ * 65:(h_ + 1) * 65].rearrange("b p d -> p b d"), [kk], [vk])

            def geom(i):
                h_, qt, kb, first, last = its[i]
                diag = kb - (32 + 4 * qt)
                c0 = max(0, diag) * 128
                return h_, qt, kb, first, last, diag, c0

            def emit_qk(i):
                h_, qt, kb, first, last, diag, c0 = geom(i)
                hs = h_ % 2
                st, sk = sps[i % 4], "sps%d" % (i % 4)
                mmg([dict(out=st[:, c0:TT], lhsT=Kh[hs][:, kb * 128:(kb + 1) * 128], rhs=Qh[hs][:, qt * TT + c0:(qt + 1) * TT], start=True, stop=True)],
                    ["Kh%d" % hs, "Qh%d" % hs, sk], [sk])

            pend = []

            def emit_rest(i):
                h_, qt, kb, first, last, diag, c0 = geom(i)
                hs = h_ % 2
                st, sk = sps[i % 4], "sps%d" % (i % 4)
                pt, ptk = Pt[i % NPT], "Pt%d" % (i % NPT)
                op_t, opk = ops_[(h_ * 8 + qt) % 2], "ops%d" % ((h_ * 8 + qt) % 2)
                act(pt[:, c0:TT], st[:, c0:TT], AF.Exp, [sk], [ptk])
                if diag >= 0:
                    tt("dve", pt[:, c0:c0 + 128], pt[:, c0:c0 + 128], maskT[:], ALU.mult, [ptk, "maskT"], [ptk])
                mmg([dict(out=op_t[0:65, c0:TT], lhsT=Vh[hs][:, kb, :], rhs=pt[:, c0:TT], start=first, stop=last)],
                    ["Vh%d" % hs, ptk, opk], [opk])
                if last:
                    cp("dve", Osb[:], op_t[0:65, :], [opk], ["Osb"])
                    pend.append((i + 3, h_, qt))

            def emit_norm(h_, qt):
                mmg([dict(out=dps[0:64, :], lhsT=SEL[:, :], rhs=Osb[:, :], start=True, stop=True)], ["SEL", "Osb", "dps"], ["dps"])
                P.c("dve", lambda h: h.reciprocal(out=Rd[:], in_=dps[0:64, :]), ["dps"], ["Rd"])
                ys, ysk = Yst[qt % 2], "Yst%d" % (qt % 2)
                tt("dve", ys[:], Osb[0:64, :], Rd[:], ALU.mult, ["Osb", "Rd"], [ysk])
                ld(YFS[h_, :, qt * TT:(qt + 1) * TT], ys[:], [ysk], [U()], eng="sp")

            head_loads(0)
            head_loads(1)
            NI = len(its)
            for i in range(min(LA, NI)):
                emit_qk(i)
            for i in range(NI):
                if i + LA < NI:
                    emit_qk(i + LA)
                emit_rest(i)
                while pend and pend[0][0] <= i:
                    _, ph, pq = pend.pop(0)
                    emit_norm(ph, pq)
                if its[i][4] and its[i][1] == 7 and its[i][0] + 2 < 12:
                    head_loads(its[i][0] + 2)
            while pend:
                _, ph, pq = pend.pop(0)
                emit_norm(ph, pq)
            P.emit()

        if upto < 4:
            return nc
        S3 = ExitStack()
        with S3:
            gfin = sb(S3, "gfin", [128, D], F32)
            ld(gfin[:], g_final.to_broadcast([128, D]), [], ["gfin"])
            wpf = sb(S3, "wpf", [128, 6, D], BF16)
            wo = sb(S3, "wo", [128, 8, D], BF16)
            yf = sb(S3, "yf", [128, 6, TT], BF16)
            sgf = sb(S3, "sgf", [128, 6, TT], BF16)
            yfg = sb(S3, "yfg", [128, 6, TT], BF16)
            g0t = sb(S3, "g0t", [128, 8, TT], BF16)
            m0t = sb(S3, "m0t", [128, 8, TT], F32)
            mg = sb(S3, "mg", [128, 8, TT], BF16)
            t3 = sb(S3, "t3", [128, TT], F32)
            x3 = sb(S3, "x3", [128, 4, D], F32)
            o3 = [sb(S3, "o3_%d" % i, [128, D], F32) for i in range(2)]
            j3 = sb(S3, "j3", [128, D], BF16)
            pf = [ps(S3, "pf%d" % i, [128, 512], F32) for i in range(2)]
            po = [ps(S3, "po%d" % i, [128, 512], F32) for i in range(4)]
            ld(wpf[:], WPF, [], ["wpf"])
            ld(wo[:], WOUT, [], ["wo"])
            for ot in range(8):
                for a2 in range(2):
                    ld(yf[a2 * 64:(a2 + 1) * 64, :, :], YFS[:, :, ot * TT:(ot + 1) * TT].rearrange("(c a) d t -> a d c t", a=2)[a2], ["yf"], ["yf"])
                ld(sgf[:], SGFS[ot], [], ["sgf"])
                ld(g0t[:], G0S[ot], [], ["g0t"])
                ld(m0t[:], M0S[ot], [], ["m0t"])
                ld(x3[:], xa[(OWN0 + ot) * TT:(OWN0 + ot + 1) * TT, :].rearrange("(a p) d -> p a d", p=128), [], ["x3"])
                for c in range(6):
                    tt("pool" if c % 2 else "dve", yfg[:, c, :], yf[:, c, :], sgf[:, c, :], ALU.mult, ["yf", "sgf"], ["yfg"])
                for cb in range(8):
                    pt_, pk = pf[cb % 2], "pf%d" % (cb % 2)
                    mmg([dict(out=pt_[:, :], lhsT=wpf[:, kt, cb * 128:(cb + 1) * 128], rhs=yfg[:, kt, :], start=(kt == 0), stop=(kt == 5)) for kt in range(6)],
                        ["wpf", "yfg", pk], [pk])
                    tt("dve", t3[:], g0t[:, cb, :], pt_[:, :], ALU.mult, ["g0t", pk], ["t3"])
                    tt("dve", mg[:, cb, :], t3[:], m0t[:, cb, :], ALU.add, ["t3", "m0t"], ["mg"])
                for a in range(4):
                    ob, obk = o3[a % 2], "o3_%d" % (a % 2)
                    for hf in range(2):
                        pt_, pk = po[(2 * a + hf) % 4], "po%d" % ((2 * a + hf) % 4)
                        mmg([dict(out=pt_[:, :], lhsT=mg[:, kt, a * 128:(a + 1) * 128], rhs=wo[:, kt, hf * 512:(hf + 1) * 512], start=(kt == 0), stop=(kt == 7)) for kt in range(8)],
                            ["mg", "wo", pk], [pk])
                        tt("dve", ob[:, hf * 512:(hf + 1) * 512], pt_[:, :], x3[:, a, hf * 512:(hf + 1) * 512], ALU.add, [pk, "x3"], [obk])
                    act(j3[:], ob[:], AF.Square, [obk], ["j3", "ssq"], accum=ssq[:, a:a + 1])
                    ts("dve", rstd[:, a:a + 1], ssq[:, a:a + 1], 1.0 / D, EPS, ALU.mult, ALU.add, ["ssq"], ["rstd"])
                    act(rstd[:, a:a + 1], rstd[:, a:a + 1], AF.Sqrt, ["rstd"], ["rstd"])
                    P.c("dve", lambda h, a=a: h.reciprocal(out=rstd[:, a:a + 1], in_=rstd[:, a:a + 1]), ["rstd"], ["rstd"])
                    stt(ob[:], ob[:], rstd[:, a:a + 1], gfin[:], ALU.mult, ALU.mult, [obk, "rstd", "gfin"], [obk])
                    ld(yout[ot * TT + a * 128:ot * TT + (a + 1) * 128, :], ob[:], [obk], [U()], eng="act")
            P.emit()
    return nc


_NC = None


def kernel(**inputs):
    global _NC
    if _NC is None:
        _NC = build_nc()
    nc = _NC
    x = np.ascontiguousarray(np.asarray(inputs["x"], dtype=np.float32))
    mem = np.asarray(inputs["mem"], dtype=np.float32)
    in_maps = []
    for c in range(8):
        b, half = c // 2, c % 2
        if half == 0:
            xa = np.concatenate([np.zeros((4096, D), np.float32), x[b, 0:4096]], axis=0)
        else:
            xa = x[b]
        km = np.zeros((NT, TT), dtype=ml_dtypes.bfloat16)
        if half == 0:
            km[0:OWN0, :] = -30000.0
        m = {"xa": np.ascontiguousarray(xa), "kmrow": km, "mem": np.ascontiguousarray(mem[b])}
        for name in ("g_norm", "g_mem_norm", "w_in", "b_forget", "b_merge", "w_mem_kv", "lam_re", "lam_im", "log_step",
                     "s5_b_re", "s5_b_im", "s5_c_re", "s5_c_im", "s5_d", "w_glu", "b_glu", "w_proj_fox", "w_proj_s5",
                     "w_proj_mem", "w_out"):
            a = np.asarray(inputs[name], dtype=np.float32)
            a = a[0]
            if a.ndim == 1:
                a = a[None, :]
            m[name] = np.ascontiguousarray(a)
        m["g_final"] = np.ascontiguousarray(np.asarray(inputs["g_final"], dtype=np.float32)[None, :])
        in_maps.append(m)
    res = run_bass_kernel_spmd(nc, in_maps, core_ids=list(range(8)))
    out = np.zeros((4, 8192, D), dtype=np.float32)
    for c in range(8):
        b, half = c // 2, c % 2
        out[b, half * 4096:(half + 1) * 4096] = np.asarray(res.results[c]["yout"], dtype=np.float32)
    return out
```

```python
import math
from contextlib import ExitStack

import ml_dtypes
import numpy as np
import concourse.bass as bass
import concourse.mybir as mybir
from concourse.bass_utils import run_bass_kernel_spmd

F32 = mybir.dt.float32
BF16 = mybir.dt.bfloat16
I32 = mybir.dt.int32
AF = mybir.ActivationFunctionType
ALU = mybir.AluOpType

ENGS = ("pe", "act", "dve", "pool", "sp")
import os as _os0
NOSELF = tuple(x for x in _os0.environ.get('NOSELF', '').split(',') if x)
D = 1024
NIN = 8716
TT = 512
NT = 16
LTOK = 8192
OWN0 = 8
C_Q, C_K, C_V, C_FL, C_GF, C_U, C_GS, C_QM, C_GM, C_GL = 0, 768, 1536, 2304, 2316, 3084, 3852, 4620, 5132, 5644
EPS = 1e-6
TWO_PI = 2.0 * math.pi


class Op:
    __slots__ = ("eng", "fn", "deps", "needs_inc", "sigval", "is_dma", "lane", "laneval")

    def __init__(self, eng, fn, is_dma=False):
        self.eng = eng
        self.fn = fn
        self.deps = []
        self.needs_inc = False
        self.sigval = None
        self.is_dma = is_dma
        self.lane = None
        self.laneval = None


class Prog:
    def __init__(self, nc, es, n_lanes=6):
        self.nc = nc
        self.n_lanes = n_lanes
        self.esem = {e: es.enter_context(nc.semaphore("s_" + e)) for e in ENGS}
        self.lsem = {}
        for e in ("sp", "act", "pool"):
            for i in range(n_lanes):
                self.lsem[(e, i)] = es.enter_context(nc.semaphore("l_%s%d" % (e, i)))
        self.ecnt = {e: 0 for e in ENGS}
        self.lane_rr = {e: 0 for e in ENGS}
        self.lane_cnt = {k: 0 for k in self.lsem}
        self.barrier = {}
        self._reset()

    def _reset(self):
        self.ops = {e: [] for e in ENGS}
        self.last_w = {}
        self.readers = {}
        self.lane_last = {}

    def _add(self, op, reads, writes):
        deps = []
        for r in reads:
            w = self.last_w.get(r)
            if w is not None:
                deps.append(w)
        for r in writes:
            w = self.last_w.get(r)
            if w is not None:
                deps.append(w)
            deps.extend(self.readers.get(r, ()))
        seen = set(id(d) for d in op.deps)
        for d in deps:
            if d is op or id(d) in seen:
                continue
            if op.eng == "pe" and d.eng == "pe" and not d.is_dma and not op.is_dma:
                continue
            if (not d.is_dma) and (not op.is_dma) and op.eng == d.eng and op.eng in NOSELF:
                continue
            seen.add(id(d))
            op.deps.append(d)
            if not d.is_dma:
                d.needs_inc = True
        for r in reads:
            self.readers.setdefault(r, []).append(op)
        for r in writes:
            self.last_w[r] = op
            self.readers[r] = []
        self.ops[op.eng].append(op)
        return op

    def c(self, eng, fn, reads=(), writes=()):
        return self._add(Op(eng, fn), reads, writes)

    def dma(self, eng, fn, reads=(), writes=()):
        op = Op(eng, fn, is_dma=True)
        lane = (eng, self.lane_rr[eng] % self.n_lanes)
        self.lane_rr[eng] += 1
        prev = self.lane_last.get(lane)
        self.lane_cnt[lane] += 1
        op.lane = lane
        op.laneval = 16 * self.lane_cnt[lane]
        if prev is not None:
            op.deps.append(prev)
        self.lane_last[lane] = op
        return self._add(op, reads, writes)

    def emit(self):
        nc = self.nc
        for e in ENGS:
            last = None
            for op in self.ops[e]:
                if not op.is_dma:
                    last = op
            if last is not None:
                last.needs_inc = True
            for op in self.ops[e]:
                if (not op.is_dma) and op.needs_inc:
                    self.ecnt[e] += 1
                    op.sigval = self.ecnt[e]
        barrier = dict(self.barrier)
        esem, lsem = self.esem, self.lsem

        def tok(d):
            if d.is_dma:
                return lsem[d.lane], d.laneval
            return esem[d.eng], d.sigval

        def run(e, h):
            waited = {}
            for s, v in barrier.values():
                if v > 0:
                    h.wait_ge(s, v)
                    waited[id(s)] = v
            for op in self.ops[e]:
                for d in op.deps:
                    s, v = tok(d)
                    if waited.get(id(s), 0) >= v:
                        continue
                    waited[id(s)] = v
                    h.wait_ge(s, v)
                inst = op.fn(h)
                if op.is_dma:
                    inst.then_inc(lsem[op.lane], 16)
                elif op.needs_inc:
                    inst.then_inc(esem[e], 1)
                import os as _os
                if _os.environ.get('DBGPRINT'):
                    print("OP", e, "dma" if op.is_dma else "c", "lane=%s val=%s" % (op.lane, op.laneval) if op.is_dma else "sig=%s" % op.sigval,
                          "deps=", [((d.lane, d.laneval) if d.is_dma else (d.eng, d.sigval)) for d in op.deps], type(inst).__name__)
            for lane, cnt in self.lane_cnt.items():
                if lane[0] == e and cnt > 0:
                    h.wait_ge(lsem[lane], 16 * cnt)

        with nc.Block() as block:
            @block.tensor
            def _(h):
                run("pe", h)

            @block.scalar
            def _(h):
                run("act", h)

            @block.vector
            def _(h):
                run("dve", h)

            @block.gpsimd
            def _(h):
                run("pool", h)

            @block.sync
            def _(h):
                run("sp", h)

        self.barrier = {}
        for e in ENGS:
            self.barrier[("e", e)] = (esem[e], self.ecnt[e])
        for lane, cnt in self.lane_cnt.items():
            self.barrier[("l", lane)] = (lsem[lane], 16 * cnt)
        self._reset()


def build_nc(dbg=False, upto=9):
    nc = bass.Bass("TRN2", target_bir_lowering=False)

    def din(name, shape, dt=F32):
        return nc.dram_tensor(name, list(shape), dt, kind="ExternalInput").ap()

    def dscr(name, shape, dt):
        return nc.dram_tensor(name, list(shape), dt, kind=("ExternalOutput" if dbg else "Internal")).ap()

    xa = din("xa", [LTOK, D])
    kmrow = din("kmrow", [NT, TT], BF16)
    mem = din("mem", [256, D])
    g_norm = din("g_norm", [1, D])
    g_mem_norm = din("g_mem_norm", [1, D])
    g_final = din("g_final", [1, D])
    w_in = din("w_in", [D, NIN])
    b_forget = din("b_forget", [1, 12])
    b_merge = din("b_merge", [1, 3072])
    w_mem_kv = din("w_mem_kv", [D, 1024])
    lam_re = din("lam_re", [48, 64])
    lam_im = din("lam_im", [48, 64])
    log_step = din("log_step", [1, 48])
    s5_b_re = din("s5_b_re", [48, 64, 16])
    s5_b_im = din("s5_b_im", [48, 64, 16])
    s5_c_re = din("s5_c_re", [48, 16, 64])
    s5_c_im = din("s5_c_im", [48, 16, 64])
    s5_d = din("s5_d", [1, 768])
    w_glu = din("w_glu", [768, 768])
    b_glu = din("b_glu", [1, 768])
    w_proj_fox = din("w_proj_fox", [768, D])
    w_proj_s5 = din("w_proj_s5", [768, D])
    w_proj_mem = din("w_proj_mem", [512, D])
    w_out = din("w_out", [D, D])
    yout = nc.dram_tensor("yout", [4096, D], F32, kind="ExternalOutput").ap()

    WB = dscr("WB", [128, 8, NIN], BF16)
    WKV = dscr("WKV", [128, 8, 1024], BF16)
    WGLU = dscr("WGLU", [128, 6, 768], BF16)
    WPF = dscr("WPF", [128, 6, D], BF16)
    WPS = dscr("WPS", [128, 6, D], BF16)
    WPM = dscr("WPM", [128, 4, D], BF16)
    WOUT = dscr("WOUT", [128, 8, D], BF16)
    BDS = dscr("BDS", [6, 128, 8, 128], BF16)
    WZS = dscr("WZS", [6, 128, 8, 2, 128], BF16)
    WYS = dscr("WYS", [6, 128, 4, 8, 2, 32], BF16)
    KT = dscr("KT", [12, 128, LTOK], BF16)
    QT = dscr("QT", [12, 128, 4096], BF16)
    VS = dscr("VS", [64, 128, 780], BF16)
    M0S = dscr("M0S", [8, 128, 8, TT], F32)
    G0S = dscr("G0S", [8, 128, 8, TT], BF16)
    SGFS = dscr("SGFS", [8, 128, 6, TT], BF16)
    YFS = dscr("YFS", [12, 64, 4096], BF16)

    es = ExitStack()
    with es:
        P = Prog(nc, es)

        def sb(stack, name, shape, dt):
            return stack.enter_context(nc.sbuf_tensor(name, list(shape), dt))

        def ps(stack, name, shape, dt=F32):
            return stack.enter_context(nc.psum_tensor(name, list(shape), dt))

        def act(out, in_, func, r, w, bias=None, scale=None, accum=None):
            kw = {}
            if bias is not None:
                kw["bias"] = bias
            if scale is not None:
                kw["scale"] = scale
            if accum is not None:
                kw["accum_out"] = accum
            return P.c("act", lambda h: h.activation(out=out, in_=in_, func=func, **kw), r, w)

        def tt(eng, out, in0, in1, op, r, w):
            return P.c(eng, lambda h: h.tensor_tensor(out=out, in0=in0, in1=in1, op=op), r, w)

        def ts(eng, out, in0, s1, s2, op0, op1, r, w):
            if op1 is None:
                return P.c(eng, lambda h: h.tensor_scalar(out=out, in0=in0, scalar1=s1, scalar2=None, op0=op0), r, w)
            return P.c(eng, lambda h: h.tensor_scalar(out=out, in0=in0, scalar1=s1, scalar2=s2, op0=op0, op1=op1), r, w)

        def stt(out, in0, scalar, in1, op0, op1, r, w):
            return P.c("dve", lambda h: h.scalar_tensor_tensor(out=out, in0=in0, scalar=scalar, in1=in1, op0=op0, op1=op1), r, w)

        def cp(eng, out, in_, r, w):
            if eng == "act":
                return P.c("act", lambda h: h.activation(out=out, in_=in_, func=AF.Identity), r, w)
            return P.c(eng, lambda h: h.tensor_copy(out=out, in_=in_), r, w)

        def mset(eng, ap, val, w):
            return P.c(eng, lambda h: h.memset(ap, val), (), w)

        def mmg(calls, r, w):
            def fn(h):
                inst = None
                for kw in calls:
                    inst = h.matmul(**kw)
                return inst
            return P.c("pe", fn, r, w)

        def tpg(calls, r, w):
            def fn(h):
                inst = None
                for (o, i, idn) in calls:
                    inst = h.transpose(o, i, idn)
                return inst
            return P.c("pe", fn, r, w)

        uq = [0]

        def U():
            uq[0] += 1
            return "u%d" % uq[0]

        def ld(out, in_, r, w, eng="sp", slow=False):
            if slow:
                return P.dma(eng, lambda h: h.dma_start(out=out, in_=in_, allow_slow_non_contiguous=True), r, w)
            return P.dma(eng, lambda h: h.dma_start(out=out, in_=in_), r, w)

        def rows_pattern(tile_ap, n, lo, hi, w):
            P.c("pool", lambda h: h.memset(tile_ap, 1.0), (), w)
            P.c("pool", lambda h: h.affine_select(out=tile_ap, in_=tile_ap, pattern=[[0, n]], compare_op=ALU.is_ge,
                                                  fill=0.0, base=-lo, channel_multiplier=1), w, w)
            P.c("pool", lambda h: h.affine_select(out=tile_ap, in_=tile_ap, pattern=[[0, n]], compare_op=ALU.is_ge,
                                                  fill=0.0, base=hi, channel_multiplier=-1), w, w)

        G = ExitStack()
        es.enter_context(G)
        ident_f = sb(G, "ident_f", [128, 128], F32)
        ident_b = sb(G, "ident_b", [128, 128], BF16)
        ones_f = sb(G, "ones_f", [128, 512], F32)
        ones_b = sb(G, "ones_b", [128, 128], BF16)
        maskT = sb(G, "maskT", [128, 128], BF16)
        gn = sb(G, "gn", [128, 8], F32)
        gmn = sb(G, "gmn", [128, 8], F32)
        bm = sb(G, "bm", [128, 24], F32)
        bglu = sb(G, "bglu", [128, 6], F32)
        SELK = sb(G, "SELK", [128, 12, 128], BF16)
        SELQ = sb(G, "SELQ", [128, 12, 128], BF16)
        qscale = sb(G, "qscale", [128, 1], F32)
        negb = sb(G, "negb", [128, 1], F32)
        nm1 = sb(G, "nm1", [128, 1], F32)
        nm2 = sb(G, "nm2", [128, 1], F32)
        WFL3 = sb(G, "WFL3", [128, 8, 76], BF16)
        MKT = sb(G, "MKT", [128, 4, 256], BF16)
        MV = sb(G, "MV", [128, 2, 512], BF16)
        A1R = sb(G, "A1R", [128, 24], F32)
        A1I = sb(G, "A1I", [128, 24], F32)
        A8R = sb(G, "A8R", [128, 24], F32)
        A8I = sb(G, "A8I", [128, 24], F32)
        APR = sb(G, "APR", [128, 8, 24], F32)
        API = sb(G, "API", [128, 8, 24], F32)
        Fcar = sb(G, "Fcar", [128, 1], F32)
        TB = sb(G, "TB", [128, 24, 2, 9], F32)
        ssq = sb(G, "ssq", [128, 8], F32)
        rstd = sb(G, "rstd", [128, 8], F32)

        mset("pool", ones_f[:], 1.0, ["ones_f"])
        mset("pool", ident_f[:], 1.0, ["ident_f"])
        P.c("pool", lambda h: h.affine_select(out=ident_f[:], in_=ident_f[:], pattern=[[-1, 128]], compare_op=ALU.is_equal,
                                              fill=0.0, base=0, channel_multiplier=1), ["ident_f"], ["ident_f"])
        cp("dve", ident_b[:], ident_f[:], ["ident_f"], ["ident_b"])
        cp("dve", ones_b[:], ones_f[:, 0:128], ["ones_f"], ["ones_b"])
        mset("pool", maskT[:], 1.0, ["maskT"])
        P.c("pool", lambda h: h.affine_select(out=maskT[:], in_=maskT[:], pattern=[[1, 128]], compare_op=ALU.is_ge,
                                              fill=0.0, base=0, channel_multiplier=-1), ["maskT"], ["maskT"])
        ld(gn[:], g_norm.rearrange("o (kt p) -> p (o kt)", p=128), [], ["gn"], slow=True)
        ld(gmn[:], g_mem_norm.rearrange("o (kt p) -> p (o kt)", p=128), [], ["gmn"], slow=True)
        ld(bm[:], b_merge.rearrange("o (c p) -> p (o c)", p=128), [], ["bm"], slow=True)
        ld(bglu[:], b_glu.rearrange("o (c p) -> p (o c)", p=128), [], ["bglu"], slow=True)
        mset("pool", qscale[:], 1.0, ["qscale"])
        mset("pool", qscale[0:64, :], 0.125, ["qscale"])
        mset("pool", negb[:], 0.0, ["negb"])
        for base in (0, 32, 64):
            ld(negb[base:base + 12, :], b_forget.rearrange("o h -> h o"), [], ["negb"], slow=True)
        ts("dve", negb[:], negb[:], -1.0, None, ALU.mult, None, ["negb"], ["negb"])
        mset("pool", nm1[:], 0.0, ["nm1"])
        mset("pool", nm1[32:64, :], -1.0, ["nm1"])
        mset("pool", nm1[64:96, :], -1.0, ["nm1"])
        mset("pool", nm2[:], 0.0, ["nm2"])
        mset("pool", nm2[64:96, :], -1.0, ["nm2"])
        mset("pool", Fcar[:], 0.0, ["Fcar"])
        mset("pool", TB[:], 0.0, ["TB"])
        mset("pool", SELK[:], 0.0, ["SELK"])
        mset("pool", SELQ[:], 0.0, ["SELQ"])
        for j, base in enumerate((0, 32, 64)):
            ts("dve", SELK[:, :, 96 + j], ident_f[:, base:base + 12], -1.0, None, ALU.mult, None, ["ident_f", "SELK"], ["SELK"])
            cp("dve", SELQ[:, :, 64 + j], ident_f[:, base:base + 12], ["ident_f", "SELQ"], ["SELQ"])
        for h_ in range(12):
            cp("dve", SELK[:, h_, 99:100], ident_f[:, 76:77], ["ident_f", "SELK"], ["SELK"])
            for c_ in (64, 65, 66):
                cp("dve", SELK[:, h_, c_:c_ + 1], ident_f[:, 96:97], ["ident_f", "SELK"], ["SELK"])
            for c_ in (96, 97, 98, 99):
                cp("dve", SELQ[:, h_, c_:c_ + 1], ident_f[:, 96:97], ["ident_f", "SELQ"], ["SELQ"])

        S0 = ExitStack()
        with S0:
            cv = [sb(S0, "cv%d" % i, [128, 8, 512], BF16) for i in range(2)]
            cvn = [0]

            def convert(src, nkt, ncols, dst, wkey=None):
                srcv = src.rearrange("(kt p) c -> p kt c", p=128)
                c0 = 0
                while c0 < ncols:
                    n = min(512, ncols - c0)
                    i = cvn[0] % 2
                    cvn[0] += 1
                    t = cv[i]
                    ld(t[:, 0:nkt, 0:n], srcv[:, :, c0:c0 + n], [], ["cv%d" % i], eng="pool")
                    ld(dst[:, :, c0:c0 + n], t[:, 0:nkt, 0:n], ["cv%d" % i], [wkey] if wkey else [U()], eng="sp")
                    c0 += n

            convert(w_in, 8, NIN, WB)
            convert(w_mem_kv, 8, 1024, WKV, "WKV")
            convert(w_glu, 6, 768, WGLU)
            convert(w_proj_fox, 6, D, WPF)
            convert(w_proj_s5, 6, D, WPS)
            convert(w_proj_mem, 4, D, WPM)
            convert(w_out, 8, D, WOUT)
            mset("pool", WFL3[:], 0.0, ["WFL3"])
            for base in (0, 32, 64):
                ld(WFL3[:, :, base:base + 12], w_in.rearrange("(kt p) c -> p kt c", p=128)[:, :, C_FL:C_FL + 12],
                   ["WFL3"], ["WFL3"], eng="pool")

            def s5t(name, shape, dt=F32):
                return sb(S0, name, shape, dt)
            LR = s5t("LR", [128, 24]); LI = s5t("LI", [128, 24]); LS = s5t("LS", [128, 24])
            ld(LR[:], lam_re.rearrange("(pr g2) p -> (g2 p) pr", g2=2), [], ["LR"], slow=True)
            ld(LI[:], lam_im.rearrange("(pr g2) p -> (g2 p) pr", g2=2), [], ["LI"], slow=True)
            for g2 in range(2):
                ld(LS[g2 * 64:(g2 + 1) * 64, :],
                   log_step.rearrange("o (pr g2) -> o g2 pr", g2=2)[:, g2, :].to_broadcast([64, 24]), [], ["LS"], slow=True)
            STP = s5t("STP", [128, 24]); MAG = s5t("MAG", [128, 24]); ANG = s5t("ANG", [128, 24])
            T0 = s5t("T0", [128, 24]); T1 = s5t("T1", [128, 24]); T2 = s5t("T2", [128, 24]); TI = s5t("TI", [128, 24], I32)
            ABR = s5t("ABR", [128, 24]); ABI = s5t("ABI", [128, 24]); SN = s5t("SN", [128, 24]); CS = s5t("CS", [128, 24])
            FR = s5t("FR", [128, 24]); FI = s5t("FI", [128, 24])
            act(STP[:], LS[:], AF.Exp, ["LS"], ["STP"])
            tt("dve", T0[:], LR[:], STP[:], ALU.mult, ["LR", "STP"], ["T0"])
            act(MAG[:], T0[:], AF.Exp, ["T0"], ["MAG"])
            tt("dve", ANG[:], LI[:], STP[:], ALU.mult, ["LI", "STP"], ["ANG"])

            def sin_of(dst, shift, key):
                ts("dve", T0[:], ANG[:], shift, None, ALU.add, None, ["ANG"], ["T0"])
                ts("dve", T1[:], T0[:], 1.0 / TWO_PI, 0.5, ALU.mult, ALU.add, ["T0"], ["T1"])
                cp("dve", TI[:], T1[:], ["T1"], ["TI"])
                cp("dve", T1[:], TI[:], ["TI"], ["T1"])
                stt(T2[:], T1[:], -TWO_PI, T0[:], ALU.mult, ALU.add, ["T1", "T0"], ["T2"])
                ts("dve", T1[:], T2[:], math.pi, -TWO_PI, ALU.is_gt, ALU.mult, ["T2"], ["T1"])
                tt("dve", T2[:], T2[:], T1[:], ALU.add, ["T2", "T1"], ["T2"])
                ts("dve", T1[:], T2[:], -math.pi, TWO_PI, ALU.is_lt, ALU.mult, ["T2"], ["T1"])
                tt("dve", T2[:], T2[:], T1[:], ALU.add, ["T2", "T1"], ["T2"])
                ts("dve", T2[:], T2[:], math.pi, -math.pi, ALU.min, ALU.max, ["T2"], ["T2"])
                act(dst[:], T2[:], AF.Sin, ["T2"], [key])
            sin_of(SN, 0.0, "SN")
            sin_of(CS, math.pi / 2.0, "CS")
            tt("dve", ABR[:], MAG[:], CS[:], ALU.mult, ["MAG", "CS"], ["ABR"])
            tt("dve", ABI[:], MAG[:], SN[:], ALU.mult, ["MAG", "SN"], ["ABI"])
            DEN = s5t("DEN", [128, 24]); NR = s5t("NR", [128, 24])
            tt("dve", T0[:], LR[:], LR[:], ALU.mult, ["LR"], ["T0"])
            tt("dve", T1[:], LI[:], LI[:], ALU.mult, ["LI"], ["T1"])
            tt("dve", DEN[:], T0[:], T1[:], ALU.add, ["T0", "T1"], ["DEN"])
            P.c("dve", lambda h: h.reciprocal(out=DEN[:], in_=DEN[:]), ["DEN"], ["DEN"])
            ts("dve", NR[:], ABR[:], -1.0, None, ALU.add, None, ["ABR"], ["NR"])
            tt("dve", T0[:], NR[:], LR[:], ALU.mult, ["NR", "LR"], ["T0"])
            tt("dve", T1[:], ABI[:], LI[:], ALU.mult, ["ABI", "LI"], ["T1"])
            tt("dve", T0[:], T0[:], T1[:], ALU.add, ["T0", "T1"], ["T0"])
            tt("dve", FR[:], T0[:], DEN[:], ALU.mult, ["T0", "DEN"], ["FR"])
            tt("dve", T0[:], ABI[:], LR[:], ALU.mult, ["ABI", "LR"], ["T0"])
            tt("dve", T1[:], NR[:], LI[:], ALU.mult, ["NR", "LI"], ["T1"])
            tt("dve", T0[:], T0[:], T1[:], ALU.subtract, ["T0", "T1"], ["T0"])
            tt("dve", FI[:], T0[:], DEN[:], ALU.mult, ["T0", "DEN"], ["FI"])
            PWR = s5t("PWR", [128, 9, 24]); PWI = s5t("PWI", [128, 9, 24])
            AWR = s5t("AWR", [128, 9, 24]); AWI = s5t("AWI", [128, 9, 24])

            def cmul(orr, oi, ar, ai, br, bi, rk, wk):
                tt("dve", T0[:], ar, br, ALU.mult, rk, ["T0"])
                tt("dve", T1[:], ai, bi, ALU.mult, rk, ["T1"])
                tt("dve", T2[:], ar, bi, ALU.mult, rk, ["T2"])
                tt("dve", orr, T0[:], T1[:], ALU.subtract, ["T0", "T1"], wk)
                tt("dve", T0[:], ai, br, ALU.mult, rk + wk, ["T0"])
                tt("dve", oi, T2[:], T0[:], ALU.add, ["T2", "T0"], wk)
            mset("pool", PWR[:, 0, :], 1.0, ["PW"])
            mset("pool", PWI[:, 0, :], 0.0, ["PW"])
            for k in range(1, 9):
                cmul(PWR[:, k, :], PWI[:, k, :], PWR[:, k - 1, :], PWI[:, k - 1, :], ABR[:], ABI[:], ["PW", "ABR", "ABI"], ["PW"])
            mset("pool", AWR[:, 0, :], 1.0, ["AW"])
            mset("pool", AWI[:, 0, :], 0.0, ["AW"])
            for k in range(1, 9):
                cmul(AWR[:, k, :], AWI[:, k, :], AWR[:, k - 1, :], AWI[:, k - 1, :], PWR[:, 8, :], PWI[:, 8, :], ["AW", "PW"], ["AW"])
            cp("dve", A1R[:], AWR[:, 1, :], ["AW"], ["A1"])
            cp("dve", A1I[:], AWI[:, 1, :], ["AW"], ["A1"])
            cp("dve", APR[:], AWR[:, 1:9, :], ["AW"], ["APR"])
            cp("dve", API[:], AWI[:, 1:9, :], ["AW"], ["API"])
            cp("dve", A8R[:], AWR[:, 8, :], ["AW"], ["A8"])
            cp("dve", A8I[:], AWI[:, 8, :], ["AW"], ["A8"])

            BRE = s5t("BRE", [128, 24, 16]); BIM = s5t("BIM", [128, 24, 16])
            ld(BRE[:], s5_b_re.rearrange("(pr g2) p h -> (g2 p) pr h", g2=2), [], ["BRE"])
            ld(BIM[:], s5_b_im.rearrange("(pr g2) p h -> (g2 p) pr h", g2=2), [], ["BIM"])
            BBR = s5t("BBR", [128, 24, 16]); BBI = s5t("BBI", [128, 24, 16])
            U0 = s5t("U0", [128, 24, 16]); U1 = s5t("U1", [128, 24, 16])
            frb = FR[:].unsqueeze(2).to_broadcast([128, 24, 16])
            fib = FI[:].unsqueeze(2).to_broadcast([128, 24, 16])
            tt("dve", U0[:], BRE[:], frb, ALU.mult, ["BRE", "FR"], ["U0"])
            tt("dve", U1[:], BIM[:], fib, ALU.mult, ["BIM", "FI"], ["U1"])
            tt("dve", BBR[:], U0[:], U1[:], ALU.subtract, ["U0", "U1"], ["BBR"])
            tt("dve", U0[:], BIM[:], frb, ALU.mult, ["BIM", "FR"], ["U0"])
            tt("dve", U1[:], BRE[:], fib, ALU.mult, ["BRE", "FI"], ["U1"])
            tt("dve", BBI[:], U0[:], U1[:], ALU.add, ["U0", "U1"], ["BBI"])
            BDR = s5t("BDR", [128, 24, 32]); BDI = s5t("BDI", [128, 24, 32])
            mset("pool", BDR[:], 0.0, ["BDR"]); mset("pool", BDI[:], 0.0, ["BDI"])
            for g2 in range(2):
                sl = slice(g2 * 64, (g2 + 1) * 64)
                cp("dve", BDR[sl, :, g2 * 16:(g2 + 1) * 16], BBR[sl, :, :], ["BBR", "BDR"], ["BDR"])
                cp("dve", BDI[sl, :, g2 * 16:(g2 + 1) * 16], BBI[sl, :, :], ["BBI", "BDI"], ["BDI"])
            BWR = s5t("BWR", [128, 24, 128]); BWI = s5t("BWI", [128, 24, 128])
            mset("pool", BWR[:], 0.0, ["BWR"]); mset("pool", BWI[:], 0.0, ["BWI"])
            for q4 in range(4):
                for o in range(6):
                    pr = 4 * o + q4
                    cp("dve", BWR[:, pr, 32 * q4:32 * q4 + 32], BDR[:, pr, :], ["BDR", "BWR"], ["BWR"])
                    cp("pool", BWI[:, pr, 32 * q4:32 * q4 + 32], BDI[:, pr, :], ["BDI", "BWI"], ["BWI"])
            CTR = s5t("CTR", [128, 24, 16]); CTI = s5t("CTI", [128, 24, 16])
            CN = s5t("CN", [128, 3, 2, 128])
            tp_ps = ps(S0, "tp_ps", [128, 512], F32)
            for ri, (csrc, cdst, key) in enumerate(((s5_c_re, CTR, "CTR"), (s5_c_im, CTI, "CTI"))):
                cv4 = csrc.rearrange("(pr g2) h p -> pr h g2 p", g2=2)
                for pr in range(24):
                    ld(CN[(pr % 8) * 16:(pr % 8) * 16 + 16, pr // 8, ri, :].rearrange("h (a p) -> h a p", a=2),
                       cv4[pr], ["CN%d" % ri], ["CN%d" % ri])
                for j in range(3):
                    tpg([(tp_ps[:, 0:128], CN[:, j, ri, :], ident_f[:])], ["CN%d" % ri, "ident_f", "tp_ps"], ["tp_ps"])
                    cp("act", cdst[:, 8 * j:8 * j + 8, :], tp_ps[:, 0:128].rearrange("p (a h) -> p a h", a=8), ["tp_ps"], [key])
            XR = s5t("XR", [128, 24, 32]); XI = s5t("XI", [128, 24, 32])
            V0 = s5t("V0", [128, 24, 32]); V1 = s5t("V1", [128, 24, 32])
            WZs = [s5t("WZs%d" % i, [128, 6, 2, 128], BF16) for i in range(2)]
            for s in range(8):
                k = 7 - s
                wzs, wzk = WZs[s % 2], "WZs%d" % (s % 2)
                pr_ = PWR[:, k, :].unsqueeze(2).to_broadcast([128, 24, 32])
                pi_ = PWI[:, k, :].unsqueeze(2).to_broadcast([128, 24, 32])
                tt("dve", V0[:], BDR[:], pr_, ALU.mult, ["BDR", "PW"], ["V0"])
                tt("dve", V1[:], BDI[:], pi_, ALU.mult, ["BDI", "PW"], ["V1"])
                tt("dve", XR[:], V0[:], V1[:], ALU.subtract, ["V0", "V1"], ["XR"])
                tt("dve", V0[:], BDI[:], pr_, ALU.mult, ["BDI", "PW"], ["V0"])
                tt("dve", V1[:], BDR[:], pi_, ALU.mult, ["BDR", "PW"], ["V1"])
                tt("dve", XI[:], V0[:], V1[:], ALU.add, ["V0", "V1"], ["XI"])
                for o in range(6):
                    for ri, X in enumerate((XR, XI)):
                        tpg([(tp_ps[:, 0:128], X[:, 4 * o:4 * o + 4, :].rearrange("p a b -> p (a b)"), ident_f[:])], ["XR", "XI", "ident_f", "tp_ps"], ["tp_ps"])
                        cp("act", wzs[:, o, ri, :], tp_ps[:, 0:128], ["tp_ps"], [wzk])
                for o in range(6):
                    ld(WZS[o, :, s, :, :], wzs[:, o, :, :], [wzk], [U()])
            QR = s5t("QR", [128, 24, 16]); QI = s5t("QI", [128, 24, 16])
            QBR = s5t("QBR", [128, 24, 32]); QBI = s5t("QBI", [128, 24, 32])
            WYk = [s5t("WYk%d" % i, [128, 24, 2, 32], BF16) for i in range(2)]
            BDj = [s5t("BDj%d" % i, [128, 6, 128], F32) for i in range(2)]
            BDjb = [s5t("BDjb%d" % i, [128, 6, 128], BF16) for i in range(2)]
            DSK = s5t("DSK", [128, 6], F32)
            ld(DSK[:], s5_d.rearrange("o (c p) -> p (o c)", p=128), [], ["DSK"], slow=True)
            mset("pool", QBR[:], 0.0, ["QBR"]); mset("pool", QBI[:], 0.0, ["QBI"])
            for i in range(2):
                mset("pool", BDj[i][:], 0.0, ["BDj%d" % i])
            bd_ps = ps(S0, "bd_ps", [128, 6, 32], F32)
            for k in range(9):
                pr_ = PWR[:, k, :].unsqueeze(2).to_broadcast([128, 24, 16])
                pi_ = PWI[:, k, :].unsqueeze(2).to_broadcast([128, 24, 16])
                tt("dve", U0[:], CTR[:], pr_, ALU.mult, ["CTR", "PW"], ["U0"])
                tt("dve", U1[:], CTI[:], pi_, ALU.mult, ["CTI", "PW"], ["U1"])
                tt("dve", QR[:], U0[:], U1[:], ALU.subtract, ["U0", "U1"], ["QR"])
                tt("dve", U0[:], CTR[:], pi_, ALU.mult, ["CTR", "PW"], ["U0"])
                tt("dve", U1[:], CTI[:], pr_, ALU.mult, ["CTI", "PW"], ["U1"])
                tt("dve", QI[:], U0[:], U1[:], ALU.add, ["U0", "U1"], ["QI"])
                for g2 in range(2):
                    sl = slice(g2 * 64, (g2 + 1) * 64)
                    cp("dve", QBR[sl, :, g2 * 16:(g2 + 1) * 16], QR[sl, :, :], ["QR", "QBR"], ["QBR"])
                    ts("dve", QBI[sl, :, g2 * 16:(g2 + 1) * 16], QI[sl, :, :], -1.0, None, ALU.mult, None, ["QI", "QBI"], ["QBI"])
                if k >= 1:
                    wyk, wykk = WYk[k % 2], "WYk%d" % (k % 2)
                    cp("act", wyk[:, :, 0, :], QBR[:], ["QBR"], [wykk])
                    cp("act", wyk[:, :, 1, :], QBI[:], ["QBI"], [wykk])
                    for o in range(6):
                        ld(WYS[o, :, :, k - 1, :, :], wyk[:, 4 * o:4 * o + 4, :, :], [wykk], [U()])
                if k <= 7:
                    bdj, bdk = BDj[k % 2], "BDj%d" % (k % 2)
                    bdjb, bdbk = BDjb[k % 2], "BDjb%d" % (k % 2)
                    for o in range(6):
                        calls = []
                        for q4 in range(4):
                            pr = 4 * o + q4
                            calls.append(dict(out=bd_ps[:, o, :], lhsT=BWR[:, pr, :], rhs=QBR[:, pr, :], start=(q4 == 0), stop=False))
                            calls.append(dict(out=bd_ps[:, o, :], lhsT=BWI[:, pr, :], rhs=QBI[:, pr, :], start=False, stop=(q4 == 3)))
                        mmg(calls, ["BWR", "BWI", "QBR", "QBI", "bd_ps"], ["bd_ps"])
                    for q4 in range(4):
                        sl = slice(32 * q4, 32 * q4 + 32)
                        cp("act", bdj[sl, :, 32 * q4:32 * q4 + 32], bd_ps[sl, :, :], ["bd_ps"], [bdk])
                    if k == 0:
                        for o in range(6):
                            stt(bdj[:, o, :], ident_f[:], DSK[:, o:o + 1], bdj[:, o, :], ALU.mult, ALU.add, ["ident_f", "DSK", bdk], [bdk])
                    cp("dve", bdjb[:], bdj[:], [bdk], [bdbk])
                    for o in range(6):
                        ld(BDS[o, :, k, :], bdjb[:, o, :], [bdbk], [U()])
            P.emit()
        if upto < 1:
            return nc
        S0b = ExitStack()
        with S0b:
            S0 = S0b

            def s5t(name, shape, dt=F32):
                return sb(S0b, name, shape, dt)
            mt_x = s5t("mt_x", [128, 2, D], F32)
            mt_n = s5t("mt_n", [128, 2, D], BF16)
            mt_j = s5t("mt_j", [128, D], BF16)
            MHT = s5t("MHT", [128, 8, 256], BF16)
            wkv_t = s5t("wkv_t", [128, 8, 1024], BF16)
            tpb_ps = ps(S0, "tpb_ps", [128, 512], BF16)
            mk_ps = ps(S0, "mk_ps", [128, 512], F32)
            ld(mt_x[:], mem.rearrange("(a p) d -> p a d", p=128), [], ["mt_x"])
            ld(wkv_t[:], WKV, ["WKV"], ["wkv_t"])
            for a in range(2):
                act(mt_j[:], mt_x[:, a, :], AF.Square, ["mt_x"], ["mt_j", "ssq"], accum=ssq[:, a:a + 1])
            ts("dve", rstd[:, 0:2], ssq[:, 0:2], 1.0 / D, EPS, ALU.mult, ALU.add, ["ssq"], ["rstd"])
            act(rstd[:, 0:2], rstd[:, 0:2], AF.Sqrt, ["rstd"], ["rstd"])
            P.c("dve", lambda h: h.reciprocal(out=rstd[:, 0:2], in_=rstd[:, 0:2]), ["rstd"], ["rstd"])
            for a in range(2):
                ts("dve", mt_n[:, a, :], mt_x[:, a, :], rstd[:, a:a + 1], None, ALU.mult, None, ["mt_x", "rstd"], ["mt_n"])
            for kt in range(8):
                tpg([(tpb_ps[:, a * 128:(a + 1) * 128], mt_n[:, a, kt * 128:(kt + 1) * 128], ident_b[:]) for a in range(2)],
                    ["mt_n", "ident_b", "tpb_ps"], ["tpb_ps"])
                ts("dve", MHT[:, kt, :], tpb_ps[:, 0:256], gmn[:, kt:kt + 1], None, ALU.mult, None, ["tpb_ps", "gmn"], ["MHT"])
            for hd in range(4):
                mmg([dict(out=mk_ps[:, 0:256], lhsT=wkv_t[:, kt, hd * 128:(hd + 1) * 128], rhs=MHT[:, kt, :],
                          start=(kt == 0), stop=(kt == 7)) for kt in range(8)], ["wkv_t", "MHT", "mk_ps"], ["mk_ps"])
                cp("act", MKT[:, hd, :], mk_ps[:, 0:256], ["mk_ps"], ["MKT"])
            for mt in range(2):
                mmg([dict(out=mk_ps[:, :], lhsT=MHT[:, kt, mt * 128:(mt + 1) * 128], rhs=wkv_t[:, kt, 512:1024],
                          start=(kt == 0), stop=(kt == 7)) for kt in range(8)], ["wkv_t", "MHT", "mk_ps"], ["mk_ps"])
                cp("act", MV[:, mt, :], mk_ps[:, :], ["mk_ps"], ["MV"])
            P.emit()

        if upto < 2:
            return nc
        S1 = ExitStack()
        with S1:
            xt = sb(S1, "xt", [128, 4, D], F32)
            xn = sb(S1, "xn", [128, 4, D], BF16)
            sqj = sb(S1, "sqj", [128, D], BF16)
            hTs = [sb(S1, "hT%d" % i, [128, 8, TT], BF16) for i in range(2)]
            NWS = 3
            wsl = [sb(S1, "ws%d" % i, [128, 4096], BF16) for i in range(NWS)]
            SPt = [sb(S1, "SPt%d" % i, [128, TT], BF16) for i in range(2)]
            HIb = sb(S1, "HIb", [128, TT], BF16)
            MIDb = sb(S1, "MIDb", [128, TT], BF16)
            kst = [sb(S1, "kst%d" % i, [128, TT], BF16) for i in range(6)]
            Vst = sb(S1, "Vst", [128, 4, 12, 65], BF16)
            uT = sb(S1, "uT", [128, 6, TT], BF16)
            s5w = [(sb(S1, "s5bd%d" % i, [128, 8, 128], BF16), sb(S1, "s5wz%d" % i, [128, 8, 2, 128], BF16),
                    sb(S1, "s5wy%d" % i, [128, 4, 8, 2, 32], BF16)) for i in range(2)]
            Z = sb(S1, "Z", [128, 24, 2, 64], F32)
            Hb = sb(S1, "Hb", [128, 24, 2, 65], BF16)
            RT = [sb(S1, "RT%d" % i, [128, 24, 8], F32) for i in range(8)]
            RS = [sb(S1, "RS%d" % i, [128, 24], F32) for i in range(8)]
            tA = sb(S1, "tA", [128, TT], F32)
            tB = sb(S1, "tB", [128, TT], F32)
            tC = sb(S1, "tC", [128, TT], F32)
            Fe = sb(S1, "Fe", [76, TT], F32)
            Ff = sb(S1, "Ff", [76, TT], F32)
            R1 = sb(S1, "R1", [76, TT], F32)
            YG = sb(S1, "YG", [128, 6, TT], BF16)
            SGS = sb(S1, "SGS", [128, 6, TT], BF16)
            YS = SGS
            QM = sb(S1, "QM", [128, 4, TT], BF16)
            SGM = sb(S1, "SGM", [128, 4, TT], BF16)
            PmT = sb(S1, "PmT", [128, 2, TT], BF16)
            YM = SGM
            M0 = [sb(S1, "M0_%d" % i, [128, TT], F32) for i in range(3)]
            G0st = [sb(S1, "G0st%d" % i, [128, TT], BF16) for i in range(4)]
            pj = [ps(S1, "pj%d" % i, [128, 512], F32) for i in range(2)]
            tpall = ps(S1, "tpall", [128, 1024], BF16)
            tpp = [tpall[:, 0:512], tpall[:, 0:512]]
            zpsb = [ps(S1, "zps%d" % i, [128, 512], F32) for i in range(4)]
            yps = ps(S1, "yps", [128, 512], F32)

            import os
            if not os.environ.get('SKIPM'):
                for i in range(2):
                    mset("pool", SPt[i][:], 0.0, ["SPt%d" % i])
                    mset("pool", SPt[i][96:97, :], 1.0, ["SPt%d" % i])
                mset("pool", Vst[:], 1.0, ["Vst"])

            wn = [0]
            pjn = [0]

            def wload(src3, nkt, ncols):
                i = wn[0] % NWS
                wn[0] += 1
                t = wsl[i][:, 0:nkt * ncols].rearrange("p (k c) -> p k c", k=nkt)
                if os.environ.get('WLDTINY'):
                    ld(t[:, 0:1, 0:16], src3[:, 0:1, 0:16], [], ["ws%d" % i])
                else:
                    ld(t, src3, [], ["ws%d" % i])
                return t, "ws%d" % i

            PJL = [(pj[0], "pj0"), (pj[1], "pj1"), (zpsb[0], "zps0"), (zpsb[1], "zps1"), (zpsb[2], "zps2"), (zpsb[3], "zps3"), (yps, "yps")]

            def nextpj():
                i = pjn[0] % len(PJL)
                pjn[0] += 1
                return PJL[i]

            import os
            P1T = int(os.environ.get('P1T', NT)); P1S = int(os.environ.get('P1S', 99)); OWNX = int(os.environ.get('OWNX', OWN0))
            def prepA(it):
                if it == 0:
                    ld(xt[:], xa[it * TT:(it + 1) * TT, :].rearrange("(a p) d -> p a d", p=128), [], ["xt"])
                for a in range(4):
                    act(sqj[:], xt[:, a, :], AF.Square, ["xt"], ["sqj", "ssq"], accum=ssq[:, a:a + 1])
                ts("dve", rstd[:, 0:4], ssq[:, 0:4], 1.0 / D, EPS, ALU.mult, ALU.add, ["ssq"], ["rstd"])
                act(rstd[:, 0:4], rstd[:, 0:4], AF.Sqrt, ["rstd"], ["rstd"])
                P.c("dve", lambda h: h.reciprocal(out=rstd[:, 0:4], in_=rstd[:, 0:4]), ["rstd"], ["rstd"])
                for a in range(4):
                    ts("dve", xn[:, a, :], xt[:, a, :], rstd[:, a:a + 1], None, ALU.mult, None, ["xt", "rstd"], ["xn"])
                if it + 1 < P1T:
                    ld(xt[:], xa[(it + 1) * TT:(it + 2) * TT, :].rearrange("(a p) d -> p a d", p=128), [], ["xt"])

            def prepB(it):
                hTn, hkn = hTs[it % 2], "hT%d" % (it % 2)
                sptn, spkn = SPt[it % 2], "SPt%d" % (it % 2)
                ld(sptn[76:77, :], kmrow[it:it + 1, :], [], [spkn], eng="pool")
                for kt in range(8):
                    tp, tk = tpp[kt % 2], "tpp0"
                    tpg([(tp[:, a * 128:(a + 1) * 128], xn[:, a, kt * 128:(kt + 1) * 128], ident_b[:]) for a in range(4)],
                        ["xn", "ident_b", tk], [tk])
                    if kt % 2 == 0:
                        ts("dve", hTn[:, kt, :], tp[:, :], gn[:, kt:kt + 1], None, ALU.mult, None, [tk, "gn"], [hkn])
                    else:
                        act(hTn[:, kt, :], tp[:, :], AF.Identity, [tk, "gn"], [hkn], scale=gn[:, kt:kt + 1])
                pjt, pk = nextpj()
                mmg([dict(out=pjt[0:76, :], lhsT=WFL3[:, kt, :], rhs=hTn[:, kt, :], start=(kt == 0), stop=(kt == 7)) for kt in range(8)],
                    ["WFL3", hkn, pk], [pk])
                act(Fe[0:76, :], pjt[0:76, :], AF.Exp, [pk, "negb"], ["Fe"], bias=negb[0:76, :], scale=-1.0)
                act(Fe[0:76, :], Fe[0:76, :], AF.Ln, ["Fe"], ["Fe"], bias=1.0)
                P.c("dve", lambda h: h.tensor_tensor_scan(out=Ff[0:76, :], data0=ones_f[0:76, :], data1=Fe[0:76, :],
                                                          initial=Fcar[0:76, :], op0=ALU.mult, op1=ALU.subtract),
                    ["ones_f", "Fe", "Fcar"], ["Ff"])
                cp("dve", Fcar[0:76, :], Ff[0:76, TT - 1:TT], ["Ff"], ["Fcar"])
                cp("dve", HIb[0:76, :], Ff[0:76, :], ["Ff"], ["HIb"])
                stt(R1[0:76, :], HIb[0:76, :], nm1[0:76, :], Ff[0:76, :], ALU.mult, ALU.add, ["HIb", "nm1", "Ff"], ["R1"])
                tt("dve", MIDb[0:76, :], Ff[0:76, :], HIb[0:76, :], ALU.subtract, ["Ff", "HIb"], ["MIDb"])
                stt(R1[0:76, :], MIDb[0:76, :], nm2[0:76, :], R1[0:76, :], ALU.mult, ALU.add, ["MIDb", "nm2", "R1"], ["R1"])
                cp("dve", sptn[0:76, :], R1[0:76, :], ["R1"], [spkn])

            prepA(0)
            prepB(0)
            for it in range(P1T):
                own = it >= OWNX
                ot = it - OWNX
                sp_i = it % 2
                spt, spk = SPt[sp_i], "SPt%d" % sp_i
                hT, hk = hTs[it % 2], "hT%d" % (it % 2)
                if P1S < 5:
                    continue
                wv1, wv1k = wload(WB[:, :, C_V:C_V + 512], 8, 512)
                wv2, wv2k = wload(WB[:, :, C_V + 512:C_V + 768], 8, 256)
                for a in range(4):
                    p1, p1k = nextpj()
                    p2, p2k = nextpj()
                    mmg([dict(out=p1[:, :], lhsT=hT[:, kt, a * 128:(a + 1) * 128], rhs=wv1[:, kt, :], start=(kt == 0), stop=(kt == 7)) for kt in range(8)],
                        [hk, wv1k, p1k], [p1k])
                    mmg([dict(out=p2[:, 0:256], lhsT=hT[:, kt, a * 128:(a + 1) * 128], rhs=wv2[:, kt, :], start=(kt == 0), stop=(kt == 7)) for kt in range(8)],
                        [hk, wv2k, p2k], [p2k])
                    cp("act", Vst[:, a, 0:8, 0:64], p1[:, :].rearrange("p (h d) -> p h d", h=8), [p1k], ["Vst"])
                    cp("dve", Vst[:, a, 8:12, 0:64], p2[:, 0:256].rearrange("p (h d) -> p h d", h=4), [p2k], ["Vst"])
                ld(VS[4 * it:4 * it + 4].rearrange("a p f -> p a f"), Vst[:].rearrange("p a h d -> p a (h d)"), ["Vst"], [U()], eng="act")
                if P1S < 6:
                    continue
                for o in range(6):
                    if o % 4 == 0:
                        n = min(512, 768 - o * 128)
                        wk, wkk = wload(WB[:, :, C_U + o * 128:C_U + o * 128 + n], 8, n)
                    pjt, pk = nextpj()
                    c0 = (o % 4) * 128
                    mmg([dict(out=pjt[:, :], lhsT=wk[:, kt, c0:c0 + 128], rhs=hT[:, kt, :], start=(kt == 0), stop=(kt == 7)) for kt in range(8)],
                        [wkk, hk, pk], [pk])
                    cp("act" if o % 2 == 0 else "dve", uT[:, o, :].rearrange("p (s c) -> p c s", s=8), pjt[:, :].rearrange("p (c s) -> p c s", s=8), [pk], ["uT"])
                if P1S < 3:
                    continue
                wk, wkk = None, None
                for h_ in range(12):
                    if h_ % 8 == 0:
                        n = min(512, 768 - h_ * 64)
                        wk, wkk = wload(WB[:, :, C_K + h_ * 64:C_K + h_ * 64 + n], 8, n)
                    pjt, pk = nextpj()
                    c0 = (h_ % 8) * 64
                    calls = [dict(out=pjt[64:128, :], lhsT=SELK[:, h_, 64:128], rhs=spt[:, :], start=True, stop=True)]
                    calls += [dict(out=pjt[0:64, :], lhsT=wk[:, kt, c0:c0 + 64], rhs=hT[:, kt, :], start=(kt == 0), stop=(kt == 7)) for kt in range(8)]
                    mmg(calls, ["SELK", spk, wkk, hk, pk], [pk])
                    ks, kk = kst[h_ % 6], "kst%d" % (h_ % 6)
                    if h_ % 2 == 0:
                        cp("act", ks[:], pjt[:, :], [pk], [kk])
                        ld(KT[h_, :, it * TT:(it + 1) * TT], ks[:], [kk], [U()], eng="act")
                    else:
                        cp("dve", ks[:], pjt[:, :], [pk], [kk])
                        ld(KT[h_, :, it * TT:(it + 1) * TT], ks[:], [kk], [U()], eng="act")
                if P1S < 4:
                    continue
                if own:
                    for h_ in range(12):
                        if h_ % 8 == 0:
                            n = min(512, 768 - h_ * 64)
                            wk, wkk = wload(WB[:, :, C_Q + h_ * 64:C_Q + h_ * 64 + n], 8, n)
                        pjt, pk = nextpj()
                        c0 = (h_ % 8) * 64
                        calls = [dict(out=pjt[64:128, :], lhsT=SELQ[:, h_, 64:128], rhs=spt[:, :], start=True, stop=True)]
                        calls += [dict(out=pjt[0:64, :], lhsT=wk[:, kt, c0:c0 + 64], rhs=hT[:, kt, :], start=(kt == 0), stop=(kt == 7)) for kt in range(8)]
                        mmg(calls, ["SELQ", spk, wkk, hk, pk], [pk])
                        ks, kk = kst[h_ % 6], "kst%d" % (h_ % 6)
                        act(ks[:], pjt[:, :], AF.Identity, [pk, "qscale"], [kk], scale=qscale[:, :])
                        ld(QT[h_, :, ot * TT:(ot + 1) * TT], ks[:], [kk], [U()], eng="act")
                if P1S < 7:
                    continue
                s5slots = []
                for o in range(6):
                    bd_t, wz_t, wy_t = s5w[o % 2]
                    sk = "s5w%d" % (o % 2)
                    ld(wz_t[:], WZS[o], [], [sk + "z"])
                    calls = []
                    for ri in range(2):
                        for s in range(8):
                            for q4 in range(4):
                                calls.append(dict(out=zpsb[q4][:, ri * 64:(ri + 1) * 64], lhsT=wz_t[32 * q4:32 * q4 + 32, s, ri, :],
                                                  rhs=uT[32 * q4:32 * q4 + 32, o, s * 64:(s + 1) * 64],
                                                  start=(s == 0), stop=(s == 7), tile_position=(32 * q4, 0)))
                    mmg(calls, [sk + "z", "uT"] + ["zps%d" % q for q in range(4)], ["zps%d" % q for q in range(4)])
                    for q4 in range(4):
                        cp("act" if q4 % 2 == 0 else "dve", Z[:, 4 * o + q4, :, :], zpsb[q4][:, 0:128].rearrange("p (i c) -> p i c", i=2), ["zps%d" % q4], ["Zj%d" % j for j in range(8)])
                if it + 1 < P1T:
                    prepA(it + 1)
                    prepB(it + 1)
                if P1S < 8:
                    continue
                if it > 0:
                    cp(os.environ.get('RENG', 'pool'), TB[:, :, :, 0], TB[:, :, :, 8], ["TB"], ["TB"])
                Zv = Z[:].rearrange("p r i (b j) -> p r i b j", j=8)

                RENG = os.environ.get('RENG', 'pool')
                cmn = [0]

                def cmac(dre, dim_, sre, sim, mr, mi, Tsets, rkeys, wkey):
                    si_ = cmn[0] % len(Tsets)
                    cmn[0] += 1
                    T = Tsets[si_]
                    tk = ["RT%d_%d" % (si_, q) for q in range(4)]
                    tt(RENG, T[0], mr, sre, ALU.mult, rkeys, [tk[0]])
                    tt(RENG, T[1], mi, sim, ALU.mult, rkeys, [tk[1]])
                    tt(RENG, T[2], mr, sim, ALU.mult, rkeys, [tk[2]])
                    tt(RENG, T[3], mi, sre, ALU.mult, rkeys, [tk[3]])
                    tt(RENG, T[0], T[0], T[1], ALU.subtract, [tk[0], tk[1]], [tk[0]])
                    tt(RENG, T[2], T[2], T[3], ALU.add, [tk[2], tk[3]], [tk[2]])
                    tt(RENG, dre, dre, T[0], ALU.add, [tk[0], wkey], [wkey])
                    tt(RENG, dim_, dim_, T[2], ALU.add, [tk[2], wkey], [wkey])
                RTs = [[t[:] for t in RT[0:4]], [t[:] for t in RT[4:8]]]
                RSs = [[t[:] for t in RS[0:4]], [t[:] for t in RS[4:8]]]
                zkeys = ["Zj%d" % j for j in range(8)]
                a1r = A1R[:].unsqueeze(2).to_broadcast([128, 24, 8])
                a1i = A1I[:].unsqueeze(2).to_broadcast([128, 24, 8])
                for j in range(1, 8):
                    cmac(Zv[:, :, 0, :, j], Zv[:, :, 1, :, j], Zv[:, :, 0, :, j - 1], Zv[:, :, 1, :, j - 1], a1r, a1i, RTs, [zkeys[j - 1]], zkeys[j])
                for b_ in range(8):
                    cp(RENG, TB[:, :, :, b_ + 1], Zv[:, :, :, b_, 7], [zkeys[7], "TB"], ["TB"])
                    cmac(TB[:, :, 0, b_ + 1], TB[:, :, 1, b_ + 1], TB[:, :, 0, b_], TB[:, :, 1, b_], A8R[:], A8I[:], RSs, ["TB"], "TB")
                for j in range(8):
                    cmac(Zv[:, :, 0, :, j], Zv[:, :, 1, :, j], TB[:, :, 0, 0:8], TB[:, :, 1, 0:8],
                         APR[:, j, :].unsqueeze(2).to_broadcast([128, 24, 8]), API[:, j, :].unsqueeze(2).to_broadcast([128, 24, 8]), RTs, ["TB"], zkeys[j])
                if own:
                    cp(RENG, Hb[:, :, :, 0], TB[:, :, :, 0], ["TB"], ["Hb"])
                    cp(RENG, Hb[:, :, :, 1:65], Z[:, :, :, :], zkeys, ["Hb"])
                if not own:
                    continue
                if P1S < 9:
                    continue
                for o in range(6):
                    if o % 4 == 0:
                        n = min(512, 768 - o * 128)
                        wk, wkk = wload(WB[:, :, C_GS + o * 128:C_GS + o * 128 + n], 8, n)
                    pjt, pk = nextpj()
                    c0 = (o % 4) * 128
                    mmg([dict(out=pjt[:, :], lhsT=wk[:, kt, c0:c0 + 128], rhs=hT[:, kt, :], start=(kt == 0), stop=(kt == 7)) for kt in range(8)],
                        [wkk, hk, pk], [pk])
                    act(SGS[:, o, :], pjt[:, :], AF.Silu, [pk], ["SGS"])
                if P1S < 12:
                    continue
                wk, wkk = wload(WB[:, :, C_QM:C_QM + 512], 8, 512)
                for hd in range(4):
                    pjt, pk = nextpj()
                    mmg([dict(out=pjt[:, :], lhsT=wk[:, kt, hd * 128:(hd + 1) * 128], rhs=hT[:, kt, :], start=(kt == 0), stop=(kt == 7)) for kt in range(8)],
                        [wkk, hk, pk], [pk])
                    act(QM[:, hd, :], pjt[:, :], AF.Identity, [pk], ["QM"], scale=128.0 ** -0.5)
                wk, wkk = wload(WB[:, :, C_GM:C_GM + 512], 8, 512)
                for hd in range(4):
                    pjt, pk = nextpj()
                    mmg([dict(out=pjt[:, :], lhsT=wk[:, kt, hd * 128:(hd + 1) * 128], rhs=hT[:, kt, :], start=(kt == 0), stop=(kt == 7)) for kt in range(8)],
                        [wkk, hk, pk], [pk])
                    act(SGM[:, hd, :], pjt[:, :], AF.Silu, [pk], ["SGM"])
                for hd in range(4):
                    for mt in range(2):
                        pjt, pk = nextpj()
                        mmg([dict(out=pjt[:, :], lhsT=MKT[:, hd, mt * 128:(mt + 1) * 128], rhs=QM[:, hd, :], start=True, stop=True)],
                            ["MKT", "QM", pk], [pk])
                        act(PmT[:, mt, :], pjt[:, :], AF.Exp, [pk], ["PmT%d" % mt])
                    pjt, pk = nextpj()
                    mmg([dict(out=pjt[:, :], lhsT=MV[:, mt, hd * 128:(hd + 1) * 128], rhs=PmT[:, mt, :], start=(mt == 0), stop=(mt == 1)) for mt in range(2)],
                        ["MV", "PmT0", "PmT1", pk], [pk])
                    axp, axk = nextpj()
                    mmg([dict(out=axp[:, :], lhsT=ones_b[:, :], rhs=PmT[:, mt, :], start=(mt == 0), stop=(mt == 1)) for mt in range(2)],
                        ["ones_b", "PmT0", "PmT1", axk], [axk])
                    P.c("dve", lambda h, axp=axp: h.reciprocal(out=tC[:], in_=axp[:, :]), [axk], ["tC"])
                    tt("dve", tB[:], tC[:], pjt[:, :], ALU.mult, ["tC", pk], ["tB"])
                    tt("dve", YM[:, hd, :], tB[:], SGM[:, hd, :], ALU.mult, ["tB", "SGM"], ["SGM"])
                if P1S < 14:
                    continue
                for cb in range(8):
                    if cb % 4 == 0:
                        wg0, wg0k = wload(WB[:, :, C_GL + cb * 128:C_GL + cb * 128 + 512], 8, 512)
                    c0 = (cb % 4) * 128
                    pjt, pk = nextpj()
                    mmg([dict(out=pjt[:, :], lhsT=wg0[:, kt, c0:c0 + 128], rhs=hT[:, kt, :], start=(kt == 0), stop=(kt == 7)) for kt in range(8)],
                        [wg0k, hk, pk], [pk])
                    g0, g0k = G0st[cb % 4], "G0st%d" % (cb % 4)
                    act(g0[:], pjt[:, :], AF.Sigmoid, [pk, "bm"], [g0k], bias=bm[:, cb:cb + 1])
                    ld(G0S[ot, :, cb, :], g0[:], [g0k], [U()], eng="act")
                for o in range(6):
                    if o % 4 == 0:
                        n = min(512, 768 - o * 128)
                        wk, wkk = wload(WB[:, :, C_GF + o * 128:C_GF + o * 128 + n], 8, n)
                    pjt, pk = nextpj()
                    c0 = (o % 4) * 128
                    mmg([dict(out=pjt[:, :], lhsT=wk[:, kt, c0:c0 + 128], rhs=hT[:, kt, :], start=(kt == 0), stop=(kt == 7)) for kt in range(8)],
                        [wkk, hk, pk], [pk])
                    ks, kk = kst[o % 6], "kst%d" % (o % 6)
                    act(ks[:], pjt[:, :], AF.Silu, [pk], [kk])
                    ld(SGFS[ot, :, o, :], ks[:], [kk], [U()], eng="act")
                if P1S < 10:
                    continue
                for o in range(6):
                    bd_t, wz_t, wy_t = s5w[o % 2]
                    sk = "s5w%d" % (o % 2)
                    ld(bd_t[:], BDS[o], [], [sk + "b"])
                    ld(wy_t[:], WYS[o], [], [sk + "y"])
                    calls = []
                    for j in range(8):
                        calls.append(dict(out=yps[:, j * 64:512], lhsT=bd_t[:, j, :], rhs=uT[:, o, 0:(8 - j) * 64], start=(j == 0), stop=False))
                    for tau in range(8):
                        for ri in range(2):
                            for q4 in range(4):
                                last = (tau == 7 and ri == 1)
                                calls.append(dict(out=yps[32 * q4:32 * q4 + 32, tau * 64:(tau + 1) * 64], lhsT=wy_t[:, q4, tau, ri, :],
                                                  rhs=Hb[:, 4 * o + q4, ri, 0:64], start=False, stop=last, tile_position=(0, 32 * q4)))
                    mmg(calls, [sk + "b", sk + "y", "uT", "Hb", "yps"], ["yps"])
                    ypv = yps[:, :].rearrange("p (s c) -> p c s", s=8)
                    v3 = lambda ap_: ap_.rearrange("p (c s) -> p c s", s=8)
                    act(v3(tA[:]), ypv, AF.Square, ["yps"], ["tA"])
                    ts("dve", tA[:], tA[:], 0.044715, 1.0, ALU.mult, ALU.add, ["tA"], ["tA"])
                    tt("dve", v3(tB[:]), v3(tA[:]), ypv, ALU.mult, ["tA", "yps"], ["tB"])
                    act(tA[:], tB[:], AF.Sigmoid, ["tB"], ["tA"], scale=1.5957691216057308)
                    tt("dve", v3(YG[:, o, :]), v3(tA[:]), ypv, ALU.mult, ["tA", "yps"], ["YG"])
                if P1S < 11:
                    continue
                for half in range(2):
                    wg, wgk = wload(WGLU[:, :, half * 384:(half + 1) * 384], 6, 384)
                    for c3 in range(3):
                        cb = half * 3 + c3
                        pjt, pk = nextpj()
                        mmg([dict(out=pjt[:, :], lhsT=wg[:, kt, c3 * 128:(c3 + 1) * 128], rhs=YG[:, kt, :], start=(kt == 0), stop=(kt == 5)) for kt in range(6)],
                            [wgk, "YG", pk], [pk])
                        act(tA[:], pjt[:, :], AF.Sigmoid, [pk, "bglu"], ["tA"], bias=bglu[:, cb:cb + 1])
                        tt("dve", tB[:], tA[:], YG[:, cb, :], ALU.mult, ["tA", "YG"], ["tB"])
                        tt("dve", YS[:, cb, :], tB[:], SGS[:, cb, :], ALU.mult, ["tB", "SGS"], ["SGS"])
                if P1S < 13:
                    continue
                for cb in range(8):
                    si = wn[0] % NWS
                    wn[0] += 1
                    wfull = wsl[si][:, 0:26 * 128].rearrange("p (k c) -> p k c", k=26)
                    wpsk = wpmk = wg1k = wg2k = "ws%d" % si
                    wps_t, wpm_t, wg1, wg2 = wfull[:, 0:6, :], wfull[:, 6:10, :], wfull[:, 10:18, :], wfull[:, 18:26, :]
                    ld(wps_t, WPS[:, :, cb * 128:(cb + 1) * 128], [], [wpsk])
                    ld(wpm_t, WPM[:, :, cb * 128:(cb + 1) * 128], [wpsk], [wpsk])
                    ld(wg1, WB[:, :, C_GL + 1024 + cb * 128:C_GL + 1024 + (cb + 1) * 128], [wpsk], [wpsk])
                    ld(wg2, WB[:, :, C_GL + 2048 + cb * 128:C_GL + 2048 + (cb + 1) * 128], [wpsk], [wpsk])
                    c0 = 0
                    m0, m0k = M0[cb % 3], "M0_%d" % (cb % 3)
                    pjt, pk = nextpj()
                    mmg([dict(out=pjt[:, :], lhsT=wg1[:, kt, c0:c0 + 128], rhs=hT[:, kt, :], start=(kt == 0), stop=(kt == 7)) for kt in range(8)],
                        [wg1k, hk, pk], [pk])
                    act(tA[:], pjt[:, :], AF.Sigmoid, [pk, "bm"], ["tA"], bias=bm[:, 8 + cb:9 + cb])
                    pjt, pk = nextpj()
                    mmg([dict(out=pjt[:, :], lhsT=wps_t[:, kt, c0:c0 + 128], rhs=YS[:, kt, :], start=(kt == 0), stop=(kt == 5)) for kt in range(6)],
                        [wpsk, "SGS", pk], [pk])
                    tt("dve", m0[:], tA[:], pjt[:, :], ALU.mult, ["tA", pk], [m0k])
                    pjt, pk = nextpj()
                    mmg([dict(out=pjt[:, :], lhsT=wg2[:, kt, c0:c0 + 128], rhs=hT[:, kt, :], start=(kt == 0), stop=(kt == 7)) for kt in range(8)],
                        [wg2k, hk, pk], [pk])
                    act(tB[:], pjt[:, :], AF.Sigmoid, [pk, "bm"], ["tB"], bias=bm[:, 16 + cb:17 + cb])
                    pjt, pk = nextpj()
                    mmg([dict(out=pjt[:, :], lhsT=wpm_t[:, kt, c0:c0 + 128], rhs=YM[:, kt, :], start=(kt == 0), stop=(kt == 3)) for kt in range(4)],
                        [wpmk, "SGM", pk], [pk])
                    tt("dve", tC[:], tB[:], pjt[:, :], ALU.mult, ["tB", pk], ["tC"])
                    tt("dve", m0[:], m0[:], tC[:], ALU.add, [m0k, "tC"], [m0k])
                    ld(M0S[ot, :, cb, :], m0[:], [m0k], [U()], eng="act")
            P.emit()

        if upto < 3:
            return nc
        S2 = ExitStack()
        with S2:
            Kh = [sb(S2, "Kh%d" % i, [128, LTOK], BF16) for i in range(2)]
            Vh = [sb(S2, "Vh%d" % i, [128, 64, 65], BF16) for i in range(2)]
            Qh = [sb(S2, "Qh%d" % i, [128, 4096], BF16) for i in range(2)]
            NPT = 4
            Pt = [sb(S2, "Pt%d" % i, [128, TT], BF16) for i in range(NPT)]
            Osb = sb(S2, "Osb", [65, TT], F32)
            Rd = sb(S2, "Rd", [64, TT], F32)
            Yst = [sb(S2, "Yst%d" % i, [64, TT], BF16) for i in range(2)]
            SEL = sb(S2, "SEL", [65, 64], F32)
            sps = [ps(S2, "sps%d" % i, [128, 512], F32) for i in range(4)]
            ops_ = [ps(S2, "ops%d" % i, [128, 512], F32) for i in range(2)]
            dps = ps(S2, "dps", [128, 512], F32)
            mset("pool", SEL[:], 0.0, ["SEL"])
            mset("pool", SEL[64:65, :], 1.0, ["SEL"])
            LA = 2
            its = []
            for h_ in range(12):
                for qt in range(8):
                    nkb = 32 + 4 * qt + 4
                    order = list(range(32 + 4 * qt, nkb)) + list(range(0, 32 + 4 * qt))
                    for n_, kb in enumerate(order):
                        its.append((h_, qt, kb, n_ == 0, n_ == nkb - 1))

            def head_loads(h_):
                hs = h_ % 2
                kh, vh, qh = Kh[hs], Vh[hs], Qh[hs]
                kk, vk, qk = "Kh%d" % hs, "Vh%d" % hs, "Qh%d" % hs
                ld(qh[:], QT[h_], [], [qk])
                for c4 in range(4):
                    ld(kh[:, c4 * 2048:(c4 + 1) * 2048], KT[h_, :, c4 * 2048:(c4 + 1) * 2048], [qk], [kk])
                for c4 in range(4):
                    ld(vh[:, c4 * 16:(c4 + 1) * 16, :], VS[c4 * 16:(c4 + 1) * 16, :, h_ * 65:(h_ + 1) * 65].rearrange("b p d -> p b d"), [kk], [vk])

            def geom(i):
                h_, qt, kb, first, last = its[i]
                diag = kb - (32 + 4 * qt)
                c0 = max(0, diag) * 128
                return h_, qt, kb, first, last, diag, c0

            def emit_qk(i):
                h_, qt, kb, first, last, diag, c0 = geom(i)
                hs = h_ % 2
                st, sk = sps[i % 4], "sps%d" % (i % 4)
                mmg([dict(out=st[:, c0:TT], lhsT=Kh[hs][:, kb * 128:(kb + 1) * 128], rhs=Qh[hs][:, qt * TT + c0:(qt + 1) * TT], start=True, stop=True)],
                    ["Kh%d" % hs, "Qh%d" % hs, sk], [sk])

            pend = []

            def emit_rest(i):
                h_, qt, kb, first, last, diag, c0 = geom(i)
                hs = h_ % 2
                st, sk = sps[i % 4], "sps%d" % (i % 4)
                pt, ptk = Pt[i % NPT], "Pt%d" % (i % NPT)
                op_t, opk = ops_[(h_ * 8 + qt) % 2], "ops%d" % ((h_ * 8 + qt) % 2)
                act(pt[:, c0:TT], st[:, c0:TT], AF.Exp, [sk], [ptk])
                if diag >= 0:
                    tt("dve", pt[:, c0:c0 + 128], pt[:, c0:c0 + 128], maskT[:], ALU.mult, [ptk, "maskT"], [ptk])
                mmg([dict(out=op_t[0:65, c0:TT], lhsT=Vh[hs][:, kb, :], rhs=pt[:, c0:TT], start=first, stop=last)],
                    ["Vh%d" % hs, ptk, opk], [opk])
                if last:
                    cp("dve", Osb[:], op_t[0:65, :], [opk], ["Osb"])
                    pend.append((i + 3, h_, qt))

            def emit_norm(h_, qt):
                mmg([dict(out=dps[0:64, :], lhsT=SEL[:, :], rhs=Osb[:, :], start=True, stop=True)], ["SEL", "Osb", "dps"], ["dps"])
                P.c("dve", lambda h: h.reciprocal(out=Rd[:], in_=dps[0:64, :]), ["dps"], ["Rd"])
                ys, ysk = Yst[qt % 2], "Yst%d" % (qt % 2)
                tt("dve", ys[:], Osb[0:64, :], Rd[:], ALU.mult, ["Osb", "Rd"], [ysk])
                ld(YFS[h_, :, qt * TT:(qt + 1) * TT], ys[:], [ysk], [U()], eng="sp")

            head_loads(0)
            head_loads(1)
            NI = len(its)
            for i in range(min(LA, NI)):
                emit_qk(i)
            for i in range(NI):
                if i + LA < NI:
                    emit_qk(i + LA)
                emit_rest(i)
                while pend and pend[0][0] <= i:
                    _, ph, pq = pend.pop(0)
                    emit_norm(ph, pq)
                if its[i][4] and its[i][1] == 7 and its[i][0] + 2 < 12:
                    head_loads(its[i][0] + 2)
            while pend:
                _, ph, pq = pend.pop(0)
                emit_norm(ph, pq)
            P.emit()

        if upto < 4:
            return nc
        S3 = ExitStack()
        with S3:
            gfin = sb(S3, "gfin", [128, D], F32)
            ld(gfin[:], g_final.to_broadcast([128, D]), [], ["gfin"])
            wpf = sb(S3, "wpf", [128, 6, D], BF16)
            wo = sb(S3, "wo", [128, 8, D], BF16)
            yf = sb(S3, "yf", [128, 6, TT], BF16)
            sgf = sb(S3, "sgf", [128, 6, TT], BF16)
            yfg = sb(S3, "yfg", [128, 6, TT], BF16)
            g0t = sb(S3, "g0t", [128, 8, TT], BF16)
            m0t = sb(S3, "m0t", [128, 8, TT], F32)
            mg = sb(S3, "mg", [128, 8, TT], BF16)
            t3 = sb(S3, "t3", [128, TT], F32)
            x3 = sb(S3, "x3", [128, 4, D], F32)
            o3 = [sb(S3, "o3_%d" % i, [128, D], F32) for i in range(2)]
            j3 = sb(S3, "j3", [128, D], BF16)
            pf = [ps(S3, "pf%d" % i, [128, 512], F32) for i in range(2)]
            po = [ps(S3, "po%d" % i, [128, 512], F32) for i in range(4)]
            ld(wpf[:], WPF, [], ["wpf"])
            ld(wo[:], WOUT, [], ["wo"])
            for ot in range(8):
                for a2 in range(2):
                    ld(yf[a2 * 64:(a2 + 1) * 64, :, :], YFS[:, :, ot * TT:(ot + 1) * TT].rearrange("(c a) d t -> a d c t", a=2)[a2], ["yf"], ["yf"])
                ld(sgf[:], SGFS[ot], [], ["sgf"])
                ld(g0t[:], G0S[ot], [], ["g0t"])
                ld(m0t[:], M0S[ot], [], ["m0t"])
                ld(x3[:], xa[(OWN0 + ot) * TT:(OWN0 + ot + 1) * TT, :].rearrange("(a p) d -> p a d", p=128), [], ["x3"])
                for c in range(6):
                    tt("pool" if c % 2 else "dve", yfg[:, c, :], yf[:, c, :], sgf[:, c, :], ALU.mult, ["yf", "sgf"], ["yfg"])
                for cb in range(8):
                    pt_, pk = pf[cb % 2], "pf%d" % (cb % 2)
                    mmg([dict(out=pt_[:, :], lhsT=wpf[:, kt, cb * 128:(cb + 1) * 128], rhs=yfg[:, kt, :], start=(kt == 0), stop=(kt == 5)) for kt in range(6)],
                        ["wpf", "yfg", pk], [pk])
                    tt("dve", t3[:], g0t[:, cb, :], pt_[:, :], ALU.mult, ["g0t", pk], ["t3"])
                    tt("dve", mg[:, cb, :], t3[:], m0t[:, cb, :], ALU.add, ["t3", "m0t"], ["mg"])
                for a in range(4):
                    ob, obk = o3[a % 2], "o3_%d" % (a % 2)
                    for hf in range(2):
                        pt_, pk = po[(2 * a + hf) % 4], "po%d" % ((2 * a + hf) % 4)
                        mmg([dict(out=pt_[:, :], lhsT=mg[:, kt, a * 128:(a + 1) * 128], rhs=wo[:, kt, hf * 512:(hf + 1) * 512], start=(kt == 0), stop=(kt == 7)) for kt in range(8)],
                            ["mg", "wo", pk], [pk])
                        tt("dve", ob[:, hf * 512:(hf + 1) * 512], pt_[:, :], x3[:, a, hf * 512:(hf + 1) * 512], ALU.add, [pk, "x3"], [obk])
                    act(j3[:], ob[:], AF.Square, [obk], ["j3", "ssq"], accum=ssq[:, a:a + 1])
                    ts("dve", rstd[:, a:a + 1], ssq[:, a:a + 1], 1.0 / D, EPS, ALU.mult, ALU.add, ["ssq"], ["rstd"])
                    act(rstd[:, a:a + 1], rstd[:, a:a + 1], AF.Sqrt, ["rstd"], ["rstd"])
                    P.c("dve", lambda h, a=a: h.reciprocal(out=rstd[:, a:a + 1], in_=rstd[:, a:a + 1]), ["rstd"], ["rstd"])
                    stt(ob[:], ob[:], rstd[:, a:a + 1], gfin[:], ALU.mult, ALU.mult, [obk, "rstd", "gfin"], [obk])
                    ld(yout[ot * TT + a * 128:ot * TT + (a + 1) * 128, :], ob[:], [obk], [U()], eng="act")
            P.emit()
    return nc


_NC = None


def kernel(**inputs):
    global _NC
    if _NC is None:
        _NC = build_nc()
    nc = _NC
    x = np.ascontiguousarray(np.asarray(inputs["x"], dtype=np.float32))
    mem = np.asarray(inputs["mem"], dtype=np.float32)
    in_maps = []
    for c in range(8):
        b, half = c // 2, c % 2
        if half == 0:
            xa = np.concatenate([np.zeros((4096, D), np.float32), x[b, 0:4096]], axis=0)
        else:
            xa = x[b]
        km = np.zeros((NT, TT), dtype=ml_dtypes.bfloat16)
        if half == 0:
            km[0:OWN0, :] = -30000.0
        m = {"xa": np.ascontiguousarray(xa), "kmrow": km, "mem": np.ascontiguousarray(mem[b])}
        for name in ("g_norm", "g_mem_norm", "w_in", "b_forget", "b_merge", "w_mem_kv", "lam_re", "lam_im", "log_step",
                     "s5_b_re", "s5_b_im", "s5_c_re", "s5_c_im", "s5_d", "w_glu", "b_glu", "w_proj_fox", "w_proj_s5",
                     "w_proj_mem", "w_out"):
            a = np.asarray(inputs[name], dtype=np.float32)
            a = a[0]
            if a.ndim == 1:
                a = a[None, :]
            m[name] = np.ascontiguousarray(a)
        m["g_final"] = np.ascontiguousarray(np.asarray(inputs["g_final"], dtype=np.float32)[None, :])
        in_maps.append(m)
    res = run_bass_kernel_spmd(nc, in_maps, core_ids=list(range(8)))
    out = np.zeros((4, 8192, D), dtype=np.float32)
    for c in range(8):
        b, half = c // 2, c % 2
        out[b, half * 4096:(half + 1) * 4096] = np.asarray(res.results[c]["yout"], dtype=np.float32)
    return out
```

```python
import math
from contextlib import ExitStack

import ml_dtypes
import numpy as np
import concourse.bass as bass
import concourse.mybir as mybir
from concourse.bass_utils import run_bass_kernel_spmd

F32 = mybir.dt.float32
BF16 = mybir.dt.bfloat16
I32 = mybir.dt.int32
AF = mybir.ActivationFunctionType
ALU = mybir.AluOpType

ENGS = ("pe", "act", "dve", "pool", "sp")
import os as _os0
NOSELF = tuple(x for x in _os0.environ.get('NOSELF', '').split(',') if x)
D = 1024
NIN = 8716
TT = 512
NT = 16
LTOK = 8192
OWN0 = 8
C_Q, C_K, C_V, C_FL, C_GF, C_U, C_GS, C_QM, C_GM, C_GL = 0, 768, 1536, 2304, 2316, 3084, 3852, 4620, 5132, 5644
EPS = 1e-6
TWO_PI = 2.0 * math.pi


class Op:
    __slots__ = ("eng", "fn", "deps", "needs_inc", "sigval", "is_dma", "lane", "laneval")

    def __init__(self, eng, fn, is_dma=False):
        self.eng = eng
        self.fn = fn
        self.deps = []
        self.needs_inc = False
        self.sigval = None
        self.is_dma = is_dma
        self.lane = None
        self.laneval = None


class Prog:
    def __init__(self, nc, es, n_lanes=6):
        self.nc = nc
        self.n_lanes = n_lanes
        self.esem = {e: es.enter_context(nc.semaphore("s_" + e)) for e in ENGS}
        self.lsem = {}
        for e in ("sp", "act", "pool"):
            for i in range(n_lanes):
                self.lsem[(e, i)] = es.enter_context(nc.semaphore("l_%s%d" % (e, i)))
        self.ecnt = {e: 0 for e in ENGS}
        self.lane_rr = {e: 0 for e in ENGS}
        self.lane_cnt = {k: 0 for k in self.lsem}
        self.barrier = {}
        self._reset()

    def _reset(self):
        self.ops = {e: [] for e in ENGS}
        self.last_w = {}
        self.readers = {}
        self.lane_last = {}

    def _add(self, op, reads, writes):
        deps = []
        for r in reads:
            w = self.last_w.get(r)
            if w is not None:
                deps.append(w)
        for r in writes:
            w = self.last_w.get(r)
            if w is not None:
                deps.append(w)
            deps.extend(self.readers.get(r, ()))
        seen = set(id(d) for d in op.deps)
        for d in deps:
            if d is op or id(d) in seen:
                continue
            if op.eng == "pe" and d.eng == "pe" and not d.is_dma and not op.is_dma:
                continue
            if (not d.is_dma) and (not op.is_dma) and op.eng == d.eng and op.eng in NOSELF:
                continue
            seen.add(id(d))
            op.deps.append(d)
            if not d.is_dma:
                d.needs_inc = True
        for r in reads:
            self.readers.setdefault(r, []).append(op)
        for r in writes:
            self.last_w[r] = op
            self.readers[r] = []
        self.ops[op.eng].append(op)
        return op

    def c(self, eng, fn, reads=(), writes=()):
        return self._add(Op(eng, fn), reads, writes)

    def dma(self, eng, fn, reads=(), writes=()):
        op = Op(eng, fn, is_dma=True)
        lane = (eng, self.lane_rr[eng] % self.n_lanes)
        self.lane_rr[eng] += 1
        prev = self.lane_last.get(lane)
        self.lane_cnt[lane] += 1
        op.lane = lane
        op.laneval = 16 * self.lane_cnt[lane]
        if prev is not None:
            op.deps.append(prev)
        self.lane_last[lane] = op
        return self._add(op, reads, writes)

    def emit(self):
        nc = self.nc
        for e in ENGS:
            last = None
            for op in self.ops[e]:
                if not op.is_dma:
                    last = op
            if last is not None:
                last.needs_inc = True
            for op in self.ops[e]:
                if (not op.is_dma) and op.needs_inc:
                    self.ecnt[e] += 1
                    op.sigval = self.ecnt[e]
        barrier = dict(self.barrier)
        esem, lsem = self.esem, self.lsem

        def tok(d):
            if d.is_dma:
                return lsem[d.lane], d.laneval
            return esem[d.eng], d.sigval

        def run(e, h):
            waited = {}
            for s, v in barrier.values():
                if v > 0:
                    h.wait_ge(s, v)
                    waited[id(s)] = v
            for op in self.ops[e]:
                for d in op.deps:
                    s, v = tok(d)
                    if waited.get(id(s), 0) >= v:
                        continue
                    waited[id(s)] = v
                    h.wait_ge(s, v)
                inst = op.fn(h)
                if op.is_dma:
                    inst.then_inc(lsem[op.lane], 16)
                elif op.needs_inc:
                    inst.then_inc(esem[e], 1)
                import os as _os
                if _os.environ.get('DBGPRINT'):
                    print("OP", e, "dma" if op.is_dma else "c", "lane=%s val=%s" % (op.lane, op.laneval) if op.is_dma else "sig=%s" % op.sigval,
                          "deps=", [((d.lane, d.laneval) if d.is_dma else (d.eng, d.sigval)) for d in op.deps], type(inst).__name__)
            for lane, cnt in self.lane_cnt.items():
                if lane[0] == e and cnt > 0:
                    h.wait_ge(lsem[lane], 16 * cnt)

        with nc.Block() as block:
            @block.tensor
            def _(h):
                run("pe", h)

            @block.scalar
            def _(h):
                run("act", h)

            @block.vector
            def _(h):
                run("dve", h)

            @block.gpsimd
            def _(h):
                run("pool", h)

            @block.sync
            def _(h):
                run("sp", h)

        self.barrier = {}
        for e in ENGS:
            self.barrier[("e", e)] = (esem[e], self.ecnt[e])
        for lane, cnt in self.lane_cnt.items():
            self.barrier[("l", lane)] = (lsem[lane], 16 * cnt)
        self._reset()


def build_nc(dbg=False, upto=9):
    nc = bass.Bass("TRN2", target_bir_lowering=False)

    def din(name, shape, dt=F32):
        return nc.dram_tensor(name, list(shape), dt, kind="ExternalInput").ap()

    def dscr(name, shape, dt):
        return nc.dram_tensor(name, list(shape), dt, kind=("ExternalOutput" if dbg else "Internal")).ap()

    xa = din("xa", [LTOK, D])
    kmrow = din("kmrow", [NT, TT], BF16)
    mem = din("mem", [256, D])
    g_norm = din("g_norm", [1, D])
    g_mem_norm = din("g_mem_norm", [1, D])
    g_final = din("g_final", [1, D])
    w_in = din("w_in", [D, NIN])
    b_forget = din("b_forget", [1, 12])
    b_merge = din("b_merge", [1, 3072])
    w_mem_kv = din("w_mem_kv", [D, 1024])
    lam_re = din("lam_re", [48, 64])
    lam_im = din("lam_im", [48, 64])
    log_step = din("log_step", [1, 48])
    s5_b_re = din("s5_b_re", [48, 64, 16])
    s5_b_im = din("s5_b_im", [48, 64, 16])
    s5_c_re = din("s5_c_re", [48, 16, 64])
    s5_c_im = din("s5_c_im", [48, 16, 64])
    s5_d = din("s5_d", [1, 768])
    w_glu = din("w_glu", [768, 768])
    b_glu = din("b_glu", [1, 768])
    w_proj_fox = din("w_proj_fox", [768, D])
    w_proj_s5 = din("w_proj_s5", [768, D])
    w_proj_mem = din("w_proj_mem", [512, D])
    w_out = din("w_out", [D, D])
    yout = nc.dram_tensor("yout", [4096, D], F32, kind="ExternalOutput").ap()

    WB = dscr("WB", [128, 8, NIN], BF16)
    WKV = dscr("WKV", [128, 8, 1024], BF16)
    WGLU = dscr("WGLU", [128, 6, 768], BF16)
    WPF = dscr("WPF", [128, 6, D], BF16)
    WPS = dscr("WPS", [128, 6, D], BF16)
    WPM = dscr("WPM", [128, 4, D], BF16)
    WOUT = dscr("WOUT", [128, 8, D], BF16)
    BDS = dscr("BDS", [6, 128, 8, 128], BF16)
    WZS = dscr("WZS", [6, 128, 8, 2, 128], BF16)
    WYS = dscr("WYS", [6, 128, 4, 8, 2, 32], BF16)
    KT = dscr("KT", [12, 128, LTOK], BF16)
    QT = dscr("QT", [12, 128, 4096], BF16)
    VS = dscr("VS", [64, 128, 780], BF16)
    M0S = dscr("M0S", [8, 128, 8, TT], F32)
    G0S = dscr("G0S", [8, 128, 8, TT], BF16)
    SGFS = dscr("SGFS", [8, 128, 6, TT], BF16)
    YFS = dscr("YFS", [12, 64, 4096], BF16)

    es = ExitStack()
    with es:
        P = Prog(nc, es)

        def sb(stack, name, shape, dt):
            return stack.enter_context(nc.sbuf_tensor(name, list(shape), dt))

        def ps(stack, name, shape, dt=F32):
            return stack.enter_context(nc.psum_tensor(name, list(shape), dt))

        def act(out, in_, func, r, w, bias=None, scale=None, accum=None):
            kw = {}
            if bias is not None:
                kw["bias"] = bias
            if scale is not None:
                kw["scale"] = scale
            if accum is not None:
                kw["accum_out"] = accum
            return P.c("act", lambda h: h.activation(out=out, in_=in_, func=func, **kw), r, w)

        def tt(eng, out, in0, in1, op, r, w):
            return P.c(eng, lambda h: h.tensor_tensor(out=out, in0=in0, in1=in1, op=op), r, w)

        def ts(eng, out, in0, s1, s2, op0, op1, r, w):
            if op1 is None:
                return P.c(eng, lambda h: h.tensor_scalar(out=out, in0=in0, scalar1=s1, scalar2=None, op0=op0), r, w)
            return P.c(eng, lambda h: h.tensor_scalar(out=out, in0=in0, scalar1=s1, scalar2=s2, op0=op0, op1=op1), r, w)

        def stt(out, in0, scalar, in1, op0, op1, r, w):
            return P.c("dve", lambda h: h.scalar_tensor_tensor(out=out, in0=in0, scalar=scalar, in1=in1, op0=op0, op1=op1), r, w)

        def cp(eng, out, in_, r, w):
            if eng == "act":
                return P.c("act", lambda h: h.activation(out=out, in_=in_, func=AF.Identity), r, w)
            return P.c(eng, lambda h: h.tensor_copy(out=out, in_=in_), r, w)

        def mset(eng, ap, val, w):
            return P.c(eng, lambda h: h.memset(ap, val), (), w)

        def mmg(calls, r, w):
            def fn(h):
                inst = None
                for kw in calls:
                    inst = h.matmul(**kw)
                return inst
            return P.c("pe", fn, r, w)

        def tpg(calls, r, w):
            def fn(h):
                inst = None
                for (o, i, idn) in calls:
                    inst = h.transpose(o, i, idn)
                return inst
            return P.c("pe", fn, r, w)

        uq = [0]

        def U():
            uq[0] += 1
            return "u%d" % uq[0]

        def ld(out, in_, r, w, eng="sp", slow=False):
            if slow:
                return P.dma(eng, lambda h: h.dma_start(out=out, in_=in_, allow_slow_non_contiguous=True), r, w)
            return P.dma(eng, lambda h: h.dma_start(out=out, in_=in_), r, w)

        def rows_pattern(tile_ap, n, lo, hi, w):
            P.c("pool", lambda h: h.memset(tile_ap, 1.0), (), w)
            P.c("pool", lambda h: h.affine_select(out=tile_ap, in_=tile_ap, pattern=[[0, n]], compare_op=ALU.is_ge,
                                                  fill=0.0, base=-lo, channel_multiplier=1), w, w)
            P.c("pool", lambda h: h.affine_select(out=tile_ap, in_=tile_ap, pattern=[[0, n]], compare_op=ALU.is_ge,
                                                  fill=0.0, base=hi, channel_multiplier=-1), w, w)

        G = ExitStack()
        es.enter_context(G)
        ident_f = sb(G, "ident_f", [128, 128], F32)
        ident_b = sb(G, "ident_b", [128, 128], BF16)
        ones_f = sb(G, "ones_f", [128, 512], F32)
        ones_b = sb(G, "ones_b", [128, 128], BF16)
        maskT = sb(G, "maskT", [128, 128], BF16)
        gn = sb(G, "gn", [128, 8], F32)
        gmn = sb(G, "gmn", [128, 8], F32)
        bm = sb(G, "bm", [128, 24], F32)
        bglu = sb(G, "bglu", [128, 6], F32)
        SELK = sb(G, "SELK", [128, 12, 128], BF16)
        SELQ = sb(G, "SELQ", [128, 12, 128], BF16)
        qscale = sb(G, "qscale", [128, 1], F32)
        negb = sb(G, "negb", [128, 1], F32)
        nm1 = sb(G, "nm1", [128, 1], F32)
        nm2 = sb(G, "nm2", [128, 1], F32)
        WFL3 = sb(G, "WFL3", [128, 8, 76], BF16)
        MKT = sb(G, "MKT", [128, 4, 256], BF16)
        MV = sb(G, "MV", [128, 2, 512], BF16)
        A1R = sb(G, "A1R", [128, 24], F32)
        A1I = sb(G, "A1I", [128, 24], F32)
        A8R = sb(G, "A8R", [128, 24], F32)
        A8I = sb(G, "A8I", [128, 24], F32)
        APR = sb(G, "APR", [128, 8, 24], F32)
        API = sb(G, "API", [128, 8, 24], F32)
        Fcar = sb(G, "Fcar", [128, 1], F32)
        TB = sb(G, "TB", [128, 24, 2, 9], F32)
        ssq = sb(G, "ssq", [128, 8], F32)
        rstd = sb(G, "rstd", [128, 8], F32)

        mset("pool", ones_f[:], 1.0, ["ones_f"])
        mset("pool", ident_f[:], 1.0, ["ident_f"])
        P.c("pool", lambda h: h.affine_select(out=ident_f[:], in_=ident_f[:], pattern=[[-1, 128]], compare_op=ALU.is_equal,
                                              fill=0.0, base=0, channel_multiplier=1), ["ident_f"], ["ident_f"])
        cp("dve", ident_b[:], ident_f[:], ["ident_f"], ["ident_b"])
        cp("dve", ones_b[:], ones_f[:, 0:128], ["ones_f"], ["ones_b"])
        mset("pool", maskT[:], 1.0, ["maskT"])
        P.c("pool", lambda h: h.affine_select(out=maskT[:], in_=maskT[:], pattern=[[1, 128]], compare_op=ALU.is_ge,
                                              fill=0.0, base=0, channel_multiplier=-1), ["maskT"], ["maskT"])
        ld(gn[:], g_norm.rearrange("o (kt p) -> p (o kt)", p=128), [], ["gn"], slow=True)
        ld(gmn[:], g_mem_norm.rearrange("o (kt p) -> p (o kt)", p=128), [], ["gmn"], slow=True)
        ld(bm[:], b_merge.rearrange("o (c p) -> p (o c)", p=128), [], ["bm"], slow=True)
        ld(bglu[:], b_glu.rearrange("o (c p) -> p (o c)", p=128), [], ["bglu"], slow=True)
        mset("pool", qscale[:], 1.0, ["qscale"])
        mset("pool", qscale[0:64, :], 0.125, ["qscale"])
        mset("pool", negb[:], 0.0, ["negb"])
        for base in (0, 32, 64):
            ld(negb[base:base + 12, :], b_forget.rearrange("o h -> h o"), [], ["negb"], slow=True)
        ts("dve", negb[:], negb[:], -1.0, None, ALU.mult, None, ["negb"], ["negb"])
        mset("pool", nm1[:], 0.0, ["nm1"])
        mset("pool", nm1[32:64, :], -1.0, ["nm1"])
        mset("pool", nm1[64:96, :], -1.0, ["nm1"])
        mset("pool", nm2[:], 0.0, ["nm2"])
        mset("pool", nm2[64:96, :], -1.0, ["nm2"])
        mset("pool", Fcar[:], 0.0, ["Fcar"])
        mset("pool", TB[:], 0.0, ["TB"])
        mset("pool", SELK[:], 0.0, ["SELK"])
        mset("pool", SELQ[:], 0.0, ["SELQ"])
        for j, base in enumerate((0, 32, 64)):
            ts("dve", SELK[:, :, 96 + j], ident_f[:, base:base + 12], -1.0, None, ALU.mult, None, ["ident_f", "SELK"], ["SELK"])
            cp("dve", SELQ[:, :, 64 + j], ident_f[:, base:base + 12], ["ident_f", "SELQ"], ["SELQ"])
        for h_ in range(12):
            cp("dve", SELK[:, h_, 99:100], ident_f[:, 76:77], ["ident_f", "SELK"], ["SELK"])
            for c_ in (64, 65, 66):
                cp("dve", SELK[:, h_, c_:c_ + 1], ident_f[:, 96:97], ["ident_f", "SELK"], ["SELK"])
            for c_ in (96, 97, 98, 99):
                cp("dve", SELQ[:, h_, c_:c_ + 1], ident_f[:, 96:97], ["ident_f", "SELQ"], ["SELQ"])

        S0 = ExitStack()
        with S0:
            cv = [sb(S0, "cv%d" % i, [128, 8, 512], BF16) for i in range(2)]
            cvn = [0]

            def convert(src, nkt, ncols, dst, wkey=None):
                srcv = src.rearrange("(kt p) c -> p kt c", p=128)
                c0 = 0
                while c0 < ncols:
                    n = min(512, ncols - c0)
                    i = cvn[0] % 2
                    cvn[0] += 1
                    t = cv[i]
                    ld(t[:, 0:nkt, 0:n], srcv[:, :, c0:c0 + n], [], ["cv%d" % i], eng="pool")
                    ld(dst[:, :, c0:c0 + n], t[:, 0:nkt, 0:n], ["cv%d" % i], [wkey] if wkey else [U()], eng="sp")
                    c0 += n

            convert(w_in, 8, NIN, WB)
            convert(w_mem_kv, 8, 1024, WKV, "WKV")
            convert(w_glu, 6, 768, WGLU)
            convert(w_proj_fox, 6, D, WPF)
            convert(w_proj_s5, 6, D, WPS)
            convert(w_proj_mem, 4, D, WPM)
            convert(w_out, 8, D, WOUT)
            mset("pool", WFL3[:], 0.0, ["WFL3"])
            for base in (0, 32, 64):
                ld(WFL3[:, :, base:base + 12], w_in.rearrange("(kt p) c -> p kt c", p=128)[:, :, C_FL:C_FL + 12],
                   ["WFL3"], ["WFL3"], eng="pool")

            def s5t(name, shape, dt=F32):
                return sb(S0, name, shape, dt)
            LR = s5t("LR", [128, 24]); LI = s5t("LI", [128, 24]); LS = s5t("LS", [128, 24])
            ld(LR[:], lam_re.rearrange("(pr g2) p -> (g2 p) pr", g2=2), [], ["LR"], slow=True)
            ld(LI[:], lam_im.rearrange("(pr g2) p -> (g2 p) pr", g2=2), [], ["LI"], slow=True)
            for g2 in range(2):
                ld(LS[g2 * 64:(g2 + 1) * 64, :],
                   log_step.rearrange("o (pr g2) -> o g2 pr", g2=2)[:, g2, :].to_broadcast([64, 24]), [], ["LS"], slow=True)
            STP = s5t("STP", [128, 24]); MAG = s5t("MAG", [128, 24]); ANG = s5t("ANG", [128, 24])
            T0 = s5t("T0", [128, 24]); T1 = s5t("T1", [128, 24]); T2 = s5t("T2", [128, 24]); TI = s5t("TI", [128, 24], I32)
            ABR = s5t("ABR", [128, 24]); ABI = s5t("ABI", [128, 24]); SN = s5t("SN", [128, 24]); CS = s5t("CS", [128, 24])
            FR = s5t("FR", [128, 24]); FI = s5t("FI", [128, 24])
            act(STP[:], LS[:], AF.Exp, ["LS"], ["STP"])
            tt("dve", T0[:], LR[:], STP[:], ALU.mult, ["LR", "STP"], ["T0"])
            act(MAG[:], T0[:], AF.Exp, ["T0"], ["MAG"])
            tt("dve", ANG[:], LI[:], STP[:], ALU.mult, ["LI", "STP"], ["ANG"])

            def sin_of(dst, shift, key):
                ts("dve", T0[:], ANG[:], shift, None, ALU.add, None, ["ANG"], ["T0"])
                ts("dve", T1[:], T0[:], 1.0 / TWO_PI, 0.5, ALU.mult, ALU.add, ["T0"], ["T1"])
                cp("dve", TI[:], T1[:], ["T1"], ["TI"])
                cp("dve", T1[:], TI[:], ["TI"], ["T1"])
                stt(T2[:], T1[:], -TWO_PI, T0[:], ALU.mult, ALU.add, ["T1", "T0"], ["T2"])
                ts("dve", T1[:], T2[:], math.pi, -TWO_PI, ALU.is_gt, ALU.mult, ["T2"], ["T1"])
                tt("dve", T2[:], T2[:], T1[:], ALU.add, ["T2", "T1"], ["T2"])
                ts("dve", T1[:], T2[:], -math.pi, TWO_PI, ALU.is_lt, ALU.mult, ["T2"], ["T1"])
                tt("dve", T2[:], T2[:], T1[:], ALU.add, ["T2", "T1"], ["T2"])
                ts("dve", T2[:], T2[:], math.pi, -math.pi, ALU.min, ALU.max, ["T2"], ["T2"])
                act(dst[:], T2[:], AF.Sin, ["T2"], [key])
            sin_of(SN, 0.0, "SN")
            sin_of(CS, math.pi / 2.0, "CS")
            tt("dve", ABR[:], MAG[:], CS[:], ALU.mult, ["MAG", "CS"], ["ABR"])
            tt("dve", ABI[:], MAG[:], SN[:], ALU.mult, ["MAG", "SN"], ["ABI"])
            DEN = s5t("DEN", [128, 24]); NR = s5t("NR", [128, 24])
            tt("dve", T0[:], LR[:], LR[:], ALU.mult, ["LR"], ["T0"])
            tt("dve", T1[:], LI[:], LI[:], ALU.mult, ["LI"], ["T1"])
            tt("dve", DEN[:], T0[:], T1[:], ALU.add, ["T0", "T1"], ["DEN"])
            P.c("dve", lambda h: h.reciprocal(out=DEN[:], in_=DEN[:]), ["DEN"], ["DEN"])
            ts("dve", NR[:], ABR[:], -1.0, None, ALU.add, None, ["ABR"], ["NR"])
            tt("dve", T0[:], NR[:], LR[:], ALU.mult, ["NR", "LR"], ["T0"])
            tt("dve", T1[:], ABI[:], LI[:], ALU.mult, ["ABI", "LI"], ["T1"])
            tt("dve", T0[:], T0[:], T1[:], ALU.add, ["T0", "T1"], ["T0"])
            tt("dve", FR[:], T0[:], DEN[:], ALU.mult, ["T0", "DEN"], ["FR"])
            tt("dve", T0[:], ABI[:], LR[:], ALU.mult, ["ABI", "LR"], ["T0"])
            tt("dve", T1[:], NR[:], LI[:], ALU.mult, ["NR", "LI"], ["T1"])
            tt("dve", T0[:], T0[:], T1[:], ALU.subtract, ["T0", "T1"], ["T0"])
            tt("dve", FI[:], T0[:], DEN[:], ALU.mult, ["T0", "DEN"], ["FI"])
            PWR = s5t("PWR", [128, 9, 24]); PWI = s5t("PWI", [128, 9, 24])
            AWR = s5t("AWR", [128, 9, 24]); AWI = s5t("AWI", [128, 9, 24])

            def cmul(orr, oi, ar, ai, br, bi, rk, wk):
                tt("dve", T0[:], ar, br, ALU.mult, rk, ["T0"])
                tt("dve", T1[:], ai, bi, ALU.mult, rk, ["T1"])
                tt("dve", T2[:], ar, bi, ALU.mult, rk, ["T2"])
                tt("dve", orr, T0[:], T1[:], ALU.subtract, ["T0", "T1"], wk)
                tt("dve", T0[:], ai, br, ALU.mult, rk + wk, ["T0"])
                tt("dve", oi, T2[:], T0[:], ALU.add, ["T2", "T0"], wk)
            mset("pool", PWR[:, 0, :], 1.0, ["PW"])
            mset("pool", PWI[:, 0, :], 0.0, ["PW"])
            for k in range(1, 9):
                cmul(PWR[:, k, :], PWI[:, k, :], PWR[:, k - 1, :], PWI[:, k - 1, :], ABR[:], ABI[:], ["PW", "ABR", "ABI"], ["PW"])
            mset("pool", AWR[:, 0, :], 1.0, ["AW"])
            mset("pool", AWI[:, 0, :], 0.0, ["AW"])
            for k in range(1, 9):
                cmul(AWR[:, k, :], AWI[:, k, :], AWR[:, k - 1, :], AWI[:, k - 1, :], PWR[:, 8, :], PWI[:, 8, :], ["AW", "PW"], ["AW"])
            cp("dve", A1R[:], AWR[:, 1, :], ["AW"], ["A1"])
            cp("dve", A1I[:], AWI[:, 1, :], ["AW"], ["A1"])
            cp("dve", APR[:], AWR[:, 1:9, :], ["AW"], ["APR"])
            cp("dve", API[:], AWI[:, 1:9, :], ["AW"], ["API"])
            cp("dve", A8R[:], AWR[:, 8, :], ["AW"], ["A8"])
            cp("dve", A8I[:], AWI[:, 8, :], ["AW"], ["A8"])

            BRE = s5t("BRE", [128, 24, 16]); BIM = s5t("BIM", [128, 24, 16])
            ld(BRE[:], s5_b_re.rearrange("(pr g2) p h -> (g2 p) pr h", g2=2), [], ["BRE"])
            ld(BIM[:], s5_b_im.rearrange("(pr g2) p h -> (g2 p) pr h", g2=2), [], ["BIM"])
            BBR = s5t("BBR", [128, 24, 16]); BBI = s5t("BBI", [128, 24, 16])
            U0 = s5t("U0", [128, 24, 16]); U1 = s5t("U1", [128, 24, 16])
            frb = FR[:].unsqueeze(2).to_broadcast([128, 24, 16])
            fib = FI[:].unsqueeze(2).to_broadcast([128, 24, 16])
            tt("dve", U0[:], BRE[:], frb, ALU.mult, ["BRE", "FR"], ["U0"])
            tt("dve", U1[:], BIM[:], fib, ALU.mult, ["BIM", "FI"], ["U1"])
            tt("dve", BBR[:], U0[:], U1[:], ALU.subtract, ["U0", "U1"], ["BBR"])
            tt("dve", U0[:], BIM[:], frb, ALU.mult, ["BIM", "FR"], ["U0"])
            tt("dve", U1[:], BRE[:], fib, ALU.mult, ["BRE", "FI"], ["U1"])
            tt("dve", BBI[:], U0[:], U1[:], ALU.add, ["U0", "U1"], ["BBI"])
            BDR = s5t("BDR", [128, 24, 32]); BDI = s5t("BDI", [128, 24, 32])
            mset("pool", BDR[:], 0.0, ["BDR"]); mset("pool", BDI[:], 0.0, ["BDI"])
            for g2 in range(2):
                sl = slice(g2 * 64, (g2 + 1) * 64)
                cp("dve", BDR[sl, :, g2 * 16:(g2 + 1) * 16], BBR[sl, :, :], ["BBR", "BDR"], ["BDR"])
                cp("dve", BDI[sl, :, g2 * 16:(g2 + 1) * 16], BBI[sl, :, :], ["BBI", "BDI"], ["BDI"])
            BWR = s5t("BWR", [128, 24, 128]); BWI = s5t("BWI", [128, 24, 128])
            mset("pool", BWR[:], 0.0, ["BWR"]); mset("pool", BWI[:], 0.0, ["BWI"])
            for q4 in range(4):
                for o in range(6):
                    pr = 4 * o + q4
                    cp("dve", BWR[:, pr, 32 * q4:32 * q4 + 32], BDR[:, pr, :], ["BDR", "BWR"], ["BWR"])
                    cp("pool", BWI[:, pr, 32 * q4:32 * q4 + 32], BDI[:, pr, :], ["BDI", "BWI"], ["BWI"])
            CTR = s5t("CTR", [128, 24, 16]); CTI = s5t("CTI", [128, 24, 16])
            CN = s5t("CN", [128, 3, 2, 128])
            tp_ps = ps(S0, "tp_ps", [128, 512], F32)
            for ri, (csrc, cdst, key) in enumerate(((s5_c_re, CTR, "CTR"), (s5_c_im, CTI, "CTI"))):
                cv4 = csrc.rearrange("(pr g2) h p -> pr h g2 p", g2=2)
                for pr in range(24):
                    ld(CN[(pr % 8) * 16:(pr % 8) * 16 + 16, pr // 8, ri, :].rearrange("h (a p) -> h a p", a=2),
                       cv4[pr], ["CN%d" % ri], ["CN%d" % ri])
                for j in range(3):
                    tpg([(tp_ps[:, 0:128], CN[:, j, ri, :], ident_f[:])], ["CN%d" % ri, "ident_f", "tp_ps"], ["tp_ps"])
                    cp("act", cdst[:, 8 * j:8 * j + 8, :], tp_ps[:, 0:128].rearrange("p (a h) -> p a h", a=8), ["tp_ps"], [key])
            XR = s5t("XR", [128, 24, 32]); XI = s5t("XI", [128, 24, 32])
            V0 = s5t("V0", [128, 24, 32]); V1 = s5t("V1", [128, 24, 32])
            WZs = [s5t("WZs%d" % i, [128, 6, 2, 128], BF16) for i in range(2)]
            for s in range(8):
                k = 7 - s
                wzs, wzk = WZs[s % 2], "WZs%d" % (s % 2)
                pr_ = PWR[:, k, :].unsqueeze(2).to_broadcast([128, 24, 32])
                pi_ = PWI[:, k, :].unsqueeze(2).to_broadcast([128, 24, 32])
                tt("dve", V0[:], BDR[:], pr_, ALU.mult, ["BDR", "PW"], ["V0"])
                tt("dve", V1[:], BDI[:], pi_, ALU.mult, ["BDI", "PW"], ["V1"])
                tt("dve", XR[:], V0[:], V1[:], ALU.subtract, ["V0", "V1"], ["XR"])
                tt("dve", V0[:], BDI[:], pr_, ALU.mult, ["BDI", "PW"], ["V0"])
                tt("dve", V1[:], BDR[:], pi_, ALU.mult, ["BDR", "PW"], ["V1"])
                tt("dve", XI[:], V0[:], V1[:], ALU.add, ["V0", "V1"], ["XI"])
                for o in range(6):
                    for ri, X in enumerate((XR, XI)):
                        tpg([(tp_ps[:, 0:128], X[:, 4 * o:4 * o + 4, :].rearrange("p a b -> p (a b)"), ident_f[:])], ["XR", "XI", "ident_f", "tp_ps"], ["tp_ps"])
                        cp("act", wzs[:, o, ri, :], tp_ps[:, 0:128], ["tp_ps"], [wzk])
                for o in range(6):
                    ld(WZS[o, :, s, :, :], wzs[:, o, :, :], [wzk], [U()])
            QR = s5t("QR", [128, 24, 16]); QI = s5t("QI", [128, 24, 16])
            QBR = s5t("QBR", [128, 24, 32]); QBI = s5t("QBI", [128, 24, 32])
            WYk = [s5t("WYk%d" % i, [128, 24, 2, 32], BF16) for i in range(2)]
            BDj = [s5t("BDj%d" % i, [128, 6, 128], F32) for i in range(2)]
            BDjb = [s5t("BDjb%d" % i, [128, 6, 128], BF16) for i in range(2)]
            DSK = s5t("DSK", [128, 6], F32)
            ld(DSK[:], s5_d.rearrange("o (c p) -> p (o c)", p=128), [], ["DSK"], slow=True)
            mset("pool", QBR[:], 0.0, ["QBR"]); mset("pool", QBI[:], 0.0, ["QBI"])
            for i in range(2):
                mset("pool", BDj[i][:], 0.0, ["BDj%d" % i])
            bd_ps = ps(S0, "bd_ps", [128, 6, 32], F32)
            for k in range(9):
                pr_ = PWR[:, k, :].unsqueeze(2).to_broadcast([128, 24, 16])
                pi_ = PWI[:, k, :].unsqueeze(2).to_broadcast([128, 24, 16])
                tt("dve", U0[:], CTR[:], pr_, ALU.mult, ["CTR", "PW"], ["U0"])
                tt("dve", U1[:], CTI[:], pi_, ALU.mult, ["CTI", "PW"], ["U1"])
                tt("dve", QR[:], U0[:], U1[:], ALU.subtract, ["U0", "U1"], ["QR"])
                tt("dve", U0[:], CTR[:], pi_, ALU.mult, ["CTR", "PW"], ["U0"])
                tt("dve", U1[:], CTI[:], pr_, ALU.mult, ["CTI", "PW"], ["U1"])
                tt("dve", QI[:], U0[:], U1[:], ALU.add, ["U0", "U1"], ["QI"])
                for g2 in range(2):
                    sl = slice(g2 * 64, (g2 + 1) * 64)
                    cp("dve", QBR[sl, :, g2 * 16:(g2 + 1) * 16], QR[sl, :, :], ["QR", "QBR"], ["QBR"])
                    ts("dve", QBI[sl, :, g2 * 16:(g2 + 1) * 16], QI[sl, :, :], -1.0, None, ALU.mult, None, ["QI", "QBI"], ["QBI"])
                if k >= 1:
                    wyk, wykk = WYk[k % 2], "WYk%d" % (k % 2)
                    cp("act", wyk[:, :, 0, :], QBR[:], ["QBR"], [wykk])
                    cp("act", wyk[:, :, 1, :], QBI[:], ["QBI"], [wykk])
                    for o in range(6):
                        ld(WYS[o, :, :, k - 1, :, :], wyk[:, 4 * o:4 * o + 4, :, :], [wykk], [U()])
                if k <= 7:
                    bdj, bdk = BDj[k % 2], "BDj%d" % (k % 2)
                    bdjb, bdbk = BDjb[k % 2], "BDjb%d" % (k % 2)
                    for o in range(6):
                        calls = []
                        for q4 in range(4):
                            pr = 4 * o + q4
                            calls.append(dict(out=bd_ps[:, o, :], lhsT=BWR[:, pr, :], rhs=QBR[:, pr, :], start=(q4 == 0), stop=False))
                            calls.append(dict(out=bd_ps[:, o, :], lhsT=BWI[:, pr, :], rhs=QBI[:, pr, :], start=False, stop=(q4 == 3)))
                        mmg(calls, ["BWR", "BWI", "QBR", "QBI", "bd_ps"], ["bd_ps"])
                    for q4 in range(4):
                        sl = slice(32 * q4, 32 * q4 + 32)
                        cp("act", bdj[sl, :, 32 * q4:32 * q4 + 32], bd_ps[sl, :, :], ["bd_ps"], [bdk])
                    if k == 0:
                        for o in range(6):
                            stt(bdj[:, o, :], ident_f[:], DSK[:, o:o + 1], bdj[:, o, :], ALU.mult, ALU.add, ["ident_f", "DSK", bdk], [bdk])
                    cp("dve", bdjb[:], bdj[:], [bdk], [bdbk])
                    for o in range(6):
                        ld(BDS[o, :, k, :], bdjb[:, o, :], [bdbk], [U()])
            P.emit()
        if upto < 1:
            return nc
        S0b = ExitStack()
        with S0b:
            S0 = S0b

            def s5t(name, shape, dt=F32):
                return sb(S0b, name, shape, dt)
            mt_x = s5t("mt_x", [128, 2, D], F32)
            mt_n = s5t("mt_n", [128, 2, D], BF16)
            mt_j = s5t("mt_j", [128, D], BF16)
            MHT = s5t("MHT", [128, 8, 256], BF16)
            wkv_t = s5t("wkv_t", [128, 8, 1024], BF16)
            tpb_ps = ps(S0, "tpb_ps", [128, 512], BF16)
            mk_ps = ps(S0, "mk_ps", [128, 512], F32)
            ld(mt_x[:], mem.rearrange("(a p) d -> p a d", p=128), [], ["mt_x"])
            ld(wkv_t[:], WKV, ["WKV"], ["wkv_t"])
            for a in range(2):
                act(mt_j[:], mt_x[:, a, :], AF.Square, ["mt_x"], ["mt_j", "ssq"], accum=ssq[:, a:a + 1])
            ts("dve", rstd[:, 0:2], ssq[:, 0:2], 1.0 / D, EPS, ALU.mult, ALU.add, ["ssq"], ["rstd"])
            act(rstd[:, 0:2], rstd[:, 0:2], AF.Sqrt, ["rstd"], ["rstd"])
            P.c("dve", lambda h: h.reciprocal(out=rstd[:, 0:2], in_=rstd[:, 0:2]), ["rstd"], ["rstd"])
            for a in range(2):
                ts("dve", mt_n[:, a, :], mt_x[:, a, :], rstd[:, a:a + 1], None, ALU.mult, None, ["mt_x", "rstd"], ["mt_n"])
            for kt in range(8):
                tpg([(tpb_ps[:, a * 128:(a + 1) * 128], mt_n[:, a, kt * 128:(kt + 1) * 128], ident_b[:]) for a in range(2)],
                    ["mt_n", "ident_b", "tpb_ps"], ["tpb_ps"])
                ts("dve", MHT[:, kt, :], tpb_ps[:, 0:256], gmn[:, kt:kt + 1], None, ALU.mult, None, ["tpb_ps", "gmn"], ["MHT"])
            for hd in range(4):
                mmg([dict(out=mk_ps[:, 0:256], lhsT=wkv_t[:, kt, hd * 128:(hd + 1) * 128], rhs=MHT[:, kt, :],
                          start=(kt == 0), stop=(kt == 7)) for kt in range(8)], ["wkv_t", "MHT", "mk_ps"], ["mk_ps"])
                cp("act", MKT[:, hd, :], mk_ps[:, 0:256], ["mk_ps"], ["MKT"])
            for mt in range(2):
                mmg([dict(out=mk_ps[:, :], lhsT=MHT[:, kt, mt * 128:(mt + 1) * 128], rhs=wkv_t[:, kt, 512:1024],
                          start=(kt == 0), stop=(kt == 7)) for kt in range(8)], ["wkv_t", "MHT", "mk_ps"], ["mk_ps"])
                cp("act", MV[:, mt, :], mk_ps[:, :], ["mk_ps"], ["MV"])
            P.emit()

        if upto < 2:
            return nc
        S1 = ExitStack()
        with S1:
            xt = sb(S1, "xt", [128, 4, D], F32)
            xn = sb(S1, "xn", [128, 4, D], BF16)
            sqj = sb(S1, "sqj", [128, D], BF16)
            hTs = [sb(S1, "hT%d" % i, [128, 8, TT], BF16) for i in range(2)]
            NWS = 3
            wsl = [sb(S1, "ws%d" % i, [128, 4096], BF16) for i in range(NWS)]
            SPt = [sb(S1, "SPt%d" % i, [128, TT], BF16) for i in range(2)]
            HIb = sb(S1, "HIb", [128, TT], BF16)
            MIDb = sb(S1, "MIDb", [128, TT], BF16)
            kst = [sb(S1, "kst%d" % i, [128, TT], BF16) for i in range(6)]
            Vst = sb(S1, "Vst", [128, 4, 12, 65], BF16)
            uT = sb(S1, "uT", [128, 6, TT], BF16)
            s5w = [(sb(S1, "s5bd%d" % i, [128, 8, 128], BF16), sb(S1, "s5wz%d" % i, [128, 8, 2, 128], BF16),
                    sb(S1, "s5wy%d" % i, [128, 4, 8, 2, 32], BF16)) for i in range(2)]
            Z = sb(S1, "Z", [128, 24, 2, 64], F32)
            Hb = sb(S1, "Hb", [128, 24, 2, 65], BF16)
            RT = [sb(S1, "RT%d" % i, [128, 24, 8], F32) for i in range(8)]
            RS = [sb(S1, "RS%d" % i, [128, 24], F32) for i in range(8)]
            tA = sb(S1, "tA", [128, TT], F32)
            tB = sb(S1, "tB", [128, TT], F32)
            tC = sb(S1, "tC", [128, TT], F32)
            Fe = sb(S1, "Fe", [76, TT], F32)
            Ff = sb(S1, "Ff", [76, TT], F32)
            R1 = sb(S1, "R1", [76, TT], F32)
            YG = sb(S1, "YG", [128, 6, TT], BF16)
            SGS = sb(S1, "SGS", [128, 6, TT], BF16)
            YS = SGS
            QM = sb(S1, "QM", [128, 4, TT], BF16)
            SGM = sb(S1, "SGM", [128, 4, TT], BF16)
            PmT = sb(S1, "PmT", [128, 2, TT], BF16)
            YM = SGM
            M0 = [sb(S1, "M0_%d" % i, [128, TT], F32) for i in range(3)]
            G0st = [sb(S1, "G0st%d" % i, [128, TT], BF16) for i in range(4)]
            pj = [ps(S1, "pj%d" % i, [128, 512], F32) for i in range(2)]
            tpall = ps(S1, "tpall", [128, 1024], BF16)
            tpp = [tpall[:, 0:512], tpall[:, 0:512]]
            zpsb = [ps(S1, "zps%d" % i, [128, 512], F32) for i in range(4)]
            yps = ps(S1, "yps", [128, 512], F32)

            import os
            if not os.environ.get('SKIPM'):
                for i in range(2):
                    mset("pool", SPt[i][:], 0.0, ["SPt%d" % i])
                    mset("pool", SPt[i][96:97, :], 1.0, ["SPt%d" % i])
                mset("pool", Vst[:], 1.0, ["Vst"])

            wn = [0]
            pjn = [0]

            def wload(src3, nkt, ncols):
                i = wn[0] % NWS
                wn[0] += 1
                t = wsl[i][:, 0:nkt * ncols].rearrange("p (k c) -> p k c", k=nkt)
                if os.environ.get('WLDTINY'):
                    ld(t[:, 0:1, 0:16], src3[:, 0:1, 0:16], [], ["ws%d" % i])
                else:
                    ld(t, src3, [], ["ws%d" % i])
                return t, "ws%d" % i

            PJL = [(pj[0], "pj0"), (pj[1], "pj1"), (zpsb[0], "zps0"), (zpsb[1], "zps1"), (zpsb[2], "zps2"), (zpsb[3], "zps3"), (yps, "yps")]

            def nextpj():
                i = pjn[0] % len(PJL)
                pjn[0] += 1
                return PJL[i]

            import os
            P1T = int(os.environ.get('P1T', NT)); P1S = int(os.environ.get('P1S', 99)); OWNX = int(os.environ.get('OWNX', OWN0))
            def prepA(it):
                if it == 0:
                    ld(xt[:], xa[it * TT:(it + 1) * TT, :].rearrange("(a p) d -> p a d", p=128), [], ["xt"])
                for a in range(4):
                    act(sqj[:], xt[:, a, :], AF.Square, ["xt"], ["sqj", "ssq"], accum=ssq[:, a:a + 1])
                ts("dve", rstd[:, 0:4], ssq[:, 0:4], 1.0 / D, EPS, ALU.mult, ALU.add, ["ssq"], ["rstd"])
                act(rstd[:, 0:4], rstd[:, 0:4], AF.Sqrt, ["rstd"], ["rstd"])
                P.c("dve", lambda h: h.reciprocal(out=rstd[:, 0:4], in_=rstd[:, 0:4]), ["rstd"], ["rstd"])
                for a in range(4):
                    ts("dve", xn[:, a, :], xt[:, a, :], rstd[:, a:a + 1], None, ALU.mult, None, ["xt", "rstd"], ["xn"])
                if it + 1 < P1T:
                    ld(xt[:], xa[(it + 1) * TT:(it + 2) * TT, :].rearrange("(a p) d -> p a d", p=128), [], ["xt"])

            def prepB(it):
                hTn, hkn = hTs[it % 2], "hT%d" % (it % 2)
                sptn, spkn = SPt[it % 2], "SPt%d" % (it % 2)
                ld(sptn[76:77, :], kmrow[it:it + 1, :], [], [spkn], eng="pool")
                for kt in range(8):
                    tp, tk = tpp[kt % 2], "tpp0"
                    tpg([(tp[:, a * 128:(a + 1) * 128], xn[:, a, kt * 128:(kt + 1) * 128], ident_b[:]) for a in range(4)],
                        ["xn", "ident_b", tk], [tk])
                    if kt % 2 == 0:
                        ts("dve", hTn[:, kt, :], tp[:, :], gn[:, kt:kt + 1], None, ALU.mult, None, [tk, "gn"], [hkn])
                    else:
                        act(hTn[:, kt, :], tp[:, :], AF.Identity, [tk, "gn"], [hkn], scale=gn[:, kt:kt + 1])
                pjt, pk = nextpj()
                mmg([dict(out=pjt[0:76, :], lhsT=WFL3[:, kt, :], rhs=hTn[:, kt, :], start=(kt == 0), stop=(kt == 7)) for kt in range(8)],
                    ["WFL3", hkn, pk], [pk])
                act(Fe[0:76, :], pjt[0:76, :], AF.Exp, [pk, "negb"], ["Fe"], bias=negb[0:76, :], scale=-1.0)
                act(Fe[0:76, :], Fe[0:76, :], AF.Ln, ["Fe"], ["Fe"], bias=1.0)
                P.c("dve", lambda h: h.tensor_tensor_scan(out=Ff[0:76, :], data0=ones_f[0:76, :], data1=Fe[0:76, :],
                                                          initial=Fcar[0:76, :], op0=ALU.mult, op1=ALU.subtract),
                    ["ones_f", "Fe", "Fcar"], ["Ff"])
                cp("dve", Fcar[0:76, :], Ff[0:76, TT - 1:TT], ["Ff"], ["Fcar"])
                cp("dve", HIb[0:76, :], Ff[0:76, :], ["Ff"], ["HIb"])
                stt(R1[0:76, :], HIb[0:76, :], nm1[0:76, :], Ff[0:76, :], ALU.mult, ALU.add, ["HIb", "nm1", "Ff"], ["R1"])
                tt("dve", MIDb[0:76, :], Ff[0:76, :], HIb[0:76, :], ALU.subtract, ["Ff", "HIb"], ["MIDb"])
                stt(R1[0:76, :], MIDb[0:76, :], nm2[0:76, :], R1[0:76, :], ALU.mult, ALU.add, ["MIDb", "nm2", "R1"], ["R1"])
                cp("dve", sptn[0:76, :], R1[0:76, :], ["R1"], [spkn])

            prepA(0)
            prepB(0)
            for it in range(P1T):
                own = it >= OWNX
                ot = it - OWNX
                sp_i = it % 2
                spt, spk = SPt[sp_i], "SPt%d" % sp_i
                hT, hk = hTs[it % 2], "hT%d" % (it % 2)
                if P1S < 5:
                    continue
                wv1, wv1k = wload(WB[:, :, C_V:C_V + 512], 8, 512)
                wv2, wv2k = wload(WB[:, :, C_V + 512:C_V + 768], 8, 256)
                for a in range(4):
                    p1, p1k = nextpj()
                    p2, p2k = nextpj()
                    mmg([dict(out=p1[:, :], lhsT=hT[:, kt, a * 128:(a + 1) * 128], rhs=wv1[:, kt, :], start=(kt == 0), stop=(kt == 7)) for kt in range(8)],
                        [hk, wv1k, p1k], [p1k])
                    mmg([dict(out=p2[:, 0:256], lhsT=hT[:, kt, a * 128:(a + 1) * 128], rhs=wv2[:, kt, :], start=(kt == 0), stop=(kt == 7)) for kt in range(8)],
                        [hk, wv2k, p2k], [p2k])
                    cp("act", Vst[:, a, 0:8, 0:64], p1[:, :].rearrange("p (h d) -> p h d", h=8), [p1k], ["Vst"])
                    cp("dve", Vst[:, a, 8:12, 0:64], p2[:, 0:256].rearrange("p (h d) -> p h d", h=4), [p2k], ["Vst"])
                ld(VS[4 * it:4 * it + 4].rearrange("a p f -> p a f"), Vst[:].rearrange("p a h d -> p a (h d)"), ["Vst"], [U()], eng="act")
                if P1S < 6:
                    continue
                for o in range(6):
                    if o % 4 == 0:
                        n = min(512, 768 - o * 128)
                        wk, wkk = wload(WB[:, :, C_U + o * 128:C_U + o * 128 + n], 8, n)
                    pjt, pk = nextpj()
                    c0 = (o % 4) * 128
                    mmg([dict(out=pjt[:, :], lhsT=wk[:, kt, c0:c0 + 128], rhs=hT[:, kt, :], start=(kt == 0), stop=(kt == 7)) for kt in range(8)],
                        [wkk, hk, pk], [pk])
                    cp("act" if o % 2 == 0 else "dve", uT[:, o, :].rearrange("p (s c) -> p c s", s=8), pjt[:, :].rearrange("p (c s) -> p c s", s=8), [pk], ["uT"])
                if P1S < 3:
                    continue
                wk, wkk = None, None
                for h_ in range(12):
                    if h_ % 8 == 0:
                        n = min(512, 768 - h_ * 64)
                        wk, wkk = wload(WB[:, :, C_K + h_ * 64:C_K + h_ * 64 + n], 8, n)
                    pjt, pk = nextpj()
                    c0 = (h_ % 8) * 64
                    calls = [dict(out=pjt[64:128, :], lhsT=SELK[:, h_, 64:128], rhs=spt[:, :], start=True, stop=True)]
                    calls += [dict(out=pjt[0:64, :], lhsT=wk[:, kt, c0:c0 + 64], rhs=hT[:, kt, :], start=(kt == 0), stop=(kt == 7)) for kt in range(8)]
                    mmg(calls, ["SELK", spk, wkk, hk, pk], [pk])
                    ks, kk = kst[h_ % 6], "kst%d" % (h_ % 6)
                    if h_ % 2 == 0:
                        cp("act", ks[:], pjt[:, :], [pk], [kk])
                        ld(KT[h_, :, it * TT:(it + 1) * TT], ks[:], [kk], [U()], eng="act")
                    else:
                        cp("dve", ks[:], pjt[:, :], [pk], [kk])
                        ld(KT[h_, :, it * TT:(it + 1) * TT], ks[:], [kk], [U()], eng="act")
                if P1S < 4:
                    continue
                if own:
                    for h_ in range(12):
                        if h_ % 8 == 0:
                            n = min(512, 768 - h_ * 64)
                            wk, wkk = wload(WB[:, :, C_Q + h_ * 64:C_Q + h_ * 64 + n], 8, n)
                        pjt, pk = nextpj()
                        c0 = (h_ % 8) * 64
                        calls = [dict(out=pjt[64:128, :], lhsT=SELQ[:, h_, 64:128], rhs=spt[:, :], start=True, stop=True)]
                        calls += [dict(out=pjt[0:64, :], lhsT=wk[:, kt, c0:c0 + 64], rhs=hT[:, kt, :], start=(kt == 0), stop=(kt == 7)) for kt in range(8)]
                        mmg(calls, ["SELQ", spk, wkk, hk, pk], [pk])
                        ks, kk = kst[h_ % 6], "kst%d" % (h_ % 6)
                        act(ks[:], pjt[:, :], AF.Identity, [pk, "qscale"], [kk], scale=qscale[:, :])
                        ld(QT[h_, :, ot * TT:(ot + 1) * TT], ks[:], [kk], [U()], eng="act")
                if P1S < 7:
                    continue
                s5slots = []
                for o in range(6):
                    bd_t, wz_t, wy_t = s5w[o % 2]
                    sk = "s5w%d" % (o % 2)
                    ld(wz_t[:], WZS[o], [], [sk + "z"])
                    calls = []
                    for ri in range(2):
                        for s in range(8):
                            for q4 in range(4):
                                calls.append(dict(out=zpsb[q4][:, ri * 64:(ri + 1) * 64], lhsT=wz_t[32 * q4:32 * q4 + 32, s, ri, :],
                                                  rhs=uT[32 * q4:32 * q4 + 32, o, s * 64:(s + 1) * 64],
                                                  start=(s == 0), stop=(s == 7), tile_position=(32 * q4, 0)))
                    mmg(calls, [sk + "z", "uT"] + ["zps%d" % q for q in range(4)], ["zps%d" % q for q in range(4)])
                    for q4 in range(4):
                        cp("act" if q4 % 2 == 0 else "dve", Z[:, 4 * o + q4, :, :], zpsb[q4][:, 0:128].rearrange("p (i c) -> p i c", i=2), ["zps%d" % q4], ["Zj%d" % j for j in range(8)])
                if it + 1 < P1T:
                    prepA(it + 1)
                    prepB(it + 1)
                if P1S < 8:
                    continue
                if it > 0:
                    cp(os.environ.get('RENG', 'pool'), TB[:, :, :, 0], TB[:, :, :, 8], ["TB"], ["TB"])
                Zv = Z[:].rearrange("p r i (b j) -> p r i b j", j=8)

                RENG = os.environ.get('RENG', 'pool')
                cmn = [0]

                def cmac(dre, dim_, sre, sim, mr, mi, Tsets, rkeys, wkey):
                    si_ = cmn[0] % len(Tsets)
                    cmn[0] += 1
                    T = Tsets[si_]
                    tk = ["RT%d_%d" % (si_, q) for q in range(4)]
                    tt(RENG, T[0], mr, sre, ALU.mult, rkeys, [tk[0]])
                    tt(RENG, T[1], mi, sim, ALU.mult, rkeys, [tk[1]])
                    tt(RENG, T[2], mr, sim, ALU.mult, rkeys, [tk[2]])
                    tt(RENG, T[3], mi, sre, ALU.mult, rkeys, [tk[3]])
                    tt(RENG, T[0], T[0], T[1], ALU.subtract, [tk[0], tk[1]], [tk[0]])
                    tt(RENG, T[2], T[2], T[3], ALU.add, [tk[2], tk[3]], [tk[2]])
                    tt(RENG, dre, dre, T[0], ALU.add, [tk[0], wkey], [wkey])
                    tt(RENG, dim_, dim_, T[2], ALU.add, [tk[2], wkey], [wkey])
                RTs = [[t[:] for t in RT[0:4]], [t[:] for t in RT[4:8]]]
                RSs = [[t[:] for t in RS[0:4]], [t[:] for t in RS[4:8]]]
                zkeys = ["Zj%d" % j for j in range(8)]
                a1r = A1R[:].unsqueeze(2).to_broadcast([128, 24, 8])
                a1i = A1I[:].unsqueeze(2).to_broadcast([128, 24, 8])
                for j in range(1, 8):
                    cmac(Zv[:, :, 0, :, j], Zv[:, :, 1, :, j], Zv[:, :, 0, :, j - 1], Zv[:, :, 1, :, j - 1], a1r, a1i, RTs, [zkeys[j - 1]], zkeys[j])
                for b_ in range(8):
                    cp(RENG, TB[:, :, :, b_ + 1], Zv[:, :, :, b_, 7], [zkeys[7], "TB"], ["TB"])
                    cmac(TB[:, :, 0, b_ + 1], TB[:, :, 1, b_ + 1], TB[:, :, 0, b_], TB[:, :, 1, b_], A8R[:], A8I[:], RSs, ["TB"], "TB")
                for j in range(8):
                    cmac(Zv[:, :, 0, :, j], Zv[:, :, 1, :, j], TB[:, :, 0, 0:8], TB[:, :, 1, 0:8],
                         APR[:, j, :].unsqueeze(2).to_broadcast([128, 24, 8]), API[:, j, :].unsqueeze(2).to_broadcast([128, 24, 8]), RTs, ["TB"], zkeys[j])
                if own:
                    cp(RENG, Hb[:, :, :, 0], TB[:, :, :, 0], ["TB"], ["Hb"])
                    cp(RENG, Hb[:, :, :, 1:65], Z[:, :, :, :], zkeys, ["Hb"])
                if not own:
                    continue
                if P1S < 9:
                    continue
                for o in range(6):
                    if o % 4 == 0:
                        n = min(512, 768 - o * 128)
                        wk, wkk = wload(WB[:, :, C_GS + o * 128:C_GS + o * 128 + n], 8, n)
                    pjt, pk = nextpj()
                    c0 = (o % 4) * 128
                    mmg([dict(out=pjt[:, :], lhsT=wk[:, kt, c0:c0 + 128], rhs=hT[:, kt, :], start=(kt == 0), stop=(kt == 7)) for kt in range(8)],
                        [wkk, hk, pk], [pk])
                    act(SGS[:, o, :], pjt[:, :], AF.Silu, [pk], ["SGS"])
                if P1S < 12:
                    continue
                wk, wkk = wload(WB[:, :, C_QM:C_QM + 512], 8, 512)
                for hd in range(4):
                    pjt, pk = nextpj()
                    mmg([dict(out=pjt[:, :], lhsT=wk[:, kt, hd * 128:(hd + 1) * 128], rhs=hT[:, kt, :], start=(kt == 0), stop=(kt == 7)) for kt in range(8)],
                        [wkk, hk, pk], [pk])
                    act(QM[:, hd, :], pjt[:, :], AF.Identity, [pk], ["QM"], scale=128.0 ** -0.5)
                wk, wkk = wload(WB[:, :, C_GM:C_GM + 512], 8, 512)
                for hd in range(4):
                    pjt, pk = nextpj()
                    mmg([dict(out=pjt[:, :], lhsT=wk[:, kt, hd * 128:(hd + 1) * 128], rhs=hT[:, kt, :], start=(kt == 0), stop=(kt == 7)) for kt in range(8)],
                        [wkk, hk, pk], [pk])
                    act(SGM[:, hd, :], pjt[:, :], AF.Silu, [pk], ["SGM"])
                for hd in range(4):
                    for mt in range(2):
                        pjt, pk = nextpj()
                        mmg([dict(out=pjt[:, :], lhsT=MKT[:, hd, mt * 128:(mt + 1) * 128], rhs=QM[:, hd, :], start=True, stop=True)],
                            ["MKT", "QM", pk], [pk])
                        act(PmT[:, mt, :], pjt[:, :], AF.Exp, [pk], ["PmT%d" % mt])
                    pjt, pk = nextpj()
                    mmg([dict(out=pjt[:, :], lhsT=MV[:, mt, hd * 128:(hd + 1) * 128], rhs=PmT[:, mt, :], start=(mt == 0), stop=(mt == 1)) for mt in range(2)],
                        ["MV", "PmT0", "PmT1", pk], [pk])
                    axp, axk = nextpj()
                    mmg([dict(out=axp[:, :], lhsT=ones_b[:, :], rhs=PmT[:, mt, :], start=(mt == 0), stop=(mt == 1)) for mt in range(2)],
                        ["ones_b", "PmT0", "PmT1", axk], [axk])
                    P.c("dve", lambda h, axp=axp: h.reciprocal(out=tC[:], in_=axp[:, :]), [axk], ["tC"])
                    tt("dve", tB[:], tC[:], pjt[:, :], ALU.mult, ["tC", pk], ["tB"])
                    tt("dve", YM[:, hd, :], tB[:], SGM[:, hd, :], ALU.mult, ["tB", "SGM"], ["SGM"])
                if P1S < 14:
                    continue
                for cb in range(8):
                    if cb % 4 == 0:
                        wg0, wg0k = wload(WB[:, :, C_GL + cb * 128:C_GL + cb * 128 + 512], 8, 512)
                    c0 = (cb % 4) * 128
                    pjt, pk = nextpj()
                    mmg([dict(out=pjt[:, :], lhsT=wg0[:, kt, c0:c0 + 128], rhs=hT[:, kt, :], start=(kt == 0), stop=(kt == 7)) for kt in range(8)],
                        [wg0k, hk, pk], [pk])
                    g0, g0k = G0st[cb % 4], "G0st%d" % (cb % 4)
                    act(g0[:], pjt[:, :], AF.Sigmoid, [pk, "bm"], [g0k], bias=bm[:, cb:cb + 1])
                    ld(G0S[ot, :, cb, :], g0[:], [g0k], [U()], eng="act")
                for o in range(6):
                    if o % 4 == 0:
                        n = min(512, 768 - o * 128)
                        wk, wkk = wload(WB[:, :, C_GF + o * 128:C_GF + o * 128 + n], 8, n)
                    pjt, pk = nextpj()
                    c0 = (o % 4) * 128
                    mmg([dict(out=pjt[:, :], lhsT=wk[:, kt, c0:c0 + 128], rhs=hT[:, kt, :], start=(kt == 0), stop=(kt == 7)) for kt in range(8)],
                        [wkk, hk, pk], [pk])
                    ks, kk = kst[o % 6], "kst%d" % (o % 6)
                    act(ks[:], pjt[:, :], AF.Silu, [pk], [kk])
                    ld(SGFS[ot, :, o, :], ks[:], [kk], [U()], eng="act")
                if P1S < 10:
                    continue
                for o in range(6):
                    bd_t, wz_t, wy_t = s5w[o % 2]
                    sk = "s5w%d" % (o % 2)
                    ld(bd_t[:], BDS[o], [], [sk + "b"])
                    ld(wy_t[:], WYS[o], [], [sk + "y"])
                    calls = []
                    for j in range(8):
                        calls.append(dict(out=yps[:, j * 64:512], lhsT=bd_t[:, j, :], rhs=uT[:, o, 0:(8 - j) * 64], start=(j == 0), stop=False))
                    for tau in range(8):
                        for ri in range(2):
                            for q4 in range(4):
                                last = (tau == 7 and ri == 1)
                                calls.append(dict(out=yps[32 * q4:32 * q4 + 32, tau * 64:(tau + 1) * 64], lhsT=wy_t[:, q4, tau, ri, :],
                                                  rhs=Hb[:, 4 * o + q4, ri, 0:64], start=False, stop=last, tile_position=(0, 32 * q4)))
                    mmg(calls, [sk + "b", sk + "y", "uT", "Hb", "yps"], ["yps"])
                    ypv = yps[:, :].rearrange("p (s c) -> p c s", s=8)
                    v3 = lambda ap_: ap_.rearrange("p (c s) -> p c s", s=8)
                    act(v3(tA[:]), ypv, AF.Square, ["yps"], ["tA"])
                    ts("dve", tA[:], tA[:], 0.044715, 1.0, ALU.mult, ALU.add, ["tA"], ["tA"])
                    tt("dve", v3(tB[:]), v3(tA[:]), ypv, ALU.mult, ["tA", "yps"], ["tB"])
                    act(tA[:], tB[:], AF.Sigmoid, ["tB"], ["tA"], scale=1.5957691216057308)
                    tt("dve", v3(YG[:, o, :]), v3(tA[:]), ypv, ALU.mult, ["tA", "yps"], ["YG"])
                if P1S < 11:
                    continue
                for half in range(2):
                    wg, wgk = wload(WGLU[:, :, half * 384:(half + 1) * 384], 6, 384)
                    for c3 in range(3):
                        cb = half * 3 + c3
                        pjt, pk = nextpj()
                        mmg([dict(out=pjt[:, :], lhsT=wg[:, kt, c3 * 128:(c3 + 1) * 128], rhs=YG[:, kt, :], start=(kt == 0), stop=(kt == 5)) for kt in range(6)],
                            [wgk, "YG", pk], [pk])
                        act(tA[:], pjt[:, :], AF.Sigmoid, [pk, "bglu"], ["tA"], bias=bglu[:, cb:cb + 1])
                        tt("dve", tB[:], tA[:], YG[:, cb, :], ALU.mult, ["tA", "YG"], ["tB"])
                        tt("dve", YS[:, cb, :], tB[:], SGS[:, cb, :], ALU.mult, ["tB", "SGS"], ["SGS"])
                if P1S < 13:
                    continue
                for cb in range(8):
                    si = wn[0] % NWS
                    wn[0] += 1
                    wfull = wsl[si][:, 0:26 * 128].rearrange("p (k c) -> p k c", k=26)
                    wpsk = wpmk = wg1k = wg2k = "ws%d" % si
                    wps_t, wpm_t, wg1, wg2 = wfull[:, 0:6, :], wfull[:, 6:10, :], wfull[:, 10:18, :], wfull[:, 18:26, :]
                    ld(wps_t, WPS[:, :, cb * 128:(cb + 1) * 128], [], [wpsk])
                    ld(wpm_t, WPM[:, :, cb * 128:(cb + 1) * 128], [wpsk], [wpsk])
                    ld(wg1, WB[:, :, C_GL + 1024 + cb * 128:C_GL + 1024 + (cb + 1) * 128], [wpsk], [wpsk])
                    ld(wg2, WB[:, :, C_GL + 2048 + cb * 128:C_GL + 2048 + (cb + 1) * 128], [wpsk], [wpsk])
                    c0 = 0
                    m0, m0k = M0[cb % 3], "M0_%d" % (cb % 3)
                    pjt, pk = nextpj()
                    mmg([dict(out=pjt[:, :], lhsT=wg1[:, kt, c0:c0 + 128], rhs=hT[:, kt, :], start=(kt == 0), stop=(kt == 7)) for kt in range(8)],
                        [wg1k, hk, pk], [pk])
                    act(tA[:], pjt[:, :], AF.Sigmoid, [pk, "bm"], ["tA"], bias=bm[:, 8 + cb:9 + cb])
                    pjt, pk = nextpj()
                    mmg([dict(out=pjt[:, :], lhsT=wps_t[:, kt, c0:c0 + 128], rhs=YS[:, kt, :], start=(kt == 0), stop=(kt == 5)) for kt in range(6)],
                        [wpsk, "SGS", pk], [pk])
                    tt("dve", m0[:], tA[:], pjt[:, :], ALU.mult, ["tA", pk], [m0k])
                    pjt, pk = nextpj()
                    mmg([dict(out=pjt[:, :], lhsT=wg2[:, kt, c0:c0 + 128], rhs=hT[:, kt, :], start=(kt == 0), stop=(kt == 7)) for kt in range(8)],
                        [wg2k, hk, pk], [pk])
                    act(tB[:], pjt[:, :], AF.Sigmoid, [pk, "bm"], ["tB"], bias=bm[:, 16 + cb:17 + cb])
                    pjt, pk = nextpj()
                    mmg([dict(out=pjt[:, :], lhsT=wpm_t[:, kt, c0:c0 + 128], rhs=YM[:, kt, :], start=(kt == 0), stop=(kt == 3)) for kt in range(4)],
                        [wpmk, "SGM", pk], [pk])
                    tt("dve", tC[:], tB[:], pjt[:, :], ALU.mult, ["tB", pk], ["tC"])
                    tt("dve", m0[:], m0[:], tC[:], ALU.add, [m0k, "tC"], [m0k])
                    ld(M0S[ot, :, cb, :], m0[:], [m0k], [U()], eng="act")
            P.emit()

        if upto < 3:
            return nc
        S2 = ExitStack()
        with S2:
            Kh = [sb(S2, "Kh%d" % i, [128, LTOK], BF16) for i in range(2)]
            Vh = [sb(S2, "Vh%d" % i, [128, 64, 128], BF16) for i in range(2)]
            Qh = [sb(S2, "Qh%d" % i, [128, 4096], BF16) for i in range(2)]
            NPT = 6
            Pt = [sb(S2, "Pt%d" % i, [128, TT], BF16) for i in range(NPT)]
            Osb = sb(S2, "Osb", [65, TT], F32)
            Rd = sb(S2, "Rd", [64, TT], F32)
            Yst = [sb(S2, "Yst%d" % i, [64, TT], BF16) for i in range(2)]
            SEL = sb(S2, "SEL", [65, 64], F32)
            sps = [ps(S2, "sps%d" % i, [128, 512], F32) for i in range(5)]
            ops_ = [ps(S2, "ops%d" % i, [128, 512], F32) for i in range(2)]
            dps = ps(S2, "dps", [128, 512], F32)
            mset("pool", SEL[:], 0.0, ["SEL"])
            for i in range(2):
                mset("pool", Vh[i][:], 0.0, ["Vh%d" % i])
            mset("pool", SEL[64:65, :], 1.0, ["SEL"])
            LA = 3
            its = []
            for h_ in range(12):
                for qt in range(8):
                    nkb = 32 + 4 * qt + 4
                    order = list(range(32 + 4 * qt, nkb)) + list(range(0, 32 + 4 * qt))
                    for n_, kb in enumerate(order):
                        its.append((h_, qt, kb, n_ == 0, n_ == nkb - 1))

            def head_loads(h_):
                hs = h_ % 2
                kh, vh, qh = Kh[hs], Vh[hs], Qh[hs]
                kk, vk, qk = "Kh%d" % hs, "Vh%d" % hs, "Qh%d" % hs
                ld(qh[:], QT[h_], [], [qk])
                for c4 in range(4):
                    ld(kh[:, c4 * 2048:(c4 + 1) * 2048], KT[h_, :, c4 * 2048:(c4 + 1) * 2048], [qk], [kk])
                for c4 in range(4):
                    ld(vh[:, c4 * 16:(c4 + 1) * 16, 0:65], VS[c4 * 16:(c4 + 1) * 16, :, h_ * 65:(h_ + 1) * 65].rearrange("b p d -> p b d"), [kk], [vk])

            def geom(i):
                h_, qt, kb, first, last = its[i]
                diag = kb - (32 + 4 * qt)
                c0 = max(0, diag) * 128
                return h_, qt, kb, first, last, diag, c0

            def emit_qk(i):
                h_, qt, kb, first, last, diag, c0 = geom(i)
                hs = h_ % 2
                st, sk = sps[i % 5], "sps%d" % (i % 5)
                mmg([dict(out=st[:, c0:TT], lhsT=Kh[hs][:, kb * 128:(kb + 1) * 128], rhs=Qh[hs][:, qt * TT + c0:(qt + 1) * TT], start=True, stop=True)],
                    ["Kh%d" % hs, "Qh%d" % hs, sk], [sk])

            pend = []

            def emit_rest(i):
                h_, qt, kb, first, last, diag, c0 = geom(i)
                hs = h_ % 2
                st, sk = sps[i % 5], "sps%d" % (i % 5)
                pt, ptk = Pt[i % NPT], "Pt%d" % (i % NPT)
                op_t, opk = ops_[(h_ * 8 + qt) % 2], "ops%d" % ((h_ * 8 + qt) % 2)
                act(pt[:, c0:TT], st[:, c0:TT], AF.Exp, [sk], [ptk])
                if diag >= 0:
                    tt("dve", pt[:, c0:c0 + 128], pt[:, c0:c0 + 128], maskT[:], ALU.mult, [ptk, "maskT"], [ptk])
                mmg([dict(out=op_t[:, c0:TT], lhsT=Vh[hs][:, kb, :], rhs=pt[:, c0:TT], start=first, stop=last)],
                    ["Vh%d" % hs, ptk, opk], [opk])
                if last:
                    cp("dve", Osb[:], op_t[0:65, :], [opk], ["Osb"])
                    pend.append((i + 3, h_, qt))

            def emit_norm(h_, qt):
                mmg([dict(out=dps[0:64, :], lhsT=SEL[:, :], rhs=Osb[:, :], start=True, stop=True)], ["SEL", "Osb", "dps"], ["dps"])
                P.c("dve", lambda h: h.reciprocal(out=Rd[:], in_=dps[0:64, :]), ["dps"], ["Rd"])
                ys, ysk = Yst[qt % 2], "Yst%d" % (qt % 2)
                tt("dve", ys[:], Osb[0:64, :], Rd[:], ALU.mult, ["Osb", "Rd"], [ysk])
                ld(YFS[h_, :, qt * TT:(qt + 1) * TT], ys[:], [ysk], [U()], eng="sp")

            head_loads(0)
            head_loads(1)
            NI = len(its)
            for i in range(min(LA, NI)):
                emit_qk(i)
            for i in range(NI):
                if i + LA < NI:
                    emit_qk(i + LA)
                emit_rest(i)
                while pend and pend[0][0] <= i:
                    _, ph, pq = pend.pop(0)
                    emit_norm(ph, pq)
                if its[i][4] and its[i][1] == 7 and its[i][0] + 2 < 12:
                    head_loads(its[i][0] + 2)
            while pend:
                _, ph, pq = pend.pop(0)
                emit_norm(ph, pq)
            P.emit()

        if upto < 4:
            return nc
        S3 = ExitStack()
        with S3:
            gfin = sb(S3, "gfin", [128, D], F32)
            ld(gfin[:], g_final.to_broadcast([128, D]), [], ["gfin"])
            wpf = sb(S3, "wpf", [128, 6, D], BF16)
            wo = sb(S3, "wo", [128, 8, D], BF16)
            yf = sb(S3, "yf", [128, 6, TT], BF16)
            sgf = sb(S3, "sgf", [128, 6, TT], BF16)
            yfg = sb(S3, "yfg", [128, 6, TT], BF16)
            g0t = sb(S3, "g0t", [128, 8, TT], BF16)
            m0t = sb(S3, "m0t", [128, 8, TT], F32)
            mg = sb(S3, "mg", [128, 8, TT], BF16)
            t3 = sb(S3, "t3", [128, TT], F32)
            x3 = sb(S3, "x3", [128, 4, D], F32)
            o3 = [sb(S3, "o3_%d" % i, [128, D], F32) for i in range(2)]
            j3 = sb(S3, "j3", [128, D], BF16)
            pf = [ps(S3, "pf%d" % i, [128, 512], F32) for i in range(2)]
            po = [ps(S3, "po%d" % i, [128, 512], F32) for i in range(4)]
            ld(wpf[:], WPF, [], ["wpf"])
            ld(wo[:], WOUT, [], ["wo"])
            for ot in range(8):
                for a2 in range(2):
                    ld(yf[a2 * 64:(a2 + 1) * 64, :, :], YFS[:, :, ot * TT:(ot + 1) * TT].rearrange("(c a) d t -> a d c t", a=2)[a2], ["yf"], ["yf"])
                ld(sgf[:], SGFS[ot], [], ["sgf"])
                ld(g0t[:], G0S[ot], [], ["g0t"])
                ld(m0t[:], M0S[ot], [], ["m0t"])
                ld(x3[:], xa[(OWN0 + ot) * TT:(OWN0 + ot + 1) * TT, :].rearrange("(a p) d -> p a d", p=128), [], ["x3"])
                for c in range(6):
                    tt("pool" if c % 2 else "dve", yfg[:, c, :], yf[:, c, :], sgf[:, c, :], ALU.mult, ["yf", "sgf"], ["yfg"])
                for cb in range(8):
                    pt_, pk = pf[cb % 2], "pf%d" % (cb % 2)
                    mmg([dict(out=pt_[:, :], lhsT=wpf[:, kt, cb * 128:(cb + 1) * 128], rhs=yfg[:, kt, :], start=(kt == 0), stop=(kt == 5)) for kt in range(6)],
                        ["wpf", "yfg", pk], [pk])
                    tt("dve", t3[:], g0t[:, cb, :], pt_[:, :], ALU.mult, ["g0t", pk], ["t3"])
                    tt("dve", mg[:, cb, :], t3[:], m0t[:, cb, :], ALU.add, ["t3", "m0t"], ["mg"])
                for a in range(4):
                    ob, obk = o3[a % 2], "o3_%d" % (a % 2)
                    for hf in range(2):
                        pt_, pk = po[(2 * a + hf) % 4], "po%d" % ((2 * a + hf) % 4)
                        mmg([dict(out=pt_[:, :], lhsT=mg[:, kt, a * 128:(a + 1) * 128], rhs=wo[:, kt, hf * 512:(hf + 1) * 512], start=(kt == 0), stop=(kt == 7)) for kt in range(8)],
                            ["mg", "wo", pk], [pk])
                        tt("dve", ob[:, hf * 512:(hf + 1) * 512], pt_[:, :], x3[:, a, hf * 512:(hf + 1) * 512], ALU.add, [pk, "x3"], [obk])
                    act(j3[:], ob[:], AF.Square, [obk], ["j3", "ssq"], accum=ssq[:, a:a + 1])
                    ts("dve", rstd[:, a:a + 1], ssq[:, a:a + 1], 1.0 / D, EPS, ALU.mult, ALU.add, ["ssq"], ["rstd"])
                    act(rstd[:, a:a + 1], rstd[:, a:a + 1], AF.Sqrt, ["rstd"], ["rstd"])
                    P.c("dve", lambda h, a=a: h.reciprocal(out=rstd[:, a:a + 1], in_=rstd[:, a:a + 1]), ["rstd"], ["rstd"])
                    stt(ob[:], ob[:], rstd[:, a:a + 1], gfin[:], ALU.mult, ALU.mult, [obk, "rstd", "gfin"], [obk])
                    ld(yout[ot * TT + a * 128:ot * TT + (a + 1) * 128, :], ob[:], [obk], [U()], eng="act")
            P.emit()
    return nc


_NC = None


def kernel(**inputs):
    global _NC
    if _NC is None:
        _NC = build_nc()
    nc = _NC
    x = np.ascontiguousarray(np.asarray(inputs["x"], dtype=np.float32))
    mem = np.asarray(inputs["mem"], dtype=np.float32)
    in_maps = []
    for c in range(8):
        b, half = c // 2, c % 2
        if half == 0:
            xa = np.concatenate([np.zeros((4096, D), np.float32), x[b, 0:4096]], axis=0)
        else:
            xa = x[b]
        km = np.zeros((NT, TT), dtype=ml_dtypes.bfloat16)
        if half == 0:
            km[0:OWN0, :] = -30000.0
        m = {"xa": np.ascontiguousarray(xa), "kmrow": km, "mem": np.ascontiguousarray(mem[b])}
        for name in ("g_norm", "g_mem_norm", "w_in", "b_forget", "b_merge", "w_mem_kv", "lam_re", "lam_im", "log_step",
                     "s5_b_re", "s5_b_im", "s5_c_re", "s5_c_im", "s5_d", "w_glu", "b_glu", "w_proj_fox", "w_proj_s5",
                     "w_proj_mem", "w_out"):
            a = np.asarray(inputs[name], dtype=np.float32)
            a = a[0]
            if a.ndim == 1:
                a = a[None, :]
            m[name] = np.ascontiguousarray(a)
        m["g_final"] = np.ascontiguousarray(np.asarray(inputs["g_final"], dtype=np.float32)[None, :])
        in_maps.append(m)
    res = run_bass_kernel_spmd(nc, in_maps, core_ids=list(range(8)))
    out = np.zeros((4, 8192, D), dtype=np.float32)
    for c in range(8):
        b, half = c // 2, c % 2
        out[b, half * 4096:(half + 1) * 4096] = np.asarray(res.results[c]["yout"], dtype=np.float32)
    return out
```
